# Optimizing a Trainium2 kernel written in Bass

```python
import math
import jax, jax.numpy as jnp
from jax import lax
import numpy as np

D_MODEL = 1024
BATCH = 8
SEQ = 2048
DEPTH = 2
DEC_BATCH = 128
DEC_SEQ = 1
PAST_LEN = 16384
PAGE_SIZE = 128

N_MIXERS = 2
N_RWKV = (DEPTH + 1) // 2
N_RET = DEPTH // 2
RW_HEAD = 64
RW_HEADS = D_MODEL // RW_HEAD
RW_DECAY_LORA = 64
RW_AAA_LORA = 64
RW_GATE_LORA = 128
RW_GN_EPS = 64e-5
RET_HEADS = 4
RET_DK = D_MODEL // RET_HEADS
RET_DV = 2 * RET_DK
RET_CHUNK = 128
RET_GN_EPS = 1e-6
ROPE_BASE = 10000.0
D_FF = 2816
FFN_RES = 0.5
RMS_EPS = 1e-6

kernel_name = 'rwkv7_retnet_macaron_hybrid_step'

RWKV_KEYS = ('rw_mu', 'rw_w0', 'rw_w1', 'rw_w2', 'rw_a0', 'rw_a1', 'rw_a2', 'rw_g1', 'rw_g2',
             'rw_kk', 'rw_ka', 'rw_rk', 'rw_wr', 'rw_wk', 'rw_wv', 'rw_wo', 'rw_lnx_g', 'rw_lnx_b')
RET_KEYS = ('ret_wq', 'ret_wk', 'ret_wv', 'ret_wg', 'ret_wo')


def _rmsnorm(x, g):
    xf = x.astype(jnp.float32)
    y = xf * lax.rsqrt(jnp.mean(xf * xf, axis=-1, keepdims=True) + RMS_EPS)
    return (y * g.astype(jnp.float32)).astype(x.dtype)


def _swiglu(x, wg, wu, wd):
    return (jax.nn.silu(x @ wg) * (x @ wu)) @ wd


def _head_norm(y, eps):
    mu = jnp.mean(y, axis=-1, keepdims=True)
    yc = y - mu
    return yc * lax.rsqrt(jnp.mean(yc * yc, axis=-1, keepdims=True) + eps)


def _rwkv7_mix(x, shift_prev, wkv0, mu, w0, w1, w2, a0, a1, a2, g1, g2, k_k, k_a, r_k,
               wr, wk, wv, wo, lnx_g, lnx_b):
    B, L, D = x.shape
    H, N = RW_HEADS, RW_HEAD
    f32 = jnp.float32
    x_prev = jnp.concatenate([shift_prev[:, None, :].astype(x.dtype), x[:, :-1]], axis=1)
    xx = x_prev - x
    xr, xw, xk, xv, xa, xg = (x + xx * mu[i] for i in range(6))
    r = xr @ wr
    k = xk @ wk
    v = xv @ wv
    w_log = -jax.nn.softplus(-(w0 + jnp.tanh(xw @ w1) @ w2)) - 0.5
    a = jax.nn.sigmoid(a0 + (xa @ a1) @ a2)
    g = jax.nn.sigmoid(xg @ g1) @ g2
    hd = lambda t: t.astype(f32).reshape(B, L, H, N)
    kk = hd(k * k_k)
    kk = kk / jnp.maximum(jnp.sqrt(jnp.sum(kk * kk, axis=-1, keepdims=True)), 1e-12)
    k = k * (1 + (a - 1) * k_a)
    r_h, k_h, v_h, a_h = hd(r), hd(k), hd(v), hd(a)
    decay = jnp.exp(-jnp.exp(hd(w_log)))

    def step(S, inp):
        r_t, d_t, k_t, v_t, kk_t, a_t = inp
        sa = jnp.einsum('bhvk,bhk->bhv', S, -kk_t)
        S = (S * d_t[:, :, None, :] + sa[..., None] * (kk_t * a_t)[:, :, None, :]
             + v_t[..., None] * k_t[:, :, None, :])
        return S, jnp.einsum('bhvk,bhk->bhv', S, r_t)

    tm = lambda t: jnp.swapaxes(t, 0, 1)
    S, y = lax.scan(step, wkv0.astype(f32),
                    (tm(r_h), tm(decay), tm(k_h), tm(v_h), tm(kk), tm(a_h)))
    y = tm(y)
    y = (_head_norm(y, RW_GN_EPS) * lnx_g.astype(f32).reshape(H, N)
         + lnx_b.astype(f32).reshape(H, N))
    y = y + jnp.sum(r_h * k_h * r_k.astype(f32), axis=-1, keepdims=True) * v_h
    out = (y.reshape(B, L, D).astype(x.dtype) * g) @ wo
    return out, x[:, -1], S


def _rotate(t, pos):
    half = t.shape[-1] // 2
    inv = ROPE_BASE ** (-jnp.arange(half, dtype=jnp.float32) / half)
    ang = pos.astype(jnp.float32)[:, None] * inv[None, :]
    cos = jnp.cos(ang)[None, :, None, :]
    sin = jnp.sin(ang)[None, :, None, :]
    t1, t2 = t[..., :half], t[..., half:]
    return jnp.concatenate([t1 * cos - t2 * sin, t1 * sin + t2 * cos], axis=-1)


def _retention_chunkwise(q, k, v, S0, chunk):
    B, L, H, dk = q.shape
    dv = v.shape[-1]
    nc = L // chunk
    log_g = jnp.log1p(-jnp.exp2(-5.0 - jnp.arange(H, dtype=jnp.float32)))
    idx = jnp.arange(chunk, dtype=jnp.float32)
    diff = idx[:, None] - idx[None, :]
    causal = diff >= 0
    dmat = jnp.where(causal[None], jnp.exp(log_g[:, None, None] * jnp.maximum(diff, 0.0)[None]), 0.0)
    q_dec = jnp.exp(log_g[None, :] * (idx[:, None] + 1.0))[None, :, :, None]
    k_dec = jnp.exp(log_g[None, :] * (chunk - 1.0 - idx[:, None]))[None, :, :, None]
    s_dec = jnp.exp(log_g * chunk)[None, :, None, None]
    cm = lambda t: jnp.swapaxes(t.reshape(B, nc, chunk, H, t.shape[-1]), 0, 1)

    def step(S, inp):
        qc, kc, vc = inp
        inner = jnp.einsum('bqhd,bkhd->bhqk', qc, kc) * dmat[None]
        o = (jnp.einsum('bhqk,bkhe->bqhe', inner, vc)
             + jnp.einsum('bqhd,bhde->bqhe', qc * q_dec, S))
        S = S * s_dec + jnp.einsum('bkhd,bkhe->bhde', kc * k_dec, vc)
        return S, o

    S, o = lax.scan(step, S0, (cm(q), cm(k), cm(v)))
    return jnp.swapaxes(o, 0, 1).reshape(B, L, H, dv), S


def _retnet_mix(x, S0, pos, wq, wk, wv, wg, wo):
    B, L, D = x.shape
    H = RET_HEADS
    f32 = jnp.float32
    q = (x @ wq).astype(f32).reshape(B, L, H, RET_DK)
    k = (x @ wk).astype(f32).reshape(B, L, H, RET_DK) * (RET_DK ** -0.5)
    v = (x @ wv).astype(f32).reshape(B, L, H, RET_DV)
    q, k = _rotate(q, pos), _rotate(k, pos)
    chunk = RET_CHUNK if L % RET_CHUNK == 0 else L
    o, S = _retention_chunkwise(q, k, v, S0.astype(f32), chunk)
    o = _head_norm(o, RET_GN_EPS).reshape(B, L, H * RET_DV).astype(x.dtype)
    out = (jax.nn.silu(x @ wg) * o) @ wo
    return out, S


def _trunk(x, pos, shift0, wkv0, ret0, w):
    new_shift, new_wkv, new_ret = [], [], []
    for i in range(DEPTH):
        h = _rmsnorm(x, w['ln_ffn1'][i])
        x = x + FFN_RES * _swiglu(h, w['ff1_wg'][i], w['ff1_wu'][i], w['ff1_wd'][i])
        h = _rmsnorm(x, w['ln_mix'][i])
        j = i // N_MIXERS
        if i % N_MIXERS == 0:
            out, sh, st = _rwkv7_mix(h, shift0[j], wkv0[j], *(w[n][j] for n in RWKV_KEYS))
            new_shift.append(sh.astype(shift0.dtype))
            new_wkv.append(st.astype(wkv0.dtype))
        else:
            out, st = _retnet_mix(h, ret0[j], pos, *(w[n][j] for n in RET_KEYS))
            new_ret.append(st.astype(ret0.dtype))
        x = x + out
        h = _rmsnorm(x, w['ln_ffn2'][i])
        x = x + FFN_RES * _swiglu(h, w['ff2_wg'][i], w['ff2_wu'][i], w['ff2_wd'][i])
    return _rmsnorm(x, w['ln_final']), jnp.stack(new_shift), jnp.stack(new_wkv), jnp.stack(new_ret)


def setup_inputs(seed: int = 0) -> dict:
    key = jax.random.key(seed)
    ks = iter(jax.random.split(key, 64))
    f32 = jnp.float32
    nrm = lambda shape, s: s * jax.random.normal(next(ks), shape, f32)
    uni = lambda shape, lo, hi: jax.random.uniform(next(ks), shape, f32, lo, hi)
    D, H, N = D_MODEL, RW_HEADS, RW_HEAD
    R = N_RWKV
    return {
        'x_prompt': nrm((BATCH, SEQ, D), 1.0),
        'x_sample': nrm((DEC_BATCH, DEC_SEQ, D), 1.0),
        'state_rwkv_shift': nrm((R, DEC_BATCH, D), 1.0),
        'state_rwkv_wkv': nrm((R, DEC_BATCH, H, N, N), 0.1),
        'state_ret': nrm((N_RET, DEC_BATCH, RET_HEADS, RET_DK, RET_DV), 0.3),
        'ln_ffn1': 1.0 + nrm((DEPTH, D), 0.05),
        'ff1_wg': nrm((DEPTH, D, D_FF), D ** -0.5),
        'ff1_wu': nrm((DEPTH, D, D_FF), D ** -0.5),
        'ff1_wd': nrm((DEPTH, D_FF, D), D_FF ** -0.5),
        'ln_mix': 1.0 + nrm((DEPTH, D), 0.05),
        'ln_ffn2': 1.0 + nrm((DEPTH, D), 0.05),
        'ff2_wg': nrm((DEPTH, D, D_FF), D ** -0.5),
        'ff2_wu': nrm((DEPTH, D, D_FF), D ** -0.5),
        'ff2_wd': nrm((DEPTH, D_FF, D), D_FF ** -0.5),
        'ln_final': 1.0 + nrm((D,), 0.05),
        'rw_mu': uni((R, 6, D), 0.0, 1.0),
        'rw_w0': uni((R, D), -6.0, -1.0),
        'rw_w1': nrm((R, D, RW_DECAY_LORA), D ** -0.5),
        'rw_w2': nrm((R, RW_DECAY_LORA, D), 0.1 * RW_DECAY_LORA ** -0.5),
        'rw_a0': nrm((R, D), 0.1),
        'rw_a1': nrm((R, D, RW_AAA_LORA), D ** -0.5),
        'rw_a2': nrm((R, RW_AAA_LORA, D), 0.1 * RW_AAA_LORA ** -0.5),
        'rw_g1': nrm((R, D, RW_GATE_LORA), D ** -0.5),
        'rw_g2': nrm((R, RW_GATE_LORA, D), RW_GATE_LORA ** -0.5),
        'rw_kk': 0.85 + nrm((R, D), 0.05),
        'rw_ka': 1.0 + nrm((R, D), 0.05),
        'rw_rk': nrm((R, H, N), 0.1),
        'rw_wr': nrm((R, D, D), D ** -0.5),
        'rw_wk': nrm((R, D, D), D ** -0.5),
        'rw_wv': nrm((R, D, D), D ** -0.5),
        'rw_wo': nrm((R, D, D), D ** -0.5),
        'rw_lnx_g': 1.0 + nrm((R, D), 0.05),
        'rw_lnx_b': nrm((R, D), 0.01),
        'ret_wq': nrm((N_RET, D, RET_HEADS * RET_DK), D ** -0.5),
        'ret_wk': nrm((N_RET, D, RET_HEADS * RET_DK), D ** -0.5),
        'ret_wv': nrm((N_RET, D, RET_HEADS * RET_DV), D ** -0.5),
        'ret_wg': nrm((N_RET, D, RET_HEADS * RET_DV), D ** -0.5),
        'ret_wo': nrm((N_RET, RET_HEADS * RET_DV, D), (RET_HEADS * RET_DV) ** -0.5),
    }


def reference(x_prompt, x_sample, state_rwkv_shift, state_rwkv_wkv, state_ret,
              ln_ffn1, ff1_wg, ff1_wu, ff1_wd, ln_mix, ln_ffn2, ff2_wg, ff2_wu, ff2_wd, ln_final,
              rw_mu, rw_w0, rw_w1, rw_w2, rw_a0, rw_a1, rw_a2, rw_g1, rw_g2, rw_kk, rw_ka, rw_rk,
              rw_wr, rw_wk, rw_wv, rw_wo, rw_lnx_g, rw_lnx_b,
              ret_wq, ret_wk, ret_wv, ret_wg, ret_wo):
    w = dict(ln_ffn1=ln_ffn1, ff1_wg=ff1_wg, ff1_wu=ff1_wu, ff1_wd=ff1_wd, ln_mix=ln_mix,
             ln_ffn2=ln_ffn2, ff2_wg=ff2_wg, ff2_wu=ff2_wu, ff2_wd=ff2_wd, ln_final=ln_final,
             rw_mu=rw_mu, rw_w0=rw_w0, rw_w1=rw_w1, rw_w2=rw_w2, rw_a0=rw_a0, rw_a1=rw_a1,
             rw_a2=rw_a2, rw_g1=rw_g1, rw_g2=rw_g2, rw_kk=rw_kk, rw_ka=rw_ka, rw_rk=rw_rk,
             rw_wr=rw_wr, rw_wk=rw_wk, rw_wv=rw_wv, rw_wo=rw_wo, rw_lnx_g=rw_lnx_g,
             rw_lnx_b=rw_lnx_b, ret_wq=ret_wq, ret_wk=ret_wk, ret_wv=ret_wv, ret_wg=ret_wg,
             ret_wo=ret_wo)
    B, L, D = x_prompt.shape
    pos_p = jnp.arange(L, dtype=jnp.float32)
    shift_p0 = jnp.zeros((N_RWKV, B, D), state_rwkv_shift.dtype)
    wkv_p0 = jnp.zeros((N_RWKV, B, RW_HEADS, RW_HEAD, RW_HEAD), state_rwkv_wkv.dtype)
    ret_p0 = jnp.zeros((N_RET, B, RET_HEADS, RET_DK, RET_DV), state_ret.dtype)
    y_prompt, p_shift, p_wkv, p_ret = _trunk(x_prompt, pos_p, shift_p0, wkv_p0, ret_p0, w)
    pos_s = PAST_LEN + jnp.arange(x_sample.shape[1], dtype=jnp.float32)
    y_sample, s_shift, s_wkv, s_ret = _trunk(x_sample, pos_s, state_rwkv_shift, state_rwkv_wkv,
                                             state_ret, w)
    return (y_prompt, y_sample, p_shift, p_wkv, p_ret, s_shift, s_wkv, s_ret)
```

```python
import contextlib
import numpy as np
import ml_dtypes
import concourse.bass as bass
import concourse.mybir as mybir
from concourse.bass_utils import run_bass_kernel_spmd

F32 = mybir.dt.float32
BF16 = mybir.dt.bfloat16
AF = mybir.ActivationFunctionType
ALU = mybir.AluOpType
AX = mybir.AxisListType

P = 128
D = 1024
DFF = 2816
NFC = DFF // P
KC = D // P
TP = 2048
NS = 16
T = TP + NS
TILES = [(0, 512), (512, 512), (1024, 512), (1536, 512), (2048, 16)]
NCH = TP // P
RMS_EPS = 1e-6
N_CORES = 8


class Reg:
    __slots__ = ("name", "last_w", "readers")

    def __init__(self, name):
        self.name = name
        self.last_w = None
        self.readers = []


class KB:
    def __init__(self, nc, es, n_dma_sems=48):
        self.nc = nc
        self.es = es
        self.eng = {"pe": nc.tensor, "act": nc.scalar, "dve": nc.vector,
                    "pool": nc.gpsimd, "sp": nc.sync}
        self.sem = {e: es.enter_context(nc.semaphore("c_" + e)) for e in self.eng}
        self.seq = {e: 0 for e in self.eng}
        self.known = {e: {} for e in self.eng}
        self.dma_sems = [es.enter_context(nc.semaphore("d%d" % i)) for i in range(n_dma_sems)]
        self.dma_cnt = [0] * n_dma_sems
        self.dma_pool = {"sw": list(range(0, n_dma_sems // 2)), "hw": list(range(n_dma_sems // 2, n_dma_sems))}
        self.dma_rr = {"sw": 0, "hw": 0}
        self.pe_pending = False
        self.n_wait = 0
        self.n_inst = 0
        self.same_engine_sync = True
        self.RAW_GAP = 10 ** 9

    def _wait(self, e, tok):
        if tok is None:
            return
        kind, who, val = tok
        key = (kind, who)
        if kind == "eng":
            if who == e and (e == "pe" or not self.same_engine_sync):
                return
            if who == e and e == "sp":
                return
        if self.known[e].get(key, 0) >= val:
            return
        self.known[e][key] = val
        sem = self.sem[who] if kind == "eng" else self.dma_sems[who]
        self.eng[e].wait_ge(sem, val)
        self.n_wait += 1

    def _deps(self, e, R, W):
        toks = []
        for r in R:
            toks.append(r.last_w)
        for w in W:
            toks.append(w.last_w)
            toks.extend(w.readers)
        seen = set()
        for t in toks:
            if t is not None and t not in seen:
                seen.add(t)
                self._wait(e, t)

    def _commit(self, tok, R, W):
        for r in R:
            r.readers.append(tok)
            if len(r.readers) > 24:
                best = {}
                for t in r.readers:
                    k = (t[0], t[1])
                    if k not in best or best[k][2] < t[2]:
                        best[k] = t
                r.readers = list(best.values())
        for w in W:
            w.last_w = tok
            w.readers = []

    def op(self, e, fn, R=(), W=(), inc=True):
        self._deps(e, R, W)
        inst = fn()
        self.n_inst += 1
        if inc:
            self.seq[e] += 1
            inst.then_inc(self.sem[e], 1)
            tok = ("eng", e, self.seq[e])
        else:
            assert e == "pe"
            tok = ("eng", e, self.seq[e] + 1)
        self._commit(tok, R, W)
        return tok

    def dma(self, q, out, in_, R=(), W=()):
        self._deps(q, R, W)
        cls = "sw" if q == "pool" else "hw"
        lst = self.dma_pool[cls]
        i = lst[self.dma_rr[cls]]
        self.dma_rr[cls] = (self.dma_rr[cls] + 1) % len(lst)
        if self.dma_cnt[i] > 0:
            self._wait(q, ("dma", i, 16 * self.dma_cnt[i]))
        self.dma_cnt[i] += 1
        self.eng[q].dma_start(out=out, in_=in_).then_inc(self.dma_sems[i], 16)
        self.n_inst += 1
        tok = ("dma", i, 16 * self.dma_cnt[i])
        self._commit(tok, R, W)
        return tok

    def finish(self, regs):
        for r in regs:
            self._wait("sp", r.last_w)


def bcast_mid(ap2, n):
    return ap2.unsqueeze(1).broadcast_to([ap2.shape[0], n, ap2.shape[1]])


def bcast_last(ap2, n):
    return ap2.unsqueeze(2).broadcast_to([ap2.shape[0], ap2.shape[1], n])


ALL_STAGES = ("ffn1_0", "mix_0", "ffn2_0", "ffn1_1", "mix_1", "ffn2_1", "samples")


def build(dbg=None, stages=ALL_STAGES):
    dbg = dbg or {}
    nc = bass.Bass("TRN2", target_bir_lowering=False)
    es = contextlib.ExitStack()
    k = KB(nc, es)

    def din(name, shape, dt=F32):
        return nc.dram_tensor(name, list(shape), dt, kind="ExternalInput").ap()

    def dout(name, shape, dt=F32):
        return nc.dram_tensor(name, list(shape), dt, kind="ExternalOutput").ap()

    xp_d = din("x_prompt", [TP, D])
    xs_d = din("x_sample", [NS, D])
    ln_ffn1_d = din("ln_ffn1", [2, D]); ln_mix_d = din("ln_mix", [2, D]); ln_ffn2_d = din("ln_ffn2", [2, D])
    ln_final_d = din("ln_final", [D])
    ffw = {}
    for nm in ("ff1_wg", "ff1_wu", "ff2_wg", "ff2_wu"):
        ffw[nm] = din(nm, [2, D, DFF])
    for nm in ("ff1_wd", "ff2_wd"):
        ffw[nm] = din(nm, [2, DFF, D])
    ident_d = din("c_ident", [P, P])
    rw = {}
    for nm_, shp_ in [("rw_mu", [6, D]), ("rw_w0", [D]), ("rw_w1", [D, 64]), ("rw_w2", [64, D]), ("rw_a0", [D]),
                      ("rw_a1", [D, 64]), ("rw_a2", [64, D]), ("rw_g1", [D, 128]), ("rw_g2", [128, D]),
                      ("rw_kk", [D]), ("rw_ka", [D]), ("rw_rk", [D]), ("rw_wr", [D, D]), ("rw_wk", [D, D]),
                      ("rw_wv", [D, D]), ("rw_wo", [D, D]), ("rw_lnx_g", [D]), ("rw_lnx_b", [D])]:
        rw[nm_] = din(nm_, shp_)
    rt = {}
    for nm_, shp_ in [("ret_wq", [D, D]), ("ret_wk", [D, D]), ("ret_wv", [D, 2 * D]), ("ret_wg", [D, 2 * D]),
                      ("ret_wo", [2 * D, D])]:
        rt[nm_] = din(nm_, shp_)
    c_cos_d = din("c_cos", [P, T]); c_sin_d = din("c_sin", [P, T])
    c_cosT_d = din("c_cosT", [TP, P]); c_sinT_d = din("c_sinT", [TP, P])
    c_dmatT_d = din("c_dmatT", [P, 4, P]); c_qd_d = din("c_qd", [P, 4, P]); c_kdec_d = din("c_kdec", [P, 4])
    c_sel_d = din("c_sel", [NS, NS, P]); c_i16_d = din("c_i16", [P, NS * NS])
    st_ret_d = din("st_ret", [NS, 4, 256, 512])
    pret_d = dout("p_ret", [4, 256, 512]); sret_d = dout("s_ret", [NS, 4, 256, 512])
    c_bones_d = din("c_bones", [P, P]); c_mask_ur_d = din("c_mask_ur", [P, 256]); c_mask_l_d = din("c_mask_l", [P, P])
    c_scanm_d = din("c_scanm", [P, 256])
    st_shift_d = din("st_shift", [NS, D]); st_wkv_d = din("st_wkv", [NS, 16, 64, 64])
    pshift_d = dout("p_shift", [D]); pwkv_d = dout("p_wkv", [16, 64, 64])
    sshift_d = dout("s_shift", [NS, D]); swkv_d = dout("s_wkv", [NS, 16, 64, 64])
    yp_d = dout("y_prompt", [TP, D])
    ys_d = dout("y_sample", [NS, D])
    out_regs = []

    def sb(name, shape, dt=F32):
        return nc.alloc_sbuf_tensor(name, list(shape), dt)

    AR_BYTES = 100 * 1024
    arena = sb("arena", [P, AR_BYTES // 2], BF16)
    carry = [[]]

    class Phase:
        def __init__(self, base=0, limit=None, carry_ref=None):
            self.off = base
            self.limit = AR_BYTES if limit is None else limit
            self.regs = []
            self.carry = carry if carry_ref is None else carry_ref

        def alloc(self, shape, dt=F32):
            esz = 4 if dt == F32 else 2
            n = 1
            for s_ in shape[1:]:
                n *= s_
            nb = (n * esz + 63) // 64 * 64
            assert self.off + nb <= self.limit, ("arena overflow", self.off, nb, self.limit)
            v = arena[0:shape[0], self.off // 2:(self.off + n * esz) // 2]
            if dt == F32:
                v = v.bitcast(F32)
            self.off += nb
            if len(shape) == 3:
                v = v.rearrange("p (a b) -> p a b", a=shape[1])
            elif len(shape) == 4:
                v = v.rearrange("p (a b c) -> p a b c", a=shape[1], b=shape[2])
            return v

        def reg(self, name):
            r = Reg(name)
            r.readers = list(self.carry[0])
            self.regs.append(r)
            return r

        def end(self):
            toks = set(self.carry[0])
            for r in self.regs:
                if r.last_w is not None:
                    toks.add(r.last_w)
                toks.update(r.readers)
            best = {}
            for t_ in toks:
                kk_ = (t_[0], t_[1])
                if kk_ not in best or best[kk_][2] < t_[2]:
                    best[kk_] = t_
            self.carry[0] = list(best.values())

    NWB = 5
    WB_ELEMS = 4096
    wbuf = [sb("wbuf%d" % i, [P, WB_ELEMS], BF16) for i in range(NWB)]
    r_wbuf = [Reg("wbuf%d" % i) for i in range(NWB)]
    wb_rr = [0]

    def next_wbuf():
        i = wb_rr[0]
        wb_rr[0] = (i + 1) % NWB
        return wbuf[i], r_wbuf[i]

    ident = sb("ident", [P, P]); r_ident = Reg("ident")
    k.dma("sp", ident[:], ident_d, W=[r_ident])
    identb = sb("identb", [P, P], BF16); r_identb = Reg("identb")
    k.op("dve", lambda: nc.vector.tensor_copy(out=identb[:], in_=ident[:]), R=[r_ident], W=[r_identb])
    ones_b = sb("ones_b", [P, P], BF16); r_ones = Reg("ones")
    k.op("dve", lambda: nc.vector.memset(ones_b[:], 1.0), W=[r_ones])
    eps_t = sb("eps_t", [P, 1]); r_eps = Reg("eps")
    k.op("dve", lambda: nc.vector.memset(eps_t[:], RMS_EPS), W=[r_eps])
    lnw = sb("lnw", [P, 7, KC]); r_lnw = Reg("lnw")
    with nc.allow_non_contiguous_dma(reason="tiny gain vectors"):
        for i, (src_, l) in enumerate([(ln_ffn1_d, 0), (ln_mix_d, 0), (ln_ffn2_d, 0),
                                       (ln_ffn1_d, 1), (ln_mix_d, 1), (ln_ffn2_d, 1)]):
            k.dma("sp", lnw[:, i, :], src_[l].rearrange("(k p) -> p k", p=P), W=[r_lnw])
        k.dma("sp", lnw[:, 6, :], ln_final_d.rearrange("(k p) -> p k", p=P), W=[r_lnw])

    NB = 8
    banks = [nc.alloc_psum_tensor("bank%d" % i, [P, 512], F32) for i in range(NB)]
    r_bank = [Reg("bank%d" % i) for i in range(NB)]
    bank_rr = [0]

    nonlocal_nb = [NB]

    def next_bank():
        i = bank_rr[0] % nonlocal_nb[0]
        bank_rr[0] = (i + 1) % nonlocal_nb[0]
        return banks[i], r_bank[i]

    x = sb("x", [P, KC, T]); r_x = [[Reg("x%d_%d" % (d, t)) for t in range(5)] for d in range(KC)]

    alt = [0]

    def evac_copy(dst_ap, src_ap, R, W, e=None):
        if e is None:
            e = "act" if alt[0] % 2 == 0 else "dve"
            alt[0] += 1
        if e == "act":
            k.op("act", lambda: nc.scalar.copy(out=dst_ap, in_=src_ap), R=R, W=W)
        else:
            k.op("dve", lambda: nc.vector.tensor_copy(out=dst_ap, in_=src_ap), R=R, W=W)

    def load_x():
        ph = Phase()
        xin = [ph.alloc([P, D]) for _ in range(2)]
        r_xin = [ph.reg("xin0"), ph.reg("xin1")]
        ld = 0
        for ti, (t0, tn) in enumerate(TILES):
            nsub = (tn + P - 1) // P
            for s in range(nsub):
                rows = min(P, tn - s * P)
                b = ld % 2; ld += 1
                src_ = xp_d[t0 + s * P: t0 + s * P + rows, :] if ti < 4 else xs_d[:, :]
                k.dma("sp", xin[b][0:rows, :], src_, W=[r_xin[b]])
                for d0 in range(0, KC, 4):
                    bk, rb = next_bank()
                    for j in range(4):
                        d = d0 + j
                        k.op("pe", lambda d=d, j=j: nc.tensor.transpose(
                            bk[:, j * P: j * P + rows], xin[b][0:rows, d * P:(d + 1) * P], ident[0:rows, 0:rows]),
                            R=[r_xin[b], r_ident], W=[rb])
                    c0 = t0 + s * P
                    src_ap = bk[:, :].rearrange("p (j c) -> p j c", j=4)[:, :, 0:rows]
                    evac_copy(x[:, d0:d0 + 4, c0:c0 + rows], src_ap, R=[rb],
                              W=[r_x[d][ti] for d in range(d0, d0 + 4)])
        ph.end()

    def rmsnorm_tile(ph_bufs, gi, ti, out_ap_fn, W_fn, rng=None, extra=None):
        sqs, r_sqs, rstd, r_rstd, cnt = ph_bufs
        t0, tn = TILES[ti] if rng is None else rng
        bk, rb = next_bank()
        for d in range(KC):
            b = cnt[0] % len(sqs); cnt[0] += 1
            k.op("act", lambda: nc.scalar.activation(
                out=sqs[b][:, 0:tn], in_=x[:, d, t0:t0 + tn], func=AF.Square),
                R=[r_x[d][ti]], W=[r_sqs[b]])
            k.op("pe", lambda: nc.tensor.matmul(
                bk[:, 0:tn], lhsT=ones_b[:], rhs=sqs[b][:, 0:tn], start=(d == 0), stop=(d == KC - 1)),
                R=[r_sqs[b], r_ones], W=[rb], inc=True)
        rb_i = cnt[1] % 2; cnt[1] += 1
        k.op("act", lambda: nc.scalar.activation(
            out=rstd[rb_i][:, 0:tn], in_=bk[:, 0:tn], func=AF.Ln, scale=1.0 / D, bias=eps_t[:, 0:1]),
            R=[rb, r_eps], W=[r_rstd[rb_i]])
        k.op("act", lambda: nc.scalar.activation(
            out=rstd[rb_i][:, 0:tn], in_=rstd[rb_i][:, 0:tn], func=AF.Exp, scale=-0.5),
            R=[r_rstd[rb_i]], W=[r_rstd[rb_i]])
        for d in range(KC):
            k.op("dve", lambda: nc.vector.scalar_tensor_tensor(
                out=out_ap_fn(d, t0, tn), in0=x[:, d, t0:t0 + tn], scalar=lnw[:, gi, d:d + 1],
                in1=rstd[rb_i][:, 0:tn], op0=ALU.mult, op1=ALU.mult),
                R=[r_x[d][ti], r_rstd[rb_i], r_lnw], W=W_fn(d))
            if extra is not None:
                extra(d, rstd[rb_i], r_rstd[rb_i])

    def norm_bufs(ph, width=512):
        sqs = [ph.alloc([P, width], BF16) for _ in range(4)]
        r_sqs = [ph.reg("sq%d" % i) for i in range(4)]
        rstd = [ph.alloc([P, width]) for _ in range(2)]
        r_rstd = [ph.reg("rstd%d" % i) for i in range(2)]
        return (sqs, r_sqs, rstd, r_rstd, [0, 0])

    FB = 4

    def ffn(l, which):
        ph = Phase()
        nb = norm_bufs(ph)
        h = ph.alloc([P, KC, T], BF16)
        r_h = [[ph.reg("h%d_%d" % (d, t)) for t in range(5)] for d in range(KC)]
        a_t = ph.alloc([P, FB, T], BF16)
        r_a = [[ph.reg("a%d_%d" % (f, t)) for t in range(5)] for f in range(FB)]
        sl_t = [ph.alloc([P, 512]) for _ in range(2)]
        r_sl = [ph.reg("sl0"), ph.reg("sl1")]
        gi = {1: 0, 2: 2}[which] + 3 * l
        wg_d = ffw["ff%d_wg" % which][l].rearrange("(k p) c -> p k c", p=P)
        wu_d = ffw["ff%d_wu" % which][l].rearrange("(k p) c -> p k c", p=P)
        wd_d = ffw["ff%d_wd" % which][l].rearrange("(f p) c -> p f c", p=P)
        for ti in range(5):
            rmsnorm_tile(nb, gi, ti, lambda d, t0, tn: h[:, d, t0:t0 + tn], lambda d, ti=ti: [r_h[d][ti]])
        nblk = (NFC + FB - 1) // FB
        sl_i = 0
        for blk in range(nblk):
            f0 = blk * FB
            nf = min(FB, NFC - f0)
            wgb, r_wg = next_wbuf()
            wub, r_wu = next_wbuf()
            wdb, r_wd = next_wbuf()
            wg_v = wgb[:, 0:KC * nf * P].rearrange("p (k c) -> p k c", k=KC)
            wu_v = wub[:, 0:KC * nf * P].rearrange("p (k c) -> p k c", k=KC)
            wd_v = wdb[:, 0:nf * D].rearrange("p (f c) -> p f c", f=nf)
            k.dma("pool", wg_v, wg_d[:, :, f0 * P:(f0 + nf) * P], W=[r_wg])
            k.dma("pool", wu_v, wu_d[:, :, f0 * P:(f0 + nf) * P], W=[r_wu])
            k.dma("pool", wd_v, wd_d[:, f0:f0 + nf, :], W=[r_wd])
            for f in range(nf):
                for ti, (t0, tn) in enumerate(TILES):
                    bg, rbg = next_bank()
                    bu, rbu = next_bank()
                    for kk in range(KC):
                        k.op("pe", lambda: nc.tensor.matmul(
                            bg[:, 0:tn], lhsT=wg_v[:, kk, f * P:(f + 1) * P], rhs=h[:, kk, t0:t0 + tn],
                            start=(kk == 0), stop=(kk == KC - 1)),
                            R=[r_wg, r_h[kk][ti]], W=[rbg], inc=(kk == KC - 1))
                    for kk in range(KC):
                        k.op("pe", lambda: nc.tensor.matmul(
                            bu[:, 0:tn], lhsT=wu_v[:, kk, f * P:(f + 1) * P], rhs=h[:, kk, t0:t0 + tn],
                            start=(kk == 0), stop=(kk == KC - 1)),
                            R=[r_wu, r_h[kk][ti]], W=[rbu], inc=(kk == KC - 1))
                    sb_i = sl_i % 2; sl_i += 1
                    k.op("act", lambda: nc.scalar.activation(
                        out=sl_t[sb_i][:, 0:tn], in_=bg[:, 0:tn], func=AF.Silu),
                        R=[rbg], W=[r_sl[sb_i]])
                    k.op("dve", lambda: nc.vector.tensor_tensor(
                        out=a_t[:, f, t0:t0 + tn], in0=bu[:, 0:tn], in1=sl_t[sb_i][:, 0:tn], op=ALU.mult),
                        R=[rbu, r_sl[sb_i]], W=[r_a[f][ti]])
            for d in range(KC):
                for ti, (t0, tn) in enumerate(TILES):
                    bk, rb = next_bank()
                    for f in range(nf):
                        k.op("pe", lambda: nc.tensor.matmul(
                            bk[:, 0:tn], lhsT=wd_v[:, f, d * P:(d + 1) * P], rhs=a_t[:, f, t0:t0 + tn],
                            start=(f == 0), stop=(f == nf - 1)),
                            R=[r_wd, r_a[f][ti]], W=[rb], inc=(f == nf - 1))
                    k.op("dve", lambda: nc.vector.scalar_tensor_tensor(
                        out=x[:, d, t0:t0 + tn], in0=bk[:, 0:tn], scalar=0.5, in1=x[:, d, t0:t0 + tn],
                        op0=ALU.mult, op1=ALU.add),
                        R=[rb, r_x[d][ti]], W=[r_x[d][ti]])
        ph.end()


    def rwkv_mixer(do_samples=True):
        ph = Phase()
        SEG = 256
        C0 = -float(np.exp(-0.5))
        mm = nc.tensor.matmul
        rv = ph.alloc([P, 14, KC]); r_rv = ph.reg("rv")
        with nc.allow_non_contiguous_dma(reason="tiny per-feature vectors"):
            for i in range(6):
                k.dma("sp", rv[:, i, :], rw["rw_mu"][i].rearrange("(k p) -> p k", p=P), W=[r_rv])
            for i, nm in enumerate(["rw_w0", "rw_a0", "rw_kk", "rw_ka", "rw_rk", "rw_lnx_g", "rw_lnx_b"]):
                k.dma("sp", rv[:, 6 + i, :], rw[nm].rearrange("(k p) -> p k", p=P), W=[r_rv])
        k.op("dve", lambda: nc.vector.tensor_scalar(out=rv[:, 13, :], in0=rv[:, 9, :], scalar1=-1.0, scalar2=1.0,
                                                    op0=ALU.mult, op1=ALU.add), R=[r_rv], W=[r_rv])
        w1b = ph.alloc([P, KC, 64], BF16); w2b = ph.alloc([64, D], BF16)
        a1b = ph.alloc([P, KC, 64], BF16); a2b = ph.alloc([64, D], BF16)
        g1b = ph.alloc([P, KC, 128], BF16); g2b = ph.alloc([P, D], BF16)
        r_lw = ph.reg("lora")
        k.dma("pool", w1b, rw["rw_w1"].rearrange("(k p) c -> p k c", p=P), W=[r_lw])
        k.dma("pool", w2b, rw["rw_w2"], W=[r_lw])
        k.dma("pool", a1b, rw["rw_a1"].rearrange("(k p) c -> p k c", p=P), W=[r_lw])
        k.dma("pool", a2b, rw["rw_a2"], W=[r_lw])
        k.dma("pool", g1b, rw["rw_g1"].rearrange("(k p) c -> p k c", p=P), W=[r_lw])
        k.dma("pool", g2b, rw["rw_g2"], W=[r_lw])
        mask_ur = ph.alloc([P, 256], BF16); mask_l = ph.alloc([P, P], BF16)
        bones = ph.alloc([P, P], BF16); bones64 = ph.alloc([P, P], BF16)
        scanm = ph.alloc([P, 256]); scanz = ph.alloc([P, NS]); eps_gn = ph.alloc([P, 1])
        r_cst = ph.reg("rwconst")
        k.dma("pool", mask_ur, c_mask_ur_d, W=[r_cst])
        k.dma("pool", mask_l, c_mask_l_d, W=[r_cst])
        k.dma("pool", bones, c_bones_d, W=[r_cst])
        k.dma("sp", scanm, c_scanm_d, W=[r_cst])
        k.op("dve", lambda: nc.vector.tensor_scalar(out=bones64, in0=bones, scalar1=1.0 / 64, scalar2=None,
                                                    op0=ALU.mult), R=[r_cst], W=[r_cst])
        k.op("dve", lambda: nc.vector.memset(scanz, 0.0), W=[r_cst])
        k.op("dve", lambda: nc.vector.memset(eps_gn, 64e-5), W=[r_cst])
        H32 = ph.alloc([P, KC, 64]); Hbd = ph.alloc([P, KC, P], BF16)
        r_H = [ph.reg("H%d" % g) for g in range(4)]
        k.op("dve", lambda: nc.vector.memset(H32, 0.0), W=r_H)
        k.op("dve", lambda: nc.vector.memset(Hbd, 0.0), W=r_H)
        shf = ph.alloc([P, KC, 1 + NS]); r_shf = ph.reg("shf")
        k.op("dve", lambda: nc.vector.memset(shf, 0.0), W=[r_shf])
        hcar = ph.alloc([P, KC, 1], BF16); r_hcar = ph.reg("hcar")
        k.op("dve", lambda: nc.vector.memset(hcar, 0.0), W=[r_hcar])
        hsp = ph.alloc([P, KC, NS], BF16); r_hsp = ph.reg("hsp")
        if do_samples:
            stg = ph.alloc([NS, D]); r_stg = ph.reg("stg")
            k.dma("sp", stg, st_shift_d, W=[r_stg])
            for half in range(2):
                bk, rb = next_bank()
                for jj in range(4):
                    d = half * 4 + jj
                    k.op("pe", lambda: nc.tensor.transpose(bk[:, jj * NS:(jj + 1) * NS], stg[0:NS, d * P:(d + 1) * P], ident[0:NS, 0:NS]),
                         R=[r_stg, r_ident], W=[rb])
                evac_copy(hsp[:, half * 4:half * 4 + 4, :], bk[:, 0:4 * NS].rearrange("p (a b) -> p a b", a=4), R=[rb], W=[r_hsp])
        ar = ph.alloc([P, KC, 2, SEG], BF16); r_ar = [ph.reg("ar%d" % d) for d in range(KC)]
        bt = ph.alloc([P, KC, SEG], BF16); r_bt = [ph.reg("bt%d" % d) for d in range(KC)]
        kt = ph.alloc([P, KC, SEG], BF16); r_kt = [ph.reg("kt%d" % d) for d in range(KC)]
        vt = ph.alloc([P, KC, SEG], BF16); r_vt = [ph.reg("vt%d" % d) for d in range(KC)]
        gt = ph.alloc([P, KC, SEG], BF16); r_gt = [ph.reg("gt%d" % d) for d in range(KC)]
        bon = ph.alloc([P, KC, SEG], BF16); r_bon = [ph.reg("bon%d" % d) for d in range(KC)]
        og = ph.alloc([P, KC, SEG], BF16); r_og = ph.reg("og")
        gC = ph.alloc([P, KC, 2]); r_gC = ph.reg("gC")
        dS = ph.alloc([P, KC, NS]); r_dS = ph.reg("dS")
        ov_base = ph.off
        ovc = [list(carry[0])]
        ybk = [banks[6], banks[7]]; r_ybk = [r_bank[6], r_bank[7]]
        nonlocal_nb[0] = 6

        def stageA(t0, n, ti, sample, last_prompt):
            pa = Phase(base=ov_base, carry_ref=ovc)
            nbuf = norm_bufs(pa, SEG)
            hn = pa.alloc([P, KC, SEG + 2], BF16); r_hn = [pa.reg("hn%d" % d) for d in range(KC)]
            dx = pa.alloc([P, KC, SEG], BF16); r_dx = pa.reg("dx")
            xi = [pa.alloc([P, KC, SEG], BF16) for _ in range(2)]; r_xi = [pa.reg("xi0"), pa.reg("xi1")]
            lt = [pa.alloc([P, SEG], BF16) for _ in range(3)]; r_lt = [pa.reg("lt%d" % i) for i in range(3)]
            T32 = [[pa.alloc([P, SEG]) for _ in range(4)] for _ in range(4)]
            rT32 = [[pa.reg("t32_%d_%d" % (i, j)) for j in range(4)] for i in range(4)]
            T16 = [[pa.alloc([P, SEG], BF16) for _ in range(4)] for _ in range(3)]
            rT16 = [[pa.reg("t16_%d_%d" % (i, j)) for j in range(4)] for i in range(3)]
            if not sample:
                k.op("dve", lambda: nc.vector.tensor_copy(out=hn[:, :, 0:1], in_=hcar[:, :, 0:1]), R=[r_hcar], W=r_hn)

            def extra(d, rstd_t, r_rstd_t):
                if last_prompt:
                    k.op("dve", lambda: nc.vector.scalar_tensor_tensor(
                        out=shf[:, d, 0:1], in0=x[:, d, t0 + n - 1:t0 + n], scalar=lnw[:, 1, d:d + 1],
                        in1=rstd_t[:, n - 1:n], op0=ALU.mult, op1=ALU.mult),
                        R=[r_x[d][ti], r_rstd_t, r_lnw], W=[r_shf])
                if sample:
                    k.op("dve", lambda: nc.vector.scalar_tensor_tensor(
                        out=shf[:, d, 1:1 + NS], in0=x[:, d, t0:t0 + n], scalar=lnw[:, 1, d:d + 1],
                        in1=rstd_t[:, 0:n], op0=ALU.mult, op1=ALU.mult),
                        R=[r_x[d][ti], r_rstd_t, r_lnw], W=[r_shf])
            rmsnorm_tile(nbuf, 1, ti, lambda d, a_, b_: hn[:, d, 1:1 + n], lambda d: [r_hn[d]], rng=(t0, n), extra=extra)
            hprev = hsp[:, :, 0:n] if sample else hn[:, :, 0:n]
            k.op("dve", lambda: nc.vector.tensor_tensor(out=dx[:, :, 0:n], in0=hprev, in1=hn[:, :, 1:1 + n], op=ALU.subtract),
                 R=r_hn + ([r_hsp] if sample else []), W=[r_dx])
            if not sample:
                k.op("dve", lambda: nc.vector.tensor_copy(out=hcar[:, :, 0:1], in_=hn[:, :, n:n + 1]), R=r_hn, W=[r_hcar])
            xi_i = [0]

            def make_xi(i):
                b = xi_i[0] % 2; xi_i[0] += 1
                for d in range(KC):
                    k.op("dve", lambda: nc.vector.scalar_tensor_tensor(
                        out=xi[b][:, d, 0:n], in0=dx[:, d, 0:n], scalar=rv[:, i, d:d + 1], in1=hn[:, d, 1:1 + n],
                        op0=ALU.mult, op1=ALU.add), R=[r_dx, r_hn[d], r_rv], W=[r_xi[b]])
                return xi[b], r_xi[b]

            def proj_big(wd, xt, r_xt, evac):
                wv_ = wd.rearrange("(k p) c -> p k c", p=P)
                for half in range(2):
                    wb, r_w = next_wbuf()
                    v = wb[:, 0:KC * 512].rearrange("p (k c) -> p k c", k=KC)
                    k.dma("pool", v, wv_[:, :, half * 512:(half + 1) * 512], W=[r_w])
                    for oc in range(4):
                        d = half * 4 + oc
                        bk, rb = next_bank()
                        for kk in range(KC):
                            k.op("pe", lambda: mm(bk[:, 0:n], lhsT=v[:, kk, oc * P:(oc + 1) * P], rhs=xt[:, kk, 0:n],
                                                  start=(kk == 0), stop=(kk == KC - 1)),
                                 R=[r_w, r_xt], W=[rb], inc=(kk == KC - 1))
                        evac(d, bk[:, 0:n], rb)

            def lora_in(wl, m, xt, r_xt, func, li):
                bk, rb = next_bank()
                for kk in range(KC):
                    k.op("pe", lambda: mm(bk[0:m, 0:n], lhsT=wl[:, kk, 0:m], rhs=xt[:, kk, 0:n],
                                          start=(kk == 0), stop=(kk == KC - 1)),
                         R=[r_lw, r_xt], W=[rb], inc=(kk == KC - 1))
                k.op("act", lambda: nc.scalar.activation(out=lt[li][0:m, 0:n], in_=bk[0:m, 0:n], func=func),
                     R=[rb], W=[r_lt[li]])

            def lora_out(wl2, m, li, d):
                bk, rb = next_bank()
                k.op("pe", lambda: mm(bk[:, 0:n], lhsT=wl2[0:m, d * P:(d + 1) * P], rhs=lt[li][0:m, 0:n],
                                      start=True, stop=True), R=[r_lw, r_lt[li]], W=[rb])
                return bk, rb

            xt, r_xt = make_xi(0)
            proj_big(rw["rw_wr"], xt, r_xt, lambda d, ps, rb: k.op(
                "act", lambda: nc.scalar.copy(out=ar[:, d, 1, 0:n], in_=ps), R=[rb], W=[r_ar[d]]))
            xt, r_xt = make_xi(2)
            proj_big(rw["rw_wk"], xt, r_xt, lambda d, ps, rb: evac_copy(kt[:, d, 0:n], ps, R=[rb], W=[r_kt[d]]))
            xt, r_xt = make_xi(4)
            lora_in(a1b, 64, xt, r_xt, AF.Copy, 0)
            for d in range(KC):
                bk, rb = lora_out(a2b, 64, 0, d)
                k.op("act", lambda: nc.scalar.activation(out=bt[:, d, 0:n], in_=bk[:, 0:n], func=AF.Sigmoid,
                                                         bias=rv[:, 7, d:d + 1]), R=[rb, r_rv], W=[r_bt[d]])
            GD = 4

            def staged(steps):
                for g0 in range(0, KC, GD):
                    st = {}
                    for step in steps:
                        for d in range(g0, g0 + GD):
                            step(d, st)

            def s_kr(d, st):
                q = d % GD
                k.op("dve", lambda: nc.vector.tensor_scalar(out=T16[0][q][:, 0:n], in0=kt[:, d, 0:n], scalar1=rv[:, 8, d:d + 1],
                                                            scalar2=None, op0=ALU.mult), R=[r_kt[d], r_rv], W=[rT16[0][q]])

            def s_sq(d, st):
                q = d % GD
                k.op("act", lambda: nc.scalar.activation(out=T16[1][q][:, 0:n], in_=T16[0][q][:, 0:n], func=AF.Square),
                     R=[rT16[0][q]], W=[rT16[1][q]])

            def s_ss(d, st):
                q = d % GD
                bk, rb = next_bank()
                st[("ss", d)] = (bk, rb)
                k.op("pe", lambda: mm(bk[:, 0:n], lhsT=bones, rhs=T16[1][q][:, 0:n], start=True, stop=True),
                     R=[r_cst, rT16[1][q]], W=[rb])

            def s_max(d, st):
                q = d % GD
                bk, rb = st[("ss", d)]
                k.op("dve", lambda: nc.vector.tensor_scalar(out=T32[0][q][:, 0:n], in0=bk[:, 0:n], scalar1=1e-24, scalar2=None,
                                                            op0=ALU.max), R=[rb], W=[rT32[0][q]])

            def s_ln(d, st):
                q = d % GD
                k.op("act", lambda: nc.scalar.activation(out=T32[0][q][:, 0:n], in_=T32[0][q][:, 0:n], func=AF.Ln),
                     R=[rT32[0][q]], W=[rT32[0][q]])

            def s_ex(d, st):
                q = d % GD
                k.op("act", lambda: nc.scalar.activation(out=T32[0][q][:, 0:n], in_=T32[0][q][:, 0:n], func=AF.Exp, scale=-0.5),
                     R=[rT32[0][q]], W=[rT32[0][q]])

            def s_kk(d, st):
                q = d % GD
                k.op("dve", lambda: nc.vector.tensor_tensor(out=ar[:, d, 0, 0:n], in0=T16[0][q][:, 0:n], in1=T32[0][q][:, 0:n], op=ALU.mult),
                     R=[rT16[0][q], rT32[0][q]], W=[r_ar[d]])
                k.op("dve", lambda: nc.vector.tensor_scalar(out=T32[1][q][:, 0:n], in0=bt[:, d, 0:n], scalar1=rv[:, 9, d:d + 1],
                                                            scalar2=rv[:, 13, d:d + 1], op0=ALU.mult, op1=ALU.add),
                     R=[r_bt[d], r_rv], W=[rT32[1][q]])

            def s_kb(d, st):
                q = d % GD
                k.op("dve", lambda: nc.vector.tensor_tensor(out=kt[:, d, 0:n], in0=kt[:, d, 0:n], in1=T32[1][q][:, 0:n], op=ALU.mult),
                     R=[rT32[1][q], r_kt[d]], W=[r_kt[d]])
                k.op("dve", lambda: nc.vector.tensor_tensor(out=bt[:, d, 0:n], in0=bt[:, d, 0:n], in1=ar[:, d, 0, 0:n], op=ALU.mult),
                     R=[r_ar[d], r_bt[d]], W=[r_bt[d]])

            def s_bq(d, st):
                q = d % GD
                k.op("dve", lambda: nc.vector.scalar_tensor_tensor(
                    out=T16[2][q][:, 0:n], in0=ar[:, d, 1, 0:n], scalar=rv[:, 10, d:d + 1], in1=kt[:, d, 0:n],
                    op0=ALU.mult, op1=ALU.mult), R=[r_ar[d], r_kt[d], r_rv], W=[rT16[2][q]])

            def s_bs(d, st):
                q = d % GD
                bk2, rb2 = next_bank()
                st[("bs", d)] = (bk2, rb2)
                k.op("pe", lambda: mm(bk2[:, 0:n], lhsT=bones, rhs=T16[2][q][:, 0:n], start=True, stop=True),
                     R=[r_cst, rT16[2][q]], W=[rb2])

            def s_bon(d, st):
                bk2, rb2 = st[("bs", d)]
                k.op("act", lambda: nc.scalar.copy(out=bon[:, d, 0:n], in_=bk2[:, 0:n]), R=[rb2], W=[r_bon[d]])

            staged([s_kr, s_sq, s_ss, s_max, s_ln, s_ex, s_kk, s_kb, s_bq, s_bs, s_bon])

            xt, r_xt = make_xi(1)
            lora_in(w1b, 64, xt, r_xt, AF.Tanh, 1)
            smask = scanz[:, 0:n] if sample else scanm[:, 0:n]
            SG, CW, E1, E2 = T32[0], T32[1], T32[2], T32[3]
            rSG, rCW, rE1, rE2 = rT32[0], rT32[1], rT32[2], rT32[3]

            def t_mm(d, st):
                st[("w", d)] = lora_out(w2b, 64, 1, d)

            def t_sig(d, st):
                q = d % GD
                bk, rb = st[("w", d)]
                k.op("act", lambda: nc.scalar.activation(out=SG[q][:, 0:n], in_=bk[:, 0:n], func=AF.Sigmoid,
                                                         bias=rv[:, 6, d:d + 1]), R=[rb, r_rv], W=[rSG[q]])

            def t_scan(d, st):
                q = d % GD
                k.op("dve", lambda: nc.vector.tensor_tensor_scan(out=CW[q][:, 0:n], data0=smask, data1=SG[q][:, 0:n], initial=0.0,
                                                                 op0=ALU.mult, op1=ALU.add), R=[rSG[q], r_cst], W=[rCW[q]])
                k.op("dve", lambda: nc.vector.tensor_tensor(out=SG[q][:, 0:n], in0=CW[q][:, 0:n], in1=SG[q][:, 0:n], op=ALU.subtract),
                     R=[rCW[q], rSG[q]], W=[rSG[q]])

            def t_e1(d, st):
                q = d % GD
                k.op("act", lambda: nc.scalar.activation(out=E1[q][:, 0:n], in_=CW[q][:, 0:n], func=AF.Exp, scale=C0),
                     R=[rCW[q]], W=[rE1[q]])
                k.op("act", lambda: nc.scalar.activation(out=E2[q][:, 0:n], in_=SG[q][:, 0:n], func=AF.Exp, scale=C0),
                     R=[rSG[q]], W=[rE2[q]])
                k.op("act", lambda: nc.scalar.activation(out=SG[q][:, 0:n], in_=CW[q][:, 0:n], func=AF.Exp, scale=-C0),
                     R=[rCW[q], rSG[q]], W=[rSG[q]])

            def t_apply(d, st):
                q = d % GD
                k.op("dve", lambda: nc.vector.tensor_tensor(out=ar[:, d, 1, 0:n], in0=ar[:, d, 1, 0:n], in1=E1[q][:, 0:n], op=ALU.mult),
                     R=[rE1[q], r_ar[d]], W=[r_ar[d]])
                if sample:
                    k.op("dve", lambda: nc.vector.tensor_copy(out=dS[:, d, 0:n], in_=E1[q][:, 0:n]), R=[rE1[q]], W=[r_dS])
                else:
                    for c in range(n // P):
                        k.op("dve", lambda: nc.vector.tensor_copy(out=gC[:, d, c:c + 1], in_=E1[q][:, c * P + P - 1:c * P + P]),
                             R=[rE1[q]], W=[r_gC])
                k.op("dve", lambda: nc.vector.scalar_tensor_tensor(
                    out=ar[:, d, 0, 0:n], in0=ar[:, d, 0, 0:n], scalar=-1.0, in1=E2[q][:, 0:n], op0=ALU.mult, op1=ALU.mult),
                    R=[rE2[q], r_ar[d]], W=[r_ar[d]])
                k.op("dve", lambda: nc.vector.tensor_tensor(out=bt[:, d, 0:n], in0=bt[:, d, 0:n], in1=SG[q][:, 0:n], op=ALU.mult),
                     R=[rSG[q], r_bt[d]], W=[r_bt[d]])
                k.op("dve", lambda: nc.vector.tensor_tensor(out=kt[:, d, 0:n], in0=kt[:, d, 0:n], in1=SG[q][:, 0:n], op=ALU.mult),
                     R=[rSG[q], r_kt[d]], W=[r_kt[d]])

            staged([t_mm, t_sig, t_scan, t_e1, t_apply])
            xt, r_xt = make_xi(3)
            proj_big(rw["rw_wv"], xt, r_xt, lambda d, ps, rb: evac_copy(vt[:, d, 0:n], ps, R=[rb], W=[r_vt[d]]))
            xt, r_xt = make_xi(5)
            lora_in(g1b, 128, xt, r_xt, AF.Sigmoid, 2)
            for d in range(KC):
                bk, rb = lora_out(g2b, 128, 2, d)
                evac_copy(gt[:, d, 0:n], bk[:, 0:n], R=[rb], W=[r_gt[d]])
            pa.end()

        def post(pb_tiles, ysrc_fn, c0, w):
            y32s, r_y32s, ybs, r_ybs, sqbs, r_sqbs, rss, r_rss = pb_tiles
            for d in range(KC):
                q = d % 2
                y32, r_y32 = y32s[q], r_y32s[q]
                yb, r_yb = ybs[q], r_ybs[q]
                sqb, r_sqb = sqbs[q], r_sqbs[q]
                rs, r_rs = rss[q], r_rss[q]
                ysrc, r_ysrc = ysrc_fn(d)
                k.op("act", lambda: nc.scalar.copy(out=y32[:, 0:w], in_=ysrc), R=[r_ysrc], W=[r_y32])
                k.op("dve", lambda: nc.vector.tensor_copy(out=yb[:, 0:w], in_=ysrc), R=[r_ysrc], W=[r_yb])
                bk, rb = next_bank()
                k.op("pe", lambda: mm(bk[:, 0:w], lhsT=bones64, rhs=yb[:, 0:w], start=True, stop=True),
                     R=[r_cst, r_yb], W=[rb])
                k.op("dve", lambda: nc.vector.tensor_tensor(out=y32[:, 0:w], in0=y32[:, 0:w], in1=bk[:, 0:w], op=ALU.subtract),
                     R=[rb, r_y32], W=[r_y32])
                k.op("act", lambda: nc.scalar.activation(out=sqb[:, 0:w], in_=y32[:, 0:w], func=AF.Square),
                     R=[r_y32], W=[r_sqb])
                bk2, rb2 = next_bank()
                k.op("pe", lambda: mm(bk2[:, 0:w], lhsT=bones64, rhs=sqb[:, 0:w], start=True, stop=True),
                     R=[r_cst, r_sqb], W=[rb2])
                k.op("act", lambda: nc.scalar.activation(out=rs[:, 0:w], in_=bk2[:, 0:w], func=AF.Ln, bias=eps_gn[:, 0:1]),
                     R=[rb2, r_cst], W=[r_rs])
                k.op("act", lambda: nc.scalar.activation(out=rs[:, 0:w], in_=rs[:, 0:w], func=AF.Exp, scale=-0.5), R=[r_rs], W=[r_rs])
                k.op("dve", lambda: nc.vector.tensor_tensor(out=y32[:, 0:w], in0=y32[:, 0:w], in1=rs[:, 0:w], op=ALU.mult),
                     R=[r_rs, r_y32], W=[r_y32])
                k.op("dve", lambda: nc.vector.tensor_scalar(out=y32[:, 0:w], in0=y32[:, 0:w], scalar1=rv[:, 11, d:d + 1],
                                                            scalar2=rv[:, 12, d:d + 1], op0=ALU.mult, op1=ALU.add),
                     R=[r_y32, r_rv], W=[r_y32])
                k.op("dve", lambda: nc.vector.tensor_tensor(out=rs[:, 0:w], in0=bon[:, d, c0:c0 + w], in1=vt[:, d, c0:c0 + w], op=ALU.mult),
                     R=[r_bon[d], r_vt[d]], W=[r_rs])
                k.op("dve", lambda: nc.vector.tensor_tensor(out=y32[:, 0:w], in0=y32[:, 0:w], in1=rs[:, 0:w], op=ALU.add),
                     R=[r_rs, r_y32], W=[r_y32])
                k.op("dve", lambda: nc.vector.tensor_tensor(out=og[:, d, c0:c0 + w], in0=y32[:, 0:w], in1=gt[:, d, c0:c0 + w], op=ALU.mult),
                     R=[r_y32, r_gt[d]], W=[r_og])

        def post_tiles(pb):
            y32s = [pb.alloc([P, P]) for _ in range(2)]; r_y32s = [pb.reg("y32a"), pb.reg("y32b")]
            ybs = [pb.alloc([P, P], BF16) for _ in range(2)]; r_ybs = [pb.reg("yba"), pb.reg("ybb")]
            sqbs = [pb.alloc([P, P], BF16) for _ in range(2)]; r_sqbs = [pb.reg("sqba"), pb.reg("sqbb")]
            rss = [pb.alloc([P, P]) for _ in range(2)]; r_rss = [pb.reg("rsa"), pb.reg("rsb")]
            return (y32s, r_y32s, ybs, r_ybs, sqbs, r_sqbs, rss, r_rss)

        def run_interleaved(gens):
            gens = list(gens)
            while gens:
                for g_ in list(gens):
                    try:
                        next(g_)
                    except StopIteration:
                        gens.remove(g_)

        def stageB(n):
            pb = Phase(base=ov_base, carry_ref=ovc)
            btok = pb.alloc([P, D], BF16); ktok = pb.alloc([P, D], BF16); vtok = pb.alloc([P, D], BF16)
            r_tok = [pb.reg("btok"), pb.reg("ktok"), pb.reg("vtok")]
            G = 4
            sets = []
            for si in range(4):
                S_ = {}
                S_["Nb"] = pb.alloc([P, G, 256], BF16); S_["Nk"] = pb.alloc([P, G, 256], BF16)
                S_["r_Nb"] = pb.reg("Nb%d" % si); S_["r_Nk"] = pb.reg("Nk%d" % si)
                _x = pb.alloc([P, G, P], BF16); _rx = pb.reg("X_%d" % si)
                _y = pb.alloc([P, G, P], BF16); _ry = pb.reg("Y_%d" % si)
                _p = pb.alloc([P, G, P], BF16); _rp = pb.reg("P_%d" % si)
                S_["Xs"] = [_x, _x]; S_["r_Xs"] = [_rx, _rx]
                S_["Ys"] = [_y, _y]; S_["r_Ys"] = [_ry, _ry]
                S_["Ps"] = [_p, _p]; S_["r_Ps"] = [_rp, _rp]
                S_["Wt"] = pb.alloc([P, 2, P], BF16); S_["r_Wt"] = pb.reg("Wt%d" % si)
                S_["Ut"] = pb.alloc([P, 2, P], BF16); S_["r_Ut"] = pb.reg("Ut%d" % si)
                S_["tH"] = pb.alloc([P, 2, 64]); S_["r_tH"] = pb.reg("tH%d" % si)
                sets.append(S_)
            ptl = post_tiles(pb)
            mur2 = mask_ur.unsqueeze(1).broadcast_to([P, 2, 256])
            ml2 = mask_l.unsqueeze(1).broadcast_to([P, 2, P])
            id4 = identb[:, :].unsqueeze(1).broadcast_to([P, G, P])

            def group_gen(c, g, S_):
                cs = c * P
                Nb, Nk, r_Nb, r_Nk = S_["Nb"], S_["Nk"], S_["r_Nb"], S_["r_Nk"]
                Xs, Ys, Ps, r_Xs, r_Ys, r_Ps = S_["Xs"], S_["Ys"], S_["Ps"], S_["r_Xs"], S_["r_Ys"], S_["r_Ps"]
                Wt, Ut, tH, r_Wt, r_Ut, r_tH = S_["Wt"], S_["Ut"], S_["tH"], S_["r_Wt"], S_["r_Ut"], S_["r_tH"]
                j0 = 2 * g
                units = [(j0 + u % 2, u // 2) for u in range(G)]
                sbk = [next_bank() for _ in range(2)]
                for u, (j, hp) in enumerate(units):
                    ps_ = slice(hp * 64, hp * 64 + 64)
                    jj = u % 2
                    if hp == 0:
                        bkA, rbA = sbk[0]
                        o_ = bkA[:, jj * 256:jj * 256 + 256]
                    else:
                        bkA, rbA = sbk[1]
                        o_ = bkA[:, jj * 256:jj * 256 + 256]
                    k.op("pe", lambda: mm(o_, lhsT=bt[ps_, j, cs:cs + P], rhs=ar[ps_, j, :, cs:cs + P], start=True, stop=True),
                         R=[r_bt[j], r_ar[j]], W=[rbA])
                for hh in range(2):
                    bkA, rbA = sbk[hh]
                    k.op("dve", lambda: nc.vector.tensor_tensor(
                        out=Nb[:, 2 * hh:2 * hh + 2, :], in0=bkA[:, :].rearrange("p (u c) -> p u c", u=2), in1=mur2, op=ALU.mult),
                        R=[rbA, r_cst], W=[r_Nb])
                yield
                sbk2 = [next_bank() for _ in range(2)]
                for u, (j, hp) in enumerate(units):
                    ps_ = slice(hp * 64, hp * 64 + 64)
                    jj = u % 2
                    bkC, rbC = sbk2[hp]
                    k.op("pe", lambda: mm(bkC[:, jj * 256:jj * 256 + 256], lhsT=kt[ps_, j, cs:cs + P],
                                          rhs=ar[ps_, j, :, cs:cs + P], start=True, stop=True),
                         R=[r_kt[j], r_ar[j]], W=[rbC])
                bkE0, rbE0 = next_bank()
                bkE1, rbE1 = next_bank()
                for u, (j, hp) in enumerate(units):
                    ps_ = slice(hp * 64, hp * 64 + 64)
                    jj = u % 2
                    bkE, rbE = (bkE0, rbE0) if hp == 0 else (bkE1, rbE1)
                    k.op("pe", lambda: mm(bkE[:, jj * P:(jj + 1) * P], lhsT=ar[ps_, j, 0, cs:cs + P],
                                          rhs=bt[ps_, j, cs:cs + P], start=True, stop=True),
                         R=[r_bt[j], r_ar[j]], W=[rbE])
                for hh in range(2):
                    bkC, rbC = sbk2[hh]
                    k.op("dve", lambda: nc.vector.tensor_tensor(
                        out=Nk[:, 2 * hh:2 * hh + 2, :], in0=bkC[:, :].rearrange("p (u c) -> p u c", u=2), in1=mur2, op=ALU.mult),
                        R=[rbC, r_cst], W=[r_Nk])
                    bkE, rbE = (bkE0, rbE0) if hh == 0 else (bkE1, rbE1)
                    k.op("dve", lambda: nc.vector.tensor_tensor(
                        out=Ys[0][:, 2 * hh:2 * hh + 2, :], in0=bkE[:, 0:2 * P].rearrange("p (u c) -> p u c", u=2), in1=ml2, op=ALU.mult),
                        R=[rbE, r_cst], W=[r_Ys[0]])
                k.op("dve", lambda: nc.vector.tensor_tensor(out=Ps[0][:, :, :], in0=Nb[:, :, 0:P], in1=id4, op=ALU.add),
                     R=[r_Nb, r_identb], W=[r_Ps[0]])
                yield
                Xc = lambda u: Nb[:, u, 0:P]
                r_Xc = r_Nb
                yi = 0; pi = 0; xi_ = 0
                for lvl in range(6):
                    bX, rbX = next_bank()
                    bY, rbY = next_bank()
                    Yc = Ys[yi]; r_Yc = r_Ys[yi]
                    for u in range(G):
                        k.op("pe", lambda: mm(bY[:, u * P:(u + 1) * P], lhsT=Xc(u), rhs=Yc[:, u, :], start=True, stop=True),
                             R=[r_Yc, r_Xc], W=[rbY])
                    if lvl < 5:
                        for u in range(G):
                            k.op("pe", lambda: mm(bX[:, u * P:(u + 1) * P], lhsT=Yc[:, u, :], rhs=Xc(u), start=True, stop=True),
                                 R=[r_Yc, r_Xc], W=[rbX])
                    Yn = Ys[1 - yi]; r_Yn = r_Ys[1 - yi]
                    evac_copy(Yn[:, :, :], bY[:, :].rearrange("p (u c) -> p u c", u=G), R=[rbY], W=[r_Yn])
                    if lvl < 5:
                        Xn = Xs[xi_]; r_Xn = r_Xs[xi_]
                        evac_copy(Xn[:, :, :], bX[:, :].rearrange("p (u c) -> p u c", u=G), R=[rbX], W=[r_Xn])
                        Xc = (lambda Xn: (lambda u: Xn[:, u, :]))(Xn)
                        r_Xc = r_Xn
                        xi_ = 1 - xi_
                    yi = 1 - yi
                    yield
                    bP, rbP = next_bank()
                    Pc = Ps[pi]; r_Pc = r_Ps[pi]
                    for u in range(G):
                        k.op("pe", lambda: mm(bP[:, u * P:(u + 1) * P], lhsT=identb[:, :], rhs=Pc[:, u, :], start=True, stop=False),
                             R=[r_identb, r_Pc], W=[rbP], inc=False)
                        k.op("pe", lambda: mm(bP[:, u * P:(u + 1) * P], lhsT=Yn[:, u, :], rhs=Pc[:, u, :], start=False, stop=True),
                             R=[r_Yn, r_Pc], W=[rbP])
                    Pn = Ps[1 - pi]; r_Pn = r_Ps[1 - pi]
                    evac_copy(Pn[:, :, :], bP[:, :].rearrange("p (u c) -> p u c", u=G), R=[rbP], W=[r_Pn])
                    pi = 1 - pi
                    yield
                Tn = Ps[pi]; r_Tn = r_Ps[pi]
                bW, rbW = next_bank()
                for jj in range(2):
                    j = j0 + jj
                    k.op("pe", lambda: mm(bW[:, jj * P:(jj + 1) * P], lhsT=ar[:, j, 0, cs:cs + P], rhs=Hbd[:, j, :],
                                          start=True, stop=False), R=[r_ar[j], r_H[g]], W=[rbW], inc=False)
                    for hp in range(2):
                        u = hp * 2 + jj
                        hc = (2 * j + hp) * 64
                        k.op("pe", lambda: mm(bW[:, jj * P + hp * 64:jj * P + hp * 64 + 64], lhsT=Nk[:, u, 0:P],
                                              rhs=vtok[:, hc:hc + 64], start=False, stop=(hp == 1)),
                             R=[r_Nk, r_tok[2]], W=[rbW], inc=(hp == 1))
                k.op("act", lambda: nc.scalar.copy(out=Wt[:, :, :], in_=bW[:, 0:2 * P].rearrange("p (u c) -> p u c", u=2)),
                     R=[rbW], W=[r_Wt])
                yield
                bU, rbU = next_bank()
                for u, (j, hp) in enumerate(units):
                    jj = u % 2
                    k.op("pe", lambda: mm(bU[:, jj * P + hp * 64:jj * P + hp * 64 + 64], lhsT=Tn[:, u, :],
                                          rhs=Wt[:, jj, hp * 64:hp * 64 + 64], start=True, stop=True),
                         R=[r_Tn, r_Wt], W=[rbU])
                k.op("act", lambda: nc.scalar.copy(out=Ut[:, :, :], in_=bU[:, 0:2 * P].rearrange("p (u c) -> p u c", u=2)),
                     R=[rbU], W=[r_Ut])
                yield
                for jj in range(2):
                    j = j0 + jj
                    yb_, ryb_ = ybk[j // 4], r_ybk[j // 4]
                    k.op("pe", lambda: mm(yb_[:, (j % 4) * P:(j % 4) * P + P], lhsT=Hbd[:, j, :], rhs=ar[:, j, 1, cs:cs + P],
                                          start=True, stop=False), R=[r_H[g], r_ar[j]], W=[ryb_], inc=False)
                    for hp in range(2):
                        u = hp * 2 + jj
                        hc = (2 * j + hp) * 64
                        yo = yb_[hp * 64:hp * 64 + 64, (j % 4) * P:(j % 4) * P + P]
                        k.op("pe", lambda: mm(yo, lhsT=Ut[:, jj, hp * 64:hp * 64 + 64], rhs=Nb[:, u, P:2 * P], start=False, stop=False),
                             R=[r_Ut, r_Nb], W=[ryb_], inc=False)
                        k.op("pe", lambda: mm(yo, lhsT=vtok[:, hc:hc + 64], rhs=Nk[:, u, P:2 * P], start=False, stop=True),
                             R=[r_tok[2], r_Nk], W=[ryb_])
                bH, rbH = next_bank()
                for u, (j, hp) in enumerate(units):
                    jj = u % 2
                    hc = (2 * j + hp) * 64
                    ho = bH[hp * 64:hp * 64 + 64, jj * 64:jj * 64 + 64]
                    k.op("pe", lambda: mm(ho, lhsT=btok[:, hc:hc + 64], rhs=Ut[:, jj, hp * 64:hp * 64 + 64], start=True, stop=False),
                         R=[r_tok[0], r_Ut], W=[rbH], inc=False)
                    k.op("pe", lambda: mm(ho, lhsT=ktok[:, hc:hc + 64], rhs=vtok[:, hc:hc + 64], start=False, stop=True),
                         R=[r_tok[1], r_tok[2]], W=[rbH])
                k.op("dve", lambda: nc.vector.tensor_tensor(out=tH[:, :, :], in0=bH[:, 0:128].rearrange("p (a b) -> p a b", a=2),
                                                            in1=H32[:, j0:j0 + 2, :], op=ALU.add), R=[rbH, r_H[g]], W=[r_tH])
                gcb = gC[:, j0:j0 + 2, c:c + 1].broadcast_to([P, 2, 64])
                k.op("dve", lambda: nc.vector.tensor_tensor(out=H32[:, j0:j0 + 2, :], in0=tH[:, :, :], in1=gcb, op=ALU.mult),
                     R=[r_tH, r_gC], W=[r_H[g]])
                k.op("act", lambda: nc.scalar.copy(out=Hbd[0:64, j0:j0 + 2, 0:64], in_=H32[0:64, j0:j0 + 2, :]), R=[r_H[g]], W=[r_H[g]])
                k.op("act", lambda: nc.scalar.copy(out=Hbd[64:128, j0:j0 + 2, 64:128], in_=H32[64:128, j0:j0 + 2, :]), R=[r_H[g]], W=[r_H[g]])
                yield

            for c in range(n // P):
                cs = c * P
                for si, (src_t, r_src, dst_t) in enumerate([(bt, r_bt, btok), (kt, r_kt, ktok), (vt, r_vt, vtok)]):
                    bk, rb = next_bank()
                    bkb = bk[:, :].bitcast(BF16)
                    for d in range(KC):
                        k.op("pe", lambda: nc.tensor.transpose(bkb[:, d * P:(d + 1) * P], src_t[:, d, cs:cs + P], identb[:, :]),
                             R=[r_src[d], r_identb], W=[rb])
                    evac_copy(dst_t[:, :], bkb[:, 0:D], R=[rb], W=[r_tok[si]])
                import os as _os3
                if _os3.environ.get("RW_NOIL"):
                    for g_ in range(4):
                        run_interleaved([group_gen(c, g_, sets[g_ % 2])])
                else:
                    if _os3.environ.get("RW_IL2"):
                        run_interleaved([group_gen(c, 0, sets[0]), group_gen(c, 2, sets[1])])
                        run_interleaved([group_gen(c, 1, sets[2]), group_gen(c, 3, sets[3])])
                    else:
                        run_interleaved([group_gen(c, g_, sets[g_]) for g_ in range(4)])
                post(ptl, lambda d: (ybk[d // 4][:, (d % 4) * P:(d % 4) * P + P], r_ybk[d // 4]), cs, P)
            pb.end()

        def out_proj(t0, n, ti):
            wv_ = rw["rw_wo"].rearrange("(k p) c -> p k c", p=P)
            for half in range(2):
                wb, r_w = next_wbuf()
                v = wb[:, 0:KC * 512].rearrange("p (k c) -> p k c", k=KC)
                k.dma("pool", v, wv_[:, :, half * 512:(half + 1) * 512], W=[r_w])
                for oc in range(4):
                    d = half * 4 + oc
                    bk, rb = next_bank()
                    for kk in range(KC):
                        k.op("pe", lambda: mm(bk[:, 0:n], lhsT=v[:, kk, oc * P:(oc + 1) * P], rhs=og[:, kk, 0:n],
                                              start=(kk == 0), stop=(kk == KC - 1)),
                             R=[r_w, r_og], W=[rb], inc=(kk == KC - 1))
                    k.op("dve", lambda: nc.vector.tensor_tensor(out=x[:, d, t0:t0 + n], in0=bk[:, 0:n], in1=x[:, d, t0:t0 + n], op=ALU.add),
                         R=[rb, r_x[d][ti]], W=[r_x[d][ti]])

        import os as _os
        _bis = _os.environ.get("RWB", "").split(",")
        nseg = TP // SEG
        for s in range(nseg):
            t0 = s * SEG
            ti = t0 // 512
            if "noA" not in _bis:
                stageA(t0, SEG, ti, False, s == nseg - 1)
            if "noB" not in _bis:
                stageB(SEG)
                out_proj(t0, SEG, ti)

        def stageB_sample():
            pb = Phase(base=ov_base, carry_ref=ovc)
            vec = {}
            for nm in ("A", "R", "B", "K", "V", "D"):
                vec[nm] = (pb.alloc([P, P]), pb.reg("sv" + nm))
            sa = pb.alloc([P, 64]); r_sa = pb.reg("sa")
            yv = pb.alloc([P, P]); r_yv = pb.reg("yv")
            Ss = pb.alloc([P, 64, 64]); r_Ss = pb.reg("Ss")
            tmp = pb.alloc([P, 64, 64]); r_tmp = pb.reg("tmp")
            ptl = post_tiles(pb)
            srcs = {"A": (lambda: ar[:, :, 0, 0:NS], r_ar, BF16), "R": (lambda: ar[:, :, 1, 0:NS], r_ar, BF16),
                    "B": (lambda: bt[:, :, 0:NS], r_bt, BF16), "K": (lambda: kt[:, :, 0:NS], r_kt, BF16),
                    "V": (lambda: vt[:, :, 0:NS], r_vt, BF16), "D": (lambda: dS[:, :, 0:NS], [r_dS], F32)}
            cb16 = [pb.alloc([P, P], BF16) for _ in range(2)]; r_cb16 = [pb.reg("cb16a"), pb.reg("cb16b")]
            cb32 = pb.alloc([P, P]); r_cb32 = pb.reg("cb32")
            for si, (nm, (fn, rr, dt_)) in enumerate(srcs.items()):
                bk, rb = next_bank()
                if dt_ == BF16:
                    cb, r_cb = cb16[si % 2], r_cb16[si % 2]
                    evac_copy(cb[:, :].rearrange("p (a b) -> p a b", a=KC), fn(), R=list(rr), W=[r_cb])
                    o_ = bk[:, :].bitcast(BF16)[:, 0:P]
                    k.op("pe", lambda: nc.tensor.transpose(o_, cb[:, :], identb[:, :]), R=[r_cb, r_identb], W=[rb])
                else:
                    evac_copy(cb32[:, :].rearrange("p (a b) -> p a b", a=KC), fn(), R=list(rr), W=[r_cb32])
                    o_ = bk[:, 0:P]
                    k.op("pe", lambda: nc.tensor.transpose(o_, cb32[:, :], ident[:, :]), R=[r_cb32, r_ident], W=[rb])
                evac_copy(vec[nm][0][:, :], o_, R=[rb], W=[vec[nm][1]])

            def kb(nm, hp):
                return vec[nm][0][:, hp * 64:(hp + 1) * 64].unsqueeze(1).broadcast_to([P, 64, 64])

            def vb(ap2):
                return ap2.unsqueeze(2).broadcast_to([P, 64, 64])
            TT = nc.vector.tensor_tensor
            for hp in range(2):
                for j in range(KC):
                    k.dma("sp", Ss[j * NS:(j + 1) * NS, :, :], st_wkv_d[:, 2 * j + hp, :, :], W=[r_Ss])
                k.op("dve", lambda: TT(out=tmp, in0=Ss, in1=kb("A", hp), op=ALU.mult), R=[r_Ss, vec["A"][1]], W=[r_tmp])
                k.op("dve", lambda: nc.vector.tensor_reduce(out=sa[:, :], in_=tmp, axis=AX.X, op=ALU.add), R=[r_tmp], W=[r_sa])
                k.op("dve", lambda: TT(out=tmp, in0=vb(sa[:, :]), in1=kb("B", hp), op=ALU.mult), R=[r_sa, vec["B"][1]], W=[r_tmp])
                k.op("dve", lambda: TT(out=Ss, in0=Ss, in1=tmp, op=ALU.add), R=[r_tmp, r_Ss], W=[r_Ss])
                k.op("dve", lambda: TT(out=tmp, in0=vb(vec["V"][0][:, hp * 64:(hp + 1) * 64]), in1=kb("K", hp), op=ALU.mult),
                     R=[vec["V"][1], vec["K"][1]], W=[r_tmp])
                k.op("dve", lambda: TT(out=Ss, in0=Ss, in1=tmp, op=ALU.add), R=[r_tmp, r_Ss], W=[r_Ss])
                k.op("dve", lambda: TT(out=tmp, in0=Ss, in1=kb("R", hp), op=ALU.mult), R=[r_Ss, vec["R"][1]], W=[r_tmp])
                k.op("dve", lambda: nc.vector.tensor_reduce(out=yv[:, hp * 64:(hp + 1) * 64], in_=tmp, axis=AX.X, op=ALU.add),
                     R=[r_tmp], W=[r_yv])
                k.op("dve", lambda: TT(out=Ss, in0=Ss, in1=kb("D", hp), op=ALU.mult), R=[r_Ss, vec["D"][1]], W=[r_Ss])
                for j in range(KC):
                    r_o = Reg("o_swkv"); out_regs.append(r_o)
                    k.dma("sp", swkv_d[:, 2 * j + hp, :, :], Ss[j * NS:(j + 1) * NS, :, :], R=[r_Ss], W=[r_o])
            k.op("pe", lambda: nc.tensor.transpose(ybk[0][:, 0:P], yv[:, :], ident[:, :]), R=[r_yv, r_ident], W=[r_ybk[0]])
            post(ptl, lambda d: (ybk[0][:, d * NS:(d + 1) * NS], r_ybk[0]), 0, NS)
            pb.end()

        if do_samples:
            stageA(TP, NS, 4, True, False)
            stageB_sample()
            out_proj(TP, NS, 4)

        pe_ = Phase(base=ov_base, carry_ref=ovc)
        shT = pe_.alloc([1 + NS, D]); r_shT = pe_.reg("shT")
        bk, rb = next_bank()
        bk2, rb2 = next_bank()
        for d in range(KC):
            bb = bk if d < 4 else bk2
            k.op("pe", lambda: nc.tensor.transpose(bb[0:1 + NS, (d % 4) * P:(d % 4) * P + P], shf[:, d, :], ident[:, :]),
                 R=[r_shf, r_ident], W=[rb if d < 4 else rb2])
        evac_copy(shT[:, 0:512], bk[0:1 + NS, :], R=[rb], W=[r_shT])
        evac_copy(shT[:, 512:1024], bk2[0:1 + NS, :], R=[rb2], W=[r_shT])
        r_o = Reg("o_pshift"); out_regs.append(r_o)
        k.dma("sp", pshift_d.rearrange("(a d) -> a d", a=1), shT[0:1, :], R=[r_shT], W=[r_o])
        if do_samples:
            r_o = Reg("o_sshift"); out_regs.append(r_o)
            k.dma("sp", sshift_d, shT[1:1 + NS, :], R=[r_shT], W=[r_o])
        ST = pe_.alloc([64, KC, P]); r_ST = pe_.reg("ST")
        for half in range(2):
            bk, rb = next_bank()
            for jj in range(4):
                j = half * 4 + jj
                k.op("pe", lambda: nc.tensor.transpose(bk[0:64, jj * P:(jj + 1) * P], H32[:, j, :], ident[:, :]),
                     R=[r_H[j // 2], r_ident], W=[rb])
            evac_copy(ST[:, half * 4:half * 4 + 4, :], bk[0:64, :].rearrange("p (a b) -> p a b", a=4), R=[rb], W=[r_ST])
        r_o = Reg("o_pwkv"); out_regs.append(r_o)
        k.dma("sp", pwkv_d.rearrange("(j hp) v kk -> v j hp kk", hp=2), ST[:, :, :].rearrange("v j (hp kk) -> v j hp kk", hp=2),
              R=[r_ST], W=[r_o])
        pe_.end()
        dmy = Reg("ovdummy"); dmy.readers = list(ovc[0]); ph.regs.append(dmy)
        nonlocal_nb[0] = 8
        ph.end()


    RET_G = [1.0 - 2.0 ** (-5 - h_) for h_ in range(4)]

    def ret_mixer(do_samples=True):
        ph = Phase()
        SEG = 256
        mm = nc.tensor.matmul
        dmatT = ph.alloc([P, 4, P], BF16); qdt = ph.alloc([P, 4, P], BF16); kdec = ph.alloc([P, 4])
        eps_r = ph.alloc([P, 1])
        r_cst = ph.reg("retconst")
        k.dma("pool", dmatT, c_dmatT_d, W=[r_cst])
        k.dma("pool", qdt, c_qd_d, W=[r_cst])
        k.dma("sp", kdec, c_kdec_d, W=[r_cst])
        k.op("dve", lambda: nc.vector.memset(eps_r, 1e-6), W=[r_cst])
        nbuf = norm_bufs(ph)
        hm = ph.alloc([P, KC, SEG], BF16); r_hm = [ph.reg("hm%d" % d) for d in range(KC)]
        gT = ph.alloc([P, 16, SEG], BF16); r_gT = [ph.reg("gT%d" % e) for e in range(16)]
        cs_fm = ph.alloc([P, 2, SEG]); r_csfm = ph.reg("csfm")
        rq = [ph.alloc([P, SEG]) for _ in range(4)]; r_rq = [ph.reg("rq%d" % i) for i in range(4)]
        rt_ = [ph.alloc([P, SEG]) for _ in range(4)]; r_rt = [ph.reg("rt%d" % i) for i in range(4)]
        on_ = [ph.alloc([P, 512], BF16) for _ in range(2)]; r_on = [ph.reg("on0"), ph.reg("on1")]
        stt = [ph.alloc([P, 8]) for _ in range(2)]; r_stt = [ph.reg("stt0"), ph.reg("stt1")]
        ov_base = ph.off
        ov_idx = len(ph.regs)
        S32 = ph.alloc([P, 4, 2, 512]); Sb = ph.alloc([P, 4, 2, 512], BF16)
        r_S = [ph.reg("S%d" % h_) for h_ in range(4)]
        k.op("dve", lambda: nc.vector.memset(S32, 0.0), W=r_S)
        k.op("dve", lambda: nc.vector.memset(Sb, 0.0), W=r_S)
        qT = ph.alloc([P, KC, SEG], BF16); r_qT = [ph.reg("qT%d" % d) for d in range(KC)]
        kT = ph.alloc([P, KC, SEG], BF16); r_kT = [ph.reg("kT%d" % d) for d in range(KC)]
        qdT = ph.alloc([P, KC, SEG], BF16); r_qdT = [ph.reg("qdT%d" % d) for d in range(KC)]
        vtok = [ph.alloc([P, 2 * D], BF16) for _ in range(2)]; r_vtok = [ph.reg("vtok0"), ph.reg("vtok1")]
        kdtok = [ph.alloc([P, D], BF16) for _ in range(2)]; r_kdtok = [ph.reg("kdtok0"), ph.reg("kdtok1")]
        cs_tok = [ph.alloc([P, 2, P]) for _ in range(2)]; r_cstok = [ph.reg("cstok0"), ph.reg("cstok1")]
        ktmp = ph.alloc([P, 512]); r_ktmp = ph.reg("ktmp")
        kt1 = ph.alloc([P, 512]); kt2 = ph.alloc([P, 512]); r_kt1 = ph.reg("kt1"); r_kt2 = ph.reg("kt2")
        kro = ph.alloc([P, 512]); r_kro = ph.reg("kro")
        innT = [ph.alloc([P, P], BF16) for _ in range(2)]; r_innT = [ph.reg("innT0"), ph.reg("innT1")]

        def load_w(view_fn):
            wb, r_w = next_wbuf()
            v_, src_ = view_fn(wb)
            k.dma("pool", v_, src_, W=[r_w])
            return v_, r_w

        def rotary_fm(bank_pair, dst, r_dst, h_, n, scale):
            (b1, rb1), (b2, rb2) = bank_pair
            i1, i2 = (2 * h_) % 4, (2 * h_ + 1) % 4
            x1, r1 = rq[i1], r_rq[i1]
            x2, r2 = rq[i2], r_rq[i2]
            k.op("act", lambda: nc.scalar.activation(out=x1[:, 0:n], in_=b1[:, 0:n], func=AF.Copy, scale=scale), R=[rb1], W=[r1])
            k.op("act", lambda: nc.scalar.activation(out=x2[:, 0:n], in_=b2[:, 0:n], func=AF.Copy, scale=scale), R=[rb2], W=[r2])
            ta, rta = rt_[i1], r_rt[i1]
            tb, rtb = rt_[i2], r_rt[i2]
            cosv = cs_fm[:, 0, 0:n]; sinv = cs_fm[:, 1, 0:n]
            k.op("dve", lambda: nc.vector.tensor_tensor(out=ta[:, 0:n], in0=x1[:, 0:n], in1=cosv, op=ALU.mult), R=[r1, r_csfm], W=[rta])
            k.op("dve", lambda: nc.vector.tensor_tensor(out=tb[:, 0:n], in0=x2[:, 0:n], in1=sinv, op=ALU.mult), R=[r2, r_csfm], W=[rtb])
            k.op("dve", lambda: nc.vector.tensor_tensor(out=dst[:, 2 * h_, 0:n], in0=ta[:, 0:n], in1=tb[:, 0:n], op=ALU.subtract),
                 R=[rta, rtb], W=[r_dst[2 * h_]])
            k.op("dve", lambda: nc.vector.tensor_tensor(out=ta[:, 0:n], in0=x1[:, 0:n], in1=sinv, op=ALU.mult), R=[r1, r_csfm], W=[rta])
            k.op("dve", lambda: nc.vector.tensor_tensor(out=tb[:, 0:n], in0=x2[:, 0:n], in1=cosv, op=ALU.mult), R=[r2, r_csfm], W=[rtb])
            k.op("dve", lambda: nc.vector.tensor_tensor(out=dst[:, 2 * h_ + 1, 0:n], in0=ta[:, 0:n], in1=tb[:, 0:n], op=ALU.add),
                 R=[rta, rtb], W=[r_dst[2 * h_ + 1]])

        def seg_front(t0, n, ti, sample, dsts=None):
            rmsnorm_tile(nbuf, 4, ti, lambda d, a_, b_: hm[:, d, 0:n], lambda d: [r_hm[d]], rng=(t0, n))
            k.dma("sp", cs_fm[:, 0, 0:n], c_cos_d[:, t0:t0 + n], W=[r_csfm])
            k.dma("sp", cs_fm[:, 1, 0:n], c_sin_d[:, t0:t0 + n], W=[r_csfm])
            dq = (qT, r_qT) if dsts is None else dsts[0]
            dk = (kT, r_kT) if dsts is None else dsts[1]
            for (wname, dst, r_dst, scale) in (("ret_wq", dq[0], dq[1], 1.0), ("ret_wk", dk[0], dk[1], 1.0 / 16.0)):
                wv_ = rt[wname].rearrange("(k p) c -> p k c", p=P)
                for half in range(2):
                    v_, r_w = load_w(lambda wb: (wb[:, 0:KC * 512].rearrange("p (k c) -> p k c", k=KC),
                                                 wv_[:, :, half * 512:(half + 1) * 512]))
                    if wname == "ret_wk" and not sample:
                        ktok_half(v_, r_w, half, n)
                    for hh in range(2):
                        h_ = half * 2 + hh
                        pair = []
                        for i in range(2):
                            oc = hh * 2 + i
                            bk, rb = next_bank()
                            for kk in range(KC):
                                k.op("pe", lambda: mm(bk[:, 0:n], lhsT=v_[:, kk, oc * P:(oc + 1) * P], rhs=hm[:, kk, 0:n],
                                                      start=(kk == 0), stop=(kk == KC - 1)),
                                     R=[r_w, r_hm[kk]], W=[rb], inc=(kk == KC - 1))
                            pair.append((bk, rb))
                        rotary_fm(pair, dst, r_dst, h_, n, scale)
            if not sample:
                for d in range(KC):
                    h_ = d // 2
                    k.op("dve", lambda: nc.vector.tensor_tensor(
                        out=qdT[:, d, 0:n].rearrange("p (c t) -> p c t", t=P), in0=qT[:, d, 0:n].rearrange("p (c t) -> p c t", t=P),
                        in1=qdt[:, h_, :].unsqueeze(1).broadcast_to([P, n // P, P]), op=ALU.mult),
                        R=[r_qT[d], r_cst], W=[r_qdT[d]])
            wg_ = rt["ret_wg"].rearrange("(k p) c -> p k c", p=P)
            for qv in range(4):
                v_, r_w = load_w(lambda wb: (wb[:, 0:KC * 512].rearrange("p (k c) -> p k c", k=KC),
                                             wg_[:, :, qv * 512:(qv + 1) * 512]))
                for oc in range(4):
                    e = qv * 4 + oc
                    bk, rb = next_bank()
                    for kk in range(KC):
                        k.op("pe", lambda: mm(bk[:, 0:n], lhsT=v_[:, kk, oc * P:(oc + 1) * P], rhs=hm[:, kk, 0:n],
                                              start=(kk == 0), stop=(kk == KC - 1)),
                             R=[r_w, r_hm[kk]], W=[rb], inc=(kk == KC - 1))
                    k.op("act", lambda: nc.scalar.activation(out=gT[:, e, 0:n], in_=bk[:, 0:n], func=AF.Silu), R=[rb], W=[r_gT[e]])

        cur_t0 = [0]

        def ktok_half(v_, r_w, half, n):
            for c in range(n // P):
                if half == 0:
                    tt0 = cur_t0[0] + c * P
                    k.dma("sp", cs_tok[c][:, 0, :], c_cosT_d[tt0:tt0 + P, :], W=[r_cstok[c]])
                    k.dma("sp", cs_tok[c][:, 1, :], c_sinT_d[tt0:tt0 + P, :], W=[r_cstok[c]])
                bk, rb = next_bank()
                for kk in range(KC):
                    k.op("pe", lambda: mm(bk[:, :], lhsT=hm[:, kk, c * P:(c + 1) * P], rhs=v_[:, kk, :],
                                          start=(kk == 0), stop=(kk == KC - 1)),
                         R=[r_w, r_hm[kk]], W=[rb], inc=(kk == KC - 1))
                k.op("act", lambda: nc.scalar.activation(out=ktmp[:, :], in_=bk[:, :], func=AF.Copy, scale=1.0 / 16.0), R=[rb], W=[r_ktmp])
                cosb = cs_tok[c][:, 0, :].unsqueeze(1).broadcast_to([P, 4, P])
                sinb = cs_tok[c][:, 1, :].unsqueeze(1).broadcast_to([P, 4, P])
                k3 = ktmp[:, :].rearrange("p (a f) -> p a f", a=4)
                k.op("dve", lambda: nc.vector.tensor_tensor(out=kt1[:, :].rearrange("p (a f) -> p a f", a=4), in0=k3, in1=cosb, op=ALU.mult),
                     R=[r_ktmp, r_cstok[c]], W=[r_kt1])
                k.op("dve", lambda: nc.vector.tensor_tensor(out=kt2[:, :].rearrange("p (a f) -> p a f", a=4), in0=k3, in1=sinb, op=ALU.mult),
                     R=[r_ktmp, r_cstok[c]], W=[r_kt2])
                t1v = kt1[:, :].rearrange("p (h two f) -> p h two f", h=2, two=2)
                t2v = kt2[:, :].rearrange("p (h two f) -> p h two f", h=2, two=2)
                kov = kro[:, :].rearrange("p (h two f) -> p h two f", h=2, two=2)
                k.op("dve", lambda: nc.vector.tensor_tensor(out=kov[:, :, 0, :], in0=t1v[:, :, 0, :], in1=t2v[:, :, 1, :], op=ALU.subtract),
                     R=[r_kt1, r_kt2], W=[r_kro])
                k.op("dve", lambda: nc.vector.tensor_tensor(out=kov[:, :, 1, :], in0=t2v[:, :, 0, :], in1=t1v[:, :, 1, :], op=ALU.add),
                     R=[r_kt1, r_kt2], W=[r_kro])
                for hh in range(2):
                    h_ = half * 2 + hh
                    k.op("dve", lambda: nc.vector.tensor_scalar(
                        out=kdtok[c][:, h_ * 256:(h_ + 1) * 256], in0=kro[:, hh * 256:(hh + 1) * 256], scalar1=kdec[:, h_:h_ + 1],
                        scalar2=None, op0=ALU.mult), R=[r_kro, r_cst], W=[r_kdtok[c]])

        def vtok_all(n):
            wv_ = rt["ret_wv"].rearrange("(k p) c -> p k c", p=P)
            for qv in range(4):
                v_, r_w = load_w(lambda wb: (wb[:, 0:KC * 512].rearrange("p (k c) -> p k c", k=KC),
                                             wv_[:, :, qv * 512:(qv + 1) * 512]))
                for c in range(n // P):
                    bk, rb = next_bank()
                    for kk in range(KC):
                        k.op("pe", lambda: mm(bk[:, :], lhsT=hm[:, kk, c * P:(c + 1) * P], rhs=v_[:, kk, :],
                                              start=(kk == 0), stop=(kk == KC - 1)),
                             R=[r_w, r_hm[kk]], W=[rb], inc=(kk == KC - 1))
                    evac_copy(vtok[c][:, qv * 512:(qv + 1) * 512], bk[:, :], R=[rb], W=[r_vtok[c]])

        def norm_gate(o_bank, rb_o, rows, h_, cs, w, ui):
            q = ui % 2
            st_, r_st = stt[q], r_stt[q]
            k.op("dve", lambda: nc.vector.bn_stats(out=st_[0:rows, 0:6], in_=o_bank[0:rows, :]), R=[rb_o], W=[r_st])
            k.op("dve", lambda: nc.vector.bn_aggr(out=st_[0:rows, 6:8], in_=st_[0:rows, 0:6]), R=[r_st], W=[r_st])
            k.op("act", lambda: nc.scalar.activation(out=st_[0:rows, 7:8], in_=st_[0:rows, 7:8], func=AF.Sqrt, bias=eps_r[0:rows, 0:1]),
                 R=[r_st, r_cst], W=[r_st])
            k.op("dve", lambda: nc.vector.reciprocal(out=st_[0:rows, 7:8], in_=st_[0:rows, 7:8]), R=[r_st], W=[r_st])
            on_t, r_on_t = on_[q], r_on[q]
            k.op("dve", lambda: nc.vector.tensor_scalar(out=on_t[0:rows, :], in0=o_bank[0:rows, :], scalar1=st_[0:rows, 6:7],
                                                        scalar2=st_[0:rows, 7:8], op0=ALU.subtract, op1=ALU.mult),
                 R=[rb_o, r_st], W=[r_on_t])
            bk, rb = next_bank()
            bkb = bk[:, :].bitcast(BF16)
            for i in range(4):
                k.op("pe", lambda: nc.tensor.transpose(bkb[:, i * P:i * P + rows], on_t[0:rows, i * P:(i + 1) * P], identb[0:rows, 0:rows]),
                     R=[r_on_t, r_identb], W=[rb])
            gv = gT[:, 4 * h_:4 * h_ + 4, cs:cs + w]
            k.op("dve", lambda: nc.vector.tensor_tensor(out=gv, in0=bkb[:, 0:4 * P].rearrange("p (a t) -> p a t", a=4)[:, :, 0:w], in1=gv, op=ALU.mult),
                 R=[rb] + [r_gT[4 * h_ + i] for i in range(4)], W=[r_gT[4 * h_ + i] for i in range(4)])

        def seg_chunks(n):
            ui = 0
            for c in range(n // P):
                cs = c * P
                for h_ in range(4):
                    q = ui % 2
                    bs, rbs = next_bank()
                    for i in range(2):
                        d = 2 * h_ + i
                        k.op("pe", lambda: mm(bs[:, 0:P], lhsT=kT[:, d, cs:cs + P], rhs=qT[:, d, cs:cs + P], start=(i == 0), stop=(i == 1)),
                             R=[r_kT[d], r_qT[d]], W=[rbs], inc=(i == 1))
                    k.op("dve", lambda: nc.vector.tensor_tensor(out=innT[q][:, :], in0=bs[:, 0:P], in1=dmatT[:, h_, :], op=ALU.mult),
                         R=[rbs, r_cst], W=[r_innT[q]])
                    bo, rbo = next_bank()
                    k.op("pe", lambda: mm(bo[:, :], lhsT=innT[q][:, :], rhs=vtok[c][:, h_ * 512:(h_ + 1) * 512], start=True, stop=False),
                         R=[r_innT[q], r_vtok[c]], W=[rbo], inc=False)
                    for i in range(2):
                        d = 2 * h_ + i
                        k.op("pe", lambda: mm(bo[:, :], lhsT=qdT[:, d, cs:cs + P], rhs=Sb[:, h_, i, :], start=False, stop=(i == 1)),
                             R=[r_qdT[d], r_S[h_]], W=[rbo], inc=(i == 1))
                    norm_gate(bo, rbo, P, h_, cs, P, ui)
                    sdec = float(np.float32(RET_G[h_]) ** 128)
                    for i in range(2):
                        bS, rbS = next_bank()
                        k.op("pe", lambda: mm(bS[:, :], lhsT=kdtok[c][:, h_ * 256 + i * P:h_ * 256 + (i + 1) * P],
                                              rhs=vtok[c][:, h_ * 512:(h_ + 1) * 512], start=True, stop=True),
                             R=[r_kdtok[c], r_vtok[c]], W=[rbS])
                        k.op("dve", lambda: nc.vector.scalar_tensor_tensor(
                            out=S32[:, h_, i, :], in0=S32[:, h_, i, :], scalar=sdec, in1=bS[:, :], op0=ALU.mult, op1=ALU.add),
                            R=[rbS, r_S[h_]], W=[r_S[h_]])
                        k.op("act", lambda: nc.scalar.copy(out=Sb[:, h_, i, :], in_=S32[:, h_, i, :]), R=[r_S[h_]], W=[r_S[h_]])
                    ui += 1

        def out_proj(t0, n, ti):
            wo_ = rt["ret_wo"].rearrange("(e p) c -> p e c", p=P)
            for piece in range(4):
                v_, r_w = load_w(lambda wb: (wb[:, 0:16 * 256].rearrange("p (e c) -> p e c", e=16),
                                             wo_[:, :, piece * 256:(piece + 1) * 256]))
                for oc in range(2):
                    d = piece * 2 + oc
                    bk, rb = next_bank()
                    for e in range(16):
                        k.op("pe", lambda: mm(bk[:, 0:n], lhsT=v_[:, e, oc * P:(oc + 1) * P], rhs=gT[:, e, 0:n],
                                              start=(e == 0), stop=(e == 15)),
                             R=[r_w, r_gT[e]], W=[rb], inc=(e == 15))
                    k.op("dve", lambda: nc.vector.tensor_tensor(out=x[:, d, t0:t0 + n], in0=bk[:, 0:n], in1=x[:, d, t0:t0 + n], op=ALU.add),
                         R=[rb, r_x[d][ti]], W=[r_x[d][ti]])

        def ret_samples(r_pret):
            toks = set()
            for r_ in ph.regs[ov_idx:] + [r_pret]:
                if r_.last_w is not None:
                    toks.add(r_.last_w)
                toks.update(r_.readers)
            ovc = [list(toks)]
            ps_ = Phase(base=ov_base, carry_ref=ovc)
            qTs = ps_.alloc([P, KC, NS]); r_qTs = [ps_.reg("qTs%d" % d) for d in range(KC)]
            kTs = ps_.alloc([P, KC, NS]); r_kTs = [ps_.reg("kTs%d" % d) for d in range(KC)]
            qpad = ps_.alloc([P, KC, NS, NS]); r_qpad = ps_.reg("qpad")
            i16 = ps_.alloc([P, NS, NS]); selb = ps_.alloc([NS, NS, P], BF16); r_sc = ps_.reg("sconst")
            k.dma("sp", i16, c_i16_d.rearrange("p (a b) -> p a b", a=NS), W=[r_sc])
            k.dma("pool", selb, c_sel_d, W=[r_sc])
            vs = ps_.alloc([NS, 2 * D], BF16); r_vs = ps_.reg("vs")
            Sin = [ps_.alloc([P, 2, 512]) for _ in range(2)]; r_Sin = [ps_.reg("Sin0"), ps_.reg("Sin1")]
            Sout = [ps_.alloc([P, 2, 512]) for _ in range(2)]; r_Sout = [ps_.reg("Sout0"), ps_.reg("Sout1")]
            kvt = [ps_.alloc([P, 512]) for _ in range(2)]; r_kvt = [ps_.reg("kvt0"), ps_.reg("kvt1")]
            n = NS
            seg_front(TP, n, 4, True, dsts=((qTs, r_qTs), (kTs, r_kTs)))
            wv_ = rt["ret_wv"].rearrange("(k p) c -> p k c", p=P)
            for qv in range(4):
                v_, r_w = load_w(lambda wb: (wb[:, 0:KC * 512].rearrange("p (k c) -> p k c", k=KC),
                                             wv_[:, :, qv * 512:(qv + 1) * 512]))
                bk, rb = next_bank()
                for kk in range(KC):
                    k.op("pe", lambda: mm(bk[0:n, :], lhsT=hm[:, kk, 0:n], rhs=v_[:, kk, :], start=(kk == 0), stop=(kk == KC - 1)),
                         R=[r_w, r_hm[kk]], W=[rb], inc=(kk == KC - 1))
                evac_copy(vs[0:n, qv * 512:(qv + 1) * 512], bk[0:n, :], R=[rb], W=[r_vs])
            k.op("dve", lambda: nc.vector.tensor_tensor(
                out=qpad, in0=qTs.unsqueeze(3).broadcast_to([P, KC, NS, NS]), in1=i16.unsqueeze(1).broadcast_to([P, KC, NS, NS]),
                op=ALU.mult), R=r_qTs + [r_sc], W=[r_qpad])
            nonlocal_nb[0] = 4
            po = [banks[4 + h_] for h_ in range(4)]; r_po = [r_bank[4 + h_] for h_ in range(4)]
            ui = 0
            for b_ in range(NS):
                for h_ in range(4):
                    q = ui % 2; ui += 1
                    k.dma("sp", Sin[q], st_ret_d[b_, h_].rearrange("(i p) v -> p i v", p=P), W=[r_Sin[q]])
                    bk, rb = next_bank()
                    k.op("pe", lambda: mm(bk[:, :], lhsT=selb[0:NS, b_, :], rhs=vs[0:NS, h_ * 512:(h_ + 1) * 512], start=True, stop=True),
                         R=[r_sc, r_vs], W=[rb])
                    for i in range(2):
                        d = 2 * h_ + i
                        k.op("act", lambda: nc.scalar.activation(out=kvt[i][:, :], in_=bk[:, :], func=AF.Copy, scale=kTs[:, d, b_:b_ + 1]),
                             R=[rb, r_kTs[d]], W=[r_kvt[i]])
                        k.op("dve", lambda: nc.vector.scalar_tensor_tensor(
                            out=Sout[q][:, i, :], in0=Sin[q][:, i, :], scalar=float(np.float32(RET_G[h_])), in1=kvt[i][:, :],
                            op0=ALU.mult, op1=ALU.add), R=[r_Sin[q], r_kvt[i]], W=[r_Sout[q]])
                    for i in range(2):
                        d = 2 * h_ + i
                        first = (b_ == 0 and i == 0); last = (b_ == NS - 1 and i == 1)
                        k.op("pe", lambda: mm(po[h_][0:NS, :], lhsT=qpad[:, d, b_, :], rhs=Sout[q][:, i, :], start=first, stop=last),
                             R=[r_qpad, r_Sout[q]], W=[r_po[h_]], inc=True)
                    r_o2 = Reg("o_sret"); out_regs.append(r_o2)
                    k.dma("act", sret_d[b_, h_].rearrange("(i p) v -> p i v", p=P), Sout[q], R=[r_Sout[q]], W=[r_o2])
            for h_ in range(4):
                norm_gate(po[h_], r_po[h_], NS, h_, 0, NS, h_)
            nonlocal_nb[0] = 8
            out_proj(TP, n, 4)
            ps_.end()
            dmy = Reg("ovdummy2"); dmy.readers = list(ovc[0]); ph.regs.append(dmy)

        nseg = TP // SEG
        for s in range(nseg):
            t0 = s * SEG
            ti = t0 // 512
            cur_t0[0] = t0
            seg_front(t0, SEG, ti, False)
            vtok_all(SEG)
            seg_chunks(SEG)
            out_proj(t0, SEG, ti)
        r_o = Reg("o_pret"); out_regs.append(r_o)
        k.dma("sp", pret_d.rearrange("h (i p) v -> p h i v", p=P), S32, R=r_S + [r_o], W=[r_o])
        if do_samples:
            ret_samples(r_o)
        ph.end()

    def final_out():
        ph = Phase()
        nb = norm_bufs(ph)
        yfm = [ph.alloc([P, KC, 512]) for _ in range(2)]
        r_yfm = [[ph.reg("yfm%d_%d" % (i, d)) for d in range(KC)] for i in range(2)]
        yout = [ph.alloc([P, D]) for _ in range(2)]
        r_yout = [ph.reg("yout0"), ph.reg("yout1")]
        st = 0
        for ti, (t0, tn) in enumerate(TILES):
            yb = ti % 2
            rmsnorm_tile(nb, 6, ti, lambda d, t0, tn: yfm[yb][:, d, 0:tn], lambda d: [r_yfm[yb][d]])
            nsub = (tn + P - 1) // P
            for s in range(nsub):
                rows = min(P, tn - s * P)
                ob = st % 2; st += 1
                for d0 in range(0, KC, 4):
                    bk2, rb2 = next_bank()
                    for j in range(4):
                        d = d0 + j
                        k.op("pe", lambda: nc.tensor.transpose(
                            bk2[0:rows, j * P:(j + 1) * P], yfm[yb][:, d, s * P:s * P + rows], ident[:, :]),
                            R=[r_yfm[yb][d], r_ident], W=[rb2])
                    evac_copy(yout[ob][0:rows, d0 * P:(d0 + 4) * P], bk2[0:rows, :], R=[rb2], W=[r_yout[ob]])
                dst = yp_d[t0 + s * P:t0 + s * P + rows, :] if ti < 4 else ys_d[:, :]
                r_o = Reg("o"); out_regs.append(r_o)
                k.dma("sp", dst, yout[ob][0:rows, :], R=[r_yout[ob]], W=[r_o])
        ph.end()

    load_x()
    for l in range(2):
        if "ffn1_%d" % l in stages:
            ffn(l, 1)
        if "mix_%d" % l in stages:
            if l == 0:
                rwkv_mixer(do_samples=("samples" in stages))
            else:
                ret_mixer(do_samples=("samples" in stages))
        if "ffn2_%d" % l in stages:
            ffn(l, 2)
    final_out()

    k.finish(out_regs)
    es.close()
    print("instructions:", k.n_inst, "waits:", k.n_wait)
    return nc


def host_consts():
    c = {"c_ident": np.eye(P, dtype=np.float32)}
    bones = np.zeros((P, P), np.float32); bones[:64, :64] = 1.0; bones[64:, 64:] = 1.0
    c["c_bones"] = bones
    s_ = np.arange(P)[:, None]; t_ = np.arange(P)[None, :]
    c["c_mask_ur"] = np.concatenate([(s_ < t_), (s_ <= t_)], axis=1).astype(np.float32)
    c["c_mask_l"] = (s_ > t_).astype(np.float32)
    sm = np.ones((P, 256), np.float32); sm[:, 0] = 0.0; sm[:, 128] = 0.0
    c["c_scanm"] = sm
    f32 = np.float32
    half = 128
    inv = (f32(10000.0) ** (-(np.arange(half, dtype=f32) / f32(half)))).astype(f32)
    pos = np.concatenate([np.arange(TP, dtype=f32), np.full(NS, 16384.0, dtype=f32)])
    ang = (pos[None, :] * inv[:, None]).astype(f32)
    c["c_cos"] = np.cos(ang.astype(np.float64)).astype(f32)
    c["c_sin"] = np.sin(ang.astype(np.float64)).astype(f32)
    c["c_cosT"] = np.ascontiguousarray(c["c_cos"][:, :TP].T)
    c["c_sinT"] = np.ascontiguousarray(c["c_sin"][:, :TP].T)
    log_g = np.log1p(-np.exp2(-5.0 - np.arange(4, dtype=f32))).astype(f32)
    idx = np.arange(P, dtype=f32)
    diff = idx[None, :] - idx[:, None]
    dm = np.where(diff[:, None, :] >= 0, np.exp(log_g[None, :, None] * np.maximum(diff, 0)[:, None, :]), 0.0)
    c["c_dmatT"] = dm.astype(f32)
    qd = np.exp(log_g[:, None] * (idx[None, :] + 1.0)).astype(f32)
    c["c_qd"] = np.ascontiguousarray(np.broadcast_to(qd[None], (P, 4, P))).astype(f32)
    c["c_kdec"] = np.exp(log_g[None, :] * (127.0 - idx[:, None])).astype(f32)
    sel = np.zeros((NS, NS, P), f32)
    for b_ in range(NS):
        sel[b_, b_, :] = 1.0
    c["c_sel"] = sel
    c["c_i16"] = np.ascontiguousarray(np.tile(np.eye(NS, dtype=f32).reshape(1, NS * NS), (P, 1)))
    return c


W_NAMES = ("ln_ffn1", "ln_mix", "ln_ffn2", "ln_final", "ff1_wg", "ff1_wu", "ff1_wd", "ff2_wg", "ff2_wu", "ff2_wd")
RT_NAMES = ("ret_wq", "ret_wk", "ret_wv", "ret_wg", "ret_wo")
RW_NAMES = ("rw_mu", "rw_w0", "rw_w1", "rw_w2", "rw_a0", "rw_a1", "rw_a2", "rw_g1", "rw_g2", "rw_kk", "rw_ka",
            "rw_rk", "rw_wr", "rw_wk", "rw_wv", "rw_wo", "rw_lnx_g", "rw_lnx_b")


def make_in_maps(inputs, n_cores=N_CORES):
    consts = host_consts()
    shared = {}
    for nm in W_NAMES:
        shared[nm] = np.ascontiguousarray(inputs[nm], dtype=np.float32)
    for nm in RW_NAMES:
        a = np.asarray(inputs[nm], dtype=np.float32)[0]
        if nm == "rw_rk":
            a = a.reshape(D)
        shared[nm] = np.ascontiguousarray(a)
    for nm in RT_NAMES:
        shared[nm] = np.ascontiguousarray(np.asarray(inputs[nm], dtype=np.float32)[0])
    in_maps = []
    for c in range(n_cores):
        m = dict(consts)
        m.update(shared)
        m["x_prompt"] = np.ascontiguousarray(inputs["x_prompt"][c])
        m["x_sample"] = np.ascontiguousarray(inputs["x_sample"][c * NS:(c + 1) * NS, 0, :])
        m["st_shift"] = np.ascontiguousarray(inputs["state_rwkv_shift"][0, c * NS:(c + 1) * NS])
        m["st_wkv"] = np.ascontiguousarray(inputs["state_rwkv_wkv"][0, c * NS:(c + 1) * NS])
        m["st_ret"] = np.ascontiguousarray(inputs["state_ret"][0, c * NS:(c + 1) * NS])
        in_maps.append(m)
    return in_maps


def kernel(**inputs):
    nc = build()
    in_maps = make_in_maps(inputs)
    res = run_bass_kernel_spmd(nc, in_maps, core_ids=list(range(N_CORES)))
    R_ = res.results
    cat = lambda nm: np.concatenate([np.asarray(R_[c][nm]) for c in range(N_CORES)], 0)
    stk = lambda nm: np.stack([np.asarray(R_[c][nm]) for c in range(N_CORES)], 0)
    y_prompt = stk("y_prompt").astype(np.float32)
    y_sample = cat("y_sample")[:, None, :].astype(np.float32)
    p_shift = stk("p_shift")[None].astype(np.float32)
    p_wkv = stk("p_wkv")[None].astype(np.float32)
    p_ret = stk("p_ret")[None].astype(np.float32)
    s_shift = cat("s_shift")[None].astype(np.float32)
    s_wkv = cat("s_wkv")[None].astype(np.float32)
    s_ret = cat("s_ret")[None].astype(np.float32)
    return (y_prompt, y_sample, p_shift, p_wkv, p_ret, s_shift, s_wkv, s_ret)
```

```python
import contextlib
import numpy as np
import ml_dtypes
import concourse.bass as bass
import concourse.mybir as mybir
from concourse.bass_utils import run_bass_kernel_spmd

F32 = mybir.dt.float32
BF16 = mybir.dt.bfloat16
AF = mybir.ActivationFunctionType
ALU = mybir.AluOpType
AX = mybir.AxisListType

P = 128
D = 1024
DFF = 2816
NFC = DFF // P
KC = D // P
TP = 2048
NS = 16
T = TP + NS
TILES = [(0, 512), (512, 512), (1024, 512), (1536, 512), (2048, 16)]
NCH = TP // P
RMS_EPS = 1e-6
N_CORES = 8


class Reg:
    __slots__ = ("name", "last_w", "readers")

    def __init__(self, name):
        self.name = name
        self.last_w = None
        self.readers = []


class KB:
    def __init__(self, nc, es, n_dma_sems=48):
        self.nc = nc
        self.es = es
        self.eng = {"pe": nc.tensor, "act": nc.scalar, "dve": nc.vector,
                    "pool": nc.gpsimd, "sp": nc.sync}
        self.sem = {e: es.enter_context(nc.semaphore("c_" + e)) for e in self.eng}
        self.seq = {e: 0 for e in self.eng}
        self.known = {e: {} for e in self.eng}
        self.dma_sems = [es.enter_context(nc.semaphore("d%d" % i)) for i in range(n_dma_sems)]
        self.dma_cnt = [0] * n_dma_sems
        self.dma_pool = {"sw": list(range(0, n_dma_sems // 2)), "hw": list(range(n_dma_sems // 2, n_dma_sems))}
        self.dma_rr = {"sw": 0, "hw": 0}
        self.pe_pending = False
        self.n_wait = 0
        self.n_inst = 0
        self.same_engine_sync = True
        self.RAW_GAP = 10 ** 9

    def _wait(self, e, tok):
        if tok is None:
            return
        kind, who, val = tok
        key = (kind, who)
        if kind == "eng":
            if who == e and (e == "pe" or not self.same_engine_sync):
                return
            if who == e and e == "sp":
                return
        if self.known[e].get(key, 0) >= val:
            return
        self.known[e][key] = val
        sem = self.sem[who] if kind == "eng" else self.dma_sems[who]
        self.eng[e].wait_ge(sem, val)
        self.n_wait += 1

    def _deps(self, e, R, W):
        toks = []
        for r in R:
            toks.append(r.last_w)
        for w in W:
            toks.append(w.last_w)
            toks.extend(w.readers)
        seen = set()
        for t in toks:
            if t is not None and t not in seen:
                seen.add(t)
                self._wait(e, t)

    def _commit(self, tok, R, W):
        for r in R:
            r.readers.append(tok)
            if len(r.readers) > 24:
                best = {}
                for t in r.readers:
                    k = (t[0], t[1])
                    if k not in best or best[k][2] < t[2]:
                        best[k] = t
                r.readers = list(best.values())
        for w in W:
            w.last_w = tok
            w.readers = []

    def op(self, e, fn, R=(), W=(), inc=True):
        self._deps(e, R, W)
        inst = fn()
        self.n_inst += 1
        if inc:
            self.seq[e] += 1
            inst.then_inc(self.sem[e], 1)
            tok = ("eng", e, self.seq[e])
        else:
            assert e == "pe"
            tok = ("eng", e, self.seq[e] + 1)
        self._commit(tok, R, W)
        return tok

    def dma(self, q, out, in_, R=(), W=()):
        self._deps(q, R, W)
        cls = "sw" if q == "pool" else "hw"
        lst = self.dma_pool[cls]
        i = lst[self.dma_rr[cls]]
        self.dma_rr[cls] = (self.dma_rr[cls] + 1) % len(lst)
        if self.dma_cnt[i] > 0:
            self._wait(q, ("dma", i, 16 * self.dma_cnt[i]))
        self.dma_cnt[i] += 1
        self.eng[q].dma_start(out=out, in_=in_).then_inc(self.dma_sems[i], 16)
        self.n_inst += 1
        tok = ("dma", i, 16 * self.dma_cnt[i])
        self._commit(tok, R, W)
        return tok

    def finish(self, regs):
        for r in regs:
            self._wait("sp", r.last_w)


def bcast_mid(ap2, n):
    return ap2.unsqueeze(1).broadcast_to([ap2.shape[0], n, ap2.shape[1]])


def bcast_last(ap2, n):
    return ap2.unsqueeze(2).broadcast_to([ap2.shape[0], ap2.shape[1], n])


ALL_STAGES = ("ffn1_0", "mix_0", "ffn2_0", "ffn1_1", "mix_1", "ffn2_1", "samples")


def build(dbg=None, stages=ALL_STAGES):
    dbg = dbg or {}
    nc = bass.Bass("TRN2", target_bir_lowering=False)
    es = contextlib.ExitStack()
    k = KB(nc, es)

    def din(name, shape, dt=F32):
        return nc.dram_tensor(name, list(shape), dt, kind="ExternalInput").ap()

    def dout(name, shape, dt=F32):
        return nc.dram_tensor(name, list(shape), dt, kind="ExternalOutput").ap()

    xp_d = din("x_prompt", [TP, D])
    xs_d = din("x_sample", [NS, D])
    ln_ffn1_d = din("ln_ffn1", [2, D]); ln_mix_d = din("ln_mix", [2, D]); ln_ffn2_d = din("ln_ffn2", [2, D])
    ln_final_d = din("ln_final", [D])
    ffw = {}
    for nm in ("ff1_wg", "ff1_wu", "ff2_wg", "ff2_wu"):
        ffw[nm] = din(nm, [2, D, DFF])
    for nm in ("ff1_wd", "ff2_wd"):
        ffw[nm] = din(nm, [2, DFF, D])
    ident_d = din("c_ident", [P, P])
    rw = {}
    for nm_, shp_ in [("rw_mu", [6, D]), ("rw_w0", [D]), ("rw_w1", [D, 64]), ("rw_w2", [64, D]), ("rw_a0", [D]),
                      ("rw_a1", [D, 64]), ("rw_a2", [64, D]), ("rw_g1", [D, 128]), ("rw_g2", [128, D]),
                      ("rw_kk", [D]), ("rw_ka", [D]), ("rw_rk", [D]), ("rw_wr", [D, D]), ("rw_wk", [D, D]),
                      ("rw_wv", [D, D]), ("rw_wo", [D, D]), ("rw_lnx_g", [D]), ("rw_lnx_b", [D])]:
        rw[nm_] = din(nm_, shp_)
    rt = {}
    for nm_, shp_ in [("ret_wq", [D, D]), ("ret_wk", [D, D]), ("ret_wv", [D, 2 * D]), ("ret_wg", [D, 2 * D]),
                      ("ret_wo", [2 * D, D])]:
        rt[nm_] = din(nm_, shp_)
    c_cos_d = din("c_cos", [P, T]); c_sin_d = din("c_sin", [P, T])
    c_cosT_d = din("c_cosT", [TP, P]); c_sinT_d = din("c_sinT", [TP, P])
    c_dmatT_d = din("c_dmatT", [P, 4, P]); c_qd_d = din("c_qd", [P, 4, P]); c_kdec_d = din("c_kdec", [P, 4])
    c_sel_d = din("c_sel", [NS, NS, P]); c_i16_d = din("c_i16", [P, NS * NS])
    st_ret_d = din("st_ret", [NS, 4, 256, 512])
    pret_d = dout("p_ret", [4, 256, 512]); sret_d = dout("s_ret", [NS, 4, 256, 512])
    c_bones_d = din("c_bones", [P, P]); c_mask_ur_d = din("c_mask_ur", [P, 256]); c_mask_l_d = din("c_mask_l", [P, P])
    c_scanm_d = din("c_scanm", [P, 256])
    st_shift_d = din("st_shift", [NS, D]); st_wkv_d = din("st_wkv", [NS, 16, 64, 64])
    pshift_d = dout("p_shift", [D]); pwkv_d = dout("p_wkv", [16, 64, 64])
    sshift_d = dout("s_shift", [NS, D]); swkv_d = dout("s_wkv", [NS, 16, 64, 64])
    yp_d = dout("y_prompt", [TP, D])
    ys_d = dout("y_sample", [NS, D])
    out_regs = []

    def sb(name, shape, dt=F32):
        return nc.alloc_sbuf_tensor(name, list(shape), dt)

    AR_BYTES = 101 * 1024 + 512
    arena = sb("arena", [P, AR_BYTES // 2], BF16)
    carry = [[]]

    class Phase:
        def __init__(self, base=0, limit=None, carry_ref=None):
            self.off = base
            self.limit = AR_BYTES if limit is None else limit
            self.regs = []
            self.carry = carry if carry_ref is None else carry_ref

        def alloc(self, shape, dt=F32):
            esz = 4 if dt == F32 else 2
            n = 1
            for s_ in shape[1:]:
                n *= s_
            nb = (n * esz + 63) // 64 * 64
            assert self.off + nb <= self.limit, ("arena overflow", self.off, nb, self.limit)
            v = arena[0:shape[0], self.off // 2:(self.off + n * esz) // 2]
            if dt == F32:
                v = v.bitcast(F32)
            self.off += nb
            if len(shape) == 3:
                v = v.rearrange("p (a b) -> p a b", a=shape[1])
            elif len(shape) == 4:
                v = v.rearrange("p (a b c) -> p a b c", a=shape[1], b=shape[2])
            return v

        def reg(self, name):
            r = Reg(name)
            r.readers = list(self.carry[0])
            self.regs.append(r)
            return r

        def end(self):
            toks = set(self.carry[0])
            for r in self.regs:
                if r.last_w is not None:
                    toks.add(r.last_w)
                toks.update(r.readers)
            best = {}
            for t_ in toks:
                kk_ = (t_[0], t_[1])
                if kk_ not in best or best[kk_][2] < t_[2]:
                    best[kk_] = t_
            self.carry[0] = list(best.values())

    NWB = 5
    WB_ELEMS = 4096
    wbuf = [sb("wbuf%d" % i, [P, WB_ELEMS], BF16) for i in range(NWB)]
    r_wbuf = [Reg("wbuf%d" % i) for i in range(NWB)]
    wb_rr = [0]

    def next_wbuf():
        i = wb_rr[0]
        wb_rr[0] = (i + 1) % NWB
        return wbuf[i], r_wbuf[i]

    ident = sb("ident", [P, P]); r_ident = Reg("ident")
    k.dma("sp", ident[:], ident_d, W=[r_ident])
    identb = sb("identb", [P, P], BF16); r_identb = Reg("identb")
    k.op("dve", lambda: nc.vector.tensor_copy(out=identb[:], in_=ident[:]), R=[r_ident], W=[r_identb])
    ones_b = sb("ones_b", [P, P], BF16); r_ones = Reg("ones")
    k.op("dve", lambda: nc.vector.memset(ones_b[:], 1.0), W=[r_ones])
    eps_t = sb("eps_t", [P, 1]); r_eps = Reg("eps")
    k.op("dve", lambda: nc.vector.memset(eps_t[:], RMS_EPS), W=[r_eps])
    lnw = sb("lnw", [P, 7, KC]); r_lnw = Reg("lnw")
    with nc.allow_non_contiguous_dma(reason="tiny gain vectors"):
        for i, (src_, l) in enumerate([(ln_ffn1_d, 0), (ln_mix_d, 0), (ln_ffn2_d, 0),
                                       (ln_ffn1_d, 1), (ln_mix_d, 1), (ln_ffn2_d, 1)]):
            k.dma("sp", lnw[:, i, :], src_[l].rearrange("(k p) -> p k", p=P), W=[r_lnw])
        k.dma("sp", lnw[:, 6, :], ln_final_d.rearrange("(k p) -> p k", p=P), W=[r_lnw])

    NB = 8
    banks = [nc.alloc_psum_tensor("bank%d" % i, [P, 512], F32) for i in range(NB)]
    r_bank = [Reg("bank%d" % i) for i in range(NB)]
    bank_rr = [0]

    nonlocal_nb = [NB]

    def next_bank():
        i = bank_rr[0] % nonlocal_nb[0]
        bank_rr[0] = (i + 1) % nonlocal_nb[0]
        return banks[i], r_bank[i]

    x = sb("x", [P, KC, T]); r_x = [[Reg("x%d_%d" % (d, t)) for t in range(5)] for d in range(KC)]

    alt = [0]

    def evac_copy(dst_ap, src_ap, R, W, e=None):
        if e is None:
            e = "act" if alt[0] % 2 == 0 else "dve"
            alt[0] += 1
        if e == "act":
            k.op("act", lambda: nc.scalar.copy(out=dst_ap, in_=src_ap), R=R, W=W)
        else:
            k.op("dve", lambda: nc.vector.tensor_copy(out=dst_ap, in_=src_ap), R=R, W=W)

    lx_carry = [[]]

    def load_x():
        ph = Phase(base=AR_BYTES - 2 * D * 4, carry_ref=lx_carry)
        xin = [ph.alloc([P, D]) for _ in range(2)]
        r_xin = [ph.reg("xin0"), ph.reg("xin1")]
        ld = 0
        for ti, (t0, tn) in enumerate(TILES):
            nsub = (tn + P - 1) // P
            for s in range(nsub):
                rows = min(P, tn - s * P)
                b = ld % 2; ld += 1
                src_ = xp_d[t0 + s * P: t0 + s * P + rows, :] if ti < 4 else xs_d[:, :]
                k.dma("sp", xin[b][0:rows, :], src_, W=[r_xin[b]])
                for d0 in range(0, KC, 4):
                    bk, rb = next_bank()
                    for j in range(4):
                        d = d0 + j
                        k.op("pe", lambda d=d, j=j: nc.tensor.transpose(
                            bk[:, j * P: j * P + rows], xin[b][0:rows, d * P:(d + 1) * P], ident[0:rows, 0:rows]),
                            R=[r_xin[b], r_ident], W=[rb])
                    c0 = t0 + s * P
                    src_ap = bk[:, :].rearrange("p (j c) -> p j c", j=4)[:, :, 0:rows]
                    evac_copy(x[:, d0:d0 + 4, c0:c0 + rows], src_ap, R=[rb],
                              W=[r_x[d][ti] for d in range(d0, d0 + 4)])
        ph.end()

    def rmsnorm_tile(ph_bufs, gi, ti, out_ap_fn, W_fn, rng=None, extra=None):
        sqs, r_sqs, rstd, r_rstd, cnt = ph_bufs
        t0, tn = TILES[ti] if rng is None else rng
        bk, rb = next_bank()
        for d in range(KC):
            b = cnt[0] % len(sqs); cnt[0] += 1
            k.op("act", lambda: nc.scalar.activation(
                out=sqs[b][:, 0:tn], in_=x[:, d, t0:t0 + tn], func=AF.Square),
                R=[r_x[d][ti]], W=[r_sqs[b]])
            k.op("pe", lambda: nc.tensor.matmul(
                bk[:, 0:tn], lhsT=ones_b[:], rhs=sqs[b][:, 0:tn], start=(d == 0), stop=(d == KC - 1)),
                R=[r_sqs[b], r_ones], W=[rb], inc=True)
        rb_i = cnt[1] % 2; cnt[1] += 1
        k.op("act", lambda: nc.scalar.activation(
            out=rstd[rb_i][:, 0:tn], in_=bk[:, 0:tn], func=AF.Ln, scale=1.0 / D, bias=eps_t[:, 0:1]),
            R=[rb, r_eps], W=[r_rstd[rb_i]])
        k.op("act", lambda: nc.scalar.activation(
            out=rstd[rb_i][:, 0:tn], in_=rstd[rb_i][:, 0:tn], func=AF.Exp, scale=-0.5),
            R=[r_rstd[rb_i]], W=[r_rstd[rb_i]])
        for d in range(KC):
            k.op("dve", lambda: nc.vector.scalar_tensor_tensor(
                out=out_ap_fn(d, t0, tn), in0=x[:, d, t0:t0 + tn], scalar=lnw[:, gi, d:d + 1],
                in1=rstd[rb_i][:, 0:tn], op0=ALU.mult, op1=ALU.mult),
                R=[r_x[d][ti], r_rstd[rb_i], r_lnw], W=W_fn(d))
            if extra is not None:
                extra(d, rstd[rb_i], r_rstd[rb_i])

    def norm_bufs(ph, width=512):
        sqs = [ph.alloc([P, width], BF16) for _ in range(4)]
        r_sqs = [ph.reg("sq%d" % i) for i in range(4)]
        rstd = [ph.alloc([P, width]) for _ in range(2)]
        r_rstd = [ph.reg("rstd%d" % i) for i in range(2)]
        return (sqs, r_sqs, rstd, r_rstd, [0, 0])

    FB = 4

    def ffn(l, which):
        ph = Phase()
        nb = norm_bufs(ph)
        h = ph.alloc([P, KC, T], BF16)
        r_h = [[ph.reg("h%d_%d" % (d, t)) for t in range(5)] for d in range(KC)]
        a_t = ph.alloc([P, FB, T], BF16)
        r_a = [[ph.reg("a%d_%d" % (f, t)) for t in range(5)] for f in range(FB)]
        sl_t = [ph.alloc([P, 512]) for _ in range(2)]
        r_sl = [ph.reg("sl0"), ph.reg("sl1")]
        gi = {1: 0, 2: 2}[which] + 3 * l
        wg_d = ffw["ff%d_wg" % which][l].rearrange("(k p) c -> p k c", p=P)
        wu_d = ffw["ff%d_wu" % which][l].rearrange("(k p) c -> p k c", p=P)
        wd_d = ffw["ff%d_wd" % which][l].rearrange("(f p) c -> p f c", p=P)
        for ti in range(5):
            rmsnorm_tile(nb, gi, ti, lambda d, t0, tn: h[:, d, t0:t0 + tn], lambda d, ti=ti: [r_h[d][ti]])
        nblk = (NFC + FB - 1) // FB
        sl_i = 0
        for blk in range(nblk):
            f0 = blk * FB
            nf = min(FB, NFC - f0)
            wgb, r_wg = next_wbuf()
            wub, r_wu = next_wbuf()
            wdb, r_wd = next_wbuf()
            wg_v = wgb[:, 0:KC * nf * P].rearrange("p (k c) -> p k c", k=KC)
            wu_v = wub[:, 0:KC * nf * P].rearrange("p (k c) -> p k c", k=KC)
            wd_v = wdb[:, 0:nf * D].rearrange("p (f c) -> p f c", f=nf)
            k.dma("pool", wg_v, wg_d[:, :, f0 * P:(f0 + nf) * P], W=[r_wg])
            k.dma("pool", wu_v, wu_d[:, :, f0 * P:(f0 + nf) * P], W=[r_wu])
            k.dma("pool", wd_v, wd_d[:, f0:f0 + nf, :], W=[r_wd])
            for f in range(nf):
                for ti, (t0, tn) in enumerate(TILES):
                    bg, rbg = next_bank()
                    bu, rbu = next_bank()
                    for kk in range(KC):
                        k.op("pe", lambda: nc.tensor.matmul(
                            bg[:, 0:tn], lhsT=wg_v[:, kk, f * P:(f + 1) * P], rhs=h[:, kk, t0:t0 + tn],
                            start=(kk == 0), stop=(kk == KC - 1)),
                            R=[r_wg, r_h[kk][ti]], W=[rbg], inc=(kk == KC - 1))
                    for kk in range(KC):
                        k.op("pe", lambda: nc.tensor.matmul(
                            bu[:, 0:tn], lhsT=wu_v[:, kk, f * P:(f + 1) * P], rhs=h[:, kk, t0:t0 + tn],
                            start=(kk == 0), stop=(kk == KC - 1)),
                            R=[r_wu, r_h[kk][ti]], W=[rbu], inc=(kk == KC - 1))
                    sb_i = sl_i % 2; sl_i += 1
                    k.op("act", lambda: nc.scalar.activation(
                        out=sl_t[sb_i][:, 0:tn], in_=bg[:, 0:tn], func=AF.Silu),
                        R=[rbg], W=[r_sl[sb_i]])
                    k.op("dve", lambda: nc.vector.tensor_tensor(
                        out=a_t[:, f, t0:t0 + tn], in0=bu[:, 0:tn], in1=sl_t[sb_i][:, 0:tn], op=ALU.mult),
                        R=[rbu, r_sl[sb_i]], W=[r_a[f][ti]])
            for d in range(KC):
                for ti, (t0, tn) in enumerate(TILES):
                    bk, rb = next_bank()
                    for f in range(nf):
                        k.op("pe", lambda: nc.tensor.matmul(
                            bk[:, 0:tn], lhsT=wd_v[:, f, d * P:(d + 1) * P], rhs=a_t[:, f, t0:t0 + tn],
                            start=(f == 0), stop=(f == nf - 1)),
                            R=[r_wd, r_a[f][ti]], W=[rb], inc=(f == nf - 1))
                    k.op("dve", lambda: nc.vector.scalar_tensor_tensor(
                        out=x[:, d, t0:t0 + tn], in0=bk[:, 0:tn], scalar=0.5, in1=x[:, d, t0:t0 + tn],
                        op0=ALU.mult, op1=ALU.add),
                        R=[rb, r_x[d][ti]], W=[r_x[d][ti]])
        ph.end()


    def rwkv_mixer(do_samples=True):
        ph = Phase()
        SEG = 256
        C0 = -float(np.exp(-0.5))
        mm = nc.tensor.matmul
        rv = ph.alloc([P, 14, KC]); r_rv = ph.reg("rv")
        with nc.allow_non_contiguous_dma(reason="tiny per-feature vectors"):
            for i in range(6):
                k.dma("sp", rv[:, i, :], rw["rw_mu"][i].rearrange("(k p) -> p k", p=P), W=[r_rv])
            for i, nm in enumerate(["rw_w0", "rw_a0", "rw_kk", "rw_ka", "rw_rk", "rw_lnx_g", "rw_lnx_b"]):
                k.dma("sp", rv[:, 6 + i, :], rw[nm].rearrange("(k p) -> p k", p=P), W=[r_rv])
        k.op("dve", lambda: nc.vector.tensor_scalar(out=rv[:, 13, :], in0=rv[:, 9, :], scalar1=-1.0, scalar2=1.0,
                                                    op0=ALU.mult, op1=ALU.add), R=[r_rv], W=[r_rv])
        w1b = ph.alloc([P, KC, 64], BF16); w2b = ph.alloc([64, D], BF16)
        a1b = ph.alloc([P, KC, 64], BF16); a2b = ph.alloc([64, D], BF16)
        g1b = ph.alloc([P, KC, 128], BF16); g2b = ph.alloc([P, D], BF16)
        r_lw = ph.reg("lora")
        k.dma("pool", w1b, rw["rw_w1"].rearrange("(k p) c -> p k c", p=P), W=[r_lw])
        k.dma("pool", w2b, rw["rw_w2"], W=[r_lw])
        k.dma("pool", a1b, rw["rw_a1"].rearrange("(k p) c -> p k c", p=P), W=[r_lw])
        k.dma("pool", a2b, rw["rw_a2"], W=[r_lw])
        k.dma("pool", g1b, rw["rw_g1"].rearrange("(k p) c -> p k c", p=P), W=[r_lw])
        k.dma("pool", g2b, rw["rw_g2"], W=[r_lw])
        mask_ur = ph.alloc([P, 256], BF16); mask_l = ph.alloc([P, P], BF16)
        bones = ph.alloc([P, P], BF16); bones64 = ph.alloc([P, P], BF16)
        scanm = ph.alloc([P, 256]); scanz = ph.alloc([P, NS]); eps_gn = ph.alloc([P, 1])
        r_cst = ph.reg("rwconst")
        k.dma("pool", mask_ur, c_mask_ur_d, W=[r_cst])
        k.dma("pool", mask_l, c_mask_l_d, W=[r_cst])
        k.dma("pool", bones, c_bones_d, W=[r_cst])
        k.dma("sp", scanm, c_scanm_d, W=[r_cst])
        k.op("dve", lambda: nc.vector.tensor_scalar(out=bones64, in0=bones, scalar1=1.0 / 64, scalar2=None,
                                                    op0=ALU.mult), R=[r_cst], W=[r_cst])
        k.op("dve", lambda: nc.vector.memset(scanz, 0.0), W=[r_cst])
        k.op("dve", lambda: nc.vector.memset(eps_gn, 64e-5), W=[r_cst])
        H32 = ph.alloc([P, KC, 64]); Hbd = ph.alloc([P, KC, P], BF16)
        r_H = [ph.reg("H%d" % g) for g in range(4)]
        k.op("dve", lambda: nc.vector.memset(H32, 0.0), W=r_H)
        k.op("dve", lambda: nc.vector.memset(Hbd, 0.0), W=r_H)
        shf = ph.alloc([P, KC, 1 + NS]); r_shf = ph.reg("shf")
        k.op("dve", lambda: nc.vector.memset(shf, 0.0), W=[r_shf])
        hcar = ph.alloc([P, KC, 1], BF16); r_hcar = ph.reg("hcar")
        k.op("dve", lambda: nc.vector.memset(hcar, 0.0), W=[r_hcar])
        hsp = ph.alloc([P, KC, NS], BF16); r_hsp = ph.reg("hsp")
        if do_samples:
            stg = ph.alloc([NS, D]); r_stg = ph.reg("stg")
            k.dma("sp", stg, st_shift_d, W=[r_stg])
            for half in range(2):
                bk, rb = next_bank()
                for jj in range(4):
                    d = half * 4 + jj
                    k.op("pe", lambda: nc.tensor.transpose(bk[:, jj * NS:(jj + 1) * NS], stg[0:NS, d * P:(d + 1) * P], ident[0:NS, 0:NS]),
                         R=[r_stg, r_ident], W=[rb])
                evac_copy(hsp[:, half * 4:half * 4 + 4, :], bk[:, 0:4 * NS].rearrange("p (a b) -> p a b", a=4), R=[rb], W=[r_hsp])
        ar = ph.alloc([P, KC, 2, SEG], BF16); r_ar = [ph.reg("ar%d" % d) for d in range(KC)]
        bt = ph.alloc([P, KC, SEG], BF16); r_bt = [ph.reg("bt%d" % d) for d in range(KC)]
        kt = ph.alloc([P, KC, SEG], BF16); r_kt = [ph.reg("kt%d" % d) for d in range(KC)]
        vt = ph.alloc([P, KC, SEG], BF16); r_vt = [ph.reg("vt%d" % d) for d in range(KC)]
        gt = ph.alloc([P, KC, SEG], BF16); r_gt = [ph.reg("gt%d" % d) for d in range(KC)]
        bon = ph.alloc([P, KC, SEG], BF16); r_bon = [ph.reg("bon%d" % d) for d in range(KC)]
        og = ph.alloc([P, KC, SEG], BF16); r_og = ph.reg("og")
        gC = ph.alloc([P, KC, 2]); r_gC = ph.reg("gC")
        dS = ph.alloc([P, KC, NS]); r_dS = ph.reg("dS")
        ov_base = ph.off
        ovc = [list(carry[0])]
        ybk = [banks[6], banks[7]]; r_ybk = [r_bank[6], r_bank[7]]
        nonlocal_nb[0] = 6

        def stageA(t0, n, ti, sample, last_prompt):
            pa = Phase(base=ov_base, carry_ref=ovc)
            nbuf = norm_bufs(pa, SEG)
            hn = pa.alloc([P, KC, SEG + 2], BF16); r_hn = [pa.reg("hn%d" % d) for d in range(KC)]
            dx = pa.alloc([P, KC, SEG], BF16); r_dx = pa.reg("dx")
            xi = [pa.alloc([P, KC, SEG], BF16) for _ in range(2)]; r_xi = [pa.reg("xi0"), pa.reg("xi1")]
            lt = [pa.alloc([P, SEG], BF16) for _ in range(3)]; r_lt = [pa.reg("lt%d" % i) for i in range(3)]
            T32 = [[pa.alloc([P, SEG]) for _ in range(4)] for _ in range(4)]
            rT32 = [[pa.reg("t32_%d_%d" % (i, j)) for j in range(4)] for i in range(4)]
            T16 = [[pa.alloc([P, SEG], BF16) for _ in range(4)] for _ in range(3)]
            rT16 = [[pa.reg("t16_%d_%d" % (i, j)) for j in range(4)] for i in range(3)]
            if not sample:
                k.op("dve", lambda: nc.vector.tensor_copy(out=hn[:, :, 0:1], in_=hcar[:, :, 0:1]), R=[r_hcar], W=r_hn)

            def extra(d, rstd_t, r_rstd_t):
                if last_prompt:
                    k.op("dve", lambda: nc.vector.scalar_tensor_tensor(
                        out=shf[:, d, 0:1], in0=x[:, d, t0 + n - 1:t0 + n], scalar=lnw[:, 1, d:d + 1],
                        in1=rstd_t[:, n - 1:n], op0=ALU.mult, op1=ALU.mult),
                        R=[r_x[d][ti], r_rstd_t, r_lnw], W=[r_shf])
                if sample:
                    k.op("dve", lambda: nc.vector.scalar_tensor_tensor(
                        out=shf[:, d, 1:1 + NS], in0=x[:, d, t0:t0 + n], scalar=lnw[:, 1, d:d + 1],
                        in1=rstd_t[:, 0:n], op0=ALU.mult, op1=ALU.mult),
                        R=[r_x[d][ti], r_rstd_t, r_lnw], W=[r_shf])
            rmsnorm_tile(nbuf, 1, ti, lambda d, a_, b_: hn[:, d, 1:1 + n], lambda d: [r_hn[d]], rng=(t0, n), extra=extra)
            hprev = hsp[:, :, 0:n] if sample else hn[:, :, 0:n]
            k.op("dve", lambda: nc.vector.tensor_tensor(out=dx[:, :, 0:n], in0=hprev, in1=hn[:, :, 1:1 + n], op=ALU.subtract),
                 R=r_hn + ([r_hsp] if sample else []), W=[r_dx])
            if not sample:
                k.op("dve", lambda: nc.vector.tensor_copy(out=hcar[:, :, 0:1], in_=hn[:, :, n:n + 1]), R=r_hn, W=[r_hcar])
            xi_i = [0]

            def make_xi(i):
                b = xi_i[0] % 2; xi_i[0] += 1
                for d in range(KC):
                    k.op("dve", lambda: nc.vector.scalar_tensor_tensor(
                        out=xi[b][:, d, 0:n], in0=dx[:, d, 0:n], scalar=rv[:, i, d:d + 1], in1=hn[:, d, 1:1 + n],
                        op0=ALU.mult, op1=ALU.add), R=[r_dx, r_hn[d], r_rv], W=[r_xi[b]])
                return xi[b], r_xi[b]

            def proj_big(wd, xt, r_xt, evac):
                wv_ = wd.rearrange("(k p) c -> p k c", p=P)
                for half in range(2):
                    wb, r_w = next_wbuf()
                    v = wb[:, 0:KC * 512].rearrange("p (k c) -> p k c", k=KC)
                    k.dma("pool", v, wv_[:, :, half * 512:(half + 1) * 512], W=[r_w])
                    for oc in range(4):
                        d = half * 4 + oc
                        bk, rb = next_bank()
                        for kk in range(KC):
                            k.op("pe", lambda: mm(bk[:, 0:n], lhsT=v[:, kk, oc * P:(oc + 1) * P], rhs=xt[:, kk, 0:n],
                                                  start=(kk == 0), stop=(kk == KC - 1)),
                                 R=[r_w, r_xt], W=[rb], inc=(kk == KC - 1))
                        evac(d, bk[:, 0:n], rb)

            def lora_in(wl, m, xt, r_xt, func, li):
                bk, rb = next_bank()
                for kk in range(KC):
                    k.op("pe", lambda: mm(bk[0:m, 0:n], lhsT=wl[:, kk, 0:m], rhs=xt[:, kk, 0:n],
                                          start=(kk == 0), stop=(kk == KC - 1)),
                         R=[r_lw, r_xt], W=[rb], inc=(kk == KC - 1))
                k.op("act", lambda: nc.scalar.activation(out=lt[li][0:m, 0:n], in_=bk[0:m, 0:n], func=func),
                     R=[rb], W=[r_lt[li]])

            def lora_out(wl2, m, li, d):
                bk, rb = next_bank()
                k.op("pe", lambda: mm(bk[:, 0:n], lhsT=wl2[0:m, d * P:(d + 1) * P], rhs=lt[li][0:m, 0:n],
                                      start=True, stop=True), R=[r_lw, r_lt[li]], W=[rb])
                return bk, rb

            xt, r_xt = make_xi(0)
            proj_big(rw["rw_wr"], xt, r_xt, lambda d, ps, rb: k.op(
                "act", lambda: nc.scalar.copy(out=ar[:, d, 1, 0:n], in_=ps), R=[rb], W=[r_ar[d]]))
            xt, r_xt = make_xi(2)
            proj_big(rw["rw_wk"], xt, r_xt, lambda d, ps, rb: evac_copy(kt[:, d, 0:n], ps, R=[rb], W=[r_kt[d]]))
            xt, r_xt = make_xi(4)
            lora_in(a1b, 64, xt, r_xt, AF.Copy, 0)
            for d in range(KC):
                bk, rb = lora_out(a2b, 64, 0, d)
                k.op("act", lambda: nc.scalar.activation(out=bt[:, d, 0:n], in_=bk[:, 0:n], func=AF.Sigmoid,
                                                         bias=rv[:, 7, d:d + 1]), R=[rb, r_rv], W=[r_bt[d]])
            GD = 4

            def staged(steps):
                for g0 in range(0, KC, GD):
                    st = {}
                    for step in steps:
                        for d in range(g0, g0 + GD):
                            step(d, st)

            def s_kr(d, st):
                q = d % GD
                k.op("dve", lambda: nc.vector.tensor_scalar(out=T16[0][q][:, 0:n], in0=kt[:, d, 0:n], scalar1=rv[:, 8, d:d + 1],
                                                            scalar2=None, op0=ALU.mult), R=[r_kt[d], r_rv], W=[rT16[0][q]])

            def s_sq(d, st):
                q = d % GD
                k.op("act", lambda: nc.scalar.activation(out=T16[1][q][:, 0:n], in_=T16[0][q][:, 0:n], func=AF.Square),
                     R=[rT16[0][q]], W=[rT16[1][q]])

            def s_ss(d, st):
                q = d % GD
                bk, rb = next_bank()
                st[("ss", d)] = (bk, rb)
                k.op("pe", lambda: mm(bk[:, 0:n], lhsT=bones, rhs=T16[1][q][:, 0:n], start=True, stop=True),
                     R=[r_cst, rT16[1][q]], W=[rb])

            def s_max(d, st):
                q = d % GD
                bk, rb = st[("ss", d)]
                k.op("dve", lambda: nc.vector.tensor_scalar(out=T32[0][q][:, 0:n], in0=bk[:, 0:n], scalar1=1e-24, scalar2=None,
                                                            op0=ALU.max), R=[rb], W=[rT32[0][q]])

            def s_ln(d, st):
                q = d % GD
                k.op("act", lambda: nc.scalar.activation(out=T32[0][q][:, 0:n], in_=T32[0][q][:, 0:n], func=AF.Ln),
                     R=[rT32[0][q]], W=[rT32[0][q]])

            def s_ex(d, st):
                q = d % GD
                k.op("act", lambda: nc.scalar.activation(out=T32[0][q][:, 0:n], in_=T32[0][q][:, 0:n], func=AF.Exp, scale=-0.5),
                     R=[rT32[0][q]], W=[rT32[0][q]])

            def s_kk(d, st):
                q = d % GD
                k.op("dve", lambda: nc.vector.tensor_tensor(out=ar[:, d, 0, 0:n], in0=T16[0][q][:, 0:n], in1=T32[0][q][:, 0:n], op=ALU.mult),
                     R=[rT16[0][q], rT32[0][q]], W=[r_ar[d]])
                k.op("dve", lambda: nc.vector.tensor_scalar(out=T32[1][q][:, 0:n], in0=bt[:, d, 0:n], scalar1=rv[:, 9, d:d + 1],
                                                            scalar2=rv[:, 13, d:d + 1], op0=ALU.mult, op1=ALU.add),
                     R=[r_bt[d], r_rv], W=[rT32[1][q]])

            def s_kb(d, st):
                q = d % GD
                k.op("dve", lambda: nc.vector.tensor_tensor(out=kt[:, d, 0:n], in0=kt[:, d, 0:n], in1=T32[1][q][:, 0:n], op=ALU.mult),
                     R=[rT32[1][q], r_kt[d]], W=[r_kt[d]])
                k.op("dve", lambda: nc.vector.tensor_tensor(out=bt[:, d, 0:n], in0=bt[:, d, 0:n], in1=ar[:, d, 0, 0:n], op=ALU.mult),
                     R=[r_ar[d], r_bt[d]], W=[r_bt[d]])

            def s_bq(d, st):
                q = d % GD
                k.op("dve", lambda: nc.vector.scalar_tensor_tensor(
                    out=T16[2][q][:, 0:n], in0=ar[:, d, 1, 0:n], scalar=rv[:, 10, d:d + 1], in1=kt[:, d, 0:n],
                    op0=ALU.mult, op1=ALU.mult), R=[r_ar[d], r_kt[d], r_rv], W=[rT16[2][q]])

            def s_bs(d, st):
                q = d % GD
                bk2, rb2 = next_bank()
                st[("bs", d)] = (bk2, rb2)
                k.op("pe", lambda: mm(bk2[:, 0:n], lhsT=bones, rhs=T16[2][q][:, 0:n], start=True, stop=True),
                     R=[r_cst, rT16[2][q]], W=[rb2])

            def s_bon(d, st):
                bk2, rb2 = st[("bs", d)]
                k.op("act", lambda: nc.scalar.copy(out=bon[:, d, 0:n], in_=bk2[:, 0:n]), R=[rb2], W=[r_bon[d]])

            staged([s_kr, s_sq, s_ss, s_max, s_ln, s_ex, s_kk, s_kb, s_bq, s_bs, s_bon])

            xt, r_xt = make_xi(1)
            lora_in(w1b, 64, xt, r_xt, AF.Tanh, 1)
            smask = scanz[:, 0:n] if sample else scanm[:, 0:n]
            SG, CW, E1, E2 = T32[0], T32[1], T32[2], T32[3]
            rSG, rCW, rE1, rE2 = rT32[0], rT32[1], rT32[2], rT32[3]

            def t_mm(d, st):
                st[("w", d)] = lora_out(w2b, 64, 1, d)

            def t_sig(d, st):
                q = d % GD
                bk, rb = st[("w", d)]
                k.op("act", lambda: nc.scalar.activation(out=SG[q][:, 0:n], in_=bk[:, 0:n], func=AF.Sigmoid,
                                                         bias=rv[:, 6, d:d + 1]), R=[rb, r_rv], W=[rSG[q]])

            def t_scan(d, st):
                q = d % GD
                k.op("dve", lambda: nc.vector.tensor_tensor_scan(out=CW[q][:, 0:n], data0=smask, data1=SG[q][:, 0:n], initial=0.0,
                                                                 op0=ALU.mult, op1=ALU.add), R=[rSG[q], r_cst], W=[rCW[q]])
                k.op("dve", lambda: nc.vector.tensor_tensor(out=SG[q][:, 0:n], in0=CW[q][:, 0:n], in1=SG[q][:, 0:n], op=ALU.subtract),
                     R=[rCW[q], rSG[q]], W=[rSG[q]])

            def t_e1(d, st):
                q = d % GD
                k.op("act", lambda: nc.scalar.activation(out=E1[q][:, 0:n], in_=CW[q][:, 0:n], func=AF.Exp, scale=C0),
                     R=[rCW[q]], W=[rE1[q]])
                k.op("act", lambda: nc.scalar.activation(out=E2[q][:, 0:n], in_=SG[q][:, 0:n], func=AF.Exp, scale=C0),
                     R=[rSG[q]], W=[rE2[q]])
                k.op("act", lambda: nc.scalar.activation(out=SG[q][:, 0:n], in_=CW[q][:, 0:n], func=AF.Exp, scale=-C0),
                     R=[rCW[q], rSG[q]], W=[rSG[q]])

            def t_apply(d, st):
                q = d % GD
                k.op("dve", lambda: nc.vector.tensor_tensor(out=ar[:, d, 1, 0:n], in0=ar[:, d, 1, 0:n], in1=E1[q][:, 0:n], op=ALU.mult),
                     R=[rE1[q], r_ar[d]], W=[r_ar[d]])
                if sample:
                    k.op("dve", lambda: nc.vector.tensor_copy(out=dS[:, d, 0:n], in_=E1[q][:, 0:n]), R=[rE1[q]], W=[r_dS])
                else:
                    for c in range(n // P):
                        k.op("dve", lambda: nc.vector.tensor_copy(out=gC[:, d, c:c + 1], in_=E1[q][:, c * P + P - 1:c * P + P]),
                             R=[rE1[q]], W=[r_gC])
                k.op("dve", lambda: nc.vector.scalar_tensor_tensor(
                    out=ar[:, d, 0, 0:n], in0=ar[:, d, 0, 0:n], scalar=-1.0, in1=E2[q][:, 0:n], op0=ALU.mult, op1=ALU.mult),
                    R=[rE2[q], r_ar[d]], W=[r_ar[d]])
                k.op("dve", lambda: nc.vector.tensor_tensor(out=bt[:, d, 0:n], in0=bt[:, d, 0:n], in1=SG[q][:, 0:n], op=ALU.mult),
                     R=[rSG[q], r_bt[d]], W=[r_bt[d]])
                k.op("dve", lambda: nc.vector.tensor_tensor(out=kt[:, d, 0:n], in0=kt[:, d, 0:n], in1=SG[q][:, 0:n], op=ALU.mult),
                     R=[rSG[q], r_kt[d]], W=[r_kt[d]])

            staged([t_mm, t_sig, t_scan, t_e1, t_apply])
            xt, r_xt = make_xi(3)
            proj_big(rw["rw_wv"], xt, r_xt, lambda d, ps, rb: evac_copy(vt[:, d, 0:n], ps, R=[rb], W=[r_vt[d]]))
            xt, r_xt = make_xi(5)
            lora_in(g1b, 128, xt, r_xt, AF.Sigmoid, 2)
            for d in range(KC):
                bk, rb = lora_out(g2b, 128, 2, d)
                evac_copy(gt[:, d, 0:n], bk[:, 0:n], R=[rb], W=[r_gt[d]])
            pa.end()

        def post(pb_tiles, ysrc_fn, c0, w):
            for _ in post_gen(pb_tiles, ysrc_fn, c0, w, None):
                pass

        def post_gen(pb_tiles, ysrc_fn, c0, w, done_flag):
            y32s, r_y32s, ybs, r_ybs, sqbs, r_sqbs, rss, r_rss = pb_tiles
            for d in range(KC):
                if d > 0:
                    yield
                q = d % 2
                y32, r_y32 = y32s[q], r_y32s[q]
                yb, r_yb = ybs[q], r_ybs[q]
                sqb, r_sqb = sqbs[q], r_sqbs[q]
                rs, r_rs = rss[q], r_rss[q]
                ysrc, r_ysrc = ysrc_fn(d)
                k.op("act", lambda: nc.scalar.copy(out=y32[:, 0:w], in_=ysrc), R=[r_ysrc], W=[r_y32])
                k.op("dve", lambda: nc.vector.tensor_copy(out=yb[:, 0:w], in_=ysrc), R=[r_ysrc], W=[r_yb])
                bk, rb = next_bank()
                k.op("pe", lambda: mm(bk[:, 0:w], lhsT=bones64, rhs=yb[:, 0:w], start=True, stop=True),
                     R=[r_cst, r_yb], W=[rb])
                k.op("dve", lambda: nc.vector.tensor_tensor(out=y32[:, 0:w], in0=y32[:, 0:w], in1=bk[:, 0:w], op=ALU.subtract),
                     R=[rb, r_y32], W=[r_y32])
                k.op("act", lambda: nc.scalar.activation(out=sqb[:, 0:w], in_=y32[:, 0:w], func=AF.Square),
                     R=[r_y32], W=[r_sqb])
                bk2, rb2 = next_bank()
                k.op("pe", lambda: mm(bk2[:, 0:w], lhsT=bones64, rhs=sqb[:, 0:w], start=True, stop=True),
                     R=[r_cst, r_sqb], W=[rb2])
                k.op("act", lambda: nc.scalar.activation(out=rs[:, 0:w], in_=bk2[:, 0:w], func=AF.Ln, bias=eps_gn[:, 0:1]),
                     R=[rb2, r_cst], W=[r_rs])
                k.op("act", lambda: nc.scalar.activation(out=rs[:, 0:w], in_=rs[:, 0:w], func=AF.Exp, scale=-0.5), R=[r_rs], W=[r_rs])
                k.op("dve", lambda: nc.vector.tensor_tensor(out=y32[:, 0:w], in0=y32[:, 0:w], in1=rs[:, 0:w], op=ALU.mult),
                     R=[r_rs, r_y32], W=[r_y32])
                k.op("dve", lambda: nc.vector.tensor_scalar(out=y32[:, 0:w], in0=y32[:, 0:w], scalar1=rv[:, 11, d:d + 1],
                                                            scalar2=rv[:, 12, d:d + 1], op0=ALU.mult, op1=ALU.add),
                     R=[r_y32, r_rv], W=[r_y32])
                k.op("dve", lambda: nc.vector.tensor_tensor(out=rs[:, 0:w], in0=bon[:, d, c0:c0 + w], in1=vt[:, d, c0:c0 + w], op=ALU.mult),
                     R=[r_bon[d], r_vt[d]], W=[r_rs])
                k.op("dve", lambda: nc.vector.tensor_tensor(out=y32[:, 0:w], in0=y32[:, 0:w], in1=rs[:, 0:w], op=ALU.add),
                     R=[r_rs, r_y32], W=[r_y32])
                k.op("dve", lambda: nc.vector.tensor_tensor(out=og[:, d, c0:c0 + w], in0=y32[:, 0:w], in1=gt[:, d, c0:c0 + w], op=ALU.mult),
                     R=[r_y32, r_gt[d]], W=[r_og])
            if done_flag is not None:
                done_flag[0] = True

        def post_tiles(pb):
            y32s = [pb.alloc([P, P]) for _ in range(2)]; r_y32s = [pb.reg("y32a"), pb.reg("y32b")]
            ybs = [pb.alloc([P, P], BF16) for _ in range(2)]; r_ybs = [pb.reg("yba"), pb.reg("ybb")]
            sqbs = [pb.alloc([P, P], BF16) for _ in range(2)]; r_sqbs = [pb.reg("sqba"), pb.reg("sqbb")]
            rss = [pb.alloc([P, P]) for _ in range(2)]; r_rss = [pb.reg("rsa"), pb.reg("rsb")]
            return (y32s, r_y32s, ybs, r_ybs, sqbs, r_sqbs, rss, r_rss)

        def run_interleaved(gens):
            gens = list(gens)
            while gens:
                for g_ in list(gens):
                    try:
                        next(g_)
                    except StopIteration:
                        gens.remove(g_)

        def stageB(n):
            pb = Phase(base=ov_base, carry_ref=ovc)
            btok = pb.alloc([P, D], BF16); ktok = pb.alloc([P, D], BF16); vtok = pb.alloc([P, D], BF16)
            r_tok = [pb.reg("btok"), pb.reg("ktok"), pb.reg("vtok")]
            G = 4
            sets = []
            for si in range(4):
                S_ = {}
                S_["Nb"] = pb.alloc([P, G, 256], BF16); S_["Nk"] = pb.alloc([P, G, 256], BF16)
                S_["r_Nb"] = pb.reg("Nb%d" % si); S_["r_Nk"] = pb.reg("Nk%d" % si)
                _x = pb.alloc([P, G, P], BF16); _rx = pb.reg("X_%d" % si)
                _y = pb.alloc([P, G, P], BF16); _ry = pb.reg("Y_%d" % si)
                _p = pb.alloc([P, G, P], BF16); _rp = pb.reg("P_%d" % si)
                S_["Xs"] = [_x, _x]; S_["r_Xs"] = [_rx, _rx]
                S_["Ys"] = [_y, _y]; S_["r_Ys"] = [_ry, _ry]
                S_["Ps"] = [_p, _p]; S_["r_Ps"] = [_rp, _rp]
                S_["Wt"] = pb.alloc([P, 2, P], BF16); S_["r_Wt"] = pb.reg("Wt%d" % si)
                S_["Ut"] = pb.alloc([P, 2, P], BF16); S_["r_Ut"] = pb.reg("Ut%d" % si)
                S_["tH"] = pb.alloc([P, 2, 64]); S_["r_tH"] = pb.reg("tH%d" % si)
                sets.append(S_)
            ptl = post_tiles(pb)
            ysb = pb.alloc([P, KC, P]); r_ysb = [pb.reg("ysb%d" % g_) for g_ in range(4)]
            nonlocal_nb[0] = 8
            post_done = [True]
            mur2 = mask_ur.unsqueeze(1).broadcast_to([P, 2, 256])
            ml2 = mask_l.unsqueeze(1).broadcast_to([P, 2, P])
            id4 = identb[:, :].unsqueeze(1).broadcast_to([P, G, P])

            def group_gen(c, g, S_):
                cs = c * P
                Nb, Nk, r_Nb, r_Nk = S_["Nb"], S_["Nk"], S_["r_Nb"], S_["r_Nk"]
                Xs, Ys, Ps, r_Xs, r_Ys, r_Ps = S_["Xs"], S_["Ys"], S_["Ps"], S_["r_Xs"], S_["r_Ys"], S_["r_Ps"]
                Wt, Ut, tH, r_Wt, r_Ut, r_tH = S_["Wt"], S_["Ut"], S_["tH"], S_["r_Wt"], S_["r_Ut"], S_["r_tH"]
                j0 = 2 * g
                units = [(j0 + u % 2, u // 2) for u in range(G)]
                sbk = [next_bank() for _ in range(2)]
                for u, (j, hp) in enumerate(units):
                    ps_ = slice(hp * 64, hp * 64 + 64)
                    jj = u % 2
                    if hp == 0:
                        bkA, rbA = sbk[0]
                        o_ = bkA[:, jj * 256:jj * 256 + 256]
                    else:
                        bkA, rbA = sbk[1]
                        o_ = bkA[:, jj * 256:jj * 256 + 256]
                    k.op("pe", lambda: mm(o_, lhsT=bt[ps_, j, cs:cs + P], rhs=ar[ps_, j, :, cs:cs + P], start=True, stop=True),
                         R=[r_bt[j], r_ar[j]], W=[rbA])
                for hh in range(2):
                    bkA, rbA = sbk[hh]
                    k.op("dve", lambda: nc.vector.tensor_tensor(
                        out=Nb[:, 2 * hh:2 * hh + 2, :], in0=bkA[:, :].rearrange("p (u c) -> p u c", u=2), in1=mur2, op=ALU.mult),
                        R=[rbA, r_cst], W=[r_Nb])
                yield
                sbk2 = [next_bank() for _ in range(2)]
                for u, (j, hp) in enumerate(units):
                    ps_ = slice(hp * 64, hp * 64 + 64)
                    jj = u % 2
                    bkC, rbC = sbk2[hp]
                    k.op("pe", lambda: mm(bkC[:, jj * 256:jj * 256 + 256], lhsT=kt[ps_, j, cs:cs + P],
                                          rhs=ar[ps_, j, :, cs:cs + P], start=True, stop=True),
                         R=[r_kt[j], r_ar[j]], W=[rbC])
                bkE0, rbE0 = next_bank()
                bkE1, rbE1 = next_bank()
                for u, (j, hp) in enumerate(units):
                    ps_ = slice(hp * 64, hp * 64 + 64)
                    jj = u % 2
                    bkE, rbE = (bkE0, rbE0) if hp == 0 else (bkE1, rbE1)
                    k.op("pe", lambda: mm(bkE[:, jj * P:(jj + 1) * P], lhsT=ar[ps_, j, 0, cs:cs + P],
                                          rhs=bt[ps_, j, cs:cs + P], start=True, stop=True),
                         R=[r_bt[j], r_ar[j]], W=[rbE])
                for hh in range(2):
                    bkC, rbC = sbk2[hh]
                    k.op("dve", lambda: nc.vector.tensor_tensor(
                        out=Nk[:, 2 * hh:2 * hh + 2, :], in0=bkC[:, :].rearrange("p (u c) -> p u c", u=2), in1=mur2, op=ALU.mult),
                        R=[rbC, r_cst], W=[r_Nk])
                    bkE, rbE = (bkE0, rbE0) if hh == 0 else (bkE1, rbE1)
                    k.op("dve", lambda: nc.vector.tensor_tensor(
                        out=Ys[0][:, 2 * hh:2 * hh + 2, :], in0=bkE[:, 0:2 * P].rearrange("p (u c) -> p u c", u=2), in1=ml2, op=ALU.mult),
                        R=[rbE, r_cst], W=[r_Ys[0]])
                k.op("dve", lambda: nc.vector.tensor_tensor(out=Ps[0][:, :, :], in0=Nb[:, :, 0:P], in1=id4, op=ALU.add),
                     R=[r_Nb, r_identb], W=[r_Ps[0]])
                yield
                Xc = lambda u: Nb[:, u, 0:P]
                r_Xc = r_Nb
                yi = 0; pi = 0; xi_ = 0
                for lvl in range(6):
                    bX, rbX = next_bank()
                    bY, rbY = next_bank()
                    Yc = Ys[yi]; r_Yc = r_Ys[yi]
                    for u in range(G):
                        k.op("pe", lambda: mm(bY[:, u * P:(u + 1) * P], lhsT=Xc(u), rhs=Yc[:, u, :], start=True, stop=True),
                             R=[r_Yc, r_Xc], W=[rbY])
                    if lvl < 5:
                        for u in range(G):
                            k.op("pe", lambda: mm(bX[:, u * P:(u + 1) * P], lhsT=Yc[:, u, :], rhs=Xc(u), start=True, stop=True),
                                 R=[r_Yc, r_Xc], W=[rbX])
                    Yn = Ys[1 - yi]; r_Yn = r_Ys[1 - yi]
                    evac_copy(Yn[:, :, :], bY[:, :].rearrange("p (u c) -> p u c", u=G), R=[rbY], W=[r_Yn])
                    if lvl < 5:
                        Xn = Xs[xi_]; r_Xn = r_Xs[xi_]
                        evac_copy(Xn[:, :, :], bX[:, :].rearrange("p (u c) -> p u c", u=G), R=[rbX], W=[r_Xn])
                        Xc = (lambda Xn: (lambda u: Xn[:, u, :]))(Xn)
                        r_Xc = r_Xn
                        xi_ = 1 - xi_
                    yi = 1 - yi
                    yield
                    bP, rbP = next_bank()
                    Pc = Ps[pi]; r_Pc = r_Ps[pi]
                    for u in range(G):
                        k.op("pe", lambda: mm(bP[:, u * P:(u + 1) * P], lhsT=identb[:, :], rhs=Pc[:, u, :], start=True, stop=False),
                             R=[r_identb, r_Pc], W=[rbP], inc=False)
                        k.op("pe", lambda: mm(bP[:, u * P:(u + 1) * P], lhsT=Yn[:, u, :], rhs=Pc[:, u, :], start=False, stop=True),
                             R=[r_Yn, r_Pc], W=[rbP])
                    Pn = Ps[1 - pi]; r_Pn = r_Ps[1 - pi]
                    evac_copy(Pn[:, :, :], bP[:, :].rearrange("p (u c) -> p u c", u=G), R=[rbP], W=[r_Pn])
                    pi = 1 - pi
                    yield
                Tn = Ps[pi]; r_Tn = r_Ps[pi]
                bW, rbW = next_bank()
                for jj in range(2):
                    j = j0 + jj
                    k.op("pe", lambda: mm(bW[:, jj * P:(jj + 1) * P], lhsT=ar[:, j, 0, cs:cs + P], rhs=Hbd[:, j, :],
                                          start=True, stop=False), R=[r_ar[j], r_H[g]], W=[rbW], inc=False)
                    for hp in range(2):
                        u = hp * 2 + jj
                        hc = (2 * j + hp) * 64
                        k.op("pe", lambda: mm(bW[:, jj * P + hp * 64:jj * P + hp * 64 + 64], lhsT=Nk[:, u, 0:P],
                                              rhs=vtok[:, hc:hc + 64], start=False, stop=(hp == 1)),
                             R=[r_Nk, r_tok[2]], W=[rbW], inc=(hp == 1))
                k.op("act", lambda: nc.scalar.copy(out=Wt[:, :, :], in_=bW[:, 0:2 * P].rearrange("p (u c) -> p u c", u=2)),
                     R=[rbW], W=[r_Wt])
                yield
                bU, rbU = next_bank()
                for u, (j, hp) in enumerate(units):
                    jj = u % 2
                    k.op("pe", lambda: mm(bU[:, jj * P + hp * 64:jj * P + hp * 64 + 64], lhsT=Tn[:, u, :],
                                          rhs=Wt[:, jj, hp * 64:hp * 64 + 64], start=True, stop=True),
                         R=[r_Tn, r_Wt], W=[rbU])
                k.op("act", lambda: nc.scalar.copy(out=Ut[:, :, :], in_=bU[:, 0:2 * P].rearrange("p (u c) -> p u c", u=2)),
                     R=[rbU], W=[r_Ut])
                yield
                assert post_done[0], "post of the previous chunk must be fully emitted before y^T is overwritten"
                yb_, ryb_ = next_bank()
                for jj in range(2):
                    j = j0 + jj
                    k.op("pe", lambda: mm(yb_[:, jj * P:(jj + 1) * P], lhsT=Hbd[:, j, :], rhs=ar[:, j, 1, cs:cs + P],
                                          start=True, stop=False), R=[r_H[g], r_ar[j]], W=[ryb_], inc=False)
                    for hp in range(2):
                        u = hp * 2 + jj
                        hc = (2 * j + hp) * 64
                        yo = yb_[hp * 64:hp * 64 + 64, jj * P:(jj + 1) * P]
                        k.op("pe", lambda: mm(yo, lhsT=Ut[:, jj, hp * 64:hp * 64 + 64], rhs=Nb[:, u, P:2 * P], start=False, stop=False),
                             R=[r_Ut, r_Nb], W=[ryb_], inc=False)
                        k.op("pe", lambda: mm(yo, lhsT=vtok[:, hc:hc + 64], rhs=Nk[:, u, P:2 * P], start=False, stop=True),
                             R=[r_tok[2], r_Nk], W=[ryb_])
                k.op("act", lambda: nc.scalar.copy(out=ysb[:, j0:j0 + 2, :], in_=yb_[:, 0:2 * P].rearrange("p (a b) -> p a b", a=2)),
                     R=[ryb_], W=[r_ysb[g]])
                bH, rbH = next_bank()
                for u, (j, hp) in enumerate(units):
                    jj = u % 2
                    hc = (2 * j + hp) * 64
                    ho = bH[hp * 64:hp * 64 + 64, jj * 64:jj * 64 + 64]
                    k.op("pe", lambda: mm(ho, lhsT=btok[:, hc:hc + 64], rhs=Ut[:, jj, hp * 64:hp * 64 + 64], start=True, stop=False),
                         R=[r_tok[0], r_Ut], W=[rbH], inc=False)
                    k.op("pe", lambda: mm(ho, lhsT=ktok[:, hc:hc + 64], rhs=vtok[:, hc:hc + 64], start=False, stop=True),
                         R=[r_tok[1], r_tok[2]], W=[rbH])
                k.op("dve", lambda: nc.vector.tensor_tensor(out=tH[:, :, :], in0=bH[:, 0:128].rearrange("p (a b) -> p a b", a=2),
                                                            in1=H32[:, j0:j0 + 2, :], op=ALU.add), R=[rbH, r_H[g]], W=[r_tH])
                gcb = gC[:, j0:j0 + 2, c:c + 1].broadcast_to([P, 2, 64])
                k.op("dve", lambda: nc.vector.tensor_tensor(out=H32[:, j0:j0 + 2, :], in0=tH[:, :, :], in1=gcb, op=ALU.mult),
                     R=[r_tH, r_gC], W=[r_H[g]])
                k.op("act", lambda: nc.scalar.copy(out=Hbd[0:64, j0:j0 + 2, 0:64], in_=H32[0:64, j0:j0 + 2, :]), R=[r_H[g]], W=[r_H[g]])
                k.op("act", lambda: nc.scalar.copy(out=Hbd[64:128, j0:j0 + 2, 64:128], in_=H32[64:128, j0:j0 + 2, :]), R=[r_H[g]], W=[r_H[g]])
                yield

            for c in range(n // P):
                cs = c * P
                for si, (src_t, r_src, dst_t) in enumerate([(bt, r_bt, btok), (kt, r_kt, ktok), (vt, r_vt, vtok)]):
                    bk, rb = next_bank()
                    bkb = bk[:, :].bitcast(BF16)
                    for d in range(KC):
                        k.op("pe", lambda: nc.tensor.transpose(bkb[:, d * P:(d + 1) * P], src_t[:, d, cs:cs + P], identb[:, :]),
                             R=[r_src[d], r_identb], W=[rb])
                    evac_copy(dst_t[:, :], bkb[:, 0:D], R=[rb], W=[r_tok[si]])
                import os as _os3
                if _os3.environ.get("RW_NOIL"):
                    for g_ in range(4):
                        run_interleaved([group_gen(c, g_, sets[g_ % 2])])
                else:
                    if _os3.environ.get("RW_IL2"):
                        run_interleaved([group_gen(c, 0, sets[0]), group_gen(c, 2, sets[1])])
                        run_interleaved([group_gen(c, 1, sets[2]), group_gen(c, 3, sets[3])])
                    else:
                        gens = [group_gen(c, g_, sets[g_]) for g_ in range(4)]
                        if c > 0:
                            pcs = (c - 1) * P
                            post_done[0] = False
                            gens = [post_gen(ptl, lambda d: (ysb[:, d, :], r_ysb[d // 2]), pcs, P, post_done)] + gens
                        run_interleaved(gens)
            post(ptl, lambda d: (ysb[:, d, :], r_ysb[d // 2]), (n // P - 1) * P, P)
            nonlocal_nb[0] = 6
            pb.end()

        def out_proj(t0, n, ti):
            wv_ = rw["rw_wo"].rearrange("(k p) c -> p k c", p=P)
            for half in range(2):
                wb, r_w = next_wbuf()
                v = wb[:, 0:KC * 512].rearrange("p (k c) -> p k c", k=KC)
                k.dma("pool", v, wv_[:, :, half * 512:(half + 1) * 512], W=[r_w])
                for oc in range(4):
                    d = half * 4 + oc
                    bk, rb = next_bank()
                    for kk in range(KC):
                        k.op("pe", lambda: mm(bk[:, 0:n], lhsT=v[:, kk, oc * P:(oc + 1) * P], rhs=og[:, kk, 0:n],
                                              start=(kk == 0), stop=(kk == KC - 1)),
                             R=[r_w, r_og], W=[rb], inc=(kk == KC - 1))
                    k.op("dve", lambda: nc.vector.tensor_tensor(out=x[:, d, t0:t0 + n], in0=bk[:, 0:n], in1=x[:, d, t0:t0 + n], op=ALU.add),
                         R=[rb, r_x[d][ti]], W=[r_x[d][ti]])

        import os as _os
        _bis = _os.environ.get("RWB", "").split(",")
        nseg = TP // SEG
        for s in range(nseg):
            t0 = s * SEG
            ti = t0 // 512
            if "noA" not in _bis:
                stageA(t0, SEG, ti, False, s == nseg - 1)
            if "noB" not in _bis:
                stageB(SEG)
                out_proj(t0, SEG, ti)

        def stageB_sample():
            pb = Phase(base=ov_base, carry_ref=ovc)
            vec = {}
            for nm in ("A", "R", "B", "K", "V", "D"):
                vec[nm] = (pb.alloc([P, P]), pb.reg("sv" + nm))
            sa = pb.alloc([P, 64]); r_sa = pb.reg("sa")
            yv = pb.alloc([P, P]); r_yv = pb.reg("yv")
            Ss = pb.alloc([P, 64, 64]); r_Ss = pb.reg("Ss")
            tmp = pb.alloc([P, 64, 64]); r_tmp = pb.reg("tmp")
            ptl = post_tiles(pb)
            srcs = {"A": (lambda: ar[:, :, 0, 0:NS], r_ar, BF16), "R": (lambda: ar[:, :, 1, 0:NS], r_ar, BF16),
                    "B": (lambda: bt[:, :, 0:NS], r_bt, BF16), "K": (lambda: kt[:, :, 0:NS], r_kt, BF16),
                    "V": (lambda: vt[:, :, 0:NS], r_vt, BF16), "D": (lambda: dS[:, :, 0:NS], [r_dS], F32)}
            cb16 = [pb.alloc([P, P], BF16) for _ in range(2)]; r_cb16 = [pb.reg("cb16a"), pb.reg("cb16b")]
            cb32 = pb.alloc([P, P]); r_cb32 = pb.reg("cb32")
            for si, (nm, (fn, rr, dt_)) in enumerate(srcs.items()):
                bk, rb = next_bank()
                if dt_ == BF16:
                    cb, r_cb = cb16[si % 2], r_cb16[si % 2]
                    evac_copy(cb[:, :].rearrange("p (a b) -> p a b", a=KC), fn(), R=list(rr), W=[r_cb])
                    o_ = bk[:, :].bitcast(BF16)[:, 0:P]
                    k.op("pe", lambda: nc.tensor.transpose(o_, cb[:, :], identb[:, :]), R=[r_cb, r_identb], W=[rb])
                else:
                    evac_copy(cb32[:, :].rearrange("p (a b) -> p a b", a=KC), fn(), R=list(rr), W=[r_cb32])
                    o_ = bk[:, 0:P]
                    k.op("pe", lambda: nc.tensor.transpose(o_, cb32[:, :], ident[:, :]), R=[r_cb32, r_ident], W=[rb])
                evac_copy(vec[nm][0][:, :], o_, R=[rb], W=[vec[nm][1]])

            def kb(nm, hp):
                return vec[nm][0][:, hp * 64:(hp + 1) * 64].unsqueeze(1).broadcast_to([P, 64, 64])

            def vb(ap2):
                return ap2.unsqueeze(2).broadcast_to([P, 64, 64])
            TT = nc.vector.tensor_tensor
            for hp in range(2):
                for j in range(KC):
                    k.dma("sp", Ss[j * NS:(j + 1) * NS, :, :], st_wkv_d[:, 2 * j + hp, :, :], W=[r_Ss])
                k.op("dve", lambda: TT(out=tmp, in0=Ss, in1=kb("A", hp), op=ALU.mult), R=[r_Ss, vec["A"][1]], W=[r_tmp])
                k.op("dve", lambda: nc.vector.tensor_reduce(out=sa[:, :], in_=tmp, axis=AX.X, op=ALU.add), R=[r_tmp], W=[r_sa])
                k.op("dve", lambda: TT(out=tmp, in0=vb(sa[:, :]), in1=kb("B", hp), op=ALU.mult), R=[r_sa, vec["B"][1]], W=[r_tmp])
                k.op("dve", lambda: TT(out=Ss, in0=Ss, in1=tmp, op=ALU.add), R=[r_tmp, r_Ss], W=[r_Ss])
                k.op("dve", lambda: TT(out=tmp, in0=vb(vec["V"][0][:, hp * 64:(hp + 1) * 64]), in1=kb("K", hp), op=ALU.mult),
                     R=[vec["V"][1], vec["K"][1]], W=[r_tmp])
                k.op("dve", lambda: TT(out=Ss, in0=Ss, in1=tmp, op=ALU.add), R=[r_tmp, r_Ss], W=[r_Ss])
                k.op("dve", lambda: TT(out=tmp, in0=Ss, in1=kb("R", hp), op=ALU.mult), R=[r_Ss, vec["R"][1]], W=[r_tmp])
                k.op("dve", lambda: nc.vector.tensor_reduce(out=yv[:, hp * 64:(hp + 1) * 64], in_=tmp, axis=AX.X, op=ALU.add),
                     R=[r_tmp], W=[r_yv])
                k.op("dve", lambda: TT(out=Ss, in0=Ss, in1=kb("D", hp), op=ALU.mult), R=[r_Ss, vec["D"][1]], W=[r_Ss])
                for j in range(KC):
                    r_o = Reg("o_swkv"); out_regs.append(r_o)
                    k.dma("sp", swkv_d[:, 2 * j + hp, :, :], Ss[j * NS:(j + 1) * NS, :, :], R=[r_Ss], W=[r_o])
            k.op("pe", lambda: nc.tensor.transpose(ybk[0][:, 0:P], yv[:, :], ident[:, :]), R=[r_yv, r_ident], W=[r_ybk[0]])
            post(ptl, lambda d: (ybk[0][:, d * NS:(d + 1) * NS], r_ybk[0]), 0, NS)
            pb.end()

        if do_samples:
            stageA(TP, NS, 4, True, False)
            stageB_sample()
            out_proj(TP, NS, 4)

        pe_ = Phase(base=ov_base, carry_ref=ovc)
        shT = pe_.alloc([1 + NS, D]); r_shT = pe_.reg("shT")
        bk, rb = next_bank()
        bk2, rb2 = next_bank()
        for d in range(KC):
            bb = bk if d < 4 else bk2
            k.op("pe", lambda: nc.tensor.transpose(bb[0:1 + NS, (d % 4) * P:(d % 4) * P + P], shf[:, d, :], ident[:, :]),
                 R=[r_shf, r_ident], W=[rb if d < 4 else rb2])
        evac_copy(shT[:, 0:512], bk[0:1 + NS, :], R=[rb], W=[r_shT])
        evac_copy(shT[:, 512:1024], bk2[0:1 + NS, :], R=[rb2], W=[r_shT])
        r_o = Reg("o_pshift"); out_regs.append(r_o)
        k.dma("sp", pshift_d.rearrange("(a d) -> a d", a=1), shT[0:1, :], R=[r_shT], W=[r_o])
        if do_samples:
            r_o = Reg("o_sshift"); out_regs.append(r_o)
            k.dma("sp", sshift_d, shT[1:1 + NS, :], R=[r_shT], W=[r_o])
        ST = pe_.alloc([64, KC, P]); r_ST = pe_.reg("ST")
        for half in range(2):
            bk, rb = next_bank()
            for jj in range(4):
                j = half * 4 + jj
                k.op("pe", lambda: nc.tensor.transpose(bk[0:64, jj * P:(jj + 1) * P], H32[:, j, :], ident[:, :]),
                     R=[r_H[j // 2], r_ident], W=[rb])
            evac_copy(ST[:, half * 4:half * 4 + 4, :], bk[0:64, :].rearrange("p (a b) -> p a b", a=4), R=[rb], W=[r_ST])
        r_o = Reg("o_pwkv"); out_regs.append(r_o)
        k.dma("sp", pwkv_d.rearrange("(j hp) v kk -> v j hp kk", hp=2), ST[:, :, :].rearrange("v j (hp kk) -> v j hp kk", hp=2),
              R=[r_ST], W=[r_o])
        pe_.end()
        dmy = Reg("ovdummy"); dmy.readers = list(ovc[0]); ph.regs.append(dmy)
        nonlocal_nb[0] = 8
        ph.end()


    RET_G = [1.0 - 2.0 ** (-5 - h_) for h_ in range(4)]

    def ret_mixer(do_samples=True):
        ph = Phase()
        SEG = 256
        mm = nc.tensor.matmul
        dmatT = ph.alloc([P, 4, P], BF16); qdt = ph.alloc([P, 4, P], BF16); kdec = ph.alloc([P, 4])
        eps_r = ph.alloc([P, 1])
        r_cst = ph.reg("retconst")
        k.dma("pool", dmatT, c_dmatT_d, W=[r_cst])
        k.dma("pool", qdt, c_qd_d, W=[r_cst])
        k.dma("sp", kdec, c_kdec_d, W=[r_cst])
        k.op("dve", lambda: nc.vector.memset(eps_r, 1e-6), W=[r_cst])
        nbuf = norm_bufs(ph)
        hm = ph.alloc([P, KC, SEG], BF16); r_hm = [ph.reg("hm%d" % d) for d in range(KC)]
        gT = ph.alloc([P, 16, SEG], BF16); r_gT = [ph.reg("gT%d" % e) for e in range(16)]
        cs_fm = ph.alloc([P, 2, SEG]); r_csfm = ph.reg("csfm")
        rq = [ph.alloc([P, SEG]) for _ in range(4)]; r_rq = [ph.reg("rq%d" % i) for i in range(4)]
        rt_ = [ph.alloc([P, SEG]) for _ in range(4)]; r_rt = [ph.reg("rt%d" % i) for i in range(4)]
        on_ = [ph.alloc([P, 512], BF16) for _ in range(2)]; r_on = [ph.reg("on0"), ph.reg("on1")]
        stt = [ph.alloc([P, 8]) for _ in range(2)]; r_stt = [ph.reg("stt0"), ph.reg("stt1")]
        ov_base = ph.off
        ov_idx = len(ph.regs)
        S32 = ph.alloc([P, 4, 2, 512]); Sb = ph.alloc([P, 4, 2, 512], BF16)
        r_S = [ph.reg("S%d" % h_) for h_ in range(4)]
        k.op("dve", lambda: nc.vector.memset(S32, 0.0), W=r_S)
        k.op("dve", lambda: nc.vector.memset(Sb, 0.0), W=r_S)
        qT = ph.alloc([P, KC, SEG], BF16); r_qT = [ph.reg("qT%d" % d) for d in range(KC)]
        kT = ph.alloc([P, KC, SEG], BF16); r_kT = [ph.reg("kT%d" % d) for d in range(KC)]
        qdT = ph.alloc([P, KC, SEG], BF16); r_qdT = [ph.reg("qdT%d" % d) for d in range(KC)]
        vtok = [ph.alloc([P, 2 * D], BF16) for _ in range(2)]; r_vtok = [ph.reg("vtok0"), ph.reg("vtok1")]
        kdtok = [ph.alloc([P, D], BF16) for _ in range(2)]; r_kdtok = [ph.reg("kdtok0"), ph.reg("kdtok1")]
        cs_tok = [ph.alloc([P, 2, P]) for _ in range(2)]; r_cstok = [ph.reg("cstok0"), ph.reg("cstok1")]
        ktmp = ph.alloc([P, 512]); r_ktmp = ph.reg("ktmp")
        kt1 = ph.alloc([P, 512]); kt2 = ph.alloc([P, 512]); r_kt1 = ph.reg("kt1"); r_kt2 = ph.reg("kt2")
        kro = ph.alloc([P, 512]); r_kro = ph.reg("kro")
        innT = [ph.alloc([P, P], BF16) for _ in range(2)]; r_innT = [ph.reg("innT0"), ph.reg("innT1")]

        def load_w(view_fn):
            wb, r_w = next_wbuf()
            v_, src_ = view_fn(wb)
            k.dma("pool", v_, src_, W=[r_w])
            return v_, r_w

        def rotary_fm(bank_pair, dst, r_dst, h_, n, scale):
            (b1, rb1), (b2, rb2) = bank_pair
            i1, i2 = (2 * h_) % 4, (2 * h_ + 1) % 4
            x1, r1 = rq[i1], r_rq[i1]
            x2, r2 = rq[i2], r_rq[i2]
            k.op("act", lambda: nc.scalar.activation(out=x1[:, 0:n], in_=b1[:, 0:n], func=AF.Copy, scale=scale), R=[rb1], W=[r1])
            k.op("act", lambda: nc.scalar.activation(out=x2[:, 0:n], in_=b2[:, 0:n], func=AF.Copy, scale=scale), R=[rb2], W=[r2])
            ta, rta = rt_[i1], r_rt[i1]
            tb, rtb = rt_[i2], r_rt[i2]
            cosv = cs_fm[:, 0, 0:n]; sinv = cs_fm[:, 1, 0:n]
            k.op("dve", lambda: nc.vector.tensor_tensor(out=ta[:, 0:n], in0=x1[:, 0:n], in1=cosv, op=ALU.mult), R=[r1, r_csfm], W=[rta])
            k.op("dve", lambda: nc.vector.tensor_tensor(out=tb[:, 0:n], in0=x2[:, 0:n], in1=sinv, op=ALU.mult), R=[r2, r_csfm], W=[rtb])
            k.op("dve", lambda: nc.vector.tensor_tensor(out=dst[:, 2 * h_, 0:n], in0=ta[:, 0:n], in1=tb[:, 0:n], op=ALU.subtract),
                 R=[rta, rtb], W=[r_dst[2 * h_]])
            k.op("dve", lambda: nc.vector.tensor_tensor(out=ta[:, 0:n], in0=x1[:, 0:n], in1=sinv, op=ALU.mult), R=[r1, r_csfm], W=[rta])
            k.op("dve", lambda: nc.vector.tensor_tensor(out=tb[:, 0:n], in0=x2[:, 0:n], in1=cosv, op=ALU.mult), R=[r2, r_csfm], W=[rtb])
            k.op("dve", lambda: nc.vector.tensor_tensor(out=dst[:, 2 * h_ + 1, 0:n], in0=ta[:, 0:n], in1=tb[:, 0:n], op=ALU.add),
                 R=[rta, rtb], W=[r_dst[2 * h_ + 1]])

        def seg_front(t0, n, ti, sample, dsts=None):
            rmsnorm_tile(nbuf, 4, ti, lambda d, a_, b_: hm[:, d, 0:n], lambda d: [r_hm[d]], rng=(t0, n))
            k.dma("sp", cs_fm[:, 0, 0:n], c_cos_d[:, t0:t0 + n], W=[r_csfm])
            k.dma("sp", cs_fm[:, 1, 0:n], c_sin_d[:, t0:t0 + n], W=[r_csfm])
            dq = (qT, r_qT) if dsts is None else dsts[0]
            dk = (kT, r_kT) if dsts is None else dsts[1]
            for (wname, dst, r_dst, scale) in (("ret_wq", dq[0], dq[1], 1.0), ("ret_wk", dk[0], dk[1], 1.0 / 16.0)):
                wv_ = rt[wname].rearrange("(k p) c -> p k c", p=P)
                for half in range(2):
                    v_, r_w = load_w(lambda wb: (wb[:, 0:KC * 512].rearrange("p (k c) -> p k c", k=KC),
                                                 wv_[:, :, half * 512:(half + 1) * 512]))
                    if wname == "ret_wk" and not sample:
                        ktok_half(v_, r_w, half, n)
                    for hh in range(2):
                        h_ = half * 2 + hh
                        pair = []
                        for i in range(2):
                            oc = hh * 2 + i
                            bk, rb = next_bank()
                            for kk in range(KC):
                                k.op("pe", lambda: mm(bk[:, 0:n], lhsT=v_[:, kk, oc * P:(oc + 1) * P], rhs=hm[:, kk, 0:n],
                                                      start=(kk == 0), stop=(kk == KC - 1)),
                                     R=[r_w, r_hm[kk]], W=[rb], inc=(kk == KC - 1))
                            pair.append((bk, rb))
                        rotary_fm(pair, dst, r_dst, h_, n, scale)
            if not sample:
                for d in range(KC):
                    h_ = d // 2
                    k.op("dve", lambda: nc.vector.tensor_tensor(
                        out=qdT[:, d, 0:n].rearrange("p (c t) -> p c t", t=P), in0=qT[:, d, 0:n].rearrange("p (c t) -> p c t", t=P),
                        in1=qdt[:, h_, :].unsqueeze(1).broadcast_to([P, n // P, P]), op=ALU.mult),
                        R=[r_qT[d], r_cst], W=[r_qdT[d]])
            wg_ = rt["ret_wg"].rearrange("(k p) c -> p k c", p=P)
            for qv in range(4):
                v_, r_w = load_w(lambda wb: (wb[:, 0:KC * 512].rearrange("p (k c) -> p k c", k=KC),
                                             wg_[:, :, qv * 512:(qv + 1) * 512]))
                for oc in range(4):
                    e = qv * 4 + oc
                    bk, rb = next_bank()
                    for kk in range(KC):
                        k.op("pe", lambda: mm(bk[:, 0:n], lhsT=v_[:, kk, oc * P:(oc + 1) * P], rhs=hm[:, kk, 0:n],
                                              start=(kk == 0), stop=(kk == KC - 1)),
                             R=[r_w, r_hm[kk]], W=[rb], inc=(kk == KC - 1))
                    k.op("act", lambda: nc.scalar.activation(out=gT[:, e, 0:n], in_=bk[:, 0:n], func=AF.Silu), R=[rb], W=[r_gT[e]])

        cur_t0 = [0]

        def ktok_half(v_, r_w, half, n):
            for c in range(n // P):
                if half == 0:
                    tt0 = cur_t0[0] + c * P
                    k.dma("sp", cs_tok[c][:, 0, :], c_cosT_d[tt0:tt0 + P, :], W=[r_cstok[c]])
                    k.dma("sp", cs_tok[c][:, 1, :], c_sinT_d[tt0:tt0 + P, :], W=[r_cstok[c]])
                bk, rb = next_bank()
                for kk in range(KC):
                    k.op("pe", lambda: mm(bk[:, :], lhsT=hm[:, kk, c * P:(c + 1) * P], rhs=v_[:, kk, :],
                                          start=(kk == 0), stop=(kk == KC - 1)),
                         R=[r_w, r_hm[kk]], W=[rb], inc=(kk == KC - 1))
                k.op("act", lambda: nc.scalar.activation(out=ktmp[:, :], in_=bk[:, :], func=AF.Copy, scale=1.0 / 16.0), R=[rb], W=[r_ktmp])
                cosb = cs_tok[c][:, 0, :].unsqueeze(1).broadcast_to([P, 4, P])
                sinb = cs_tok[c][:, 1, :].unsqueeze(1).broadcast_to([P, 4, P])
                k3 = ktmp[:, :].rearrange("p (a f) -> p a f", a=4)
                k.op("dve", lambda: nc.vector.tensor_tensor(out=kt1[:, :].rearrange("p (a f) -> p a f", a=4), in0=k3, in1=cosb, op=ALU.mult),
                     R=[r_ktmp, r_cstok[c]], W=[r_kt1])
                k.op("dve", lambda: nc.vector.tensor_tensor(out=kt2[:, :].rearrange("p (a f) -> p a f", a=4), in0=k3, in1=sinb, op=ALU.mult),
                     R=[r_ktmp, r_cstok[c]], W=[r_kt2])
                t1v = kt1[:, :].rearrange("p (h two f) -> p h two f", h=2, two=2)
                t2v = kt2[:, :].rearrange("p (h two f) -> p h two f", h=2, two=2)
                kov = kro[:, :].rearrange("p (h two f) -> p h two f", h=2, two=2)
                k.op("dve", lambda: nc.vector.tensor_tensor(out=kov[:, :, 0, :], in0=t1v[:, :, 0, :], in1=t2v[:, :, 1, :], op=ALU.subtract),
                     R=[r_kt1, r_kt2], W=[r_kro])
                k.op("dve", lambda: nc.vector.tensor_tensor(out=kov[:, :, 1, :], in0=t2v[:, :, 0, :], in1=t1v[:, :, 1, :], op=ALU.add),
                     R=[r_kt1, r_kt2], W=[r_kro])
                for hh in range(2):
                    h_ = half * 2 + hh
                    k.op("dve", lambda: nc.vector.tensor_scalar(
                        out=kdtok[c][:, h_ * 256:(h_ + 1) * 256], in0=kro[:, hh * 256:(hh + 1) * 256], scalar1=kdec[:, h_:h_ + 1],
                        scalar2=None, op0=ALU.mult), R=[r_kro, r_cst], W=[r_kdtok[c]])

        def vtok_all(n):
            wv_ = rt["ret_wv"].rearrange("(k p) c -> p k c", p=P)
            for qv in range(4):
                v_, r_w = load_w(lambda wb: (wb[:, 0:KC * 512].rearrange("p (k c) -> p k c", k=KC),
                                             wv_[:, :, qv * 512:(qv + 1) * 512]))
                for c in range(n // P):
                    bk, rb = next_bank()
                    for kk in range(KC):
                        k.op("pe", lambda: mm(bk[:, :], lhsT=hm[:, kk, c * P:(c + 1) * P], rhs=v_[:, kk, :],
                                              start=(kk == 0), stop=(kk == KC - 1)),
                             R=[r_w, r_hm[kk]], W=[rb], inc=(kk == KC - 1))
                    evac_copy(vtok[c][:, qv * 512:(qv + 1) * 512], bk[:, :], R=[rb], W=[r_vtok[c]])

        def norm_gate(o_bank, rb_o, rows, h_, cs, w, ui):
            q = ui % 2
            st_, r_st = stt[q], r_stt[q]
            k.op("dve", lambda: nc.vector.bn_stats(out=st_[0:rows, 0:6], in_=o_bank[0:rows, :]), R=[rb_o], W=[r_st])
            k.op("dve", lambda: nc.vector.bn_aggr(out=st_[0:rows, 6:8], in_=st_[0:rows, 0:6]), R=[r_st], W=[r_st])
            k.op("act", lambda: nc.scalar.activation(out=st_[0:rows, 7:8], in_=st_[0:rows, 7:8], func=AF.Sqrt, bias=eps_r[0:rows, 0:1]),
                 R=[r_st, r_cst], W=[r_st])
            k.op("dve", lambda: nc.vector.reciprocal(out=st_[0:rows, 7:8], in_=st_[0:rows, 7:8]), R=[r_st], W=[r_st])
            on_t, r_on_t = on_[q], r_on[q]
            k.op("dve", lambda: nc.vector.tensor_scalar(out=on_t[0:rows, :], in0=o_bank[0:rows, :], scalar1=st_[0:rows, 6:7],
                                                        scalar2=st_[0:rows, 7:8], op0=ALU.subtract, op1=ALU.mult),
                 R=[rb_o, r_st], W=[r_on_t])
            bk, rb = next_bank()
            bkb = bk[:, :].bitcast(BF16)
            for i in range(4):
                k.op("pe", lambda: nc.tensor.transpose(bkb[:, i * P:i * P + rows], on_t[0:rows, i * P:(i + 1) * P], identb[0:rows, 0:rows]),
                     R=[r_on_t, r_identb], W=[rb])
            gv = gT[:, 4 * h_:4 * h_ + 4, cs:cs + w]
            k.op("dve", lambda: nc.vector.tensor_tensor(out=gv, in0=bkb[:, 0:4 * P].rearrange("p (a t) -> p a t", a=4)[:, :, 0:w], in1=gv, op=ALU.mult),
                 R=[rb] + [r_gT[4 * h_ + i] for i in range(4)], W=[r_gT[4 * h_ + i] for i in range(4)])

        def seg_chunks(n):
            ui = 0
            for c in range(n // P):
                cs = c * P
                for h_ in range(4):
                    q = ui % 2
                    bs, rbs = next_bank()
                    for i in range(2):
                        d = 2 * h_ + i
                        k.op("pe", lambda: mm(bs[:, 0:P], lhsT=kT[:, d, cs:cs + P], rhs=qT[:, d, cs:cs + P], start=(i == 0), stop=(i == 1)),
                             R=[r_kT[d], r_qT[d]], W=[rbs], inc=(i == 1))
                    k.op("dve", lambda: nc.vector.tensor_tensor(out=innT[q][:, :], in0=bs[:, 0:P], in1=dmatT[:, h_, :], op=ALU.mult),
                         R=[rbs, r_cst], W=[r_innT[q]])
                    bo, rbo = next_bank()
                    k.op("pe", lambda: mm(bo[:, :], lhsT=innT[q][:, :], rhs=vtok[c][:, h_ * 512:(h_ + 1) * 512], start=True, stop=False),
                         R=[r_innT[q], r_vtok[c]], W=[rbo], inc=False)
                    for i in range(2):
                        d = 2 * h_ + i
                        k.op("pe", lambda: mm(bo[:, :], lhsT=qdT[:, d, cs:cs + P], rhs=Sb[:, h_, i, :], start=False, stop=(i == 1)),
                             R=[r_qdT[d], r_S[h_]], W=[rbo], inc=(i == 1))
                    norm_gate(bo, rbo, P, h_, cs, P, ui)
                    sdec = float(np.float32(RET_G[h_]) ** 128)
                    for i in range(2):
                        bS, rbS = next_bank()
                        k.op("pe", lambda: mm(bS[:, :], lhsT=kdtok[c][:, h_ * 256 + i * P:h_ * 256 + (i + 1) * P],
                                              rhs=vtok[c][:, h_ * 512:(h_ + 1) * 512], start=True, stop=True),
                             R=[r_kdtok[c], r_vtok[c]], W=[rbS])
                        k.op("dve", lambda: nc.vector.scalar_tensor_tensor(
                            out=S32[:, h_, i, :], in0=S32[:, h_, i, :], scalar=sdec, in1=bS[:, :], op0=ALU.mult, op1=ALU.add),
                            R=[rbS, r_S[h_]], W=[r_S[h_]])
                        k.op("act", lambda: nc.scalar.copy(out=Sb[:, h_, i, :], in_=S32[:, h_, i, :]), R=[r_S[h_]], W=[r_S[h_]])
                    ui += 1

        def out_proj(t0, n, ti):
            wo_ = rt["ret_wo"].rearrange("(e p) c -> p e c", p=P)
            for piece in range(4):
                v_, r_w = load_w(lambda wb: (wb[:, 0:16 * 256].rearrange("p (e c) -> p e c", e=16),
                                             wo_[:, :, piece * 256:(piece + 1) * 256]))
                for oc in range(2):
                    d = piece * 2 + oc
                    bk, rb = next_bank()
                    for e in range(16):
                        k.op("pe", lambda: mm(bk[:, 0:n], lhsT=v_[:, e, oc * P:(oc + 1) * P], rhs=gT[:, e, 0:n],
                                              start=(e == 0), stop=(e == 15)),
                             R=[r_w, r_gT[e]], W=[rb], inc=(e == 15))
                    k.op("dve", lambda: nc.vector.tensor_tensor(out=x[:, d, t0:t0 + n], in0=bk[:, 0:n], in1=x[:, d, t0:t0 + n], op=ALU.add),
                         R=[rb, r_x[d][ti]], W=[r_x[d][ti]])

        def ret_samples(r_pret):
            toks = set()
            for r_ in ph.regs[ov_idx:] + [r_pret]:
                if r_.last_w is not None:
                    toks.add(r_.last_w)
                toks.update(r_.readers)
            ovc = [list(toks)]
            ps_ = Phase(base=ov_base, carry_ref=ovc)
            qTs = ps_.alloc([P, KC, NS]); r_qTs = [ps_.reg("qTs%d" % d) for d in range(KC)]
            kTs = ps_.alloc([P, KC, NS]); r_kTs = [ps_.reg("kTs%d" % d) for d in range(KC)]
            qpad = ps_.alloc([P, KC, NS, NS], BF16); r_qpad = ps_.reg("qpad")
            i16 = ps_.alloc([P, NS, NS]); selb = ps_.alloc([NS, NS, P], BF16); r_sc = ps_.reg("sconst")
            k.dma("sp", i16, c_i16_d.rearrange("p (a b) -> p a b", a=NS), W=[r_sc])
            k.dma("pool", selb, c_sel_d, W=[r_sc])
            vs = ps_.alloc([NS, 2 * D], BF16); r_vs = ps_.reg("vs")
            Sring = [ps_.alloc([P, 2, 512]) for _ in range(4)]; r_Sring = [ps_.reg("Sring%d" % i) for i in range(4)]
            Sb16 = [ps_.alloc([P, 2, 512], BF16) for _ in range(2)]; r_Sb16 = [ps_.reg("Sb16_0"), ps_.reg("Sb16_1")]
            kvt = [ps_.alloc([P, 512]) for _ in range(4)]; r_kvt = [ps_.reg("kvt%d" % i) for i in range(4)]
            n = NS
            seg_front(TP, n, 4, True, dsts=((qTs, r_qTs), (kTs, r_kTs)))
            wv_ = rt["ret_wv"].rearrange("(k p) c -> p k c", p=P)
            for qv in range(4):
                v_, r_w = load_w(lambda wb: (wb[:, 0:KC * 512].rearrange("p (k c) -> p k c", k=KC),
                                             wv_[:, :, qv * 512:(qv + 1) * 512]))
                bk, rb = next_bank()
                for kk in range(KC):
                    k.op("pe", lambda: mm(bk[0:n, :], lhsT=hm[:, kk, 0:n], rhs=v_[:, kk, :], start=(kk == 0), stop=(kk == KC - 1)),
                         R=[r_w, r_hm[kk]], W=[rb], inc=(kk == KC - 1))
                evac_copy(vs[0:n, qv * 512:(qv + 1) * 512], bk[0:n, :], R=[rb], W=[r_vs])
            k.op("dve", lambda: nc.vector.tensor_tensor(
                out=qpad, in0=qTs.unsqueeze(3).broadcast_to([P, KC, NS, NS]), in1=i16.unsqueeze(1).broadcast_to([P, KC, NS, NS]),
                op=ALU.mult), R=r_qTs + [r_sc], W=[r_qpad])
            nonlocal_nb[0] = 4
            po = [banks[4 + h_] for h_ in range(4)]; r_po = [r_bank[4 + h_] for h_ in range(4)]
            DEPTH = 4
            units = [(b_, h_) for b_ in range(NS) for h_ in range(4)]
            NU = len(units)

            def load(u):
                b_, h_ = units[u]
                k.dma("sp", Sring[u % DEPTH], st_ret_d[b_, h_].rearrange("(i p) v -> p i v", p=P), W=[r_Sring[u % DEPTH]])

            def vbc(u):
                b_, h_ = units[u]
                bk, rb = next_bank()
                k.op("pe", lambda: mm(bk[:, :], lhsT=selb[0:NS, b_, :], rhs=vs[0:NS, h_ * 512:(h_ + 1) * 512], start=True, stop=True),
                     R=[r_sc, r_vs], W=[rb])
                return bk, rb
            for u in range(min(DEPTH - 1, NU)):
                load(u)
            nxt = vbc(0)
            for u in range(NU):
                b_, h_ = units[u]
                St, r_St = Sring[u % DEPTH], r_Sring[u % DEPTH]
                Sh, r_Sh = Sb16[u % 2], r_Sb16[u % 2]
                bk, rb = nxt
                if u + DEPTH - 1 < NU:
                    load(u + DEPTH - 1)
                kv_, r_kv = kvt[u % 4], r_kvt[u % 4]
                d0_, d1_ = 2 * h_, 2 * h_ + 1
                k.op("act", lambda: nc.scalar.activation(out=kv_[:, :], in_=bk[:, :], func=AF.Copy, scale=kTs[:, d1_, b_:b_ + 1]),
                     R=[rb, r_kTs[d1_]], W=[r_kv])
                k.op("act", lambda: nc.scalar.activation(out=bk[:, :], in_=bk[:, :], func=AF.Copy, scale=kTs[:, d0_, b_:b_ + 1]),
                     R=[rb, r_kTs[d0_]], W=[rb])
                k.op("dve", lambda: nc.vector.scalar_tensor_tensor(
                    out=St[:, 0, :], in0=St[:, 0, :], scalar=float(np.float32(RET_G[h_])), in1=bk[:, :],
                    op0=ALU.mult, op1=ALU.add), R=[r_St, rb], W=[r_St])
                k.op("dve", lambda: nc.vector.scalar_tensor_tensor(
                    out=St[:, 1, :], in0=St[:, 1, :], scalar=float(np.float32(RET_G[h_])), in1=kv_[:, :],
                    op0=ALU.mult, op1=ALU.add), R=[r_St, r_kv], W=[r_St])
                if u + 1 < NU:
                    nxt = vbc(u + 1)
                k.op("act", lambda: nc.scalar.copy(out=Sh[:, :, :], in_=St[:, :, :]), R=[r_St], W=[r_Sh])
                r_o2 = Reg("o_sret"); out_regs.append(r_o2)
                k.dma("pool", sret_d[b_, h_].rearrange("(i p) v -> p i v", p=P), St, R=[r_St], W=[r_o2])
                for i in range(2):
                    d = 2 * h_ + i
                    first = (b_ == 0 and i == 0); last = (b_ == NS - 1 and i == 1)
                    k.op("pe", lambda: mm(po[h_][0:NS, :], lhsT=qpad[:, d, b_, :], rhs=Sh[:, i, :], start=first, stop=last),
                         R=[r_qpad, r_Sh], W=[r_po[h_]], inc=True)
            for h_ in range(4):
                norm_gate(po[h_], r_po[h_], NS, h_, 0, NS, h_)
            nonlocal_nb[0] = 8
            out_proj(TP, n, 4)
            ps_.end()
            dmy = Reg("ovdummy2"); dmy.readers = list(ovc[0]); ph.regs.append(dmy)

        nseg = TP // SEG
        for s in range(nseg):
            t0 = s * SEG
            ti = t0 // 512
            cur_t0[0] = t0
            seg_front(t0, SEG, ti, False)
            vtok_all(SEG)
            seg_chunks(SEG)
            out_proj(t0, SEG, ti)
        r_o = Reg("o_pret"); out_regs.append(r_o)
        k.dma("sp", pret_d.rearrange("h (i p) v -> p h i v", p=P), S32, R=r_S + [r_o], W=[r_o])
        if do_samples:
            ret_samples(r_o)
        ph.end()

    def final_out():
        ph = Phase()
        nb = norm_bufs(ph)
        yfm = [ph.alloc([P, KC, 512]) for _ in range(2)]
        r_yfm = [[ph.reg("yfm%d_%d" % (i, d)) for d in range(KC)] for i in range(2)]
        yout = [ph.alloc([P, D]) for _ in range(2)]
        r_yout = [ph.reg("yout0"), ph.reg("yout1")]
        st = 0
        for ti, (t0, tn) in enumerate(TILES):
            yb = ti % 2
            rmsnorm_tile(nb, 6, ti, lambda d, t0, tn: yfm[yb][:, d, 0:tn], lambda d: [r_yfm[yb][d]])
            nsub = (tn + P - 1) // P
            for s in range(nsub):
                rows = min(P, tn - s * P)
                ob = st % 2; st += 1
                for d0 in range(0, KC, 4):
                    bk2, rb2 = next_bank()
                    for j in range(4):
                        d = d0 + j
                        k.op("pe", lambda: nc.tensor.transpose(
                            bk2[0:rows, j * P:(j + 1) * P], yfm[yb][:, d, s * P:s * P + rows], ident[:, :]),
                            R=[r_yfm[yb][d], r_ident], W=[rb2])
                    evac_copy(yout[ob][0:rows, d0 * P:(d0 + 4) * P], bk2[0:rows, :], R=[rb2], W=[r_yout[ob]])
                dst = yp_d[t0 + s * P:t0 + s * P + rows, :] if ti < 4 else ys_d[:, :]
                r_o = Reg("o"); out_regs.append(r_o)
                k.dma("sp", dst, yout[ob][0:rows, :], R=[r_yout[ob]], W=[r_o])
        ph.end()

    load_x()
    for l in range(2):
        if "ffn1_%d" % l in stages:
            ffn(l, 1)
        if l == 0 and lx_carry[0]:
            best = {}
            for t_ in list(carry[0]) + list(lx_carry[0]):
                kk_ = (t_[0], t_[1])
                if kk_ not in best or best[kk_][2] < t_[2]:
                    best[kk_] = t_
            carry[0] = list(best.values())
            lx_carry[0] = []
        if "mix_%d" % l in stages:
            if l == 0:
                rwkv_mixer(do_samples=("samples" in stages))
            else:
                ret_mixer(do_samples=("samples" in stages))
        if "ffn2_%d" % l in stages:
            ffn(l, 2)
    final_out()

    k.finish(out_regs)
    es.close()
    print("instructions:", k.n_inst, "waits:", k.n_wait)
    return nc


def host_consts():
    c = {"c_ident": np.eye(P, dtype=np.float32)}
    bones = np.zeros((P, P), np.float32); bones[:64, :64] = 1.0; bones[64:, 64:] = 1.0
    c["c_bones"] = bones
    s_ = np.arange(P)[:, None]; t_ = np.arange(P)[None, :]
    c["c_mask_ur"] = np.concatenate([(s_ < t_), (s_ <= t_)], axis=1).astype(np.float32)
    c["c_mask_l"] = (s_ > t_).astype(np.float32)
    sm = np.ones((P, 256), np.float32); sm[:, 0] = 0.0; sm[:, 128] = 0.0
    c["c_scanm"] = sm
    f32 = np.float32
    half = 128
    inv = (f32(10000.0) ** (-(np.arange(half, dtype=f32) / f32(half)))).astype(f32)
    pos = np.concatenate([np.arange(TP, dtype=f32), np.full(NS, 16384.0, dtype=f32)])
    ang = (pos[None, :] * inv[:, None]).astype(f32)
    c["c_cos"] = np.cos(ang.astype(np.float64)).astype(f32)
    c["c_sin"] = np.sin(ang.astype(np.float64)).astype(f32)
    c["c_cosT"] = np.ascontiguousarray(c["c_cos"][:, :TP].T)
    c["c_sinT"] = np.ascontiguousarray(c["c_sin"][:, :TP].T)
    log_g = np.log1p(-np.exp2(-5.0 - np.arange(4, dtype=f32))).astype(f32)
    idx = np.arange(P, dtype=f32)
    diff = idx[None, :] - idx[:, None]
    dm = np.where(diff[:, None, :] >= 0, np.exp(log_g[None, :, None] * np.maximum(diff, 0)[:, None, :]), 0.0)
    c["c_dmatT"] = dm.astype(f32)
    qd = np.exp(log_g[:, None] * (idx[None, :] + 1.0)).astype(f32)
    c["c_qd"] = np.ascontiguousarray(np.broadcast_to(qd[None], (P, 4, P))).astype(f32)
    c["c_kdec"] = np.exp(log_g[None, :] * (127.0 - idx[:, None])).astype(f32)
    sel = np.zeros((NS, NS, P), f32)
    for b_ in range(NS):
        sel[b_, b_, :] = 1.0
    c["c_sel"] = sel
    c["c_i16"] = np.ascontiguousarray(np.tile(np.eye(NS, dtype=f32).reshape(1, NS * NS), (P, 1)))
    return c


W_NAMES = ("ln_ffn1", "ln_mix", "ln_ffn2", "ln_final", "ff1_wg", "ff1_wu", "ff1_wd", "ff2_wg", "ff2_wu", "ff2_wd")
RT_NAMES = ("ret_wq", "ret_wk", "ret_wv", "ret_wg", "ret_wo")
RW_NAMES = ("rw_mu", "rw_w0", "rw_w1", "rw_w2", "rw_a0", "rw_a1", "rw_a2", "rw_g1", "rw_g2", "rw_kk", "rw_ka",
            "rw_rk", "rw_wr", "rw_wk", "rw_wv", "rw_wo", "rw_lnx_g", "rw_lnx_b")


def make_in_maps(inputs, n_cores=N_CORES):
    consts = host_consts()
    shared = {}
    for nm in W_NAMES:
        shared[nm] = np.ascontiguousarray(inputs[nm], dtype=np.float32)
    for nm in RW_NAMES:
        a = np.asarray(inputs[nm], dtype=np.float32)[0]
        if nm == "rw_rk":
            a = a.reshape(D)
        shared[nm] = np.ascontiguousarray(a)
    for nm in RT_NAMES:
        shared[nm] = np.ascontiguousarray(np.asarray(inputs[nm], dtype=np.float32)[0])
    in_maps = []
    for c in range(n_cores):
        m = dict(consts)
        m.update(shared)
        m["x_prompt"] = np.ascontiguousarray(inputs["x_prompt"][c])
        m["x_sample"] = np.ascontiguousarray(inputs["x_sample"][c * NS:(c + 1) * NS, 0, :])
        m["st_shift"] = np.ascontiguousarray(inputs["state_rwkv_shift"][0, c * NS:(c + 1) * NS])
        m["st_wkv"] = np.ascontiguousarray(inputs["state_rwkv_wkv"][0, c * NS:(c + 1) * NS])
        m["st_ret"] = np.ascontiguousarray(inputs["state_ret"][0, c * NS:(c + 1) * NS])
        in_maps.append(m)
    return in_maps


def kernel(**inputs):
    nc = build()
    in_maps = make_in_maps(inputs)
    res = run_bass_kernel_spmd(nc, in_maps, core_ids=list(range(N_CORES)))
    R_ = res.results
    cat = lambda nm: np.concatenate([np.asarray(R_[c][nm]) for c in range(N_CORES)], 0)
    stk = lambda nm: np.stack([np.asarray(R_[c][nm]) for c in range(N_CORES)], 0)
    y_prompt = stk("y_prompt").astype(np.float32)
    y_sample = cat("y_sample")[:, None, :].astype(np.float32)
    p_shift = stk("p_shift")[None].astype(np.float32)
    p_wkv = stk("p_wkv")[None].astype(np.float32)
    p_ret = stk("p_ret")[None].astype(np.float32)
    s_shift = cat("s_shift")[None].astype(np.float32)
    s_wkv = cat("s_wkv")[None].astype(np.float32)
    s_ret = cat("s_ret")[None].astype(np.float32)
    return (y_prompt, y_sample, p_shift, p_wkv, p_ret, s_shift, s_wkv, s_ret)
```

```python
import contextlib
import numpy as np
import ml_dtypes
import concourse.bass as bass
import concourse.mybir as mybir
from concourse.bass_utils import run_bass_kernel_spmd

F32 = mybir.dt.float32
BF16 = mybir.dt.bfloat16
AF = mybir.ActivationFunctionType
ALU = mybir.AluOpType
AX = mybir.AxisListType

P = 128
D = 1024
DFF = 2816
NFC = DFF // P
KC = D // P
TP = 2048
NS = 16
T = TP + NS
TILES = [(0, 512), (512, 512), (1024, 512), (1536, 512), (2048, 16)]
NCH = TP // P
RMS_EPS = 1e-6
N_CORES = 8


class Reg:
    __slots__ = ("name", "last_w", "readers")

    def __init__(self, name):
        self.name = name
        self.last_w = None
        self.readers = []


class KB:
    def __init__(self, nc, es, n_dma_sems=48):
        self.nc = nc
        self.es = es
        self.eng = {"pe": nc.tensor, "act": nc.scalar, "dve": nc.vector,
                    "pool": nc.gpsimd, "sp": nc.sync}
        self.sem = {e: es.enter_context(nc.semaphore("c_" + e)) for e in self.eng}
        self.seq = {e: 0 for e in self.eng}
        self.known = {e: {} for e in self.eng}
        self.dma_sems = [es.enter_context(nc.semaphore("d%d" % i)) for i in range(n_dma_sems)]
        self.dma_cnt = [0] * n_dma_sems
        self.dma_pool = {"sw": list(range(0, n_dma_sems // 2)), "hw": list(range(n_dma_sems // 2, n_dma_sems))}
        self.dma_rr = {"sw": 0, "hw": 0}
        self.pe_pending = False
        self.n_wait = 0
        self.n_inst = 0
        self.same_engine_sync = True
        self.RAW_GAP = 10 ** 9

    def _wait(self, e, tok):
        if tok is None:
            return
        kind, who, val = tok
        key = (kind, who)
        if kind == "eng":
            if who == e and (e == "pe" or not self.same_engine_sync):
                return
            if who == e and e == "sp":
                return
        if self.known[e].get(key, 0) >= val:
            return
        self.known[e][key] = val
        sem = self.sem[who] if kind == "eng" else self.dma_sems[who]
        self.eng[e].wait_ge(sem, val)
        self.n_wait += 1

    def _deps(self, e, R, W):
        toks = []
        for r in R:
            toks.append(r.last_w)
        for w in W:
            toks.append(w.last_w)
            toks.extend(w.readers)
        seen = set()
        for t in toks:
            if t is not None and t not in seen:
                seen.add(t)
                self._wait(e, t)

    def _commit(self, tok, R, W):
        for r in R:
            r.readers.append(tok)
            if len(r.readers) > 24:
                best = {}
                for t in r.readers:
                    k = (t[0], t[1])
                    if k not in best or best[k][2] < t[2]:
                        best[k] = t
                r.readers = list(best.values())
        for w in W:
            w.last_w = tok
            w.readers = []

    def op(self, e, fn, R=(), W=(), inc=True):
        self._deps(e, R, W)
        inst = fn()
        self.n_inst += 1
        if inc:
            self.seq[e] += 1
            inst.then_inc(self.sem[e], 1)
            tok = ("eng", e, self.seq[e])
        else:
            assert e == "pe"
            tok = ("eng", e, self.seq[e] + 1)
        self._commit(tok, R, W)
        return tok

    def dma(self, q, out, in_, R=(), W=()):
        self._deps(q, R, W)
        cls = "sw" if q == "pool" else "hw"
        lst = self.dma_pool[cls]
        i = lst[self.dma_rr[cls]]
        self.dma_rr[cls] = (self.dma_rr[cls] + 1) % len(lst)
        if self.dma_cnt[i] > 0:
            self._wait(q, ("dma", i, 16 * self.dma_cnt[i]))
        self.dma_cnt[i] += 1
        self.eng[q].dma_start(out=out, in_=in_).then_inc(self.dma_sems[i], 16)
        self.n_inst += 1
        tok = ("dma", i, 16 * self.dma_cnt[i])
        self._commit(tok, R, W)
        return tok

    def finish(self, regs):
        for r in regs:
            self._wait("sp", r.last_w)


def bcast_mid(ap2, n):
    return ap2.unsqueeze(1).broadcast_to([ap2.shape[0], n, ap2.shape[1]])


def bcast_last(ap2, n):
    return ap2.unsqueeze(2).broadcast_to([ap2.shape[0], ap2.shape[1], n])


ALL_STAGES = ("ffn1_0", "mix_0", "ffn2_0", "ffn1_1", "mix_1", "ffn2_1", "samples")


def build(dbg=None, stages=ALL_STAGES):
    dbg = dbg or {}
    nc = bass.Bass("TRN2", target_bir_lowering=False)
    es = contextlib.ExitStack()
    k = KB(nc, es)

    def din(name, shape, dt=F32):
        return nc.dram_tensor(name, list(shape), dt, kind="ExternalInput").ap()

    def dout(name, shape, dt=F32):
        return nc.dram_tensor(name, list(shape), dt, kind="ExternalOutput").ap()

    xp_d = din("x_prompt", [TP, D])
    xs_d = din("x_sample", [NS, D])
    ln_ffn1_d = din("ln_ffn1", [2, D]); ln_mix_d = din("ln_mix", [2, D]); ln_ffn2_d = din("ln_ffn2", [2, D])
    ln_final_d = din("ln_final", [D])
    ffw = {}
    for nm in ("ff1_wg", "ff1_wu", "ff2_wg", "ff2_wu"):
        ffw[nm] = din(nm, [2, D, DFF])
    for nm in ("ff1_wd", "ff2_wd"):
        ffw[nm] = din(nm, [2, DFF, D])
    ident_d = din("c_ident", [P, P])
    rw = {}
    for nm_, shp_ in [("rw_mu", [6, D]), ("rw_w0", [D]), ("rw_w1", [D, 64]), ("rw_w2", [64, D]), ("rw_a0", [D]),
                      ("rw_a1", [D, 64]), ("rw_a2", [64, D]), ("rw_g1", [D, 128]), ("rw_g2", [128, D]),
                      ("rw_kk", [D]), ("rw_ka", [D]), ("rw_rk", [D]), ("rw_wr", [D, D]), ("rw_wk", [D, D]),
                      ("rw_wv", [D, D]), ("rw_wo", [D, D]), ("rw_lnx_g", [D]), ("rw_lnx_b", [D])]:
        rw[nm_] = din(nm_, shp_)
    rt = {}
    for nm_, shp_ in [("ret_wq", [D, D]), ("ret_wk", [D, D]), ("ret_wv", [D, 2 * D]), ("ret_wg", [D, 2 * D]),
                      ("ret_wo", [2 * D, D])]:
        rt[nm_] = din(nm_, shp_)
    c_cos_d = din("c_cos", [P, T]); c_sin_d = din("c_sin", [P, T])
    c_cosT_d = din("c_cosT", [TP, P]); c_sinT_d = din("c_sinT", [TP, P])
    c_dmatT_d = din("c_dmatT", [P, 4, P]); c_qd_d = din("c_qd", [P, 4, P]); c_kdec_d = din("c_kdec", [P, 4])
    c_sel_d = din("c_sel", [NS, NS, P]); c_i16_d = din("c_i16", [P, NS * NS])
    st_ret_d = din("st_ret", [NS, 4, 256, 512])
    pret_d = dout("p_ret", [4, 256, 512]); sret_d = dout("s_ret", [NS, 4, 256, 512])
    c_bones_d = din("c_bones", [P, P]); c_mask_ur_d = din("c_mask_ur", [P, 256]); c_mask_l_d = din("c_mask_l", [P, P])
    c_scanm_d = din("c_scanm", [P, 256])
    st_shift_d = din("st_shift", [NS, D]); st_wkv_d = din("st_wkv", [NS, 16, 64, 64])
    pshift_d = dout("p_shift", [D]); pwkv_d = dout("p_wkv", [16, 64, 64])
    sshift_d = dout("s_shift", [NS, D]); swkv_d = dout("s_wkv", [NS, 16, 64, 64])
    yp_d = dout("y_prompt", [TP, D])
    ys_d = dout("y_sample", [NS, D])
    out_regs = []

    def sb(name, shape, dt=F32):
        return nc.alloc_sbuf_tensor(name, list(shape), dt)

    AR_BYTES = 101 * 1024 + 512
    arena = sb("arena", [P, AR_BYTES // 2], BF16)
    carry = [[]]

    class Phase:
        def __init__(self, base=0, limit=None, carry_ref=None):
            self.off = base
            self.limit = AR_BYTES if limit is None else limit
            self.regs = []
            self.carry = carry if carry_ref is None else carry_ref

        def alloc(self, shape, dt=F32):
            esz = 4 if dt == F32 else 2
            n = 1
            for s_ in shape[1:]:
                n *= s_
            nb = (n * esz + 63) // 64 * 64
            assert self.off + nb <= self.limit, ("arena overflow", self.off, nb, self.limit)
            v = arena[0:shape[0], self.off // 2:(self.off + n * esz) // 2]
            if dt == F32:
                v = v.bitcast(F32)
            self.off += nb
            if len(shape) == 3:
                v = v.rearrange("p (a b) -> p a b", a=shape[1])
            elif len(shape) == 4:
                v = v.rearrange("p (a b c) -> p a b c", a=shape[1], b=shape[2])
            return v

        def reg(self, name):
            r = Reg(name)
            r.readers = list(self.carry[0])
            self.regs.append(r)
            return r

        def end(self):
            toks = set(self.carry[0])
            for r in self.regs:
                if r.last_w is not None:
                    toks.add(r.last_w)
                toks.update(r.readers)
            best = {}
            for t_ in toks:
                kk_ = (t_[0], t_[1])
                if kk_ not in best or best[kk_][2] < t_[2]:
                    best[kk_] = t_
            self.carry[0] = list(best.values())

    NWB = 5
    WB_ELEMS = 4096
    wbuf = [sb("wbuf%d" % i, [P, WB_ELEMS], BF16) for i in range(NWB)]
    r_wbuf = [Reg("wbuf%d" % i) for i in range(NWB)]
    wb_rr = [0]

    def next_wbuf():
        i = wb_rr[0]
        wb_rr[0] = (i + 1) % NWB
        return wbuf[i], r_wbuf[i]

    ident = sb("ident", [P, P]); r_ident = Reg("ident")
    k.dma("sp", ident[:], ident_d, W=[r_ident])
    identb = sb("identb", [P, P], BF16); r_identb = Reg("identb")
    k.op("dve", lambda: nc.vector.tensor_copy(out=identb[:], in_=ident[:]), R=[r_ident], W=[r_identb])
    ones_b = sb("ones_b", [P, P], BF16); r_ones = Reg("ones")
    k.op("dve", lambda: nc.vector.memset(ones_b[:], 1.0), W=[r_ones])
    eps_t = sb("eps_t", [P, 1]); r_eps = Reg("eps")
    k.op("dve", lambda: nc.vector.memset(eps_t[:], RMS_EPS), W=[r_eps])
    lnw = sb("lnw", [P, 7, KC]); r_lnw = Reg("lnw")
    with nc.allow_non_contiguous_dma(reason="tiny gain vectors"):
        for i, (src_, l) in enumerate([(ln_ffn1_d, 0), (ln_mix_d, 0), (ln_ffn2_d, 0),
                                       (ln_ffn1_d, 1), (ln_mix_d, 1), (ln_ffn2_d, 1)]):
            k.dma("sp", lnw[:, i, :], src_[l].rearrange("(k p) -> p k", p=P), W=[r_lnw])
        k.dma("sp", lnw[:, 6, :], ln_final_d.rearrange("(k p) -> p k", p=P), W=[r_lnw])

    NB = 8
    banks = [nc.alloc_psum_tensor("bank%d" % i, [P, 512], F32) for i in range(NB)]
    r_bank = [Reg("bank%d" % i) for i in range(NB)]
    bank_rr = [0]

    nonlocal_nb = [NB]

    def next_bank():
        i = bank_rr[0] % nonlocal_nb[0]
        bank_rr[0] = (i + 1) % nonlocal_nb[0]
        return banks[i], r_bank[i]

    x = sb("x", [P, KC, T]); r_x = [[Reg("x%d_%d" % (d, t)) for t in range(5)] for d in range(KC)]

    alt = [0]

    def evac_copy(dst_ap, src_ap, R, W, e=None):
        if e is None:
            e = "act" if alt[0] % 2 == 0 else "dve"
            alt[0] += 1
        if e == "act":
            k.op("act", lambda: nc.scalar.copy(out=dst_ap, in_=src_ap), R=R, W=W)
        else:
            k.op("dve", lambda: nc.vector.tensor_copy(out=dst_ap, in_=src_ap), R=R, W=W)

    lx_carry = [[]]

    def load_x():
        ph = Phase(base=AR_BYTES - 2 * D * 4, carry_ref=lx_carry)
        xin = [ph.alloc([P, D]) for _ in range(2)]
        r_xin = [ph.reg("xin0"), ph.reg("xin1")]
        ld = 0
        for ti, (t0, tn) in enumerate(TILES):
            nsub = (tn + P - 1) // P
            for s in range(nsub):
                rows = min(P, tn - s * P)
                b = ld % 2; ld += 1
                src_ = xp_d[t0 + s * P: t0 + s * P + rows, :] if ti < 4 else xs_d[:, :]
                k.dma("sp", xin[b][0:rows, :], src_, W=[r_xin[b]])
                for d0 in range(0, KC, 4):
                    bk, rb = next_bank()
                    for j in range(4):
                        d = d0 + j
                        k.op("pe", lambda d=d, j=j: nc.tensor.transpose(
                            bk[:, j * P: j * P + rows], xin[b][0:rows, d * P:(d + 1) * P], ident[0:rows, 0:rows]),
                            R=[r_xin[b], r_ident], W=[rb])
                    c0 = t0 + s * P
                    src_ap = bk[:, :].rearrange("p (j c) -> p j c", j=4)[:, :, 0:rows]
                    evac_copy(x[:, d0:d0 + 4, c0:c0 + rows], src_ap, R=[rb],
                              W=[r_x[d][ti] for d in range(d0, d0 + 4)])
        ph.end()

    def rmsnorm_tile(ph_bufs, gi, ti, out_ap_fn, W_fn, rng=None, extra=None, xregs=None):
        sqs, r_sqs, rstd, r_rstd, cnt = ph_bufs
        t0, tn = TILES[ti] if rng is None else rng
        xr_ = (lambda d: [r_x[d][ti]]) if xregs is None else xregs
        bk, rb = next_bank()
        for d in range(KC):
            b = cnt[0] % len(sqs); cnt[0] += 1
            k.op("act", lambda: nc.scalar.activation(
                out=sqs[b][:, 0:tn], in_=x[:, d, t0:t0 + tn], func=AF.Square),
                R=xr_(d), W=[r_sqs[b]])
            k.op("pe", lambda: nc.tensor.matmul(
                bk[:, 0:tn], lhsT=ones_b[:], rhs=sqs[b][:, 0:tn], start=(d == 0), stop=(d == KC - 1)),
                R=[r_sqs[b], r_ones], W=[rb], inc=True)
        rb_i = cnt[1] % 2; cnt[1] += 1
        k.op("act", lambda: nc.scalar.activation(
            out=rstd[rb_i][:, 0:tn], in_=bk[:, 0:tn], func=AF.Ln, scale=1.0 / D, bias=eps_t[:, 0:1]),
            R=[rb, r_eps], W=[r_rstd[rb_i]])
        k.op("act", lambda: nc.scalar.activation(
            out=rstd[rb_i][:, 0:tn], in_=rstd[rb_i][:, 0:tn], func=AF.Exp, scale=-0.5),
            R=[r_rstd[rb_i]], W=[r_rstd[rb_i]])
        for d in range(KC):
            k.op("dve", lambda: nc.vector.scalar_tensor_tensor(
                out=out_ap_fn(d, t0, tn), in0=x[:, d, t0:t0 + tn], scalar=lnw[:, gi, d:d + 1],
                in1=rstd[rb_i][:, 0:tn], op0=ALU.mult, op1=ALU.mult),
                R=xr_(d) + [r_rstd[rb_i], r_lnw], W=W_fn(d))
            if extra is not None:
                extra(d, rstd[rb_i], r_rstd[rb_i])

    def norm_bufs(ph, width=512):
        sqs = [ph.alloc([P, width], BF16) for _ in range(4)]
        r_sqs = [ph.reg("sq%d" % i) for i in range(4)]
        rstd = [ph.alloc([P, width]) for _ in range(2)]
        r_rstd = [ph.reg("rstd%d" % i) for i in range(2)]
        return (sqs, r_sqs, rstd, r_rstd, [0, 0])

    FB = 4

    def ffn(l, which):
        ph = Phase()
        nb = norm_bufs(ph)
        h = ph.alloc([P, KC, T], BF16)
        r_h = [[ph.reg("h%d_%d" % (d, t)) for t in range(5)] for d in range(KC)]
        a_t = ph.alloc([P, FB, T], BF16)
        r_a = [[ph.reg("a%d_%d" % (f, t)) for t in range(5)] for f in range(FB)]
        sl_t = [ph.alloc([P, 512]) for _ in range(2)]
        r_sl = [ph.reg("sl0"), ph.reg("sl1")]
        gi = {1: 0, 2: 2}[which] + 3 * l
        wg_d = ffw["ff%d_wg" % which][l].rearrange("(k p) c -> p k c", p=P)
        wu_d = ffw["ff%d_wu" % which][l].rearrange("(k p) c -> p k c", p=P)
        wd_d = ffw["ff%d_wd" % which][l].rearrange("(f p) c -> p f c", p=P)
        FT = [(0, 512), (512, 512), (1024, 512), (1536, 264), (1800, 264)]

        def xr(d, t0, tn):
            return [r_x[d][ti_] for ti_, (a_, b_) in enumerate(TILES) if a_ < t0 + tn and t0 < a_ + b_]
        for ti in range(5):
            rmsnorm_tile(nb, gi, ti, lambda d, t0, tn: h[:, d, t0:t0 + tn], lambda d, ti=ti: [r_h[d][ti]], rng=FT[ti],
                         xregs=lambda d, ti=ti: xr(d, *FT[ti]))
        nblk = (NFC + FB - 1) // FB
        sl_i = 0
        for blk in range(nblk):
            f0 = blk * FB
            nf = min(FB, NFC - f0)
            wgb, r_wg = next_wbuf()
            wub, r_wu = next_wbuf()
            wdb, r_wd = next_wbuf()
            wg_v = wgb[:, 0:KC * nf * P].rearrange("p (k c) -> p k c", k=KC)
            wu_v = wub[:, 0:KC * nf * P].rearrange("p (k c) -> p k c", k=KC)
            wd_v = wdb[:, 0:nf * D].rearrange("p (f c) -> p f c", f=nf)
            k.dma("pool", wg_v, wg_d[:, :, f0 * P:(f0 + nf) * P], W=[r_wg])
            k.dma("pool", wu_v, wu_d[:, :, f0 * P:(f0 + nf) * P], W=[r_wu])
            k.dma("pool", wd_v, wd_d[:, f0:f0 + nf, :], W=[r_wd])
            for f in range(nf):
                for ti, (t0, tn) in enumerate(FT):
                    bg, rbg = next_bank()
                    bu, rbu = next_bank()
                    for kk in range(KC):
                        k.op("pe", lambda: nc.tensor.matmul(
                            bg[:, 0:tn], lhsT=wg_v[:, kk, f * P:(f + 1) * P], rhs=h[:, kk, t0:t0 + tn],
                            start=(kk == 0), stop=(kk == KC - 1)),
                            R=[r_wg, r_h[kk][ti]], W=[rbg], inc=(kk == KC - 1))
                    for kk in range(KC):
                        k.op("pe", lambda: nc.tensor.matmul(
                            bu[:, 0:tn], lhsT=wu_v[:, kk, f * P:(f + 1) * P], rhs=h[:, kk, t0:t0 + tn],
                            start=(kk == 0), stop=(kk == KC - 1)),
                            R=[r_wu, r_h[kk][ti]], W=[rbu], inc=(kk == KC - 1))
                    sb_i = sl_i % 2; sl_i += 1
                    k.op("act", lambda: nc.scalar.activation(
                        out=sl_t[sb_i][:, 0:tn], in_=bg[:, 0:tn], func=AF.Silu),
                        R=[rbg], W=[r_sl[sb_i]])
                    k.op("dve", lambda: nc.vector.tensor_tensor(
                        out=a_t[:, f, t0:t0 + tn], in0=bu[:, 0:tn], in1=sl_t[sb_i][:, 0:tn], op=ALU.mult),
                        R=[rbu, r_sl[sb_i]], W=[r_a[f][ti]])
            for d in range(KC):
                for ti, (t0, tn) in enumerate(FT):
                    bk, rb = next_bank()
                    for f in range(nf):
                        k.op("pe", lambda: nc.tensor.matmul(
                            bk[:, 0:tn], lhsT=wd_v[:, f, d * P:(d + 1) * P], rhs=a_t[:, f, t0:t0 + tn],
                            start=(f == 0), stop=(f == nf - 1)),
                            R=[r_wd, r_a[f][ti]], W=[rb], inc=(f == nf - 1))
                    k.op("dve", lambda: nc.vector.scalar_tensor_tensor(
                        out=x[:, d, t0:t0 + tn], in0=bk[:, 0:tn], scalar=0.5, in1=x[:, d, t0:t0 + tn],
                        op0=ALU.mult, op1=ALU.add),
                        R=[rb] + xr(d, t0, tn), W=xr(d, t0, tn))
        ph.end()


    def rwkv_mixer(do_samples=True):
        ph = Phase()
        SEG = 256
        C0 = -float(np.exp(-0.5))
        mm = nc.tensor.matmul
        rv = ph.alloc([P, 14, KC]); r_rv = ph.reg("rv")
        with nc.allow_non_contiguous_dma(reason="tiny per-feature vectors"):
            for i in range(6):
                k.dma("sp", rv[:, i, :], rw["rw_mu"][i].rearrange("(k p) -> p k", p=P), W=[r_rv])
            for i, nm in enumerate(["rw_w0", "rw_a0", "rw_kk", "rw_ka", "rw_rk", "rw_lnx_g", "rw_lnx_b"]):
                k.dma("sp", rv[:, 6 + i, :], rw[nm].rearrange("(k p) -> p k", p=P), W=[r_rv])
        k.op("dve", lambda: nc.vector.tensor_scalar(out=rv[:, 13, :], in0=rv[:, 9, :], scalar1=-1.0, scalar2=1.0,
                                                    op0=ALU.mult, op1=ALU.add), R=[r_rv], W=[r_rv])
        w1b = ph.alloc([P, KC, 64], BF16); w2b = ph.alloc([64, D], BF16)
        a1b = ph.alloc([P, KC, 64], BF16); a2b = ph.alloc([64, D], BF16)
        g1b = ph.alloc([P, KC, 128], BF16); g2b = ph.alloc([P, D], BF16)
        r_lw = ph.reg("lora")
        k.dma("pool", w1b, rw["rw_w1"].rearrange("(k p) c -> p k c", p=P), W=[r_lw])
        k.dma("pool", w2b, rw["rw_w2"], W=[r_lw])
        k.dma("pool", a1b, rw["rw_a1"].rearrange("(k p) c -> p k c", p=P), W=[r_lw])
        k.dma("pool", a2b, rw["rw_a2"], W=[r_lw])
        k.dma("pool", g1b, rw["rw_g1"].rearrange("(k p) c -> p k c", p=P), W=[r_lw])
        k.dma("pool", g2b, rw["rw_g2"], W=[r_lw])
        mask_ur = ph.alloc([P, 256], BF16); mask_l = ph.alloc([P, P], BF16)
        bones = ph.alloc([P, P], BF16); bones64 = ph.alloc([P, P], BF16)
        scanm = ph.alloc([P, 256]); scanz = ph.alloc([P, NS]); eps_gn = ph.alloc([P, 1])
        r_cst = ph.reg("rwconst")
        k.dma("pool", mask_ur, c_mask_ur_d, W=[r_cst])
        k.dma("pool", mask_l, c_mask_l_d, W=[r_cst])
        k.dma("pool", bones, c_bones_d, W=[r_cst])
        k.dma("sp", scanm, c_scanm_d, W=[r_cst])
        k.op("dve", lambda: nc.vector.tensor_scalar(out=bones64, in0=bones, scalar1=1.0 / 64, scalar2=None,
                                                    op0=ALU.mult), R=[r_cst], W=[r_cst])
        k.op("dve", lambda: nc.vector.memset(scanz, 0.0), W=[r_cst])
        k.op("dve", lambda: nc.vector.memset(eps_gn, 64e-5), W=[r_cst])
        H32 = ph.alloc([P, KC, 64]); Hbd = ph.alloc([P, KC, P], BF16)
        r_H = [ph.reg("H%d" % g) for g in range(4)]
        k.op("dve", lambda: nc.vector.memset(H32, 0.0), W=r_H)
        k.op("dve", lambda: nc.vector.memset(Hbd, 0.0), W=r_H)
        shf = ph.alloc([P, KC, 1 + NS]); r_shf = ph.reg("shf")
        k.op("dve", lambda: nc.vector.memset(shf, 0.0), W=[r_shf])
        hcar = ph.alloc([P, KC, 1], BF16); r_hcar = ph.reg("hcar")
        k.op("dve", lambda: nc.vector.memset(hcar, 0.0), W=[r_hcar])
        hsp = ph.alloc([P, KC, NS], BF16); r_hsp = ph.reg("hsp")
        if do_samples:
            stg = ph.alloc([NS, D]); r_stg = ph.reg("stg")
            k.dma("sp", stg, st_shift_d, W=[r_stg])
            for half in range(2):
                bk, rb = next_bank()
                for jj in range(4):
                    d = half * 4 + jj
                    k.op("pe", lambda: nc.tensor.transpose(bk[:, jj * NS:(jj + 1) * NS], stg[0:NS, d * P:(d + 1) * P], ident[0:NS, 0:NS]),
                         R=[r_stg, r_ident], W=[rb])
                evac_copy(hsp[:, half * 4:half * 4 + 4, :], bk[:, 0:4 * NS].rearrange("p (a b) -> p a b", a=4), R=[rb], W=[r_hsp])
        ar = ph.alloc([P, KC, 2, SEG], BF16); r_ar = [ph.reg("ar%d" % d) for d in range(KC)]
        bt = ph.alloc([P, KC, SEG], BF16); r_bt = [ph.reg("bt%d" % d) for d in range(KC)]
        kt = ph.alloc([P, KC, SEG], BF16); r_kt = [ph.reg("kt%d" % d) for d in range(KC)]
        vt = ph.alloc([P, KC, SEG], BF16); r_vt = [ph.reg("vt%d" % d) for d in range(KC)]
        gt = ph.alloc([P, KC, SEG], BF16); r_gt = [ph.reg("gt%d" % d) for d in range(KC)]
        bon = ph.alloc([P, KC, SEG], BF16); r_bon = [ph.reg("bon%d" % d) for d in range(KC)]
        og = ph.alloc([P, KC, SEG], BF16); r_og = ph.reg("og")
        gC = ph.alloc([P, KC, 2]); r_gC = ph.reg("gC")
        dS = ph.alloc([P, KC, NS]); r_dS = ph.reg("dS")
        ov_base = ph.off
        ovc = [list(carry[0])]
        ybk = [banks[6], banks[7]]; r_ybk = [r_bank[6], r_bank[7]]
        nonlocal_nb[0] = 6

        def stageA(t0, n, ti, sample, last_prompt):
            pa = Phase(base=ov_base, carry_ref=ovc)
            nbuf = norm_bufs(pa, SEG)
            hn = pa.alloc([P, KC, SEG + 2], BF16); r_hn = [pa.reg("hn%d" % d) for d in range(KC)]
            dx = pa.alloc([P, KC, SEG], BF16); r_dx = pa.reg("dx")
            xi = [pa.alloc([P, KC, SEG], BF16) for _ in range(2)]; r_xi = [pa.reg("xi0"), pa.reg("xi1")]
            lt = [pa.alloc([P, SEG], BF16) for _ in range(3)]; r_lt = [pa.reg("lt%d" % i) for i in range(3)]
            T32 = [[pa.alloc([P, SEG]) for _ in range(4)] for _ in range(4)]
            rT32 = [[pa.reg("t32_%d_%d" % (i, j)) for j in range(4)] for i in range(4)]
            T16 = [[pa.alloc([P, SEG], BF16) for _ in range(4)] for _ in range(3)]
            rT16 = [[pa.reg("t16_%d_%d" % (i, j)) for j in range(4)] for i in range(3)]
            if not sample:
                k.op("dve", lambda: nc.vector.tensor_copy(out=hn[:, :, 0:1], in_=hcar[:, :, 0:1]), R=[r_hcar], W=r_hn)

            def extra(d, rstd_t, r_rstd_t):
                if last_prompt:
                    k.op("dve", lambda: nc.vector.scalar_tensor_tensor(
                        out=shf[:, d, 0:1], in0=x[:, d, t0 + n - 1:t0 + n], scalar=lnw[:, 1, d:d + 1],
                        in1=rstd_t[:, n - 1:n], op0=ALU.mult, op1=ALU.mult),
                        R=[r_x[d][ti], r_rstd_t, r_lnw], W=[r_shf])
                if sample:
                    k.op("dve", lambda: nc.vector.scalar_tensor_tensor(
                        out=shf[:, d, 1:1 + NS], in0=x[:, d, t0:t0 + n], scalar=lnw[:, 1, d:d + 1],
                        in1=rstd_t[:, 0:n], op0=ALU.mult, op1=ALU.mult),
                        R=[r_x[d][ti], r_rstd_t, r_lnw], W=[r_shf])
            rmsnorm_tile(nbuf, 1, ti, lambda d, a_, b_: hn[:, d, 1:1 + n], lambda d: [r_hn[d]], rng=(t0, n), extra=extra)
            hprev = hsp[:, :, 0:n] if sample else hn[:, :, 0:n]
            k.op("dve", lambda: nc.vector.tensor_tensor(out=dx[:, :, 0:n], in0=hprev, in1=hn[:, :, 1:1 + n], op=ALU.subtract),
                 R=r_hn + ([r_hsp] if sample else []), W=[r_dx])
            if not sample:
                k.op("dve", lambda: nc.vector.tensor_copy(out=hcar[:, :, 0:1], in_=hn[:, :, n:n + 1]), R=r_hn, W=[r_hcar])
            xi_i = [0]

            def make_xi(i):
                b = xi_i[0] % 2; xi_i[0] += 1
                for d in range(KC):
                    k.op("dve", lambda: nc.vector.scalar_tensor_tensor(
                        out=xi[b][:, d, 0:n], in0=dx[:, d, 0:n], scalar=rv[:, i, d:d + 1], in1=hn[:, d, 1:1 + n],
                        op0=ALU.mult, op1=ALU.add), R=[r_dx, r_hn[d], r_rv], W=[r_xi[b]])
                return xi[b], r_xi[b]

            def proj_big(wd, xt, r_xt, evac):
                wv_ = wd.rearrange("(k p) c -> p k c", p=P)
                for half in range(2):
                    wb, r_w = next_wbuf()
                    v = wb[:, 0:KC * 512].rearrange("p (k c) -> p k c", k=KC)
                    k.dma("pool", v, wv_[:, :, half * 512:(half + 1) * 512], W=[r_w])
                    for oc in range(4):
                        d = half * 4 + oc
                        bk, rb = next_bank()
                        for kk in range(KC):
                            k.op("pe", lambda: mm(bk[:, 0:n], lhsT=v[:, kk, oc * P:(oc + 1) * P], rhs=xt[:, kk, 0:n],
                                                  start=(kk == 0), stop=(kk == KC - 1)),
                                 R=[r_w, r_xt], W=[rb], inc=(kk == KC - 1))
                        evac(d, bk[:, 0:n], rb)

            def lora_in(wl, m, xt, r_xt, func, li):
                bk, rb = next_bank()
                for kk in range(KC):
                    k.op("pe", lambda: mm(bk[0:m, 0:n], lhsT=wl[:, kk, 0:m], rhs=xt[:, kk, 0:n],
                                          start=(kk == 0), stop=(kk == KC - 1)),
                         R=[r_lw, r_xt], W=[rb], inc=(kk == KC - 1))
                k.op("act", lambda: nc.scalar.activation(out=lt[li][0:m, 0:n], in_=bk[0:m, 0:n], func=func),
                     R=[rb], W=[r_lt[li]])

            def lora_out(wl2, m, li, d):
                bk, rb = next_bank()
                k.op("pe", lambda: mm(bk[:, 0:n], lhsT=wl2[0:m, d * P:(d + 1) * P], rhs=lt[li][0:m, 0:n],
                                      start=True, stop=True), R=[r_lw, r_lt[li]], W=[rb])
                return bk, rb

            xt, r_xt = make_xi(0)
            proj_big(rw["rw_wr"], xt, r_xt, lambda d, ps, rb: k.op(
                "act", lambda: nc.scalar.copy(out=ar[:, d, 1, 0:n], in_=ps), R=[rb], W=[r_ar[d]]))
            xt, r_xt = make_xi(2)
            proj_big(rw["rw_wk"], xt, r_xt, lambda d, ps, rb: evac_copy(kt[:, d, 0:n], ps, R=[rb], W=[r_kt[d]]))
            xt, r_xt = make_xi(4)
            lora_in(a1b, 64, xt, r_xt, AF.Copy, 0)
            for d in range(KC):
                bk, rb = lora_out(a2b, 64, 0, d)
                k.op("act", lambda: nc.scalar.activation(out=bt[:, d, 0:n], in_=bk[:, 0:n], func=AF.Sigmoid,
                                                         bias=rv[:, 7, d:d + 1]), R=[rb, r_rv], W=[r_bt[d]])
            GD = 4

            def staged(steps):
                for g0 in range(0, KC, GD):
                    st = {}
                    for step in steps:
                        for d in range(g0, g0 + GD):
                            step(d, st)

            def s_kr(d, st):
                q = d % GD
                k.op("dve", lambda: nc.vector.tensor_scalar(out=T16[0][q][:, 0:n], in0=kt[:, d, 0:n], scalar1=rv[:, 8, d:d + 1],
                                                            scalar2=None, op0=ALU.mult), R=[r_kt[d], r_rv], W=[rT16[0][q]])

            def s_sq(d, st):
                q = d % GD
                k.op("act", lambda: nc.scalar.activation(out=T16[1][q][:, 0:n], in_=T16[0][q][:, 0:n], func=AF.Square),
                     R=[rT16[0][q]], W=[rT16[1][q]])

            def s_ss(d, st):
                q = d % GD
                bk, rb = next_bank()
                st[("ss", d)] = (bk, rb)
                k.op("pe", lambda: mm(bk[:, 0:n], lhsT=bones, rhs=T16[1][q][:, 0:n], start=True, stop=True),
                     R=[r_cst, rT16[1][q]], W=[rb])

            def s_max(d, st):
                q = d % GD
                bk, rb = st[("ss", d)]
                k.op("dve", lambda: nc.vector.tensor_scalar(out=T32[0][q][:, 0:n], in0=bk[:, 0:n], scalar1=1e-24, scalar2=None,
                                                            op0=ALU.max), R=[rb], W=[rT32[0][q]])

            def s_ln(d, st):
                q = d % GD
                k.op("act", lambda: nc.scalar.activation(out=T32[0][q][:, 0:n], in_=T32[0][q][:, 0:n], func=AF.Ln),
                     R=[rT32[0][q]], W=[rT32[0][q]])

            def s_ex(d, st):
                q = d % GD
                k.op("act", lambda: nc.scalar.activation(out=T32[0][q][:, 0:n], in_=T32[0][q][:, 0:n], func=AF.Exp, scale=-0.5),
                     R=[rT32[0][q]], W=[rT32[0][q]])

            def s_kk(d, st):
                q = d % GD
                k.op("dve", lambda: nc.vector.tensor_tensor(out=ar[:, d, 0, 0:n], in0=T16[0][q][:, 0:n], in1=T32[0][q][:, 0:n], op=ALU.mult),
                     R=[rT16[0][q], rT32[0][q]], W=[r_ar[d]])
                k.op("dve", lambda: nc.vector.tensor_scalar(out=T32[1][q][:, 0:n], in0=bt[:, d, 0:n], scalar1=rv[:, 9, d:d + 1],
                                                            scalar2=rv[:, 13, d:d + 1], op0=ALU.mult, op1=ALU.add),
                     R=[r_bt[d], r_rv], W=[rT32[1][q]])

            def s_kb(d, st):
                q = d % GD
                k.op("dve", lambda: nc.vector.tensor_tensor(out=kt[:, d, 0:n], in0=kt[:, d, 0:n], in1=T32[1][q][:, 0:n], op=ALU.mult),
                     R=[rT32[1][q], r_kt[d]], W=[r_kt[d]])
                k.op("dve", lambda: nc.vector.tensor_tensor(out=bt[:, d, 0:n], in0=bt[:, d, 0:n], in1=ar[:, d, 0, 0:n], op=ALU.mult),
                     R=[r_ar[d], r_bt[d]], W=[r_bt[d]])

            def s_bq(d, st):
                q = d % GD
                k.op("dve", lambda: nc.vector.scalar_tensor_tensor(
                    out=T16[2][q][:, 0:n], in0=ar[:, d, 1, 0:n], scalar=rv[:, 10, d:d + 1], in1=kt[:, d, 0:n],
                    op0=ALU.mult, op1=ALU.mult), R=[r_ar[d], r_kt[d], r_rv], W=[rT16[2][q]])

            def s_bs(d, st):
                q = d % GD
                bk2, rb2 = next_bank()
                st[("bs", d)] = (bk2, rb2)
                k.op("pe", lambda: mm(bk2[:, 0:n], lhsT=bones, rhs=T16[2][q][:, 0:n], start=True, stop=True),
                     R=[r_cst, rT16[2][q]], W=[rb2])

            def s_bon(d, st):
                bk2, rb2 = st[("bs", d)]
                k.op("act", lambda: nc.scalar.copy(out=bon[:, d, 0:n], in_=bk2[:, 0:n]), R=[rb2], W=[r_bon[d]])

            staged([s_kr, s_sq, s_ss, s_max, s_ln, s_ex, s_kk, s_kb, s_bq, s_bs, s_bon])

            xt, r_xt = make_xi(1)
            lora_in(w1b, 64, xt, r_xt, AF.Tanh, 1)
            smask = scanz[:, 0:n] if sample else scanm[:, 0:n]
            SG, CW, E1, E2 = T32[0], T32[1], T32[2], T32[3]
            rSG, rCW, rE1, rE2 = rT32[0], rT32[1], rT32[2], rT32[3]

            def t_mm(d, st):
                st[("w", d)] = lora_out(w2b, 64, 1, d)

            def t_sig(d, st):
                q = d % GD
                bk, rb = st[("w", d)]
                k.op("act", lambda: nc.scalar.activation(out=SG[q][:, 0:n], in_=bk[:, 0:n], func=AF.Sigmoid,
                                                         bias=rv[:, 6, d:d + 1]), R=[rb, r_rv], W=[rSG[q]])

            def t_scan(d, st):
                q = d % GD
                k.op("dve", lambda: nc.vector.tensor_tensor_scan(out=CW[q][:, 0:n], data0=smask, data1=SG[q][:, 0:n], initial=0.0,
                                                                 op0=ALU.mult, op1=ALU.add), R=[rSG[q], r_cst], W=[rCW[q]])
                k.op("dve", lambda: nc.vector.tensor_tensor(out=SG[q][:, 0:n], in0=CW[q][:, 0:n], in1=SG[q][:, 0:n], op=ALU.subtract),
                     R=[rCW[q], rSG[q]], W=[rSG[q]])

            def t_e1(d, st):
                q = d % GD
                k.op("act", lambda: nc.scalar.activation(out=E1[q][:, 0:n], in_=CW[q][:, 0:n], func=AF.Exp, scale=C0),
                     R=[rCW[q]], W=[rE1[q]])
                k.op("act", lambda: nc.scalar.activation(out=E2[q][:, 0:n], in_=SG[q][:, 0:n], func=AF.Exp, scale=C0),
                     R=[rSG[q]], W=[rE2[q]])
                k.op("act", lambda: nc.scalar.activation(out=SG[q][:, 0:n], in_=CW[q][:, 0:n], func=AF.Exp, scale=-C0),
                     R=[rCW[q], rSG[q]], W=[rSG[q]])

            def t_apply(d, st):
                q = d % GD
                k.op("dve", lambda: nc.vector.tensor_tensor(out=ar[:, d, 1, 0:n], in0=ar[:, d, 1, 0:n], in1=E1[q][:, 0:n], op=ALU.mult),
                     R=[rE1[q], r_ar[d]], W=[r_ar[d]])
                if sample:
                    k.op("dve", lambda: nc.vector.tensor_copy(out=dS[:, d, 0:n], in_=E1[q][:, 0:n]), R=[rE1[q]], W=[r_dS])
                else:
                    for c in range(n // P):
                        k.op("dve", lambda: nc.vector.tensor_copy(out=gC[:, d, c:c + 1], in_=E1[q][:, c * P + P - 1:c * P + P]),
                             R=[rE1[q]], W=[r_gC])
                k.op("dve", lambda: nc.vector.scalar_tensor_tensor(
                    out=ar[:, d, 0, 0:n], in0=ar[:, d, 0, 0:n], scalar=-1.0, in1=E2[q][:, 0:n], op0=ALU.mult, op1=ALU.mult),
                    R=[rE2[q], r_ar[d]], W=[r_ar[d]])
                k.op("dve", lambda: nc.vector.tensor_tensor(out=bt[:, d, 0:n], in0=bt[:, d, 0:n], in1=SG[q][:, 0:n], op=ALU.mult),
                     R=[rSG[q], r_bt[d]], W=[r_bt[d]])
                k.op("dve", lambda: nc.vector.tensor_tensor(out=kt[:, d, 0:n], in0=kt[:, d, 0:n], in1=SG[q][:, 0:n], op=ALU.mult),
                     R=[rSG[q], r_kt[d]], W=[r_kt[d]])

            staged([t_mm, t_sig, t_scan, t_e1, t_apply])
            xt, r_xt = make_xi(3)
            proj_big(rw["rw_wv"], xt, r_xt, lambda d, ps, rb: evac_copy(vt[:, d, 0:n], ps, R=[rb], W=[r_vt[d]]))
            xt, r_xt = make_xi(5)
            lora_in(g1b, 128, xt, r_xt, AF.Sigmoid, 2)
            for d in range(KC):
                bk, rb = lora_out(g2b, 128, 2, d)
                evac_copy(gt[:, d, 0:n], bk[:, 0:n], R=[rb], W=[r_gt[d]])
            pa.end()

        def post(pb_tiles, ysrc_fn, c0, w):
            for _ in post_gen(pb_tiles, ysrc_fn, c0, w, None):
                pass

        def post_gen(pb_tiles, ysrc_fn, c0, w, done_flag):
            y32s, r_y32s, ybs, r_ybs, sqbs, r_sqbs, rss, r_rss = pb_tiles
            for d in range(KC):
                if d > 0:
                    yield
                q = d % 2
                y32, r_y32 = y32s[q], r_y32s[q]
                yb, r_yb = ybs[q], r_ybs[q]
                sqb, r_sqb = sqbs[q], r_sqbs[q]
                rs, r_rs = rss[q], r_rss[q]
                ysrc, r_ysrc = ysrc_fn(d)
                k.op("act", lambda: nc.scalar.copy(out=y32[:, 0:w], in_=ysrc), R=[r_ysrc], W=[r_y32])
                k.op("dve", lambda: nc.vector.tensor_copy(out=yb[:, 0:w], in_=ysrc), R=[r_ysrc], W=[r_yb])
                bk, rb = next_bank()
                k.op("pe", lambda: mm(bk[:, 0:w], lhsT=bones64, rhs=yb[:, 0:w], start=True, stop=True),
                     R=[r_cst, r_yb], W=[rb])
                k.op("dve", lambda: nc.vector.tensor_tensor(out=y32[:, 0:w], in0=y32[:, 0:w], in1=bk[:, 0:w], op=ALU.subtract),
                     R=[rb, r_y32], W=[r_y32])
                k.op("act", lambda: nc.scalar.activation(out=sqb[:, 0:w], in_=y32[:, 0:w], func=AF.Square),
                     R=[r_y32], W=[r_sqb])
                bk2, rb2 = next_bank()
                k.op("pe", lambda: mm(bk2[:, 0:w], lhsT=bones64, rhs=sqb[:, 0:w], start=True, stop=True),
                     R=[r_cst, r_sqb], W=[rb2])
                k.op("act", lambda: nc.scalar.activation(out=rs[:, 0:w], in_=bk2[:, 0:w], func=AF.Ln, bias=eps_gn[:, 0:1]),
                     R=[rb2, r_cst], W=[r_rs])
                k.op("act", lambda: nc.scalar.activation(out=rs[:, 0:w], in_=rs[:, 0:w], func=AF.Exp, scale=-0.5), R=[r_rs], W=[r_rs])
                k.op("dve", lambda: nc.vector.tensor_tensor(out=y32[:, 0:w], in0=y32[:, 0:w], in1=rs[:, 0:w], op=ALU.mult),
                     R=[r_rs, r_y32], W=[r_y32])
                k.op("dve", lambda: nc.vector.tensor_scalar(out=y32[:, 0:w], in0=y32[:, 0:w], scalar1=rv[:, 11, d:d + 1],
                                                            scalar2=rv[:, 12, d:d + 1], op0=ALU.mult, op1=ALU.add),
                     R=[r_y32, r_rv], W=[r_y32])
                k.op("dve", lambda: nc.vector.tensor_tensor(out=rs[:, 0:w], in0=bon[:, d, c0:c0 + w], in1=vt[:, d, c0:c0 + w], op=ALU.mult),
                     R=[r_bon[d], r_vt[d]], W=[r_rs])
                k.op("dve", lambda: nc.vector.tensor_tensor(out=y32[:, 0:w], in0=y32[:, 0:w], in1=rs[:, 0:w], op=ALU.add),
                     R=[r_rs, r_y32], W=[r_y32])
                k.op("dve", lambda: nc.vector.tensor_tensor(out=og[:, d, c0:c0 + w], in0=y32[:, 0:w], in1=gt[:, d, c0:c0 + w], op=ALU.mult),
                     R=[r_y32, r_gt[d]], W=[r_og])
            if done_flag is not None:
                done_flag[0] = True

        def post_tiles(pb):
            y32s = [pb.alloc([P, P]) for _ in range(2)]; r_y32s = [pb.reg("y32a"), pb.reg("y32b")]
            ybs = [pb.alloc([P, P], BF16) for _ in range(2)]; r_ybs = [pb.reg("yba"), pb.reg("ybb")]
            sqbs = [pb.alloc([P, P], BF16) for _ in range(2)]; r_sqbs = [pb.reg("sqba"), pb.reg("sqbb")]
            rss = [pb.alloc([P, P]) for _ in range(2)]; r_rss = [pb.reg("rsa"), pb.reg("rsb")]
            return (y32s, r_y32s, ybs, r_ybs, sqbs, r_sqbs, rss, r_rss)

        def run_interleaved(gens):
            gens = list(gens)
            while gens:
                for g_ in list(gens):
                    try:
                        next(g_)
                    except StopIteration:
                        gens.remove(g_)

        def stageB(n):
            pb = Phase(base=ov_base, carry_ref=ovc)
            btok = pb.alloc([P, D], BF16); ktok = pb.alloc([P, D], BF16); vtok = pb.alloc([P, D], BF16)
            r_tok = [pb.reg("btok"), pb.reg("ktok"), pb.reg("vtok")]
            G = 4
            sets = []
            for si in range(4):
                S_ = {}
                S_["Nb"] = pb.alloc([P, G, 256], BF16); S_["Nk"] = pb.alloc([P, G, 256], BF16)
                S_["r_Nb"] = pb.reg("Nb%d" % si); S_["r_Nk"] = pb.reg("Nk%d" % si)
                _x = pb.alloc([P, G, P], BF16); _rx = pb.reg("X_%d" % si)
                _y = pb.alloc([P, G, P], BF16); _ry = pb.reg("Y_%d" % si)
                _p = pb.alloc([P, G, P], BF16); _rp = pb.reg("P_%d" % si)
                S_["Xs"] = [_x, _x]; S_["r_Xs"] = [_rx, _rx]
                S_["Ys"] = [_y, _y]; S_["r_Ys"] = [_ry, _ry]
                S_["Ps"] = [_p, _p]; S_["r_Ps"] = [_rp, _rp]
                S_["Wt"] = pb.alloc([P, 2, P], BF16); S_["r_Wt"] = pb.reg("Wt%d" % si)
                S_["Ut"] = pb.alloc([P, 2, P], BF16); S_["r_Ut"] = pb.reg("Ut%d" % si)
                S_["tH"] = pb.alloc([P, 2, 64]); S_["r_tH"] = pb.reg("tH%d" % si)
                sets.append(S_)
            ptl = post_tiles(pb)
            ysb = pb.alloc([P, KC, P]); r_ysb = [pb.reg("ysb%d" % g_) for g_ in range(4)]
            nonlocal_nb[0] = 8
            post_done = [True]
            mur2 = mask_ur.unsqueeze(1).broadcast_to([P, 2, 256])
            ml2 = mask_l.unsqueeze(1).broadcast_to([P, 2, P])
            id4 = identb[:, :].unsqueeze(1).broadcast_to([P, G, P])

            def group_gen(c, g, S_):
                cs = c * P
                Nb, Nk, r_Nb, r_Nk = S_["Nb"], S_["Nk"], S_["r_Nb"], S_["r_Nk"]
                Xs, Ys, Ps, r_Xs, r_Ys, r_Ps = S_["Xs"], S_["Ys"], S_["Ps"], S_["r_Xs"], S_["r_Ys"], S_["r_Ps"]
                Wt, Ut, tH, r_Wt, r_Ut, r_tH = S_["Wt"], S_["Ut"], S_["tH"], S_["r_Wt"], S_["r_Ut"], S_["r_tH"]
                j0 = 2 * g
                units = [(j0 + u % 2, u // 2) for u in range(G)]
                sbk = [next_bank() for _ in range(2)]
                for u, (j, hp) in enumerate(units):
                    ps_ = slice(hp * 64, hp * 64 + 64)
                    jj = u % 2
                    if hp == 0:
                        bkA, rbA = sbk[0]
                        o_ = bkA[:, jj * 256:jj * 256 + 256]
                    else:
                        bkA, rbA = sbk[1]
                        o_ = bkA[:, jj * 256:jj * 256 + 256]
                    k.op("pe", lambda: mm(o_, lhsT=bt[ps_, j, cs:cs + P], rhs=ar[ps_, j, :, cs:cs + P], start=True, stop=True),
                         R=[r_bt[j], r_ar[j]], W=[rbA])
                for hh in range(2):
                    bkA, rbA = sbk[hh]
                    k.op("dve", lambda: nc.vector.tensor_tensor(
                        out=Nb[:, 2 * hh:2 * hh + 2, :], in0=bkA[:, :].rearrange("p (u c) -> p u c", u=2), in1=mur2, op=ALU.mult),
                        R=[rbA, r_cst], W=[r_Nb])
                yield
                sbk2 = [next_bank() for _ in range(2)]
                for u, (j, hp) in enumerate(units):
                    ps_ = slice(hp * 64, hp * 64 + 64)
                    jj = u % 2
                    bkC, rbC = sbk2[hp]
                    k.op("pe", lambda: mm(bkC[:, jj * 256:jj * 256 + 256], lhsT=kt[ps_, j, cs:cs + P],
                                          rhs=ar[ps_, j, :, cs:cs + P], start=True, stop=True),
                         R=[r_kt[j], r_ar[j]], W=[rbC])
                bkE0, rbE0 = next_bank()
                bkE1, rbE1 = next_bank()
                for u, (j, hp) in enumerate(units):
                    ps_ = slice(hp * 64, hp * 64 + 64)
                    jj = u % 2
                    bkE, rbE = (bkE0, rbE0) if hp == 0 else (bkE1, rbE1)
                    k.op("pe", lambda: mm(bkE[:, jj * P:(jj + 1) * P], lhsT=ar[ps_, j, 0, cs:cs + P],
                                          rhs=bt[ps_, j, cs:cs + P], start=True, stop=True),
                         R=[r_bt[j], r_ar[j]], W=[rbE])
                for hh in range(2):
                    bkC, rbC = sbk2[hh]
                    k.op("dve", lambda: nc.vector.tensor_tensor(
                        out=Nk[:, 2 * hh:2 * hh + 2, :], in0=bkC[:, :].rearrange("p (u c) -> p u c", u=2), in1=mur2, op=ALU.mult),
                        R=[rbC, r_cst], W=[r_Nk])
                    bkE, rbE = (bkE0, rbE0) if hh == 0 else (bkE1, rbE1)
                    k.op("dve", lambda: nc.vector.tensor_tensor(
                        out=Ys[0][:, 2 * hh:2 * hh + 2, :], in0=bkE[:, 0:2 * P].rearrange("p (u c) -> p u c", u=2), in1=ml2, op=ALU.mult),
                        R=[rbE, r_cst], W=[r_Ys[0]])
                k.op("dve", lambda: nc.vector.tensor_tensor(out=Ps[0][:, :, :], in0=Nb[:, :, 0:P], in1=id4, op=ALU.add),
                     R=[r_Nb, r_identb], W=[r_Ps[0]])
                yield
                Xc = lambda u: Nb[:, u, 0:P]
                r_Xc = r_Nb
                yi = 0; pi = 0; xi_ = 0
                for lvl in range(6):
                    bX, rbX = next_bank()
                    bY, rbY = next_bank()
                    Yc = Ys[yi]; r_Yc = r_Ys[yi]
                    for u in range(G):
                        k.op("pe", lambda: mm(bY[:, u * P:(u + 1) * P], lhsT=Xc(u), rhs=Yc[:, u, :], start=True, stop=True),
                             R=[r_Yc, r_Xc], W=[rbY])
                    if lvl < 5:
                        for u in range(G):
                            k.op("pe", lambda: mm(bX[:, u * P:(u + 1) * P], lhsT=Yc[:, u, :], rhs=Xc(u), start=True, stop=True),
                                 R=[r_Yc, r_Xc], W=[rbX])
                    Yn = Ys[1 - yi]; r_Yn = r_Ys[1 - yi]
                    evac_copy(Yn[:, :, :], bY[:, :].rearrange("p (u c) -> p u c", u=G), R=[rbY], W=[r_Yn])
                    if lvl < 5:
                        Xn = Xs[xi_]; r_Xn = r_Xs[xi_]
                        evac_copy(Xn[:, :, :], bX[:, :].rearrange("p (u c) -> p u c", u=G), R=[rbX], W=[r_Xn])
                        Xc = (lambda Xn: (lambda u: Xn[:, u, :]))(Xn)
                        r_Xc = r_Xn
                        xi_ = 1 - xi_
                    yi = 1 - yi
                    yield
                    bP, rbP = next_bank()
                    Pc = Ps[pi]; r_Pc = r_Ps[pi]
                    for u in range(G):
                        k.op("pe", lambda: mm(bP[:, u * P:(u + 1) * P], lhsT=identb[:, :], rhs=Pc[:, u, :], start=True, stop=False),
                             R=[r_identb, r_Pc], W=[rbP], inc=False)
                        k.op("pe", lambda: mm(bP[:, u * P:(u + 1) * P], lhsT=Yn[:, u, :], rhs=Pc[:, u, :], start=False, stop=True),
                             R=[r_Yn, r_Pc], W=[rbP])
                    Pn = Ps[1 - pi]; r_Pn = r_Ps[1 - pi]
                    evac_copy(Pn[:, :, :], bP[:, :].rearrange("p (u c) -> p u c", u=G), R=[rbP], W=[r_Pn])
                    pi = 1 - pi
                    yield
                Tn = Ps[pi]; r_Tn = r_Ps[pi]
                bW, rbW = next_bank()
                for jj in range(2):
                    j = j0 + jj
                    k.op("pe", lambda: mm(bW[:, jj * P:(jj + 1) * P], lhsT=ar[:, j, 0, cs:cs + P], rhs=Hbd[:, j, :],
                                          start=True, stop=False), R=[r_ar[j], r_H[g]], W=[rbW], inc=False)
                    for hp in range(2):
                        u = hp * 2 + jj
                        hc = (2 * j + hp) * 64
                        k.op("pe", lambda: mm(bW[:, jj * P + hp * 64:jj * P + hp * 64 + 64], lhsT=Nk[:, u, 0:P],
                                              rhs=vtok[:, hc:hc + 64], start=False, stop=(hp == 1)),
                             R=[r_Nk, r_tok[2]], W=[rbW], inc=(hp == 1))
                k.op("act", lambda: nc.scalar.copy(out=Wt[:, :, :], in_=bW[:, 0:2 * P].rearrange("p (u c) -> p u c", u=2)),
                     R=[rbW], W=[r_Wt])
                yield
                bU, rbU = next_bank()
                for u, (j, hp) in enumerate(units):
                    jj = u % 2
                    k.op("pe", lambda: mm(bU[:, jj * P + hp * 64:jj * P + hp * 64 + 64], lhsT=Tn[:, u, :],
                                          rhs=Wt[:, jj, hp * 64:hp * 64 + 64], start=True, stop=True),
                         R=[r_Tn, r_Wt], W=[rbU])
                k.op("act", lambda: nc.scalar.copy(out=Ut[:, :, :], in_=bU[:, 0:2 * P].rearrange("p (u c) -> p u c", u=2)),
                     R=[rbU], W=[r_Ut])
                yield
                assert post_done[0], "post of the previous chunk must be fully emitted before y^T is overwritten"
                yb_, ryb_ = next_bank()
                for jj in range(2):
                    j = j0 + jj
                    k.op("pe", lambda: mm(yb_[:, jj * P:(jj + 1) * P], lhsT=Hbd[:, j, :], rhs=ar[:, j, 1, cs:cs + P],
                                          start=True, stop=False), R=[r_H[g], r_ar[j]], W=[ryb_], inc=False)
                    for hp in range(2):
                        u = hp * 2 + jj
                        hc = (2 * j + hp) * 64
                        yo = yb_[hp * 64:hp * 64 + 64, jj * P:(jj + 1) * P]
                        k.op("pe", lambda: mm(yo, lhsT=Ut[:, jj, hp * 64:hp * 64 + 64], rhs=Nb[:, u, P:2 * P], start=False, stop=False),
                             R=[r_Ut, r_Nb], W=[ryb_], inc=False)
                        k.op("pe", lambda: mm(yo, lhsT=vtok[:, hc:hc + 64], rhs=Nk[:, u, P:2 * P], start=False, stop=True),
                             R=[r_tok[2], r_Nk], W=[ryb_])
                k.op("act", lambda: nc.scalar.copy(out=ysb[:, j0:j0 + 2, :], in_=yb_[:, 0:2 * P].rearrange("p (a b) -> p a b", a=2)),
                     R=[ryb_], W=[r_ysb[g]])
                bH, rbH = next_bank()
                for u, (j, hp) in enumerate(units):
                    jj = u % 2
                    hc = (2 * j + hp) * 64
                    ho = bH[hp * 64:hp * 64 + 64, jj * 64:jj * 64 + 64]
                    k.op("pe", lambda: mm(ho, lhsT=btok[:, hc:hc + 64], rhs=Ut[:, jj, hp * 64:hp * 64 + 64], start=True, stop=False),
                         R=[r_tok[0], r_Ut], W=[rbH], inc=False)
                    k.op("pe", lambda: mm(ho, lhsT=ktok[:, hc:hc + 64], rhs=vtok[:, hc:hc + 64], start=False, stop=True),
                         R=[r_tok[1], r_tok[2]], W=[rbH])
                k.op("dve", lambda: nc.vector.tensor_tensor(out=tH[:, :, :], in0=bH[:, 0:128].rearrange("p (a b) -> p a b", a=2),
                                                            in1=H32[:, j0:j0 + 2, :], op=ALU.add), R=[rbH, r_H[g]], W=[r_tH])
                gcb = gC[:, j0:j0 + 2, c:c + 1].broadcast_to([P, 2, 64])
                k.op("dve", lambda: nc.vector.tensor_tensor(out=H32[:, j0:j0 + 2, :], in0=tH[:, :, :], in1=gcb, op=ALU.mult),
                     R=[r_tH, r_gC], W=[r_H[g]])
                k.op("act", lambda: nc.scalar.copy(out=Hbd[0:64, j0:j0 + 2, 0:64], in_=H32[0:64, j0:j0 + 2, :]), R=[r_H[g]], W=[r_H[g]])
                k.op("act", lambda: nc.scalar.copy(out=Hbd[64:128, j0:j0 + 2, 64:128], in_=H32[64:128, j0:j0 + 2, :]), R=[r_H[g]], W=[r_H[g]])
                yield

            for c in range(n // P):
                cs = c * P
                for si, (src_t, r_src, dst_t) in enumerate([(bt, r_bt, btok), (kt, r_kt, ktok), (vt, r_vt, vtok)]):
                    bk, rb = next_bank()
                    bkb = bk[:, :].bitcast(BF16)
                    for d in range(KC):
                        k.op("pe", lambda: nc.tensor.transpose(bkb[:, d * P:(d + 1) * P], src_t[:, d, cs:cs + P], identb[:, :]),
                             R=[r_src[d], r_identb], W=[rb])
                    evac_copy(dst_t[:, :], bkb[:, 0:D], R=[rb], W=[r_tok[si]])
                import os as _os3
                if _os3.environ.get("RW_NOIL"):
                    for g_ in range(4):
                        run_interleaved([group_gen(c, g_, sets[g_ % 2])])
                else:
                    if _os3.environ.get("RW_IL2"):
                        run_interleaved([group_gen(c, 0, sets[0]), group_gen(c, 2, sets[1])])
                        run_interleaved([group_gen(c, 1, sets[2]), group_gen(c, 3, sets[3])])
                    else:
                        gens = [group_gen(c, g_, sets[g_]) for g_ in range(4)]
                        if c > 0:
                            pcs = (c - 1) * P
                            post_done[0] = False
                            gens = [post_gen(ptl, lambda d: (ysb[:, d, :], r_ysb[d // 2]), pcs, P, post_done)] + gens
                        run_interleaved(gens)
            post(ptl, lambda d: (ysb[:, d, :], r_ysb[d // 2]), (n // P - 1) * P, P)
            nonlocal_nb[0] = 6
            pb.end()

        def out_proj(t0, n, ti):
            wv_ = rw["rw_wo"].rearrange("(k p) c -> p k c", p=P)
            for half in range(2):
                wb, r_w = next_wbuf()
                v = wb[:, 0:KC * 512].rearrange("p (k c) -> p k c", k=KC)
                k.dma("pool", v, wv_[:, :, half * 512:(half + 1) * 512], W=[r_w])
                for oc in range(4):
                    d = half * 4 + oc
                    bk, rb = next_bank()
                    for kk in range(KC):
                        k.op("pe", lambda: mm(bk[:, 0:n], lhsT=v[:, kk, oc * P:(oc + 1) * P], rhs=og[:, kk, 0:n],
                                              start=(kk == 0), stop=(kk == KC - 1)),
                             R=[r_w, r_og], W=[rb], inc=(kk == KC - 1))
                    k.op("dve", lambda: nc.vector.tensor_tensor(out=x[:, d, t0:t0 + n], in0=bk[:, 0:n], in1=x[:, d, t0:t0 + n], op=ALU.add),
                         R=[rb, r_x[d][ti]], W=[r_x[d][ti]])

        import os as _os
        _bis = _os.environ.get("RWB", "").split(",")
        nseg = TP // SEG
        for s in range(nseg):
            t0 = s * SEG
            ti = t0 // 512
            if "noA" not in _bis:
                stageA(t0, SEG, ti, False, s == nseg - 1)
            if "noB" not in _bis:
                stageB(SEG)
                out_proj(t0, SEG, ti)

        def stageB_sample():
            pb = Phase(base=ov_base, carry_ref=ovc)
            vec = {}
            for nm in ("A", "R", "B", "K", "V", "D"):
                vec[nm] = (pb.alloc([P, P]), pb.reg("sv" + nm))
            sa = pb.alloc([P, 64]); r_sa = pb.reg("sa")
            yv = pb.alloc([P, P]); r_yv = pb.reg("yv")
            Ss = pb.alloc([P, 64, 64]); r_Ss = pb.reg("Ss")
            tmp = pb.alloc([P, 64, 64]); r_tmp = pb.reg("tmp")
            ptl = post_tiles(pb)
            srcs = {"A": (lambda: ar[:, :, 0, 0:NS], r_ar, BF16), "R": (lambda: ar[:, :, 1, 0:NS], r_ar, BF16),
                    "B": (lambda: bt[:, :, 0:NS], r_bt, BF16), "K": (lambda: kt[:, :, 0:NS], r_kt, BF16),
                    "V": (lambda: vt[:, :, 0:NS], r_vt, BF16), "D": (lambda: dS[:, :, 0:NS], [r_dS], F32)}
            cb16 = [pb.alloc([P, P], BF16) for _ in range(2)]; r_cb16 = [pb.reg("cb16a"), pb.reg("cb16b")]
            cb32 = pb.alloc([P, P]); r_cb32 = pb.reg("cb32")
            for si, (nm, (fn, rr, dt_)) in enumerate(srcs.items()):
                bk, rb = next_bank()
                if dt_ == BF16:
                    cb, r_cb = cb16[si % 2], r_cb16[si % 2]
                    evac_copy(cb[:, :].rearrange("p (a b) -> p a b", a=KC), fn(), R=list(rr), W=[r_cb])
                    o_ = bk[:, :].bitcast(BF16)[:, 0:P]
                    k.op("pe", lambda: nc.tensor.transpose(o_, cb[:, :], identb[:, :]), R=[r_cb, r_identb], W=[rb])
                else:
                    evac_copy(cb32[:, :].rearrange("p (a b) -> p a b", a=KC), fn(), R=list(rr), W=[r_cb32])
                    o_ = bk[:, 0:P]
                    k.op("pe", lambda: nc.tensor.transpose(o_, cb32[:, :], ident[:, :]), R=[r_cb32, r_ident], W=[rb])
                evac_copy(vec[nm][0][:, :], o_, R=[rb], W=[vec[nm][1]])

            def kb(nm, hp):
                return vec[nm][0][:, hp * 64:(hp + 1) * 64].unsqueeze(1).broadcast_to([P, 64, 64])

            def vb(ap2):
                return ap2.unsqueeze(2).broadcast_to([P, 64, 64])
            TT = nc.vector.tensor_tensor
            for hp in range(2):
                for j in range(KC):
                    k.dma("sp", Ss[j * NS:(j + 1) * NS, :, :], st_wkv_d[:, 2 * j + hp, :, :], W=[r_Ss])
                k.op("dve", lambda: TT(out=tmp, in0=Ss, in1=kb("A", hp), op=ALU.mult), R=[r_Ss, vec["A"][1]], W=[r_tmp])
                k.op("dve", lambda: nc.vector.tensor_reduce(out=sa[:, :], in_=tmp, axis=AX.X, op=ALU.add), R=[r_tmp], W=[r_sa])
                k.op("dve", lambda: TT(out=tmp, in0=vb(sa[:, :]), in1=kb("B", hp), op=ALU.mult), R=[r_sa, vec["B"][1]], W=[r_tmp])
                k.op("dve", lambda: TT(out=Ss, in0=Ss, in1=tmp, op=ALU.add), R=[r_tmp, r_Ss], W=[r_Ss])
                k.op("dve", lambda: TT(out=tmp, in0=vb(vec["V"][0][:, hp * 64:(hp + 1) * 64]), in1=kb("K", hp), op=ALU.mult),
                     R=[vec["V"][1], vec["K"][1]], W=[r_tmp])
                k.op("dve", lambda: TT(out=Ss, in0=Ss, in1=tmp, op=ALU.add), R=[r_tmp, r_Ss], W=[r_Ss])
                k.op("dve", lambda: TT(out=tmp, in0=Ss, in1=kb("R", hp), op=ALU.mult), R=[r_Ss, vec["R"][1]], W=[r_tmp])
                k.op("dve", lambda: nc.vector.tensor_reduce(out=yv[:, hp * 64:(hp + 1) * 64], in_=tmp, axis=AX.X, op=ALU.add),
                     R=[r_tmp], W=[r_yv])
                k.op("dve", lambda: TT(out=Ss, in0=Ss, in1=kb("D", hp), op=ALU.mult), R=[r_Ss, vec["D"][1]], W=[r_Ss])
                for j in range(KC):
                    r_o = Reg("o_swkv"); out_regs.append(r_o)
                    k.dma("sp", swkv_d[:, 2 * j + hp, :, :], Ss[j * NS:(j + 1) * NS, :, :], R=[r_Ss], W=[r_o])
            k.op("pe", lambda: nc.tensor.transpose(ybk[0][:, 0:P], yv[:, :], ident[:, :]), R=[r_yv, r_ident], W=[r_ybk[0]])
            post(ptl, lambda d: (ybk[0][:, d * NS:(d + 1) * NS], r_ybk[0]), 0, NS)
            pb.end()

        if do_samples:
            stageA(TP, NS, 4, True, False)
            stageB_sample()
            out_proj(TP, NS, 4)

        pe_ = Phase(base=ov_base, carry_ref=ovc)
        shT = pe_.alloc([1 + NS, D]); r_shT = pe_.reg("shT")
        bk, rb = next_bank()
        bk2, rb2 = next_bank()
        for d in range(KC):
            bb = bk if d < 4 else bk2
            k.op("pe", lambda: nc.tensor.transpose(bb[0:1 + NS, (d % 4) * P:(d % 4) * P + P], shf[:, d, :], ident[:, :]),
                 R=[r_shf, r_ident], W=[rb if d < 4 else rb2])
        evac_copy(shT[:, 0:512], bk[0:1 + NS, :], R=[rb], W=[r_shT])
        evac_copy(shT[:, 512:1024], bk2[0:1 + NS, :], R=[rb2], W=[r_shT])
        r_o = Reg("o_pshift"); out_regs.append(r_o)
        k.dma("sp", pshift_d.rearrange("(a d) -> a d", a=1), shT[0:1, :], R=[r_shT], W=[r_o])
        if do_samples:
            r_o = Reg("o_sshift"); out_regs.append(r_o)
            k.dma("sp", sshift_d, shT[1:1 + NS, :], R=[r_shT], W=[r_o])
        ST = pe_.alloc([64, KC, P]); r_ST = pe_.reg("ST")
        for half in range(2):
            bk, rb = next_bank()
            for jj in range(4):
                j = half * 4 + jj
                k.op("pe", lambda: nc.tensor.transpose(bk[0:64, jj * P:(jj + 1) * P], H32[:, j, :], ident[:, :]),
                     R=[r_H[j // 2], r_ident], W=[rb])
            evac_copy(ST[:, half * 4:half * 4 + 4, :], bk[0:64, :].rearrange("p (a b) -> p a b", a=4), R=[rb], W=[r_ST])
        r_o = Reg("o_pwkv"); out_regs.append(r_o)
        k.dma("sp", pwkv_d.rearrange("(j hp) v kk -> v j hp kk", hp=2), ST[:, :, :].rearrange("v j (hp kk) -> v j hp kk", hp=2),
              R=[r_ST], W=[r_o])
        pe_.end()
        dmy = Reg("ovdummy"); dmy.readers = list(ovc[0]); ph.regs.append(dmy)
        nonlocal_nb[0] = 8
        ph.end()


    RET_G = [1.0 - 2.0 ** (-5 - h_) for h_ in range(4)]

    def ret_mixer(do_samples=True):
        ph = Phase()
        SEG = 256
        mm = nc.tensor.matmul
        dmatT = ph.alloc([P, 4, P], BF16); qdt = ph.alloc([P, 4, P], BF16); kdec = ph.alloc([P, 4])
        eps_r = ph.alloc([P, 1])
        r_cst = ph.reg("retconst")
        k.dma("pool", dmatT, c_dmatT_d, W=[r_cst])
        k.dma("pool", qdt, c_qd_d, W=[r_cst])
        k.dma("sp", kdec, c_kdec_d, W=[r_cst])
        k.op("dve", lambda: nc.vector.memset(eps_r, 1e-6), W=[r_cst])
        nbuf = norm_bufs(ph)
        hm = ph.alloc([P, KC, SEG], BF16); r_hm = [ph.reg("hm%d" % d) for d in range(KC)]
        gT = ph.alloc([P, 16, SEG], BF16); r_gT = [ph.reg("gT%d" % e) for e in range(16)]
        cs_fm = ph.alloc([P, 2, SEG]); r_csfm = ph.reg("csfm")
        rq = [ph.alloc([P, SEG]) for _ in range(4)]; r_rq = [ph.reg("rq%d" % i) for i in range(4)]
        rt_ = [ph.alloc([P, SEG]) for _ in range(4)]; r_rt = [ph.reg("rt%d" % i) for i in range(4)]
        on_ = [ph.alloc([P, 512], BF16) for _ in range(2)]; r_on = [ph.reg("on0"), ph.reg("on1")]
        stt = [ph.alloc([P, 8]) for _ in range(2)]; r_stt = [ph.reg("stt0"), ph.reg("stt1")]
        ov_base = ph.off
        ov_idx = len(ph.regs)
        S32 = ph.alloc([P, 4, 2, 512]); Sb = ph.alloc([P, 4, 2, 512], BF16)
        r_S = [ph.reg("S%d" % h_) for h_ in range(4)]
        k.op("dve", lambda: nc.vector.memset(S32, 0.0), W=r_S)
        k.op("dve", lambda: nc.vector.memset(Sb, 0.0), W=r_S)
        qT = ph.alloc([P, KC, SEG], BF16); r_qT = [ph.reg("qT%d" % d) for d in range(KC)]
        kT = ph.alloc([P, KC, SEG], BF16); r_kT = [ph.reg("kT%d" % d) for d in range(KC)]
        qdT = ph.alloc([P, KC, SEG], BF16); r_qdT = [ph.reg("qdT%d" % d) for d in range(KC)]
        vtok = [ph.alloc([P, 2 * D], BF16) for _ in range(2)]; r_vtok = [ph.reg("vtok0"), ph.reg("vtok1")]
        kdtok = [ph.alloc([P, D], BF16) for _ in range(2)]; r_kdtok = [ph.reg("kdtok0"), ph.reg("kdtok1")]
        cs_tok = [ph.alloc([P, 2, P]) for _ in range(2)]; r_cstok = [ph.reg("cstok0"), ph.reg("cstok1")]
        ktmp = ph.alloc([P, 512]); r_ktmp = ph.reg("ktmp")
        kt1 = ph.alloc([P, 512]); kt2 = ph.alloc([P, 512]); r_kt1 = ph.reg("kt1"); r_kt2 = ph.reg("kt2")
        kro = ph.alloc([P, 512]); r_kro = ph.reg("kro")
        innT = [ph.alloc([P, P], BF16) for _ in range(2)]; r_innT = [ph.reg("innT0"), ph.reg("innT1")]

        def load_w(view_fn):
            wb, r_w = next_wbuf()
            v_, src_ = view_fn(wb)
            k.dma("pool", v_, src_, W=[r_w])
            return v_, r_w

        def rotary_fm(bank_pair, dst, r_dst, h_, n, scale):
            (b1, rb1), (b2, rb2) = bank_pair
            i1, i2 = (2 * h_) % 4, (2 * h_ + 1) % 4
            x1, r1 = rq[i1], r_rq[i1]
            x2, r2 = rq[i2], r_rq[i2]
            k.op("act", lambda: nc.scalar.activation(out=x1[:, 0:n], in_=b1[:, 0:n], func=AF.Copy, scale=scale), R=[rb1], W=[r1])
            k.op("act", lambda: nc.scalar.activation(out=x2[:, 0:n], in_=b2[:, 0:n], func=AF.Copy, scale=scale), R=[rb2], W=[r2])
            ta, rta = rt_[i1], r_rt[i1]
            tb, rtb = rt_[i2], r_rt[i2]
            cosv = cs_fm[:, 0, 0:n]; sinv = cs_fm[:, 1, 0:n]
            k.op("dve", lambda: nc.vector.tensor_tensor(out=ta[:, 0:n], in0=x1[:, 0:n], in1=cosv, op=ALU.mult), R=[r1, r_csfm], W=[rta])
            k.op("dve", lambda: nc.vector.tensor_tensor(out=tb[:, 0:n], in0=x2[:, 0:n], in1=sinv, op=ALU.mult), R=[r2, r_csfm], W=[rtb])
            k.op("dve", lambda: nc.vector.tensor_tensor(out=dst[:, 2 * h_, 0:n], in0=ta[:, 0:n], in1=tb[:, 0:n], op=ALU.subtract),
                 R=[rta, rtb], W=[r_dst[2 * h_]])
            k.op("dve", lambda: nc.vector.tensor_tensor(out=ta[:, 0:n], in0=x1[:, 0:n], in1=sinv, op=ALU.mult), R=[r1, r_csfm], W=[rta])
            k.op("dve", lambda: nc.vector.tensor_tensor(out=tb[:, 0:n], in0=x2[:, 0:n], in1=cosv, op=ALU.mult), R=[r2, r_csfm], W=[rtb])
            k.op("dve", lambda: nc.vector.tensor_tensor(out=dst[:, 2 * h_ + 1, 0:n], in0=ta[:, 0:n], in1=tb[:, 0:n], op=ALU.add),
                 R=[rta, rtb], W=[r_dst[2 * h_ + 1]])

        def seg_front(t0, n, ti, sample, dsts=None):
            rmsnorm_tile(nbuf, 4, ti, lambda d, a_, b_: hm[:, d, 0:n], lambda d: [r_hm[d]], rng=(t0, n))
            k.dma("sp", cs_fm[:, 0, 0:n], c_cos_d[:, t0:t0 + n], W=[r_csfm])
            k.dma("sp", cs_fm[:, 1, 0:n], c_sin_d[:, t0:t0 + n], W=[r_csfm])
            dq = (qT, r_qT) if dsts is None else dsts[0]
            dk = (kT, r_kT) if dsts is None else dsts[1]
            for (wname, dst, r_dst, scale) in (("ret_wq", dq[0], dq[1], 1.0), ("ret_wk", dk[0], dk[1], 1.0 / 16.0)):
                wv_ = rt[wname].rearrange("(k p) c -> p k c", p=P)
                for half in range(2):
                    v_, r_w = load_w(lambda wb: (wb[:, 0:KC * 512].rearrange("p (k c) -> p k c", k=KC),
                                                 wv_[:, :, half * 512:(half + 1) * 512]))
                    if wname == "ret_wk" and not sample:
                        ktok_half(v_, r_w, half, n)
                    for hh in range(2):
                        h_ = half * 2 + hh
                        pair = []
                        for i in range(2):
                            oc = hh * 2 + i
                            bk, rb = next_bank()
                            for kk in range(KC):
                                k.op("pe", lambda: mm(bk[:, 0:n], lhsT=v_[:, kk, oc * P:(oc + 1) * P], rhs=hm[:, kk, 0:n],
                                                      start=(kk == 0), stop=(kk == KC - 1)),
                                     R=[r_w, r_hm[kk]], W=[rb], inc=(kk == KC - 1))
                            pair.append((bk, rb))
                        rotary_fm(pair, dst, r_dst, h_, n, scale)
            if not sample:
                for d in range(KC):
                    h_ = d // 2
                    k.op("dve", lambda: nc.vector.tensor_tensor(
                        out=qdT[:, d, 0:n].rearrange("p (c t) -> p c t", t=P), in0=qT[:, d, 0:n].rearrange("p (c t) -> p c t", t=P),
                        in1=qdt[:, h_, :].unsqueeze(1).broadcast_to([P, n // P, P]), op=ALU.mult),
                        R=[r_qT[d], r_cst], W=[r_qdT[d]])
            wg_ = rt["ret_wg"].rearrange("(k p) c -> p k c", p=P)
            for qv in range(4):
                v_, r_w = load_w(lambda wb: (wb[:, 0:KC * 512].rearrange("p (k c) -> p k c", k=KC),
                                             wg_[:, :, qv * 512:(qv + 1) * 512]))
                for oc in range(4):
                    e = qv * 4 + oc
                    bk, rb = next_bank()
                    for kk in range(KC):
                        k.op("pe", lambda: mm(bk[:, 0:n], lhsT=v_[:, kk, oc * P:(oc + 1) * P], rhs=hm[:, kk, 0:n],
                                              start=(kk == 0), stop=(kk == KC - 1)),
                             R=[r_w, r_hm[kk]], W=[rb], inc=(kk == KC - 1))
                    k.op("act", lambda: nc.scalar.activation(out=gT[:, e, 0:n], in_=bk[:, 0:n], func=AF.Silu), R=[rb], W=[r_gT[e]])

        cur_t0 = [0]

        def ktok_half(v_, r_w, half, n):
            for c in range(n // P):
                if half == 0:
                    tt0 = cur_t0[0] + c * P
                    k.dma("sp", cs_tok[c][:, 0, :], c_cosT_d[tt0:tt0 + P, :], W=[r_cstok[c]])
                    k.dma("sp", cs_tok[c][:, 1, :], c_sinT_d[tt0:tt0 + P, :], W=[r_cstok[c]])
                bk, rb = next_bank()
                for kk in range(KC):
                    k.op("pe", lambda: mm(bk[:, :], lhsT=hm[:, kk, c * P:(c + 1) * P], rhs=v_[:, kk, :],
                                          start=(kk == 0), stop=(kk == KC - 1)),
                         R=[r_w, r_hm[kk]], W=[rb], inc=(kk == KC - 1))
                k.op("act", lambda: nc.scalar.activation(out=ktmp[:, :], in_=bk[:, :], func=AF.Copy, scale=1.0 / 16.0), R=[rb], W=[r_ktmp])
                cosb = cs_tok[c][:, 0, :].unsqueeze(1).broadcast_to([P, 4, P])
                sinb = cs_tok[c][:, 1, :].unsqueeze(1).broadcast_to([P, 4, P])
                k3 = ktmp[:, :].rearrange("p (a f) -> p a f", a=4)
                k.op("dve", lambda: nc.vector.tensor_tensor(out=kt1[:, :].rearrange("p (a f) -> p a f", a=4), in0=k3, in1=cosb, op=ALU.mult),
                     R=[r_ktmp, r_cstok[c]], W=[r_kt1])
                k.op("dve", lambda: nc.vector.tensor_tensor(out=kt2[:, :].rearrange("p (a f) -> p a f", a=4), in0=k3, in1=sinb, op=ALU.mult),
                     R=[r_ktmp, r_cstok[c]], W=[r_kt2])
                t1v = kt1[:, :].rearrange("p (h two f) -> p h two f", h=2, two=2)
                t2v = kt2[:, :].rearrange("p (h two f) -> p h two f", h=2, two=2)
                kov = kro[:, :].rearrange("p (h two f) -> p h two f", h=2, two=2)
                k.op("dve", lambda: nc.vector.tensor_tensor(out=kov[:, :, 0, :], in0=t1v[:, :, 0, :], in1=t2v[:, :, 1, :], op=ALU.subtract),
                     R=[r_kt1, r_kt2], W=[r_kro])
                k.op("dve", lambda: nc.vector.tensor_tensor(out=kov[:, :, 1, :], in0=t2v[:, :, 0, :], in1=t1v[:, :, 1, :], op=ALU.add),
                     R=[r_kt1, r_kt2], W=[r_kro])
                for hh in range(2):
                    h_ = half * 2 + hh
                    k.op("dve", lambda: nc.vector.tensor_scalar(
                        out=kdtok[c][:, h_ * 256:(h_ + 1) * 256], in0=kro[:, hh * 256:(hh + 1) * 256], scalar1=kdec[:, h_:h_ + 1],
                        scalar2=None, op0=ALU.mult), R=[r_kro, r_cst], W=[r_kdtok[c]])

        def vtok_all(n):
            wv_ = rt["ret_wv"].rearrange("(k p) c -> p k c", p=P)
            for qv in range(4):
                v_, r_w = load_w(lambda wb: (wb[:, 0:KC * 512].rearrange("p (k c) -> p k c", k=KC),
                                             wv_[:, :, qv * 512:(qv + 1) * 512]))
                for c in range(n // P):
                    bk, rb = next_bank()
                    for kk in range(KC):
                        k.op("pe", lambda: mm(bk[:, :], lhsT=hm[:, kk, c * P:(c + 1) * P], rhs=v_[:, kk, :],
                                              start=(kk == 0), stop=(kk == KC - 1)),
                             R=[r_w, r_hm[kk]], W=[rb], inc=(kk == KC - 1))
                    evac_copy(vtok[c][:, qv * 512:(qv + 1) * 512], bk[:, :], R=[rb], W=[r_vtok[c]])

        def norm_gate(o_bank, rb_o, rows, h_, cs, w, ui):
            q = ui % 2
            st_, r_st = stt[q], r_stt[q]
            k.op("dve", lambda: nc.vector.bn_stats(out=st_[0:rows, 0:6], in_=o_bank[0:rows, :]), R=[rb_o], W=[r_st])
            k.op("dve", lambda: nc.vector.bn_aggr(out=st_[0:rows, 6:8], in_=st_[0:rows, 0:6]), R=[r_st], W=[r_st])
            k.op("act", lambda: nc.scalar.activation(out=st_[0:rows, 7:8], in_=st_[0:rows, 7:8], func=AF.Sqrt, bias=eps_r[0:rows, 0:1]),
                 R=[r_st, r_cst], W=[r_st])
            k.op("dve", lambda: nc.vector.reciprocal(out=st_[0:rows, 7:8], in_=st_[0:rows, 7:8]), R=[r_st], W=[r_st])
            on_t, r_on_t = on_[q], r_on[q]
            k.op("dve", lambda: nc.vector.tensor_scalar(out=on_t[0:rows, :], in0=o_bank[0:rows, :], scalar1=st_[0:rows, 6:7],
                                                        scalar2=st_[0:rows, 7:8], op0=ALU.subtract, op1=ALU.mult),
                 R=[rb_o, r_st], W=[r_on_t])
            bk, rb = next_bank()
            bkb = bk[:, :].bitcast(BF16)
            for i in range(4):
                k.op("pe", lambda: nc.tensor.transpose(bkb[:, i * P:i * P + rows], on_t[0:rows, i * P:(i + 1) * P], identb[0:rows, 0:rows]),
                     R=[r_on_t, r_identb], W=[rb])
            gv = gT[:, 4 * h_:4 * h_ + 4, cs:cs + w]
            k.op("dve", lambda: nc.vector.tensor_tensor(out=gv, in0=bkb[:, 0:4 * P].rearrange("p (a t) -> p a t", a=4)[:, :, 0:w], in1=gv, op=ALU.mult),
                 R=[rb] + [r_gT[4 * h_ + i] for i in range(4)], W=[r_gT[4 * h_ + i] for i in range(4)])

        def seg_chunks(n):
            ui = 0
            for c in range(n // P):
                cs = c * P
                for h_ in range(4):
                    q = ui % 2
                    bs, rbs = next_bank()
                    for i in range(2):
                        d = 2 * h_ + i
                        k.op("pe", lambda: mm(bs[:, 0:P], lhsT=kT[:, d, cs:cs + P], rhs=qT[:, d, cs:cs + P], start=(i == 0), stop=(i == 1)),
                             R=[r_kT[d], r_qT[d]], W=[rbs], inc=(i == 1))
                    k.op("dve", lambda: nc.vector.tensor_tensor(out=innT[q][:, :], in0=bs[:, 0:P], in1=dmatT[:, h_, :], op=ALU.mult),
                         R=[rbs, r_cst], W=[r_innT[q]])
                    bo, rbo = next_bank()
                    k.op("pe", lambda: mm(bo[:, :], lhsT=innT[q][:, :], rhs=vtok[c][:, h_ * 512:(h_ + 1) * 512], start=True, stop=False),
                         R=[r_innT[q], r_vtok[c]], W=[rbo], inc=False)
                    for i in range(2):
                        d = 2 * h_ + i
                        k.op("pe", lambda: mm(bo[:, :], lhsT=qdT[:, d, cs:cs + P], rhs=Sb[:, h_, i, :], start=False, stop=(i == 1)),
                             R=[r_qdT[d], r_S[h_]], W=[rbo], inc=(i == 1))
                    norm_gate(bo, rbo, P, h_, cs, P, ui)
                    sdec = float(np.float32(RET_G[h_]) ** 128)
                    for i in range(2):
                        bS, rbS = next_bank()
                        k.op("pe", lambda: mm(bS[:, :], lhsT=kdtok[c][:, h_ * 256 + i * P:h_ * 256 + (i + 1) * P],
                                              rhs=vtok[c][:, h_ * 512:(h_ + 1) * 512], start=True, stop=True),
                             R=[r_kdtok[c], r_vtok[c]], W=[rbS])
                        k.op("dve", lambda: nc.vector.scalar_tensor_tensor(
                            out=S32[:, h_, i, :], in0=S32[:, h_, i, :], scalar=sdec, in1=bS[:, :], op0=ALU.mult, op1=ALU.add),
                            R=[rbS, r_S[h_]], W=[r_S[h_]])
                        k.op("act", lambda: nc.scalar.copy(out=Sb[:, h_, i, :], in_=S32[:, h_, i, :]), R=[r_S[h_]], W=[r_S[h_]])
                    ui += 1

        def out_proj(t0, n, ti):
            wo_ = rt["ret_wo"].rearrange("(e p) c -> p e c", p=P)
            for piece in range(4):
                v_, r_w = load_w(lambda wb: (wb[:, 0:16 * 256].rearrange("p (e c) -> p e c", e=16),
                                             wo_[:, :, piece * 256:(piece + 1) * 256]))
                for oc in range(2):
                    d = piece * 2 + oc
                    bk, rb = next_bank()
                    for e in range(16):
                        k.op("pe", lambda: mm(bk[:, 0:n], lhsT=v_[:, e, oc * P:(oc + 1) * P], rhs=gT[:, e, 0:n],
                                              start=(e == 0), stop=(e == 15)),
                             R=[r_w, r_gT[e]], W=[rb], inc=(e == 15))
                    k.op("dve", lambda: nc.vector.tensor_tensor(out=x[:, d, t0:t0 + n], in0=bk[:, 0:n], in1=x[:, d, t0:t0 + n], op=ALU.add),
                         R=[rb, r_x[d][ti]], W=[r_x[d][ti]])

        def ret_samples(r_pret):
            toks = set()
            for r_ in ph.regs[ov_idx:] + [r_pret]:
                if r_.last_w is not None:
                    toks.add(r_.last_w)
                toks.update(r_.readers)
            ovc = [list(toks)]
            ps_ = Phase(base=ov_base, carry_ref=ovc)
            qTs = ps_.alloc([P, KC, NS]); r_qTs = [ps_.reg("qTs%d" % d) for d in range(KC)]
            kTs = ps_.alloc([P, KC, NS]); r_kTs = [ps_.reg("kTs%d" % d) for d in range(KC)]
            qpad = ps_.alloc([P, KC, NS, NS], BF16); r_qpad = ps_.reg("qpad")
            i16 = ps_.alloc([P, NS, NS]); selb = ps_.alloc([NS, NS, P], BF16); r_sc = ps_.reg("sconst")
            k.dma("sp", i16, c_i16_d.rearrange("p (a b) -> p a b", a=NS), W=[r_sc])
            k.dma("pool", selb, c_sel_d, W=[r_sc])
            vs = ps_.alloc([NS, 2 * D], BF16); r_vs = ps_.reg("vs")
            Sring = [ps_.alloc([P, 2, 512]) for _ in range(4)]; r_Sring = [ps_.reg("Sring%d" % i) for i in range(4)]
            Sb16 = [ps_.alloc([P, 2, 512], BF16) for _ in range(2)]; r_Sb16 = [ps_.reg("Sb16_0"), ps_.reg("Sb16_1")]
            kvt = [ps_.alloc([P, 512]) for _ in range(4)]; r_kvt = [ps_.reg("kvt%d" % i) for i in range(4)]
            n = NS
            seg_front(TP, n, 4, True, dsts=((qTs, r_qTs), (kTs, r_kTs)))
            wv_ = rt["ret_wv"].rearrange("(k p) c -> p k c", p=P)
            for qv in range(4):
                v_, r_w = load_w(lambda wb: (wb[:, 0:KC * 512].rearrange("p (k c) -> p k c", k=KC),
                                             wv_[:, :, qv * 512:(qv + 1) * 512]))
                bk, rb = next_bank()
                for kk in range(KC):
                    k.op("pe", lambda: mm(bk[0:n, :], lhsT=hm[:, kk, 0:n], rhs=v_[:, kk, :], start=(kk == 0), stop=(kk == KC - 1)),
                         R=[r_w, r_hm[kk]], W=[rb], inc=(kk == KC - 1))
                evac_copy(vs[0:n, qv * 512:(qv + 1) * 512], bk[0:n, :], R=[rb], W=[r_vs])
            k.op("dve", lambda: nc.vector.tensor_tensor(
                out=qpad, in0=qTs.unsqueeze(3).broadcast_to([P, KC, NS, NS]), in1=i16.unsqueeze(1).broadcast_to([P, KC, NS, NS]),
                op=ALU.mult), R=r_qTs + [r_sc], W=[r_qpad])
            nonlocal_nb[0] = 4
            po = [banks[4 + h_] for h_ in range(4)]; r_po = [r_bank[4 + h_] for h_ in range(4)]
            DEPTH = 4
            units = [(b_, h_) for b_ in range(NS) for h_ in range(4)]
            NU = len(units)

            def load(u):
                b_, h_ = units[u]
                k.dma("sp", Sring[u % DEPTH], st_ret_d[b_, h_].rearrange("(i p) v -> p i v", p=P), W=[r_Sring[u % DEPTH]])

            def vbc(u):
                b_, h_ = units[u]
                bk, rb = next_bank()
                k.op("pe", lambda: mm(bk[:, :], lhsT=selb[0:NS, b_, :], rhs=vs[0:NS, h_ * 512:(h_ + 1) * 512], start=True, stop=True),
                     R=[r_sc, r_vs], W=[rb])
                return bk, rb
            for u in range(min(DEPTH - 1, NU)):
                load(u)
            nxt = vbc(0)
            for u in range(NU):
                b_, h_ = units[u]
                St, r_St = Sring[u % DEPTH], r_Sring[u % DEPTH]
                Sh, r_Sh = Sb16[u % 2], r_Sb16[u % 2]
                bk, rb = nxt
                if u + DEPTH - 1 < NU:
                    load(u + DEPTH - 1)
                kv_, r_kv = kvt[u % 4], r_kvt[u % 4]
                d0_, d1_ = 2 * h_, 2 * h_ + 1
                k.op("act", lambda: nc.scalar.activation(out=kv_[:, :], in_=bk[:, :], func=AF.Copy, scale=kTs[:, d1_, b_:b_ + 1]),
                     R=[rb, r_kTs[d1_]], W=[r_kv])
                k.op("act", lambda: nc.scalar.activation(out=bk[:, :], in_=bk[:, :], func=AF.Copy, scale=kTs[:, d0_, b_:b_ + 1]),
                     R=[rb, r_kTs[d0_]], W=[rb])
                k.op("dve", lambda: nc.vector.scalar_tensor_tensor(
                    out=St[:, 0, :], in0=St[:, 0, :], scalar=float(np.float32(RET_G[h_])), in1=bk[:, :],
                    op0=ALU.mult, op1=ALU.add), R=[r_St, rb], W=[r_St])
                k.op("dve", lambda: nc.vector.scalar_tensor_tensor(
                    out=St[:, 1, :], in0=St[:, 1, :], scalar=float(np.float32(RET_G[h_])), in1=kv_[:, :],
                    op0=ALU.mult, op1=ALU.add), R=[r_St, r_kv], W=[r_St])
                if u + 1 < NU:
                    nxt = vbc(u + 1)
                k.op("act", lambda: nc.scalar.copy(out=Sh[:, :, :], in_=St[:, :, :]), R=[r_St], W=[r_Sh])
                r_o2 = Reg("o_sret"); out_regs.append(r_o2)
                k.dma("pool", sret_d[b_, h_].rearrange("(i p) v -> p i v", p=P), St, R=[r_St], W=[r_o2])
                for i in range(2):
                    d = 2 * h_ + i
                    first = (b_ == 0 and i == 0); last = (b_ == NS - 1 and i == 1)
                    k.op("pe", lambda: mm(po[h_][0:NS, :], lhsT=qpad[:, d, b_, :], rhs=Sh[:, i, :], start=first, stop=last),
                         R=[r_qpad, r_Sh], W=[r_po[h_]], inc=True)
            for h_ in range(4):
                norm_gate(po[h_], r_po[h_], NS, h_, 0, NS, h_)
            nonlocal_nb[0] = 8
            out_proj(TP, n, 4)
            ps_.end()
            dmy = Reg("ovdummy2"); dmy.readers = list(ovc[0]); ph.regs.append(dmy)

        nseg = TP // SEG
        for s in range(nseg):
            t0 = s * SEG
            ti = t0 // 512
            cur_t0[0] = t0
            seg_front(t0, SEG, ti, False)
            vtok_all(SEG)
            seg_chunks(SEG)
            out_proj(t0, SEG, ti)
        r_o = Reg("o_pret"); out_regs.append(r_o)
        k.dma("sp", pret_d.rearrange("h (i p) v -> p h i v", p=P), S32, R=r_S + [r_o], W=[r_o])
        if do_samples:
            ret_samples(r_o)
        ph.end()

    def final_out():
        ph = Phase()
        nb = norm_bufs(ph)
        yfm = [ph.alloc([P, KC, 512]) for _ in range(2)]
        r_yfm = [[ph.reg("yfm%d_%d" % (i, d)) for d in range(KC)] for i in range(2)]
        yout = [ph.alloc([P, D]) for _ in range(2)]
        r_yout = [ph.reg("yout0"), ph.reg("yout1")]
        st = 0
        for ti, (t0, tn) in enumerate(TILES):
            yb = ti % 2
            rmsnorm_tile(nb, 6, ti, lambda d, t0, tn: yfm[yb][:, d, 0:tn], lambda d: [r_yfm[yb][d]])
            nsub = (tn + P - 1) // P
            for s in range(nsub):
                rows = min(P, tn - s * P)
                ob = st % 2; st += 1
                for d0 in range(0, KC, 4):
                    bk2, rb2 = next_bank()
                    for j in range(4):
                        d = d0 + j
                        k.op("pe", lambda: nc.tensor.transpose(
                            bk2[0:rows, j * P:(j + 1) * P], yfm[yb][:, d, s * P:s * P + rows], ident[:, :]),
                            R=[r_yfm[yb][d], r_ident], W=[rb2])
                    evac_copy(yout[ob][0:rows, d0 * P:(d0 + 4) * P], bk2[0:rows, :], R=[rb2], W=[r_yout[ob]])
                dst = yp_d[t0 + s * P:t0 + s * P + rows, :] if ti < 4 else ys_d[:, :]
                r_o = Reg("o"); out_regs.append(r_o)
                k.dma("sp", dst, yout[ob][0:rows, :], R=[r_yout[ob]], W=[r_o])
        ph.end()

    load_x()
    for l in range(2):
        if "ffn1_%d" % l in stages:
            ffn(l, 1)
        if l == 0 and lx_carry[0]:
            best = {}
            for t_ in list(carry[0]) + list(lx_carry[0]):
                kk_ = (t_[0], t_[1])
                if kk_ not in best or best[kk_][2] < t_[2]:
                    best[kk_] = t_
            carry[0] = list(best.values())
            lx_carry[0] = []
        if "mix_%d" % l in stages:
            if l == 0:
                rwkv_mixer(do_samples=("samples" in stages))
            else:
                ret_mixer(do_samples=("samples" in stages))
        if "ffn2_%d" % l in stages:
            ffn(l, 2)
    final_out()

    k.finish(out_regs)
    es.close()
    print("instructions:", k.n_inst, "waits:", k.n_wait)
    return nc


def host_consts():
    c = {"c_ident": np.eye(P, dtype=np.float32)}
    bones = np.zeros((P, P), np.float32); bones[:64, :64] = 1.0; bones[64:, 64:] = 1.0
    c["c_bones"] = bones
    s_ = np.arange(P)[:, None]; t_ = np.arange(P)[None, :]
    c["c_mask_ur"] = np.concatenate([(s_ < t_), (s_ <= t_)], axis=1).astype(np.float32)
    c["c_mask_l"] = (s_ > t_).astype(np.float32)
    sm = np.ones((P, 256), np.float32); sm[:, 0] = 0.0; sm[:, 128] = 0.0
    c["c_scanm"] = sm
    f32 = np.float32
    half = 128
    inv = (f32(10000.0) ** (-(np.arange(half, dtype=f32) / f32(half)))).astype(f32)
    pos = np.concatenate([np.arange(TP, dtype=f32), np.full(NS, 16384.0, dtype=f32)])
    ang = (pos[None, :] * inv[:, None]).astype(f32)
    c["c_cos"] = np.cos(ang.astype(np.float64)).astype(f32)
    c["c_sin"] = np.sin(ang.astype(np.float64)).astype(f32)
    c["c_cosT"] = np.ascontiguousarray(c["c_cos"][:, :TP].T)
    c["c_sinT"] = np.ascontiguousarray(c["c_sin"][:, :TP].T)
    log_g = np.log1p(-np.exp2(-5.0 - np.arange(4, dtype=f32))).astype(f32)
    idx = np.arange(P, dtype=f32)
    diff = idx[None, :] - idx[:, None]
    dm = np.where(diff[:, None, :] >= 0, np.exp(log_g[None, :, None] * np.maximum(diff, 0)[:, None, :]), 0.0)
    c["c_dmatT"] = dm.astype(f32)
    qd = np.exp(log_g[:, None] * (idx[None, :] + 1.0)).astype(f32)
    c["c_qd"] = np.ascontiguousarray(np.broadcast_to(qd[None], (P, 4, P))).astype(f32)
    c["c_kdec"] = np.exp(log_g[None, :] * (127.0 - idx[:, None])).astype(f32)
    sel = np.zeros((NS, NS, P), f32)
    for b_ in range(NS):
        sel[b_, b_, :] = 1.0
    c["c_sel"] = sel
    c["c_i16"] = np.ascontiguousarray(np.tile(np.eye(NS, dtype=f32).reshape(1, NS * NS), (P, 1)))
    return c


W_NAMES = ("ln_ffn1", "ln_mix", "ln_ffn2", "ln_final", "ff1_wg", "ff1_wu", "ff1_wd", "ff2_wg", "ff2_wu", "ff2_wd")
RT_NAMES = ("ret_wq", "ret_wk", "ret_wv", "ret_wg", "ret_wo")
RW_NAMES = ("rw_mu", "rw_w0", "rw_w1", "rw_w2", "rw_a0", "rw_a1", "rw_a2", "rw_g1", "rw_g2", "rw_kk", "rw_ka",
            "rw_rk", "rw_wr", "rw_wk", "rw_wv", "rw_wo", "rw_lnx_g", "rw_lnx_b")


def make_in_maps(inputs, n_cores=N_CORES):
    consts = host_consts()
    shared = {}
    for nm in W_NAMES:
        shared[nm] = np.ascontiguousarray(inputs[nm], dtype=np.float32)
    for nm in RW_NAMES:
        a = np.asarray(inputs[nm], dtype=np.float32)[0]
        if nm == "rw_rk":
            a = a.reshape(D)
        shared[nm] = np.ascontiguousarray(a)
    for nm in RT_NAMES:
        shared[nm] = np.ascontiguousarray(np.asarray(inputs[nm], dtype=np.float32)[0])
    in_maps = []
    for c in range(n_cores):
        m = dict(consts)
        m.update(shared)
        m["x_prompt"] = np.ascontiguousarray(inputs["x_prompt"][c])
        m["x_sample"] = np.ascontiguousarray(inputs["x_sample"][c * NS:(c + 1) * NS, 0, :])
        m["st_shift"] = np.ascontiguousarray(inputs["state_rwkv_shift"][0, c * NS:(c + 1) * NS])
        m["st_wkv"] = np.ascontiguousarray(inputs["state_rwkv_wkv"][0, c * NS:(c + 1) * NS])
        m["st_ret"] = np.ascontiguousarray(inputs["state_ret"][0, c * NS:(c + 1) * NS])
        in_maps.append(m)
    return in_maps


def kernel(**inputs):
    nc = build()
    in_maps = make_in_maps(inputs)
    res = run_bass_kernel_spmd(nc, in_maps, core_ids=list(range(N_CORES)))
    R_ = res.results
    cat = lambda nm: np.concatenate([np.asarray(R_[c][nm]) for c in range(N_CORES)], 0)
    stk = lambda nm: np.stack([np.asarray(R_[c][nm]) for c in range(N_CORES)], 0)
    y_prompt = stk("y_prompt").astype(np.float32)
    y_sample = cat("y_sample")[:, None, :].astype(np.float32)
    p_shift = stk("p_shift")[None].astype(np.float32)
    p_wkv = stk("p_wkv")[None].astype(np.float32)
    p_ret = stk("p_ret")[None].astype(np.float32)
    s_shift = cat("s_shift")[None].astype(np.float32)
    s_wkv = cat("s_wkv")[None].astype(np.float32)
    s_ret = cat("s_ret")[None].astype(np.float32)
    return (y_prompt, y_sample, p_shift, p_wkv, p_ret, s_shift, s_wkv, s_ret)
```

```python
import contextlib
import numpy as np
import ml_dtypes
import concourse.bass as bass
import concourse.mybir as mybir
from concourse.bass_utils import run_bass_kernel_spmd

F32 = mybir.dt.float32
BF16 = mybir.dt.bfloat16
AF = mybir.ActivationFunctionType
ALU = mybir.AluOpType
AX = mybir.AxisListType

P = 128
D = 1024
DFF = 2816
NFC = DFF // P
KC = D // P
TP = 2048
NS = 16
T = TP + NS
TILES = [(0, 512), (512, 512), (1024, 512), (1536, 512), (2048, 16)]
NCH = TP // P
RMS_EPS = 1e-6
N_CORES = 8


class Reg:
    __slots__ = ("name", "last_w", "readers")

    def __init__(self, name):
        self.name = name
        self.last_w = None
        self.readers = []


class KB:
    def __init__(self, nc, es, n_dma_sems=48):
        self.nc = nc
        self.es = es
        self.eng = {"pe": nc.tensor, "act": nc.scalar, "dve": nc.vector,
                    "pool": nc.gpsimd, "sp": nc.sync}
        self.sem = {e: es.enter_context(nc.semaphore("c_" + e)) for e in self.eng}
        self.seq = {e: 0 for e in self.eng}
        self.known = {e: {} for e in self.eng}
        self.dma_sems = [es.enter_context(nc.semaphore("d%d" % i)) for i in range(n_dma_sems)]
        self.dma_cnt = [0] * n_dma_sems
        self.dma_pool = {"sw": list(range(0, n_dma_sems // 2)), "hw": list(range(n_dma_sems // 2, n_dma_sems))}
        self.dma_rr = {"sw": 0, "hw": 0}
        self.pe_pending = False
        self.n_wait = 0
        self.n_inst = 0
        self.same_engine_sync = True
        self.RAW_GAP = 10 ** 9

    def _wait(self, e, tok):
        if tok is None:
            return
        kind, who, val = tok
        key = (kind, who)
        if kind == "eng":
            if who == e and (e == "pe" or not self.same_engine_sync):
                return
            if who == e and e == "sp":
                return
        if self.known[e].get(key, 0) >= val:
            return
        self.known[e][key] = val
        sem = self.sem[who] if kind == "eng" else self.dma_sems[who]
        self.eng[e].wait_ge(sem, val)
        self.n_wait += 1

    def _deps(self, e, R, W):
        toks = []
        for r in R:
            toks.append(r.last_w)
        for w in W:
            toks.append(w.last_w)
            toks.extend(w.readers)
        seen = set()
        for t in toks:
            if t is not None and t not in seen:
                seen.add(t)
                self._wait(e, t)

    def _commit(self, tok, R, W):
        for r in R:
            r.readers.append(tok)
            if len(r.readers) > 24:
                best = {}
                for t in r.readers:
                    k = (t[0], t[1])
                    if k not in best or best[k][2] < t[2]:
                        best[k] = t
                r.readers = list(best.values())
        for w in W:
            w.last_w = tok
            w.readers = []

    def op(self, e, fn, R=(), W=(), inc=True):
        self._deps(e, R, W)
        inst = fn()
        self.n_inst += 1
        if inc:
            self.seq[e] += 1
            inst.then_inc(self.sem[e], 1)
            tok = ("eng", e, self.seq[e])
        else:
            assert e == "pe"
            tok = ("eng", e, self.seq[e] + 1)
        self._commit(tok, R, W)
        return tok

    def dma(self, q, out, in_, R=(), W=()):
        self._deps(q, R, W)
        cls = "sw" if q == "pool" else "hw"
        lst = self.dma_pool[cls]
        i = lst[self.dma_rr[cls]]
        self.dma_rr[cls] = (self.dma_rr[cls] + 1) % len(lst)
        if self.dma_cnt[i] > 0:
            self._wait(q, ("dma", i, 16 * self.dma_cnt[i]))
        self.dma_cnt[i] += 1
        self.eng[q].dma_start(out=out, in_=in_).then_inc(self.dma_sems[i], 16)
        self.n_inst += 1
        tok = ("dma", i, 16 * self.dma_cnt[i])
        self._commit(tok, R, W)
        return tok

    def finish(self, regs):
        for r in regs:
            self._wait("sp", r.last_w)


def bcast_mid(ap2, n):
    return ap2.unsqueeze(1).broadcast_to([ap2.shape[0], n, ap2.shape[1]])


def bcast_last(ap2, n):
    return ap2.unsqueeze(2).broadcast_to([ap2.shape[0], ap2.shape[1], n])


ALL_STAGES = ("ffn1_0", "mix_0", "ffn2_0", "ffn1_1", "mix_1", "ffn2_1", "samples")


def build(dbg=None, stages=ALL_STAGES):
    dbg = dbg or {}
    nc = bass.Bass("TRN2", target_bir_lowering=False)
    es = contextlib.ExitStack()
    k = KB(nc, es)

    def din(name, shape, dt=F32):
        return nc.dram_tensor(name, list(shape), dt, kind="ExternalInput").ap()

    def dout(name, shape, dt=F32):
        return nc.dram_tensor(name, list(shape), dt, kind="ExternalOutput").ap()

    xp_d = din("x_prompt", [TP, D])
    xs_d = din("x_sample", [NS, D])
    ln_ffn1_d = din("ln_ffn1", [2, D]); ln_mix_d = din("ln_mix", [2, D]); ln_ffn2_d = din("ln_ffn2", [2, D])
    ln_final_d = din("ln_final", [D])
    ffw = {}
    for nm in ("ff1_wg", "ff1_wu", "ff2_wg", "ff2_wu"):
        ffw[nm] = din(nm, [2, D, DFF])
    for nm in ("ff1_wd", "ff2_wd"):
        ffw[nm] = din(nm, [2, DFF, D])
    ident_d = din("c_ident", [P, P])
    rw = {}
    for nm_, shp_ in [("rw_mu", [6, D]), ("rw_w0", [D]), ("rw_w1", [D, 64]), ("rw_w2", [64, D]), ("rw_a0", [D]),
                      ("rw_a1", [D, 64]), ("rw_a2", [64, D]), ("rw_g1", [D, 128]), ("rw_g2", [128, D]),
                      ("rw_kk", [D]), ("rw_ka", [D]), ("rw_rk", [D]), ("rw_wr", [D, D]), ("rw_wk", [D, D]),
                      ("rw_wv", [D, D]), ("rw_wo", [D, D]), ("rw_lnx_g", [D]), ("rw_lnx_b", [D])]:
        rw[nm_] = din(nm_, shp_)
    rt = {}
    for nm_, shp_ in [("ret_wq", [D, D]), ("ret_wk", [D, D]), ("ret_wv", [D, 2 * D]), ("ret_wg", [D, 2 * D]),
                      ("ret_wo", [2 * D, D])]:
        rt[nm_] = din(nm_, shp_)
    c_cos_d = din("c_cos", [P, T]); c_sin_d = din("c_sin", [P, T])
    c_cosT_d = din("c_cosT", [TP, P]); c_sinT_d = din("c_sinT", [TP, P])
    c_dmatT_d = din("c_dmatT", [P, 4, P]); c_qd_d = din("c_qd", [P, 4, P]); c_kdec_d = din("c_kdec", [P, 4])
    c_sel_d = din("c_sel", [NS, NS, P]); c_i16_d = din("c_i16", [P, NS * NS])
    st_ret_d = din("st_ret", [NS, 4, 256, 512])
    pret_d = dout("p_ret", [4, 256, 512]); sret_d = dout("s_ret", [NS, 4, 256, 512])
    c_bones_d = din("c_bones", [P, P]); c_mask_ur_d = din("c_mask_ur", [P, 256]); c_mask_l_d = din("c_mask_l", [P, P])
    c_scanm_d = din("c_scanm", [P, 256])
    st_shift_d = din("st_shift", [NS, D]); st_wkv_d = din("st_wkv", [NS, 16, 64, 64])
    pshift_d = dout("p_shift", [D]); pwkv_d = dout("p_wkv", [16, 64, 64])
    sshift_d = dout("s_shift", [NS, D]); swkv_d = dout("s_wkv", [NS, 16, 64, 64])
    yp_d = dout("y_prompt", [TP, D])
    ys_d = dout("y_sample", [NS, D])
    out_regs = []

    def sb(name, shape, dt=F32):
        return nc.alloc_sbuf_tensor(name, list(shape), dt)

    AR_BYTES = 101 * 1024 + 512
    arena = sb("arena", [P, AR_BYTES // 2], BF16)
    carry = [[]]

    class Phase:
        def __init__(self, base=0, limit=None, carry_ref=None):
            self.off = base
            self.limit = AR_BYTES if limit is None else limit
            self.regs = []
            self.carry = carry if carry_ref is None else carry_ref

        def alloc(self, shape, dt=F32):
            esz = 4 if dt == F32 else 2
            n = 1
            for s_ in shape[1:]:
                n *= s_
            nb = (n * esz + 63) // 64 * 64
            assert self.off + nb <= self.limit, ("arena overflow", self.off, nb, self.limit)
            v = arena[0:shape[0], self.off // 2:(self.off + n * esz) // 2]
            if dt == F32:
                v = v.bitcast(F32)
            self.off += nb
            if len(shape) == 3:
                v = v.rearrange("p (a b) -> p a b", a=shape[1])
            elif len(shape) == 4:
                v = v.rearrange("p (a b c) -> p a b c", a=shape[1], b=shape[2])
            return v

        def reg(self, name):
            r = Reg(name)
            r.readers = list(self.carry[0])
            self.regs.append(r)
            return r

        def end(self):
            toks = set(self.carry[0])
            for r in self.regs:
                if r.last_w is not None:
                    toks.add(r.last_w)
                toks.update(r.readers)
            best = {}
            for t_ in toks:
                kk_ = (t_[0], t_[1])
                if kk_ not in best or best[kk_][2] < t_[2]:
                    best[kk_] = t_
            self.carry[0] = list(best.values())

    NWB = 5
    WB_ELEMS = 4096
    wbuf = [sb("wbuf%d" % i, [P, WB_ELEMS], BF16) for i in range(NWB)]
    r_wbuf = [Reg("wbuf%d" % i) for i in range(NWB)]
    wb_rr = [0]

    def next_wbuf():
        i = wb_rr[0]
        wb_rr[0] = (i + 1) % NWB
        return wbuf[i], r_wbuf[i]

    ident = sb("ident", [P, P]); r_ident = Reg("ident")
    k.dma("sp", ident[:], ident_d, W=[r_ident])
    identb = sb("identb", [P, P], BF16); r_identb = Reg("identb")
    k.op("dve", lambda: nc.vector.tensor_copy(out=identb[:], in_=ident[:]), R=[r_ident], W=[r_identb])
    ones_b = sb("ones_b", [P, P], BF16); r_ones = Reg("ones")
    k.op("dve", lambda: nc.vector.memset(ones_b[:], 1.0), W=[r_ones])
    eps_t = sb("eps_t", [P, 1]); r_eps = Reg("eps")
    k.op("dve", lambda: nc.vector.memset(eps_t[:], RMS_EPS), W=[r_eps])
    lnw = sb("lnw", [P, 7, KC]); r_lnw = Reg("lnw")
    with nc.allow_non_contiguous_dma(reason="tiny gain vectors"):
        for i, (src_, l) in enumerate([(ln_ffn1_d, 0), (ln_mix_d, 0), (ln_ffn2_d, 0),
                                       (ln_ffn1_d, 1), (ln_mix_d, 1), (ln_ffn2_d, 1)]):
            k.dma("sp", lnw[:, i, :], src_[l].rearrange("(k p) -> p k", p=P), W=[r_lnw])
        k.dma("sp", lnw[:, 6, :], ln_final_d.rearrange("(k p) -> p k", p=P), W=[r_lnw])

    NB = 8
    banks = [nc.alloc_psum_tensor("bank%d" % i, [P, 512], F32) for i in range(NB)]
    r_bank = [Reg("bank%d" % i) for i in range(NB)]
    bank_rr = [0]

    nonlocal_nb = [NB]

    def next_bank():
        i = bank_rr[0] % nonlocal_nb[0]
        bank_rr[0] = (i + 1) % nonlocal_nb[0]
        return banks[i], r_bank[i]

    x = sb("x", [P, KC, T]); r_x = [[Reg("x%d_%d" % (d, t)) for t in range(5)] for d in range(KC)]

    alt = [0]

    def evac_copy(dst_ap, src_ap, R, W, e=None):
        if e is None:
            e = "act" if alt[0] % 2 == 0 else "dve"
            alt[0] += 1
        if e == "act":
            k.op("act", lambda: nc.scalar.copy(out=dst_ap, in_=src_ap), R=R, W=W)
        else:
            k.op("dve", lambda: nc.vector.tensor_copy(out=dst_ap, in_=src_ap), R=R, W=W)

    lx_carry = [[]]

    def load_x():
        ph = Phase(base=AR_BYTES - 2 * D * 4, carry_ref=lx_carry)
        xin = [ph.alloc([P, D]) for _ in range(2)]
        r_xin = [ph.reg("xin0"), ph.reg("xin1")]
        ld = 0
        for ti, (t0, tn) in enumerate(TILES):
            nsub = (tn + P - 1) // P
            for s in range(nsub):
                rows = min(P, tn - s * P)
                b = ld % 2; ld += 1
                src_ = xp_d[t0 + s * P: t0 + s * P + rows, :] if ti < 4 else xs_d[:, :]
                k.dma("sp", xin[b][0:rows, :], src_, W=[r_xin[b]])
                for d0 in range(0, KC, 4):
                    bk, rb = next_bank()
                    for j in range(4):
                        d = d0 + j
                        k.op("pe", lambda d=d, j=j: nc.tensor.transpose(
                            bk[:, j * P: j * P + rows], xin[b][0:rows, d * P:(d + 1) * P], ident[0:rows, 0:rows]),
                            R=[r_xin[b], r_ident], W=[rb])
                    c0 = t0 + s * P
                    src_ap = bk[:, :].rearrange("p (j c) -> p j c", j=4)[:, :, 0:rows]
                    evac_copy(x[:, d0:d0 + 4, c0:c0 + rows], src_ap, R=[rb],
                              W=[r_x[d][ti] for d in range(d0, d0 + 4)])
        ph.end()

    def rmsnorm_tile(ph_bufs, gi, ti, out_ap_fn, W_fn, rng=None, extra=None, xregs=None):
        sqs, r_sqs, rstd, r_rstd, cnt = ph_bufs
        t0, tn = TILES[ti] if rng is None else rng
        xr_ = (lambda d: [r_x[d][ti]]) if xregs is None else xregs
        bk, rb = next_bank()
        for d in range(KC):
            b = cnt[0] % len(sqs); cnt[0] += 1
            k.op("act", lambda: nc.scalar.activation(
                out=sqs[b][:, 0:tn], in_=x[:, d, t0:t0 + tn], func=AF.Square),
                R=xr_(d), W=[r_sqs[b]])
            k.op("pe", lambda: nc.tensor.matmul(
                bk[:, 0:tn], lhsT=ones_b[:], rhs=sqs[b][:, 0:tn], start=(d == 0), stop=(d == KC - 1)),
                R=[r_sqs[b], r_ones], W=[rb], inc=True)
        rb_i = cnt[1] % 2; cnt[1] += 1
        k.op("act", lambda: nc.scalar.activation(
            out=rstd[rb_i][:, 0:tn], in_=bk[:, 0:tn], func=AF.Ln, scale=1.0 / D, bias=eps_t[:, 0:1]),
            R=[rb, r_eps], W=[r_rstd[rb_i]])
        k.op("act", lambda: nc.scalar.activation(
            out=rstd[rb_i][:, 0:tn], in_=rstd[rb_i][:, 0:tn], func=AF.Exp, scale=-0.5),
            R=[r_rstd[rb_i]], W=[r_rstd[rb_i]])
        for d in range(KC):
            k.op("dve", lambda: nc.vector.scalar_tensor_tensor(
                out=out_ap_fn(d, t0, tn), in0=x[:, d, t0:t0 + tn], scalar=lnw[:, gi, d:d + 1],
                in1=rstd[rb_i][:, 0:tn], op0=ALU.mult, op1=ALU.mult),
                R=xr_(d) + [r_rstd[rb_i], r_lnw], W=W_fn(d))
            if extra is not None:
                extra(d, rstd[rb_i], r_rstd[rb_i])

    def norm_bufs(ph, width=512):
        sqs = [ph.alloc([P, width], BF16) for _ in range(4)]
        r_sqs = [ph.reg("sq%d" % i) for i in range(4)]
        rstd = [ph.alloc([P, width]) for _ in range(2)]
        r_rstd = [ph.reg("rstd%d" % i) for i in range(2)]
        return (sqs, r_sqs, rstd, r_rstd, [0, 0])

    FB = 4

    def ffn(l, which):
        ph = Phase()
        nb = norm_bufs(ph)
        h = ph.alloc([P, KC, T], BF16)
        r_h = [[ph.reg("h%d_%d" % (d, t)) for t in range(5)] for d in range(KC)]
        a_t = ph.alloc([P, FB, T], BF16)
        r_a = [[ph.reg("a%d_%d" % (f, t)) for t in range(5)] for f in range(FB)]
        sl_t = [ph.alloc([P, 512]) for _ in range(2)]
        r_sl = [ph.reg("sl0"), ph.reg("sl1")]
        gi = {1: 0, 2: 2}[which] + 3 * l
        wg_d = ffw["ff%d_wg" % which][l].rearrange("(k p) c -> p k c", p=P)
        wu_d = ffw["ff%d_wu" % which][l].rearrange("(k p) c -> p k c", p=P)
        wd_d = ffw["ff%d_wd" % which][l].rearrange("(f p) c -> p f c", p=P)
        FT = [(0, 512), (512, 512), (1024, 512), (1536, 264), (1800, 264)]

        def xr(d, t0, tn):
            return [r_x[d][ti_] for ti_, (a_, b_) in enumerate(TILES) if a_ < t0 + tn and t0 < a_ + b_]
        for ti in range(5):
            rmsnorm_tile(nb, gi, ti, lambda d, t0, tn: h[:, d, t0:t0 + tn], lambda d, ti=ti: [r_h[d][ti]], rng=FT[ti],
                         xregs=lambda d, ti=ti: xr(d, *FT[ti]))
        blk_sizes = [4, 4, 4, 4, 3, 3]
        assert sum(blk_sizes) == NFC and max(blk_sizes) <= FB
        sl_i = 0
        f0 = -blk_sizes[0]
        for blk, nf in enumerate(blk_sizes):
            f0 = sum(blk_sizes[:blk])
            wgb, r_wg = next_wbuf()
            wub, r_wu = next_wbuf()
            wdb, r_wd = next_wbuf()
            wg_v = wgb[:, 0:KC * nf * P].rearrange("p (k c) -> p k c", k=KC)
            wu_v = wub[:, 0:KC * nf * P].rearrange("p (k c) -> p k c", k=KC)
            wd_v = wdb[:, 0:nf * D].rearrange("p (f c) -> p f c", f=nf)
            k.dma("pool", wg_v, wg_d[:, :, f0 * P:(f0 + nf) * P], W=[r_wg])
            k.dma("pool", wu_v, wu_d[:, :, f0 * P:(f0 + nf) * P], W=[r_wu])
            k.dma("pool", wd_v, wd_d[:, f0:f0 + nf, :], W=[r_wd])
            for f in range(nf):
                for ti, (t0, tn) in enumerate(FT):
                    bg, rbg = next_bank()
                    bu, rbu = next_bank()
                    for kk in range(KC):
                        k.op("pe", lambda: nc.tensor.matmul(
                            bg[:, 0:tn], lhsT=wg_v[:, kk, f * P:(f + 1) * P], rhs=h[:, kk, t0:t0 + tn],
                            start=(kk == 0), stop=(kk == KC - 1)),
                            R=[r_wg, r_h[kk][ti]], W=[rbg], inc=(kk == KC - 1))
                    for kk in range(KC):
                        k.op("pe", lambda: nc.tensor.matmul(
                            bu[:, 0:tn], lhsT=wu_v[:, kk, f * P:(f + 1) * P], rhs=h[:, kk, t0:t0 + tn],
                            start=(kk == 0), stop=(kk == KC - 1)),
                            R=[r_wu, r_h[kk][ti]], W=[rbu], inc=(kk == KC - 1))
                    sb_i = sl_i % 2; sl_i += 1
                    k.op("act", lambda: nc.scalar.activation(
                        out=sl_t[sb_i][:, 0:tn], in_=bg[:, 0:tn], func=AF.Silu),
                        R=[rbg], W=[r_sl[sb_i]])
                    k.op("dve", lambda: nc.vector.tensor_tensor(
                        out=a_t[:, f, t0:t0 + tn], in0=bu[:, 0:tn], in1=sl_t[sb_i][:, 0:tn], op=ALU.mult),
                        R=[rbu, r_sl[sb_i]], W=[r_a[f][ti]])
            for d in range(KC):
                for ti, (t0, tn) in enumerate(FT):
                    bk, rb = next_bank()
                    for f in range(nf):
                        k.op("pe", lambda: nc.tensor.matmul(
                            bk[:, 0:tn], lhsT=wd_v[:, f, d * P:(d + 1) * P], rhs=a_t[:, f, t0:t0 + tn],
                            start=(f == 0), stop=(f == nf - 1)),
                            R=[r_wd, r_a[f][ti]], W=[rb], inc=(f == nf - 1))
                    k.op("dve", lambda: nc.vector.scalar_tensor_tensor(
                        out=x[:, d, t0:t0 + tn], in0=bk[:, 0:tn], scalar=0.5, in1=x[:, d, t0:t0 + tn],
                        op0=ALU.mult, op1=ALU.add),
                        R=[rb] + xr(d, t0, tn), W=xr(d, t0, tn))
        ph.end()


    def rwkv_mixer(do_samples=True):
        ph = Phase()
        SEG = 256
        C0 = -float(np.exp(-0.5))
        mm = nc.tensor.matmul
        rv = ph.alloc([P, 14, KC]); r_rv = ph.reg("rv")
        with nc.allow_non_contiguous_dma(reason="tiny per-feature vectors"):
            for i in range(6):
                k.dma("sp", rv[:, i, :], rw["rw_mu"][i].rearrange("(k p) -> p k", p=P), W=[r_rv])
            for i, nm in enumerate(["rw_w0", "rw_a0", "rw_kk", "rw_ka", "rw_rk", "rw_lnx_g", "rw_lnx_b"]):
                k.dma("sp", rv[:, 6 + i, :], rw[nm].rearrange("(k p) -> p k", p=P), W=[r_rv])
        k.op("dve", lambda: nc.vector.tensor_scalar(out=rv[:, 13, :], in0=rv[:, 9, :], scalar1=-1.0, scalar2=1.0,
                                                    op0=ALU.mult, op1=ALU.add), R=[r_rv], W=[r_rv])
        w1b = ph.alloc([P, KC, 64], BF16); w2b = ph.alloc([64, D], BF16)
        a1b = ph.alloc([P, KC, 64], BF16); a2b = ph.alloc([64, D], BF16)
        g1b = ph.alloc([P, KC, 128], BF16); g2b = ph.alloc([P, D], BF16)
        r_lw = ph.reg("lora")
        k.dma("pool", w1b, rw["rw_w1"].rearrange("(k p) c -> p k c", p=P), W=[r_lw])
        k.dma("pool", w2b, rw["rw_w2"], W=[r_lw])
        k.dma("pool", a1b, rw["rw_a1"].rearrange("(k p) c -> p k c", p=P), W=[r_lw])
        k.dma("pool", a2b, rw["rw_a2"], W=[r_lw])
        k.dma("pool", g1b, rw["rw_g1"].rearrange("(k p) c -> p k c", p=P), W=[r_lw])
        k.dma("pool", g2b, rw["rw_g2"], W=[r_lw])
        mask_ur = ph.alloc([P, 256], BF16); mask_l = ph.alloc([P, P], BF16)
        bones = ph.alloc([P, P], BF16); bones64 = ph.alloc([P, P], BF16)
        scanm = ph.alloc([P, 256]); scanz = ph.alloc([P, NS]); eps_gn = ph.alloc([P, 1])
        r_cst = ph.reg("rwconst")
        k.dma("pool", mask_ur, c_mask_ur_d, W=[r_cst])
        k.dma("pool", mask_l, c_mask_l_d, W=[r_cst])
        k.dma("pool", bones, c_bones_d, W=[r_cst])
        k.dma("sp", scanm, c_scanm_d, W=[r_cst])
        k.op("dve", lambda: nc.vector.tensor_scalar(out=bones64, in0=bones, scalar1=1.0 / 64, scalar2=None,
                                                    op0=ALU.mult), R=[r_cst], W=[r_cst])
        k.op("dve", lambda: nc.vector.memset(scanz, 0.0), W=[r_cst])
        k.op("dve", lambda: nc.vector.memset(eps_gn, 64e-5), W=[r_cst])
        H32 = ph.alloc([P, KC, 64]); Hbd = ph.alloc([P, KC, P], BF16)
        r_H = [ph.reg("H%d" % g) for g in range(4)]
        k.op("dve", lambda: nc.vector.memset(H32, 0.0), W=r_H)
        k.op("dve", lambda: nc.vector.memset(Hbd, 0.0), W=r_H)
        shf = ph.alloc([P, KC, 1 + NS]); r_shf = ph.reg("shf")
        k.op("dve", lambda: nc.vector.memset(shf, 0.0), W=[r_shf])
        hcar = ph.alloc([P, KC, 1], BF16); r_hcar = ph.reg("hcar")
        k.op("dve", lambda: nc.vector.memset(hcar, 0.0), W=[r_hcar])
        hsp = ph.alloc([P, KC, NS], BF16); r_hsp = ph.reg("hsp")
        if do_samples:
            stg = ph.alloc([NS, D]); r_stg = ph.reg("stg")
            k.dma("sp", stg, st_shift_d, W=[r_stg])
            for half in range(2):
                bk, rb = next_bank()
                for jj in range(4):
                    d = half * 4 + jj
                    k.op("pe", lambda: nc.tensor.transpose(bk[:, jj * NS:(jj + 1) * NS], stg[0:NS, d * P:(d + 1) * P], ident[0:NS, 0:NS]),
                         R=[r_stg, r_ident], W=[rb])
                evac_copy(hsp[:, half * 4:half * 4 + 4, :], bk[:, 0:4 * NS].rearrange("p (a b) -> p a b", a=4), R=[rb], W=[r_hsp])
        ar = ph.alloc([P, KC, 2, SEG], BF16); r_ar = [ph.reg("ar%d" % d) for d in range(KC)]
        bt = ph.alloc([P, KC, SEG], BF16); r_bt = [ph.reg("bt%d" % d) for d in range(KC)]
        kt = ph.alloc([P, KC, SEG], BF16); r_kt = [ph.reg("kt%d" % d) for d in range(KC)]
        vt = ph.alloc([P, KC, SEG], BF16); r_vt = [ph.reg("vt%d" % d) for d in range(KC)]
        gt = ph.alloc([P, KC, SEG], BF16); r_gt = [ph.reg("gt%d" % d) for d in range(KC)]
        bon = ph.alloc([P, KC, SEG], BF16); r_bon = [ph.reg("bon%d" % d) for d in range(KC)]
        og = ph.alloc([P, KC, SEG], BF16); r_og = ph.reg("og")
        gC = ph.alloc([P, KC, 2]); r_gC = ph.reg("gC")
        dS = ph.alloc([P, KC, NS]); r_dS = ph.reg("dS")
        ov_base = ph.off
        ovc = [list(carry[0])]
        ybk = [banks[6], banks[7]]; r_ybk = [r_bank[6], r_bank[7]]
        nonlocal_nb[0] = 6

        def stageA(t0, n, ti, sample, last_prompt):
            pa = Phase(base=ov_base, carry_ref=ovc)
            nbuf = norm_bufs(pa, SEG)
            hn = pa.alloc([P, KC, SEG + 2], BF16); r_hn = [pa.reg("hn%d" % d) for d in range(KC)]
            dx = pa.alloc([P, KC, SEG], BF16); r_dx = pa.reg("dx")
            xi = [pa.alloc([P, KC, SEG], BF16) for _ in range(2)]; r_xi = [pa.reg("xi0"), pa.reg("xi1")]
            lt = [pa.alloc([P, SEG], BF16) for _ in range(3)]; r_lt = [pa.reg("lt%d" % i) for i in range(3)]
            T32 = [[pa.alloc([P, SEG]) for _ in range(4)] for _ in range(4)]
            rT32 = [[pa.reg("t32_%d_%d" % (i, j)) for j in range(4)] for i in range(4)]
            T16 = [[pa.alloc([P, SEG], BF16) for _ in range(4)] for _ in range(3)]
            rT16 = [[pa.reg("t16_%d_%d" % (i, j)) for j in range(4)] for i in range(3)]
            if not sample:
                k.op("dve", lambda: nc.vector.tensor_copy(out=hn[:, :, 0:1], in_=hcar[:, :, 0:1]), R=[r_hcar], W=r_hn)

            def extra(d, rstd_t, r_rstd_t):
                if last_prompt:
                    k.op("dve", lambda: nc.vector.scalar_tensor_tensor(
                        out=shf[:, d, 0:1], in0=x[:, d, t0 + n - 1:t0 + n], scalar=lnw[:, 1, d:d + 1],
                        in1=rstd_t[:, n - 1:n], op0=ALU.mult, op1=ALU.mult),
                        R=[r_x[d][ti], r_rstd_t, r_lnw], W=[r_shf])
                if sample:
                    k.op("dve", lambda: nc.vector.scalar_tensor_tensor(
                        out=shf[:, d, 1:1 + NS], in0=x[:, d, t0:t0 + n], scalar=lnw[:, 1, d:d + 1],
                        in1=rstd_t[:, 0:n], op0=ALU.mult, op1=ALU.mult),
                        R=[r_x[d][ti], r_rstd_t, r_lnw], W=[r_shf])
            rmsnorm_tile(nbuf, 1, ti, lambda d, a_, b_: hn[:, d, 1:1 + n], lambda d: [r_hn[d]], rng=(t0, n), extra=extra)
            hprev = hsp[:, :, 0:n] if sample else hn[:, :, 0:n]
            k.op("dve", lambda: nc.vector.tensor_tensor(out=dx[:, :, 0:n], in0=hprev, in1=hn[:, :, 1:1 + n], op=ALU.subtract),
                 R=r_hn + ([r_hsp] if sample else []), W=[r_dx])
            if not sample:
                k.op("dve", lambda: nc.vector.tensor_copy(out=hcar[:, :, 0:1], in_=hn[:, :, n:n + 1]), R=r_hn, W=[r_hcar])
            xi_i = [0]

            def make_xi(i):
                b = xi_i[0] % 2; xi_i[0] += 1
                for d in range(KC):
                    k.op("dve", lambda: nc.vector.scalar_tensor_tensor(
                        out=xi[b][:, d, 0:n], in0=dx[:, d, 0:n], scalar=rv[:, i, d:d + 1], in1=hn[:, d, 1:1 + n],
                        op0=ALU.mult, op1=ALU.add), R=[r_dx, r_hn[d], r_rv], W=[r_xi[b]])
                return xi[b], r_xi[b]

            def proj_big(wd, xt, r_xt, evac):
                wv_ = wd.rearrange("(k p) c -> p k c", p=P)
                for half in range(2):
                    wb, r_w = next_wbuf()
                    v = wb[:, 0:KC * 512].rearrange("p (k c) -> p k c", k=KC)
                    k.dma("pool", v, wv_[:, :, half * 512:(half + 1) * 512], W=[r_w])
                    for oc in range(4):
                        d = half * 4 + oc
                        bk, rb = next_bank()
                        for kk in range(KC):
                            k.op("pe", lambda: mm(bk[:, 0:n], lhsT=v[:, kk, oc * P:(oc + 1) * P], rhs=xt[:, kk, 0:n],
                                                  start=(kk == 0), stop=(kk == KC - 1)),
                                 R=[r_w, r_xt], W=[rb], inc=(kk == KC - 1))
                        evac(d, bk[:, 0:n], rb)

            def lora_in(wl, m, xt, r_xt, func, li):
                bk, rb = next_bank()
                for kk in range(KC):
                    k.op("pe", lambda: mm(bk[0:m, 0:n], lhsT=wl[:, kk, 0:m], rhs=xt[:, kk, 0:n],
                                          start=(kk == 0), stop=(kk == KC - 1)),
                         R=[r_lw, r_xt], W=[rb], inc=(kk == KC - 1))
                k.op("act", lambda: nc.scalar.activation(out=lt[li][0:m, 0:n], in_=bk[0:m, 0:n], func=func),
                     R=[rb], W=[r_lt[li]])

            def lora_out(wl2, m, li, d):
                bk, rb = next_bank()
                k.op("pe", lambda: mm(bk[:, 0:n], lhsT=wl2[0:m, d * P:(d + 1) * P], rhs=lt[li][0:m, 0:n],
                                      start=True, stop=True), R=[r_lw, r_lt[li]], W=[rb])
                return bk, rb

            xt, r_xt = make_xi(0)
            proj_big(rw["rw_wr"], xt, r_xt, lambda d, ps, rb: k.op(
                "act", lambda: nc.scalar.copy(out=ar[:, d, 1, 0:n], in_=ps), R=[rb], W=[r_ar[d]]))
            xt, r_xt = make_xi(2)
            proj_big(rw["rw_wk"], xt, r_xt, lambda d, ps, rb: evac_copy(kt[:, d, 0:n], ps, R=[rb], W=[r_kt[d]]))
            xt, r_xt = make_xi(4)
            lora_in(a1b, 64, xt, r_xt, AF.Copy, 0)
            for d in range(KC):
                bk, rb = lora_out(a2b, 64, 0, d)
                k.op("act", lambda: nc.scalar.activation(out=bt[:, d, 0:n], in_=bk[:, 0:n], func=AF.Sigmoid,
                                                         bias=rv[:, 7, d:d + 1]), R=[rb, r_rv], W=[r_bt[d]])
            GD = 4

            def staged(steps):
                for g0 in range(0, KC, GD):
                    st = {}
                    for step in steps:
                        for d in range(g0, g0 + GD):
                            step(d, st)

            def s_kr(d, st):
                q = d % GD
                k.op("dve", lambda: nc.vector.tensor_scalar(out=T16[0][q][:, 0:n], in0=kt[:, d, 0:n], scalar1=rv[:, 8, d:d + 1],
                                                            scalar2=None, op0=ALU.mult), R=[r_kt[d], r_rv], W=[rT16[0][q]])

            def s_sq(d, st):
                q = d % GD
                k.op("act", lambda: nc.scalar.activation(out=T16[1][q][:, 0:n], in_=T16[0][q][:, 0:n], func=AF.Square),
                     R=[rT16[0][q]], W=[rT16[1][q]])

            def s_ss(d, st):
                q = d % GD
                bk, rb = next_bank()
                st[("ss", d)] = (bk, rb)
                k.op("pe", lambda: mm(bk[:, 0:n], lhsT=bones, rhs=T16[1][q][:, 0:n], start=True, stop=True),
                     R=[r_cst, rT16[1][q]], W=[rb])

            def s_max(d, st):
                q = d % GD
                bk, rb = st[("ss", d)]
                k.op("dve", lambda: nc.vector.tensor_scalar(out=T32[0][q][:, 0:n], in0=bk[:, 0:n], scalar1=1e-24, scalar2=None,
                                                            op0=ALU.max), R=[rb], W=[rT32[0][q]])

            def s_ln(d, st):
                q = d % GD
                k.op("act", lambda: nc.scalar.activation(out=T32[0][q][:, 0:n], in_=T32[0][q][:, 0:n], func=AF.Ln),
                     R=[rT32[0][q]], W=[rT32[0][q]])

            def s_ex(d, st):
                q = d % GD
                k.op("act", lambda: nc.scalar.activation(out=T32[0][q][:, 0:n], in_=T32[0][q][:, 0:n], func=AF.Exp, scale=-0.5),
                     R=[rT32[0][q]], W=[rT32[0][q]])

            def s_kk(d, st):
                q = d % GD
                k.op("dve", lambda: nc.vector.tensor_tensor(out=ar[:, d, 0, 0:n], in0=T16[0][q][:, 0:n], in1=T32[0][q][:, 0:n], op=ALU.mult),
                     R=[rT16[0][q], rT32[0][q]], W=[r_ar[d]])
                k.op("dve", lambda: nc.vector.tensor_scalar(out=T32[1][q][:, 0:n], in0=bt[:, d, 0:n], scalar1=rv[:, 9, d:d + 1],
                                                            scalar2=rv[:, 13, d:d + 1], op0=ALU.mult, op1=ALU.add),
                     R=[r_bt[d], r_rv], W=[rT32[1][q]])

            def s_kb(d, st):
                q = d % GD
                k.op("dve", lambda: nc.vector.tensor_tensor(out=kt[:, d, 0:n], in0=kt[:, d, 0:n], in1=T32[1][q][:, 0:n], op=ALU.mult),
                     R=[rT32[1][q], r_kt[d]], W=[r_kt[d]])
                k.op("dve", lambda: nc.vector.tensor_tensor(out=bt[:, d, 0:n], in0=bt[:, d, 0:n], in1=ar[:, d, 0, 0:n], op=ALU.mult),
                     R=[r_ar[d], r_bt[d]], W=[r_bt[d]])

            def s_bq(d, st):
                q = d % GD
                k.op("dve", lambda: nc.vector.scalar_tensor_tensor(
                    out=T16[2][q][:, 0:n], in0=ar[:, d, 1, 0:n], scalar=rv[:, 10, d:d + 1], in1=kt[:, d, 0:n],
                    op0=ALU.mult, op1=ALU.mult), R=[r_ar[d], r_kt[d], r_rv], W=[rT16[2][q]])

            def s_bs(d, st):
                q = d % GD
                bk2, rb2 = next_bank()
                st[("bs", d)] = (bk2, rb2)
                k.op("pe", lambda: mm(bk2[:, 0:n], lhsT=bones, rhs=T16[2][q][:, 0:n], start=True, stop=True),
                     R=[r_cst, rT16[2][q]], W=[rb2])

            def s_bon(d, st):
                bk2, rb2 = st[("bs", d)]
                k.op("act", lambda: nc.scalar.copy(out=bon[:, d, 0:n], in_=bk2[:, 0:n]), R=[rb2], W=[r_bon[d]])

            staged([s_kr, s_sq, s_ss, s_max, s_ln, s_ex, s_kk, s_kb, s_bq, s_bs, s_bon])

            xt, r_xt = make_xi(1)
            lora_in(w1b, 64, xt, r_xt, AF.Tanh, 1)
            smask = scanz[:, 0:n] if sample else scanm[:, 0:n]
            SG, CW, E1, E2 = T32[0], T32[1], T32[2], T32[3]
            rSG, rCW, rE1, rE2 = rT32[0], rT32[1], rT32[2], rT32[3]

            def t_mm(d, st):
                st[("w", d)] = lora_out(w2b, 64, 1, d)

            def t_sig(d, st):
                q = d % GD
                bk, rb = st[("w", d)]
                k.op("act", lambda: nc.scalar.activation(out=SG[q][:, 0:n], in_=bk[:, 0:n], func=AF.Sigmoid,
                                                         bias=rv[:, 6, d:d + 1]), R=[rb, r_rv], W=[rSG[q]])

            def t_scan(d, st):
                q = d % GD
                k.op("dve", lambda: nc.vector.tensor_tensor_scan(out=CW[q][:, 0:n], data0=smask, data1=SG[q][:, 0:n], initial=0.0,
                                                                 op0=ALU.mult, op1=ALU.add), R=[rSG[q], r_cst], W=[rCW[q]])
                k.op("dve", lambda: nc.vector.tensor_tensor(out=SG[q][:, 0:n], in0=CW[q][:, 0:n], in1=SG[q][:, 0:n], op=ALU.subtract),
                     R=[rCW[q], rSG[q]], W=[rSG[q]])

            def t_e1(d, st):
                q = d % GD
                k.op("act", lambda: nc.scalar.activation(out=E1[q][:, 0:n], in_=CW[q][:, 0:n], func=AF.Exp, scale=C0),
                     R=[rCW[q]], W=[rE1[q]])
                k.op("act", lambda: nc.scalar.activation(out=E2[q][:, 0:n], in_=SG[q][:, 0:n], func=AF.Exp, scale=C0),
                     R=[rSG[q]], W=[rE2[q]])
                k.op("act", lambda: nc.scalar.activation(out=SG[q][:, 0:n], in_=CW[q][:, 0:n], func=AF.Exp, scale=-C0),
                     R=[rCW[q], rSG[q]], W=[rSG[q]])

            def t_apply(d, st):
                q = d % GD
                k.op("dve", lambda: nc.vector.tensor_tensor(out=ar[:, d, 1, 0:n], in0=ar[:, d, 1, 0:n], in1=E1[q][:, 0:n], op=ALU.mult),
                     R=[rE1[q], r_ar[d]], W=[r_ar[d]])
                if sample:
                    k.op("dve", lambda: nc.vector.tensor_copy(out=dS[:, d, 0:n], in_=E1[q][:, 0:n]), R=[rE1[q]], W=[r_dS])
                else:
                    for c in range(n // P):
                        k.op("dve", lambda: nc.vector.tensor_copy(out=gC[:, d, c:c + 1], in_=E1[q][:, c * P + P - 1:c * P + P]),
                             R=[rE1[q]], W=[r_gC])
                k.op("dve", lambda: nc.vector.scalar_tensor_tensor(
                    out=ar[:, d, 0, 0:n], in0=ar[:, d, 0, 0:n], scalar=-1.0, in1=E2[q][:, 0:n], op0=ALU.mult, op1=ALU.mult),
                    R=[rE2[q], r_ar[d]], W=[r_ar[d]])
                k.op("dve", lambda: nc.vector.tensor_tensor(out=bt[:, d, 0:n], in0=bt[:, d, 0:n], in1=SG[q][:, 0:n], op=ALU.mult),
                     R=[rSG[q], r_bt[d]], W=[r_bt[d]])
                k.op("dve", lambda: nc.vector.tensor_tensor(out=kt[:, d, 0:n], in0=kt[:, d, 0:n], in1=SG[q][:, 0:n], op=ALU.mult),
                     R=[rSG[q], r_kt[d]], W=[r_kt[d]])

            staged([t_mm, t_sig, t_scan, t_e1, t_apply])
            xt, r_xt = make_xi(3)
            proj_big(rw["rw_wv"], xt, r_xt, lambda d, ps, rb: evac_copy(vt[:, d, 0:n], ps, R=[rb], W=[r_vt[d]]))
            xt, r_xt = make_xi(5)
            lora_in(g1b, 128, xt, r_xt, AF.Sigmoid, 2)
            for d in range(KC):
                bk, rb = lora_out(g2b, 128, 2, d)
                evac_copy(gt[:, d, 0:n], bk[:, 0:n], R=[rb], W=[r_gt[d]])
            pa.end()

        def post(pb_tiles, ysrc_fn, c0, w):
            for _ in post_gen(pb_tiles, ysrc_fn, c0, w, None):
                pass

        def post_gen(pb_tiles, ysrc_fn, c0, w, done_flag):
            y32s, r_y32s, ybs, r_ybs, sqbs, r_sqbs, rss, r_rss = pb_tiles
            for d in range(KC):
                if d > 0:
                    yield
                q = d % 2
                y32, r_y32 = y32s[q], r_y32s[q]
                yb, r_yb = ybs[q], r_ybs[q]
                sqb, r_sqb = sqbs[q], r_sqbs[q]
                rs, r_rs = rss[q], r_rss[q]
                ysrc, r_ysrc = ysrc_fn(d)
                k.op("act", lambda: nc.scalar.copy(out=y32[:, 0:w], in_=ysrc), R=[r_ysrc], W=[r_y32])
                k.op("dve", lambda: nc.vector.tensor_copy(out=yb[:, 0:w], in_=ysrc), R=[r_ysrc], W=[r_yb])
                bk, rb = next_bank()
                k.op("pe", lambda: mm(bk[:, 0:w], lhsT=bones64, rhs=yb[:, 0:w], start=True, stop=True),
                     R=[r_cst, r_yb], W=[rb])
                k.op("dve", lambda: nc.vector.tensor_tensor(out=y32[:, 0:w], in0=y32[:, 0:w], in1=bk[:, 0:w], op=ALU.subtract),
                     R=[rb, r_y32], W=[r_y32])
                k.op("act", lambda: nc.scalar.activation(out=sqb[:, 0:w], in_=y32[:, 0:w], func=AF.Square),
                     R=[r_y32], W=[r_sqb])
                bk2, rb2 = next_bank()
                k.op("pe", lambda: mm(bk2[:, 0:w], lhsT=bones64, rhs=sqb[:, 0:w], start=True, stop=True),
                     R=[r_cst, r_sqb], W=[rb2])
                k.op("act", lambda: nc.scalar.activation(out=rs[:, 0:w], in_=bk2[:, 0:w], func=AF.Ln, bias=eps_gn[:, 0:1]),
                     R=[rb2, r_cst], W=[r_rs])
                k.op("act", lambda: nc.scalar.activation(out=rs[:, 0:w], in_=rs[:, 0:w], func=AF.Exp, scale=-0.5), R=[r_rs], W=[r_rs])
                k.op("dve", lambda: nc.vector.tensor_tensor(out=y32[:, 0:w], in0=y32[:, 0:w], in1=rs[:, 0:w], op=ALU.mult),
                     R=[r_rs, r_y32], W=[r_y32])
                k.op("dve", lambda: nc.vector.tensor_scalar(out=y32[:, 0:w], in0=y32[:, 0:w], scalar1=rv[:, 11, d:d + 1],
                                                            scalar2=rv[:, 12, d:d + 1], op0=ALU.mult, op1=ALU.add),
                     R=[r_y32, r_rv], W=[r_y32])
                k.op("dve", lambda: nc.vector.tensor_tensor(out=rs[:, 0:w], in0=bon[:, d, c0:c0 + w], in1=vt[:, d, c0:c0 + w], op=ALU.mult),
                     R=[r_bon[d], r_vt[d]], W=[r_rs])
                k.op("dve", lambda: nc.vector.tensor_tensor(out=y32[:, 0:w], in0=y32[:, 0:w], in1=rs[:, 0:w], op=ALU.add),
                     R=[r_rs, r_y32], W=[r_y32])
                k.op("dve", lambda: nc.vector.tensor_tensor(out=og[:, d, c0:c0 + w], in0=y32[:, 0:w], in1=gt[:, d, c0:c0 + w], op=ALU.mult),
                     R=[r_y32, r_gt[d]], W=[r_og])
            if done_flag is not None:
                done_flag[0] = True

        def post_tiles(pb):
            y32s = [pb.alloc([P, P]) for _ in range(2)]; r_y32s = [pb.reg("y32a"), pb.reg("y32b")]
            ybs = [pb.alloc([P, P], BF16) for _ in range(2)]; r_ybs = [pb.reg("yba"), pb.reg("ybb")]
            sqbs = [pb.alloc([P, P], BF16) for _ in range(2)]; r_sqbs = [pb.reg("sqba"), pb.reg("sqbb")]
            rss = [pb.alloc([P, P]) for _ in range(2)]; r_rss = [pb.reg("rsa"), pb.reg("rsb")]
            return (y32s, r_y32s, ybs, r_ybs, sqbs, r_sqbs, rss, r_rss)

        def run_interleaved(gens):
            gens = list(gens)
            while gens:
                for g_ in list(gens):
                    try:
                        next(g_)
                    except StopIteration:
                        gens.remove(g_)

        def stageB(n):
            pb = Phase(base=ov_base, carry_ref=ovc)
            btok = pb.alloc([P, D], BF16); ktok = pb.alloc([P, D], BF16); vtok = pb.alloc([P, D], BF16)
            r_tok = [pb.reg("btok"), pb.reg("ktok"), pb.reg("vtok")]
            G = 4
            sets = []
            for si in range(4):
                S_ = {}
                S_["Nb"] = pb.alloc([P, G, 256], BF16); S_["Nk"] = pb.alloc([P, G, 256], BF16)
                S_["r_Nb"] = pb.reg("Nb%d" % si); S_["r_Nk"] = pb.reg("Nk%d" % si)
                _x = pb.alloc([P, G, P], BF16); _rx = pb.reg("X_%d" % si)
                _y = pb.alloc([P, G, P], BF16); _ry = pb.reg("Y_%d" % si)
                _p = pb.alloc([P, G, P], BF16); _rp = pb.reg("P_%d" % si)
                S_["Xs"] = [_x, _x]; S_["r_Xs"] = [_rx, _rx]
                S_["Ys"] = [_y, _y]; S_["r_Ys"] = [_ry, _ry]
                S_["Ps"] = [_p, _p]; S_["r_Ps"] = [_rp, _rp]
                S_["Wt"] = pb.alloc([P, 2, P], BF16); S_["r_Wt"] = pb.reg("Wt%d" % si)
                S_["Ut"] = pb.alloc([P, 2, P], BF16); S_["r_Ut"] = pb.reg("Ut%d" % si)
                S_["tH"] = pb.alloc([P, 2, 64]); S_["r_tH"] = pb.reg("tH%d" % si)
                sets.append(S_)
            ptl = post_tiles(pb)
            ysb = pb.alloc([P, KC, P]); r_ysb = [pb.reg("ysb%d" % g_) for g_ in range(4)]
            nonlocal_nb[0] = 8
            post_done = [True]
            mur2 = mask_ur.unsqueeze(1).broadcast_to([P, 2, 256])
            ml2 = mask_l.unsqueeze(1).broadcast_to([P, 2, P])
            id4 = identb[:, :].unsqueeze(1).broadcast_to([P, G, P])

            def group_gen(c, g, S_):
                cs = c * P
                Nb, Nk, r_Nb, r_Nk = S_["Nb"], S_["Nk"], S_["r_Nb"], S_["r_Nk"]
                Xs, Ys, Ps, r_Xs, r_Ys, r_Ps = S_["Xs"], S_["Ys"], S_["Ps"], S_["r_Xs"], S_["r_Ys"], S_["r_Ps"]
                Wt, Ut, tH, r_Wt, r_Ut, r_tH = S_["Wt"], S_["Ut"], S_["tH"], S_["r_Wt"], S_["r_Ut"], S_["r_tH"]
                j0 = 2 * g
                units = [(j0 + u % 2, u // 2) for u in range(G)]
                sbk = [next_bank() for _ in range(2)]
                for u, (j, hp) in enumerate(units):
                    ps_ = slice(hp * 64, hp * 64 + 64)
                    jj = u % 2
                    if hp == 0:
                        bkA, rbA = sbk[0]
                        o_ = bkA[:, jj * 256:jj * 256 + 256]
                    else:
                        bkA, rbA = sbk[1]
                        o_ = bkA[:, jj * 256:jj * 256 + 256]
                    k.op("pe", lambda: mm(o_, lhsT=bt[ps_, j, cs:cs + P], rhs=ar[ps_, j, :, cs:cs + P], start=True, stop=True),
                         R=[r_bt[j], r_ar[j]], W=[rbA])
                for hh in range(2):
                    bkA, rbA = sbk[hh]
                    k.op("dve", lambda: nc.vector.tensor_tensor(
                        out=Nb[:, 2 * hh:2 * hh + 2, :], in0=bkA[:, :].rearrange("p (u c) -> p u c", u=2), in1=mur2, op=ALU.mult),
                        R=[rbA, r_cst], W=[r_Nb])
                yield
                sbk2 = [next_bank() for _ in range(2)]
                for u, (j, hp) in enumerate(units):
                    ps_ = slice(hp * 64, hp * 64 + 64)
                    jj = u % 2
                    bkC, rbC = sbk2[hp]
                    k.op("pe", lambda: mm(bkC[:, jj * 256:jj * 256 + 256], lhsT=kt[ps_, j, cs:cs + P],
                                          rhs=ar[ps_, j, :, cs:cs + P], start=True, stop=True),
                         R=[r_kt[j], r_ar[j]], W=[rbC])
                bkE0, rbE0 = next_bank()
                bkE1, rbE1 = next_bank()
                for u, (j, hp) in enumerate(units):
                    ps_ = slice(hp * 64, hp * 64 + 64)
                    jj = u % 2
                    bkE, rbE = (bkE0, rbE0) if hp == 0 else (bkE1, rbE1)
                    k.op("pe", lambda: mm(bkE[:, jj * P:(jj + 1) * P], lhsT=ar[ps_, j, 0, cs:cs + P],
                                          rhs=bt[ps_, j, cs:cs + P], start=True, stop=True),
                         R=[r_bt[j], r_ar[j]], W=[rbE])
                for hh in range(2):
                    bkC, rbC = sbk2[hh]
                    k.op("dve", lambda: nc.vector.tensor_tensor(
                        out=Nk[:, 2 * hh:2 * hh + 2, :], in0=bkC[:, :].rearrange("p (u c) -> p u c", u=2), in1=mur2, op=ALU.mult),
                        R=[rbC, r_cst], W=[r_Nk])
                    bkE, rbE = (bkE0, rbE0) if hh == 0 else (bkE1, rbE1)
                    k.op("dve", lambda: nc.vector.tensor_tensor(
                        out=Ys[0][:, 2 * hh:2 * hh + 2, :], in0=bkE[:, 0:2 * P].rearrange("p (u c) -> p u c", u=2), in1=ml2, op=ALU.mult),
                        R=[rbE, r_cst], W=[r_Ys[0]])
                k.op("dve", lambda: nc.vector.tensor_tensor(out=Ps[0][:, :, :], in0=Nb[:, :, 0:P], in1=id4, op=ALU.add),
                     R=[r_Nb, r_identb], W=[r_Ps[0]])
                yield
                Xc = lambda u: Nb[:, u, 0:P]
                r_Xc = r_Nb
                yi = 0; pi = 0; xi_ = 0
                for lvl in range(6):
                    bX, rbX = next_bank()
                    bY, rbY = next_bank()
                    Yc = Ys[yi]; r_Yc = r_Ys[yi]
                    for u in range(G):
                        k.op("pe", lambda: mm(bY[:, u * P:(u + 1) * P], lhsT=Xc(u), rhs=Yc[:, u, :], start=True, stop=True),
                             R=[r_Yc, r_Xc], W=[rbY])
                    if lvl < 5:
                        for u in range(G):
                            k.op("pe", lambda: mm(bX[:, u * P:(u + 1) * P], lhsT=Yc[:, u, :], rhs=Xc(u), start=True, stop=True),
                                 R=[r_Yc, r_Xc], W=[rbX])
                    Yn = Ys[1 - yi]; r_Yn = r_Ys[1 - yi]
                    evac_copy(Yn[:, :, :], bY[:, :].rearrange("p (u c) -> p u c", u=G), R=[rbY], W=[r_Yn])
                    if lvl < 5:
                        Xn = Xs[xi_]; r_Xn = r_Xs[xi_]
                        evac_copy(Xn[:, :, :], bX[:, :].rearrange("p (u c) -> p u c", u=G), R=[rbX], W=[r_Xn])
                        Xc = (lambda Xn: (lambda u: Xn[:, u, :]))(Xn)
                        r_Xc = r_Xn
                        xi_ = 1 - xi_
                    yi = 1 - yi
                    yield
                    bP, rbP = next_bank()
                    Pc = Ps[pi]; r_Pc = r_Ps[pi]
                    for u in range(G):
                        k.op("pe", lambda: mm(bP[:, u * P:(u + 1) * P], lhsT=identb[:, :], rhs=Pc[:, u, :], start=True, stop=False),
                             R=[r_identb, r_Pc], W=[rbP], inc=False)
                        k.op("pe", lambda: mm(bP[:, u * P:(u + 1) * P], lhsT=Yn[:, u, :], rhs=Pc[:, u, :], start=False, stop=True),
                             R=[r_Yn, r_Pc], W=[rbP])
                    Pn = Ps[1 - pi]; r_Pn = r_Ps[1 - pi]
                    evac_copy(Pn[:, :, :], bP[:, :].rearrange("p (u c) -> p u c", u=G), R=[rbP], W=[r_Pn])
                    pi = 1 - pi
                    yield
                Tn = Ps[pi]; r_Tn = r_Ps[pi]
                bW, rbW = next_bank()
                for jj in range(2):
                    j = j0 + jj
                    k.op("pe", lambda: mm(bW[:, jj * P:(jj + 1) * P], lhsT=ar[:, j, 0, cs:cs + P], rhs=Hbd[:, j, :],
                                          start=True, stop=False), R=[r_ar[j], r_H[g]], W=[rbW], inc=False)
                    for hp in range(2):
                        u = hp * 2 + jj
                        hc = (2 * j + hp) * 64
                        k.op("pe", lambda: mm(bW[:, jj * P + hp * 64:jj * P + hp * 64 + 64], lhsT=Nk[:, u, 0:P],
                                              rhs=vtok[:, hc:hc + 64], start=False, stop=(hp == 1)),
                             R=[r_Nk, r_tok[2]], W=[rbW], inc=(hp == 1))
                k.op("act", lambda: nc.scalar.copy(out=Wt[:, :, :], in_=bW[:, 0:2 * P].rearrange("p (u c) -> p u c", u=2)),
                     R=[rbW], W=[r_Wt])
                yield
                bU, rbU = next_bank()
                for u, (j, hp) in enumerate(units):
                    jj = u % 2
                    k.op("pe", lambda: mm(bU[:, jj * P + hp * 64:jj * P + hp * 64 + 64], lhsT=Tn[:, u, :],
                                          rhs=Wt[:, jj, hp * 64:hp * 64 + 64], start=True, stop=True),
                         R=[r_Tn, r_Wt], W=[rbU])
                k.op("act", lambda: nc.scalar.copy(out=Ut[:, :, :], in_=bU[:, 0:2 * P].rearrange("p (u c) -> p u c", u=2)),
                     R=[rbU], W=[r_Ut])
                yield
                assert post_done[0], "post of the previous chunk must be fully emitted before y^T is overwritten"
                yb_, ryb_ = next_bank()
                for jj in range(2):
                    j = j0 + jj
                    k.op("pe", lambda: mm(yb_[:, jj * P:(jj + 1) * P], lhsT=Hbd[:, j, :], rhs=ar[:, j, 1, cs:cs + P],
                                          start=True, stop=False), R=[r_H[g], r_ar[j]], W=[ryb_], inc=False)
                    for hp in range(2):
                        u = hp * 2 + jj
                        hc = (2 * j + hp) * 64
                        yo = yb_[hp * 64:hp * 64 + 64, jj * P:(jj + 1) * P]
                        k.op("pe", lambda: mm(yo, lhsT=Ut[:, jj, hp * 64:hp * 64 + 64], rhs=Nb[:, u, P:2 * P], start=False, stop=False),
                             R=[r_Ut, r_Nb], W=[ryb_], inc=False)
                        k.op("pe", lambda: mm(yo, lhsT=vtok[:, hc:hc + 64], rhs=Nk[:, u, P:2 * P], start=False, stop=True),
                             R=[r_tok[2], r_Nk], W=[ryb_])
                k.op("act", lambda: nc.scalar.copy(out=ysb[:, j0:j0 + 2, :], in_=yb_[:, 0:2 * P].rearrange("p (a b) -> p a b", a=2)),
                     R=[ryb_], W=[r_ysb[g]])
                bH, rbH = next_bank()
                for u, (j, hp) in enumerate(units):
                    jj = u % 2
                    hc = (2 * j + hp) * 64
                    ho = bH[hp * 64:hp * 64 + 64, jj * 64:jj * 64 + 64]
                    k.op("pe", lambda: mm(ho, lhsT=btok[:, hc:hc + 64], rhs=Ut[:, jj, hp * 64:hp * 64 + 64], start=True, stop=False),
                         R=[r_tok[0], r_Ut], W=[rbH], inc=False)
                    k.op("pe", lambda: mm(ho, lhsT=ktok[:, hc:hc + 64], rhs=vtok[:, hc:hc + 64], start=False, stop=True),
                         R=[r_tok[1], r_tok[2]], W=[rbH])
                k.op("dve", lambda: nc.vector.tensor_tensor(out=tH[:, :, :], in0=bH[:, 0:128].rearrange("p (a b) -> p a b", a=2),
                                                            in1=H32[:, j0:j0 + 2, :], op=ALU.add), R=[rbH, r_H[g]], W=[r_tH])
                gcb = gC[:, j0:j0 + 2, c:c + 1].broadcast_to([P, 2, 64])
                k.op("dve", lambda: nc.vector.tensor_tensor(out=H32[:, j0:j0 + 2, :], in0=tH[:, :, :], in1=gcb, op=ALU.mult),
                     R=[r_tH, r_gC], W=[r_H[g]])
                k.op("act", lambda: nc.scalar.copy(out=Hbd[0:64, j0:j0 + 2, 0:64], in_=H32[0:64, j0:j0 + 2, :]), R=[r_H[g]], W=[r_H[g]])
                k.op("act", lambda: nc.scalar.copy(out=Hbd[64:128, j0:j0 + 2, 64:128], in_=H32[64:128, j0:j0 + 2, :]), R=[r_H[g]], W=[r_H[g]])
                yield

            for c in range(n // P):
                cs = c * P
                for si, (src_t, r_src, dst_t) in enumerate([(bt, r_bt, btok), (kt, r_kt, ktok), (vt, r_vt, vtok)]):
                    bk, rb = next_bank()
                    bkb = bk[:, :].bitcast(BF16)
                    for d in range(KC):
                        k.op("pe", lambda: nc.tensor.transpose(bkb[:, d * P:(d + 1) * P], src_t[:, d, cs:cs + P], identb[:, :]),
                             R=[r_src[d], r_identb], W=[rb])
                    evac_copy(dst_t[:, :], bkb[:, 0:D], R=[rb], W=[r_tok[si]])
                import os as _os3
                if _os3.environ.get("RW_NOIL"):
                    for g_ in range(4):
                        run_interleaved([group_gen(c, g_, sets[g_ % 2])])
                else:
                    if _os3.environ.get("RW_IL2"):
                        run_interleaved([group_gen(c, 0, sets[0]), group_gen(c, 2, sets[1])])
                        run_interleaved([group_gen(c, 1, sets[2]), group_gen(c, 3, sets[3])])
                    else:
                        gens = [group_gen(c, g_, sets[g_]) for g_ in range(4)]
                        if c > 0:
                            pcs = (c - 1) * P
                            post_done[0] = False
                            gens = [post_gen(ptl, lambda d: (ysb[:, d, :], r_ysb[d // 2]), pcs, P, post_done)] + gens
                        run_interleaved(gens)
            post(ptl, lambda d: (ysb[:, d, :], r_ysb[d // 2]), (n // P - 1) * P, P)
            nonlocal_nb[0] = 6
            pb.end()

        def out_proj(t0, n, ti):
            wv_ = rw["rw_wo"].rearrange("(k p) c -> p k c", p=P)
            for half in range(2):
                wb, r_w = next_wbuf()
                v = wb[:, 0:KC * 512].rearrange("p (k c) -> p k c", k=KC)
                k.dma("pool", v, wv_[:, :, half * 512:(half + 1) * 512], W=[r_w])
                for oc in range(4):
                    d = half * 4 + oc
                    bk, rb = next_bank()
                    for kk in range(KC):
                        k.op("pe", lambda: mm(bk[:, 0:n], lhsT=v[:, kk, oc * P:(oc + 1) * P], rhs=og[:, kk, 0:n],
                                              start=(kk == 0), stop=(kk == KC - 1)),
                             R=[r_w, r_og], W=[rb], inc=(kk == KC - 1))
                    k.op("dve", lambda: nc.vector.tensor_tensor(out=x[:, d, t0:t0 + n], in0=bk[:, 0:n], in1=x[:, d, t0:t0 + n], op=ALU.add),
                         R=[rb, r_x[d][ti]], W=[r_x[d][ti]])

        import os as _os
        _bis = _os.environ.get("RWB", "").split(",")
        nseg = TP // SEG
        for s in range(nseg):
            t0 = s * SEG
            ti = t0 // 512
            if "noA" not in _bis:
                stageA(t0, SEG, ti, False, s == nseg - 1)
            if "noB" not in _bis:
                stageB(SEG)
                out_proj(t0, SEG, ti)

        def stageB_sample():
            pb = Phase(base=ov_base, carry_ref=ovc)
            vec = {}
            for nm in ("A", "R", "B", "K", "V", "D"):
                vec[nm] = (pb.alloc([P, P]), pb.reg("sv" + nm))
            sa = pb.alloc([P, 64]); r_sa = pb.reg("sa")
            yv = pb.alloc([P, P]); r_yv = pb.reg("yv")
            Ss = pb.alloc([P, 64, 64]); r_Ss = pb.reg("Ss")
            tmp = pb.alloc([P, 64, 64]); r_tmp = pb.reg("tmp")
            ptl = post_tiles(pb)
            srcs = {"A": (lambda: ar[:, :, 0, 0:NS], r_ar, BF16), "R": (lambda: ar[:, :, 1, 0:NS], r_ar, BF16),
                    "B": (lambda: bt[:, :, 0:NS], r_bt, BF16), "K": (lambda: kt[:, :, 0:NS], r_kt, BF16),
                    "V": (lambda: vt[:, :, 0:NS], r_vt, BF16), "D": (lambda: dS[:, :, 0:NS], [r_dS], F32)}
            cb16 = [pb.alloc([P, P], BF16) for _ in range(2)]; r_cb16 = [pb.reg("cb16a"), pb.reg("cb16b")]
            cb32 = pb.alloc([P, P]); r_cb32 = pb.reg("cb32")
            for si, (nm, (fn, rr, dt_)) in enumerate(srcs.items()):
                bk, rb = next_bank()
                if dt_ == BF16:
                    cb, r_cb = cb16[si % 2], r_cb16[si % 2]
                    evac_copy(cb[:, :].rearrange("p (a b) -> p a b", a=KC), fn(), R=list(rr), W=[r_cb])
                    o_ = bk[:, :].bitcast(BF16)[:, 0:P]
                    k.op("pe", lambda: nc.tensor.transpose(o_, cb[:, :], identb[:, :]), R=[r_cb, r_identb], W=[rb])
                else:
                    evac_copy(cb32[:, :].rearrange("p (a b) -> p a b", a=KC), fn(), R=list(rr), W=[r_cb32])
                    o_ = bk[:, 0:P]
                    k.op("pe", lambda: nc.tensor.transpose(o_, cb32[:, :], ident[:, :]), R=[r_cb32, r_ident], W=[rb])
                evac_copy(vec[nm][0][:, :], o_, R=[rb], W=[vec[nm][1]])

            def kb(nm, hp):
                return vec[nm][0][:, hp * 64:(hp + 1) * 64].unsqueeze(1).broadcast_to([P, 64, 64])

            def vb(ap2):
                return ap2.unsqueeze(2).broadcast_to([P, 64, 64])
            TT = nc.vector.tensor_tensor
            for hp in range(2):
                for j in range(KC):
                    k.dma("sp", Ss[j * NS:(j + 1) * NS, :, :], st_wkv_d[:, 2 * j + hp, :, :], W=[r_Ss])
                k.op("dve", lambda: TT(out=tmp, in0=Ss, in1=kb("A", hp), op=ALU.mult), R=[r_Ss, vec["A"][1]], W=[r_tmp])
                k.op("dve", lambda: nc.vector.tensor_reduce(out=sa[:, :], in_=tmp, axis=AX.X, op=ALU.add), R=[r_tmp], W=[r_sa])
                k.op("dve", lambda: TT(out=tmp, in0=vb(sa[:, :]), in1=kb("B", hp), op=ALU.mult), R=[r_sa, vec["B"][1]], W=[r_tmp])
                k.op("dve", lambda: TT(out=Ss, in0=Ss, in1=tmp, op=ALU.add), R=[r_tmp, r_Ss], W=[r_Ss])
                k.op("dve", lambda: TT(out=tmp, in0=vb(vec["V"][0][:, hp * 64:(hp + 1) * 64]), in1=kb("K", hp), op=ALU.mult),
                     R=[vec["V"][1], vec["K"][1]], W=[r_tmp])
                k.op("dve", lambda: TT(out=Ss, in0=Ss, in1=tmp, op=ALU.add), R=[r_tmp, r_Ss], W=[r_Ss])
                k.op("dve", lambda: TT(out=tmp, in0=Ss, in1=kb("R", hp), op=ALU.mult), R=[r_Ss, vec["R"][1]], W=[r_tmp])
                k.op("dve", lambda: nc.vector.tensor_reduce(out=yv[:, hp * 64:(hp + 1) * 64], in_=tmp, axis=AX.X, op=ALU.add),
                     R=[r_tmp], W=[r_yv])
                k.op("dve", lambda: TT(out=Ss, in0=Ss, in1=kb("D", hp), op=ALU.mult), R=[r_Ss, vec["D"][1]], W=[r_Ss])
                for j in range(KC):
                    r_o = Reg("o_swkv"); out_regs.append(r_o)
                    k.dma("sp", swkv_d[:, 2 * j + hp, :, :], Ss[j * NS:(j + 1) * NS, :, :], R=[r_Ss], W=[r_o])
            k.op("pe", lambda: nc.tensor.transpose(ybk[0][:, 0:P], yv[:, :], ident[:, :]), R=[r_yv, r_ident], W=[r_ybk[0]])
            post(ptl, lambda d: (ybk[0][:, d * NS:(d + 1) * NS], r_ybk[0]), 0, NS)
            pb.end()

        if do_samples:
            stageA(TP, NS, 4, True, False)
            stageB_sample()
            out_proj(TP, NS, 4)

        pe_ = Phase(base=ov_base, carry_ref=ovc)
        shT = pe_.alloc([1 + NS, D]); r_shT = pe_.reg("shT")
        bk, rb = next_bank()
        bk2, rb2 = next_bank()
        for d in range(KC):
            bb = bk if d < 4 else bk2
            k.op("pe", lambda: nc.tensor.transpose(bb[0:1 + NS, (d % 4) * P:(d % 4) * P + P], shf[:, d, :], ident[:, :]),
                 R=[r_shf, r_ident], W=[rb if d < 4 else rb2])
        evac_copy(shT[:, 0:512], bk[0:1 + NS, :], R=[rb], W=[r_shT])
        evac_copy(shT[:, 512:1024], bk2[0:1 + NS, :], R=[rb2], W=[r_shT])
        r_o = Reg("o_pshift"); out_regs.append(r_o)
        k.dma("sp", pshift_d.rearrange("(a d) -> a d", a=1), shT[0:1, :], R=[r_shT], W=[r_o])
        if do_samples:
            r_o = Reg("o_sshift"); out_regs.append(r_o)
            k.dma("sp", sshift_d, shT[1:1 + NS, :], R=[r_shT], W=[r_o])
        ST = pe_.alloc([64, KC, P]); r_ST = pe_.reg("ST")
        for half in range(2):
            bk, rb = next_bank()
            for jj in range(4):
                j = half * 4 + jj
                k.op("pe", lambda: nc.tensor.transpose(bk[0:64, jj * P:(jj + 1) * P], H32[:, j, :], ident[:, :]),
                     R=[r_H[j // 2], r_ident], W=[rb])
            evac_copy(ST[:, half * 4:half * 4 + 4, :], bk[0:64, :].rearrange("p (a b) -> p a b", a=4), R=[rb], W=[r_ST])
        r_o = Reg("o_pwkv"); out_regs.append(r_o)
        k.dma("sp", pwkv_d.rearrange("(j hp) v kk -> v j hp kk", hp=2), ST[:, :, :].rearrange("v j (hp kk) -> v j hp kk", hp=2),
              R=[r_ST], W=[r_o])
        pe_.end()
        dmy = Reg("ovdummy"); dmy.readers = list(ovc[0]); ph.regs.append(dmy)
        nonlocal_nb[0] = 8
        ph.end()


    RET_G = [1.0 - 2.0 ** (-5 - h_) for h_ in range(4)]

    def ret_mixer(do_samples=True):
        ph = Phase()
        SEG = 256
        mm = nc.tensor.matmul
        dmatT = ph.alloc([P, 4, P], BF16); qdt = ph.alloc([P, 4, P], BF16); kdec = ph.alloc([P, 4])
        eps_r = ph.alloc([P, 1])
        r_cst = ph.reg("retconst")
        k.dma("pool", dmatT, c_dmatT_d, W=[r_cst])
        k.dma("pool", qdt, c_qd_d, W=[r_cst])
        k.dma("sp", kdec, c_kdec_d, W=[r_cst])
        k.op("dve", lambda: nc.vector.memset(eps_r, 1e-6), W=[r_cst])
        nbuf = norm_bufs(ph)
        hm = ph.alloc([P, KC, SEG], BF16); r_hm = [ph.reg("hm%d" % d) for d in range(KC)]
        gT = ph.alloc([P, 16, SEG], BF16); r_gT = [ph.reg("gT%d" % e) for e in range(16)]
        cs_fm = ph.alloc([P, 2, SEG]); r_csfm = ph.reg("csfm")
        rq = [ph.alloc([P, SEG]) for _ in range(4)]; r_rq = [ph.reg("rq%d" % i) for i in range(4)]
        rt_ = [ph.alloc([P, SEG]) for _ in range(4)]; r_rt = [ph.reg("rt%d" % i) for i in range(4)]
        on_ = [ph.alloc([P, 512], BF16) for _ in range(2)]; r_on = [ph.reg("on0"), ph.reg("on1")]
        stt = [ph.alloc([P, 8]) for _ in range(2)]; r_stt = [ph.reg("stt0"), ph.reg("stt1")]
        ov_base = ph.off
        ov_idx = len(ph.regs)
        S32 = ph.alloc([P, 4, 2, 512]); Sb = ph.alloc([P, 4, 2, 512], BF16)
        r_S = [ph.reg("S%d" % h_) for h_ in range(4)]
        k.op("dve", lambda: nc.vector.memset(S32, 0.0), W=r_S)
        k.op("dve", lambda: nc.vector.memset(Sb, 0.0), W=r_S)
        qT = ph.alloc([P, KC, SEG], BF16); r_qT = [ph.reg("qT%d" % d) for d in range(KC)]
        kT = ph.alloc([P, KC, SEG], BF16); r_kT = [ph.reg("kT%d" % d) for d in range(KC)]
        qdT = ph.alloc([P, KC, SEG], BF16); r_qdT = [ph.reg("qdT%d" % d) for d in range(KC)]
        vtok = [ph.alloc([P, 2 * D], BF16) for _ in range(2)]; r_vtok = [ph.reg("vtok0"), ph.reg("vtok1")]
        kdtok = [ph.alloc([P, D], BF16) for _ in range(2)]; r_kdtok = [ph.reg("kdtok0"), ph.reg("kdtok1")]
        cs_tok = [ph.alloc([P, 2, P]) for _ in range(2)]; r_cstok = [ph.reg("cstok0"), ph.reg("cstok1")]
        ktmp = ph.alloc([P, 512]); r_ktmp = ph.reg("ktmp")
        kt1 = ph.alloc([P, 512]); kt2 = ph.alloc([P, 512]); r_kt1 = ph.reg("kt1"); r_kt2 = ph.reg("kt2")
        kro = ph.alloc([P, 512]); r_kro = ph.reg("kro")
        innT = [ph.alloc([P, P], BF16) for _ in range(2)]; r_innT = [ph.reg("innT0"), ph.reg("innT1")]

        def load_w(view_fn):
            wb, r_w = next_wbuf()
            v_, src_ = view_fn(wb)
            k.dma("pool", v_, src_, W=[r_w])
            return v_, r_w

        def rotary_fm(bank_pair, dst, r_dst, h_, n, scale):
            (b1, rb1), (b2, rb2) = bank_pair
            i1, i2 = (2 * h_) % 4, (2 * h_ + 1) % 4
            x1, r1 = rq[i1], r_rq[i1]
            x2, r2 = rq[i2], r_rq[i2]
            k.op("act", lambda: nc.scalar.activation(out=x1[:, 0:n], in_=b1[:, 0:n], func=AF.Copy, scale=scale), R=[rb1], W=[r1])
            k.op("act", lambda: nc.scalar.activation(out=x2[:, 0:n], in_=b2[:, 0:n], func=AF.Copy, scale=scale), R=[rb2], W=[r2])
            ta, rta = rt_[i1], r_rt[i1]
            tb, rtb = rt_[i2], r_rt[i2]
            cosv = cs_fm[:, 0, 0:n]; sinv = cs_fm[:, 1, 0:n]
            k.op("dve", lambda: nc.vector.tensor_tensor(out=ta[:, 0:n], in0=x1[:, 0:n], in1=cosv, op=ALU.mult), R=[r1, r_csfm], W=[rta])
            k.op("dve", lambda: nc.vector.tensor_tensor(out=tb[:, 0:n], in0=x2[:, 0:n], in1=sinv, op=ALU.mult), R=[r2, r_csfm], W=[rtb])
            k.op("dve", lambda: nc.vector.tensor_tensor(out=dst[:, 2 * h_, 0:n], in0=ta[:, 0:n], in1=tb[:, 0:n], op=ALU.subtract),
                 R=[rta, rtb], W=[r_dst[2 * h_]])
            k.op("dve", lambda: nc.vector.tensor_tensor(out=ta[:, 0:n], in0=x1[:, 0:n], in1=sinv, op=ALU.mult), R=[r1, r_csfm], W=[rta])
            k.op("dve", lambda: nc.vector.tensor_tensor(out=tb[:, 0:n], in0=x2[:, 0:n], in1=cosv, op=ALU.mult), R=[r2, r_csfm], W=[rtb])
            k.op("dve", lambda: nc.vector.tensor_tensor(out=dst[:, 2 * h_ + 1, 0:n], in0=ta[:, 0:n], in1=tb[:, 0:n], op=ALU.add),
                 R=[rta, rtb], W=[r_dst[2 * h_ + 1]])

        def seg_front(t0, n, ti, sample, dsts=None):
            rmsnorm_tile(nbuf, 4, ti, lambda d, a_, b_: hm[:, d, 0:n], lambda d: [r_hm[d]], rng=(t0, n))
            k.dma("sp", cs_fm[:, 0, 0:n], c_cos_d[:, t0:t0 + n], W=[r_csfm])
            k.dma("sp", cs_fm[:, 1, 0:n], c_sin_d[:, t0:t0 + n], W=[r_csfm])
            dq = (qT, r_qT) if dsts is None else dsts[0]
            dk = (kT, r_kT) if dsts is None else dsts[1]
            for (wname, dst, r_dst, scale) in (("ret_wq", dq[0], dq[1], 1.0), ("ret_wk", dk[0], dk[1], 1.0 / 16.0)):
                wv_ = rt[wname].rearrange("(k p) c -> p k c", p=P)
                for half in range(2):
                    v_, r_w = load_w(lambda wb: (wb[:, 0:KC * 512].rearrange("p (k c) -> p k c", k=KC),
                                                 wv_[:, :, half * 512:(half + 1) * 512]))
                    if wname == "ret_wk" and not sample:
                        ktok_half(v_, r_w, half, n)
                    for hh in range(2):
                        h_ = half * 2 + hh
                        pair = []
                        for i in range(2):
                            oc = hh * 2 + i
                            bk, rb = next_bank()
                            for kk in range(KC):
                                k.op("pe", lambda: mm(bk[:, 0:n], lhsT=v_[:, kk, oc * P:(oc + 1) * P], rhs=hm[:, kk, 0:n],
                                                      start=(kk == 0), stop=(kk == KC - 1)),
                                     R=[r_w, r_hm[kk]], W=[rb], inc=(kk == KC - 1))
                            pair.append((bk, rb))
                        rotary_fm(pair, dst, r_dst, h_, n, scale)
            if not sample:
                for d in range(KC):
                    h_ = d // 2
                    k.op("dve", lambda: nc.vector.tensor_tensor(
                        out=qdT[:, d, 0:n].rearrange("p (c t) -> p c t", t=P), in0=qT[:, d, 0:n].rearrange("p (c t) -> p c t", t=P),
                        in1=qdt[:, h_, :].unsqueeze(1).broadcast_to([P, n // P, P]), op=ALU.mult),
                        R=[r_qT[d], r_cst], W=[r_qdT[d]])
            wg_ = rt["ret_wg"].rearrange("(k p) c -> p k c", p=P)
            for qv in range(4):
                v_, r_w = load_w(lambda wb: (wb[:, 0:KC * 512].rearrange("p (k c) -> p k c", k=KC),
                                             wg_[:, :, qv * 512:(qv + 1) * 512]))
                for oc in range(4):
                    e = qv * 4 + oc
                    bk, rb = next_bank()
                    for kk in range(KC):
                        k.op("pe", lambda: mm(bk[:, 0:n], lhsT=v_[:, kk, oc * P:(oc + 1) * P], rhs=hm[:, kk, 0:n],
                                              start=(kk == 0), stop=(kk == KC - 1)),
                             R=[r_w, r_hm[kk]], W=[rb], inc=(kk == KC - 1))
                    k.op("act", lambda: nc.scalar.activation(out=gT[:, e, 0:n], in_=bk[:, 0:n], func=AF.Silu), R=[rb], W=[r_gT[e]])

        cur_t0 = [0]

        def ktok_half(v_, r_w, half, n):
            for c in range(n // P):
                if half == 0:
                    tt0 = cur_t0[0] + c * P
                    k.dma("sp", cs_tok[c][:, 0, :], c_cosT_d[tt0:tt0 + P, :], W=[r_cstok[c]])
                    k.dma("sp", cs_tok[c][:, 1, :], c_sinT_d[tt0:tt0 + P, :], W=[r_cstok[c]])
                bk, rb = next_bank()
                for kk in range(KC):
                    k.op("pe", lambda: mm(bk[:, :], lhsT=hm[:, kk, c * P:(c + 1) * P], rhs=v_[:, kk, :],
                                          start=(kk == 0), stop=(kk == KC - 1)),
                         R=[r_w, r_hm[kk]], W=[rb], inc=(kk == KC - 1))
                k.op("act", lambda: nc.scalar.activation(out=ktmp[:, :], in_=bk[:, :], func=AF.Copy, scale=1.0 / 16.0), R=[rb], W=[r_ktmp])
                cosb = cs_tok[c][:, 0, :].unsqueeze(1).broadcast_to([P, 4, P])
                sinb = cs_tok[c][:, 1, :].unsqueeze(1).broadcast_to([P, 4, P])
                k3 = ktmp[:, :].rearrange("p (a f) -> p a f", a=4)
                k.op("dve", lambda: nc.vector.tensor_tensor(out=kt1[:, :].rearrange("p (a f) -> p a f", a=4), in0=k3, in1=cosb, op=ALU.mult),
                     R=[r_ktmp, r_cstok[c]], W=[r_kt1])
                k.op("dve", lambda: nc.vector.tensor_tensor(out=kt2[:, :].rearrange("p (a f) -> p a f", a=4), in0=k3, in1=sinb, op=ALU.mult),
                     R=[r_ktmp, r_cstok[c]], W=[r_kt2])
                t1v = kt1[:, :].rearrange("p (h two f) -> p h two f", h=2, two=2)
                t2v = kt2[:, :].rearrange("p (h two f) -> p h two f", h=2, two=2)
                kov = kro[:, :].rearrange("p (h two f) -> p h two f", h=2, two=2)
                k.op("dve", lambda: nc.vector.tensor_tensor(out=kov[:, :, 0, :], in0=t1v[:, :, 0, :], in1=t2v[:, :, 1, :], op=ALU.subtract),
                     R=[r_kt1, r_kt2], W=[r_kro])
                k.op("dve", lambda: nc.vector.tensor_tensor(out=kov[:, :, 1, :], in0=t2v[:, :, 0, :], in1=t1v[:, :, 1, :], op=ALU.add),
                     R=[r_kt1, r_kt2], W=[r_kro])
                for hh in range(2):
                    h_ = half * 2 + hh
                    k.op("dve", lambda: nc.vector.tensor_scalar(
                        out=kdtok[c][:, h_ * 256:(h_ + 1) * 256], in0=kro[:, hh * 256:(hh + 1) * 256], scalar1=kdec[:, h_:h_ + 1],
                        scalar2=None, op0=ALU.mult), R=[r_kro, r_cst], W=[r_kdtok[c]])

        def vtok_all(n):
            wv_ = rt["ret_wv"].rearrange("(k p) c -> p k c", p=P)
            for qv in range(4):
                v_, r_w = load_w(lambda wb: (wb[:, 0:KC * 512].rearrange("p (k c) -> p k c", k=KC),
                                             wv_[:, :, qv * 512:(qv + 1) * 512]))
                for c in range(n // P):
                    bk, rb = next_bank()
                    for kk in range(KC):
                        k.op("pe", lambda: mm(bk[:, :], lhsT=hm[:, kk, c * P:(c + 1) * P], rhs=v_[:, kk, :],
                                              start=(kk == 0), stop=(kk == KC - 1)),
                             R=[r_w, r_hm[kk]], W=[rb], inc=(kk == KC - 1))
                    evac_copy(vtok[c][:, qv * 512:(qv + 1) * 512], bk[:, :], R=[rb], W=[r_vtok[c]])

        def norm_gate(o_bank, rb_o, rows, h_, cs, w, ui):
            q = ui % 2
            st_, r_st = stt[q], r_stt[q]
            k.op("dve", lambda: nc.vector.bn_stats(out=st_[0:rows, 0:6], in_=o_bank[0:rows, :]), R=[rb_o], W=[r_st])
            k.op("dve", lambda: nc.vector.bn_aggr(out=st_[0:rows, 6:8], in_=st_[0:rows, 0:6]), R=[r_st], W=[r_st])
            k.op("act", lambda: nc.scalar.activation(out=st_[0:rows, 7:8], in_=st_[0:rows, 7:8], func=AF.Sqrt, bias=eps_r[0:rows, 0:1]),
                 R=[r_st, r_cst], W=[r_st])
            k.op("dve", lambda: nc.vector.reciprocal(out=st_[0:rows, 7:8], in_=st_[0:rows, 7:8]), R=[r_st], W=[r_st])
            on_t, r_on_t = on_[q], r_on[q]
            k.op("dve", lambda: nc.vector.tensor_scalar(out=on_t[0:rows, :], in0=o_bank[0:rows, :], scalar1=st_[0:rows, 6:7],
                                                        scalar2=st_[0:rows, 7:8], op0=ALU.subtract, op1=ALU.mult),
                 R=[rb_o, r_st], W=[r_on_t])
            bk, rb = next_bank()
            bkb = bk[:, :].bitcast(BF16)
            for i in range(4):
                k.op("pe", lambda: nc.tensor.transpose(bkb[:, i * P:i * P + rows], on_t[0:rows, i * P:(i + 1) * P], identb[0:rows, 0:rows]),
                     R=[r_on_t, r_identb], W=[rb])
            gv = gT[:, 4 * h_:4 * h_ + 4, cs:cs + w]
            k.op("dve", lambda: nc.vector.tensor_tensor(out=gv, in0=bkb[:, 0:4 * P].rearrange("p (a t) -> p a t", a=4)[:, :, 0:w], in1=gv, op=ALU.mult),
                 R=[rb] + [r_gT[4 * h_ + i] for i in range(4)], W=[r_gT[4 * h_ + i] for i in range(4)])

        def seg_chunks(n):
            ui = 0
            for c in range(n // P):
                cs = c * P
                for h_ in range(4):
                    q = ui % 2
                    bs, rbs = next_bank()
                    for i in range(2):
                        d = 2 * h_ + i
                        k.op("pe", lambda: mm(bs[:, 0:P], lhsT=kT[:, d, cs:cs + P], rhs=qT[:, d, cs:cs + P], start=(i == 0), stop=(i == 1)),
                             R=[r_kT[d], r_qT[d]], W=[rbs], inc=(i == 1))
                    k.op("dve", lambda: nc.vector.tensor_tensor(out=innT[q][:, :], in0=bs[:, 0:P], in1=dmatT[:, h_, :], op=ALU.mult),
                         R=[rbs, r_cst], W=[r_innT[q]])
                    bo, rbo = next_bank()
                    k.op("pe", lambda: mm(bo[:, :], lhsT=innT[q][:, :], rhs=vtok[c][:, h_ * 512:(h_ + 1) * 512], start=True, stop=False),
                         R=[r_innT[q], r_vtok[c]], W=[rbo], inc=False)
                    for i in range(2):
                        d = 2 * h_ + i
                        k.op("pe", lambda: mm(bo[:, :], lhsT=qdT[:, d, cs:cs + P], rhs=Sb[:, h_, i, :], start=False, stop=(i == 1)),
                             R=[r_qdT[d], r_S[h_]], W=[rbo], inc=(i == 1))
                    norm_gate(bo, rbo, P, h_, cs, P, ui)
                    sdec = float(np.float32(RET_G[h_]) ** 128)
                    for i in range(2):
                        bS, rbS = next_bank()
                        k.op("pe", lambda: mm(bS[:, :], lhsT=kdtok[c][:, h_ * 256 + i * P:h_ * 256 + (i + 1) * P],
                                              rhs=vtok[c][:, h_ * 512:(h_ + 1) * 512], start=True, stop=True),
                             R=[r_kdtok[c], r_vtok[c]], W=[rbS])
                        k.op("dve", lambda: nc.vector.scalar_tensor_tensor(
                            out=S32[:, h_, i, :], in0=S32[:, h_, i, :], scalar=sdec, in1=bS[:, :], op0=ALU.mult, op1=ALU.add),
                            R=[rbS, r_S[h_]], W=[r_S[h_]])
                        k.op("act", lambda: nc.scalar.copy(out=Sb[:, h_, i, :], in_=S32[:, h_, i, :]), R=[r_S[h_]], W=[r_S[h_]])
                    ui += 1

        def out_proj(t0, n, ti):
            wo_ = rt["ret_wo"].rearrange("(e p) c -> p e c", p=P)
            for piece in range(4):
                v_, r_w = load_w(lambda wb: (wb[:, 0:16 * 256].rearrange("p (e c) -> p e c", e=16),
                                             wo_[:, :, piece * 256:(piece + 1) * 256]))
                for oc in range(2):
                    d = piece * 2 + oc
                    bk, rb = next_bank()
                    for e in range(16):
                        k.op("pe", lambda: mm(bk[:, 0:n], lhsT=v_[:, e, oc * P:(oc + 1) * P], rhs=gT[:, e, 0:n],
                                              start=(e == 0), stop=(e == 15)),
                             R=[r_w, r_gT[e]], W=[rb], inc=(e == 15))
                    k.op("dve", lambda: nc.vector.tensor_tensor(out=x[:, d, t0:t0 + n], in0=bk[:, 0:n], in1=x[:, d, t0:t0 + n], op=ALU.add),
                         R=[rb, r_x[d][ti]], W=[r_x[d][ti]])

        def ret_samples(r_pret):
            toks = set()
            for r_ in ph.regs[ov_idx:] + [r_pret]:
                if r_.last_w is not None:
                    toks.add(r_.last_w)
                toks.update(r_.readers)
            ovc = [list(toks)]
            ps_ = Phase(base=ov_base, carry_ref=ovc)
            qTs = ps_.alloc([P, KC, NS]); r_qTs = [ps_.reg("qTs%d" % d) for d in range(KC)]
            kTs = ps_.alloc([P, KC, NS]); r_kTs = [ps_.reg("kTs%d" % d) for d in range(KC)]
            qpad = ps_.alloc([P, KC, NS, NS], BF16); r_qpad = ps_.reg("qpad")
            i16 = ps_.alloc([P, NS, NS]); selb = ps_.alloc([NS, NS, P], BF16); r_sc = ps_.reg("sconst")
            k.dma("sp", i16, c_i16_d.rearrange("p (a b) -> p a b", a=NS), W=[r_sc])
            k.dma("pool", selb, c_sel_d, W=[r_sc])
            vs = ps_.alloc([NS, 2 * D], BF16); r_vs = ps_.reg("vs")
            Sring = [ps_.alloc([P, 2, 512]) for _ in range(4)]; r_Sring = [ps_.reg("Sring%d" % i) for i in range(4)]
            Sb16 = [ps_.alloc([P, 2, 512], BF16) for _ in range(2)]; r_Sb16 = [ps_.reg("Sb16_0"), ps_.reg("Sb16_1")]
            kvt = [ps_.alloc([P, 512]) for _ in range(4)]; r_kvt = [ps_.reg("kvt%d" % i) for i in range(4)]
            n = NS
            seg_front(TP, n, 4, True, dsts=((qTs, r_qTs), (kTs, r_kTs)))
            wv_ = rt["ret_wv"].rearrange("(k p) c -> p k c", p=P)
            for qv in range(4):
                v_, r_w = load_w(lambda wb: (wb[:, 0:KC * 512].rearrange("p (k c) -> p k c", k=KC),
                                             wv_[:, :, qv * 512:(qv + 1) * 512]))
                bk, rb = next_bank()
                for kk in range(KC):
                    k.op("pe", lambda: mm(bk[0:n, :], lhsT=hm[:, kk, 0:n], rhs=v_[:, kk, :], start=(kk == 0), stop=(kk == KC - 1)),
                         R=[r_w, r_hm[kk]], W=[rb], inc=(kk == KC - 1))
                evac_copy(vs[0:n, qv * 512:(qv + 1) * 512], bk[0:n, :], R=[rb], W=[r_vs])
            k.op("dve", lambda: nc.vector.tensor_tensor(
                out=qpad, in0=qTs.unsqueeze(3).broadcast_to([P, KC, NS, NS]), in1=i16.unsqueeze(1).broadcast_to([P, KC, NS, NS]),
                op=ALU.mult), R=r_qTs + [r_sc], W=[r_qpad])
            nonlocal_nb[0] = 4
            po = [banks[4 + h_] for h_ in range(4)]; r_po = [r_bank[4 + h_] for h_ in range(4)]
            DEPTH = 4
            units = [(b_, h_) for b_ in range(NS) for h_ in range(4)]
            NU = len(units)

            def load(u):
                b_, h_ = units[u]
                k.dma("sp", Sring[u % DEPTH], st_ret_d[b_, h_].rearrange("(i p) v -> p i v", p=P), W=[r_Sring[u % DEPTH]])

            def vbc(u):
                b_, h_ = units[u]
                bk, rb = next_bank()
                k.op("pe", lambda: mm(bk[:, :], lhsT=selb[0:NS, b_, :], rhs=vs[0:NS, h_ * 512:(h_ + 1) * 512], start=True, stop=True),
                     R=[r_sc, r_vs], W=[rb])
                return bk, rb
            for u in range(min(DEPTH - 1, NU)):
                load(u)
            nxt = vbc(0)
            for u in range(NU):
                b_, h_ = units[u]
                St, r_St = Sring[u % DEPTH], r_Sring[u % DEPTH]
                Sh, r_Sh = Sb16[u % 2], r_Sb16[u % 2]
                bk, rb = nxt
                if u + DEPTH - 1 < NU:
                    load(u + DEPTH - 1)
                kv_, r_kv = kvt[u % 4], r_kvt[u % 4]
                d0_, d1_ = 2 * h_, 2 * h_ + 1
                k.op("act", lambda: nc.scalar.activation(out=kv_[:, :], in_=bk[:, :], func=AF.Copy, scale=kTs[:, d1_, b_:b_ + 1]),
                     R=[rb, r_kTs[d1_]], W=[r_kv])
                k.op("act", lambda: nc.scalar.activation(out=bk[:, :], in_=bk[:, :], func=AF.Copy, scale=kTs[:, d0_, b_:b_ + 1]),
                     R=[rb, r_kTs[d0_]], W=[rb])
                k.op("dve", lambda: nc.vector.scalar_tensor_tensor(
                    out=St[:, 0, :], in0=St[:, 0, :], scalar=float(np.float32(RET_G[h_])), in1=bk[:, :],
                    op0=ALU.mult, op1=ALU.add), R=[r_St, rb], W=[r_St])
                k.op("dve", lambda: nc.vector.scalar_tensor_tensor(
                    out=St[:, 1, :], in0=St[:, 1, :], scalar=float(np.float32(RET_G[h_])), in1=kv_[:, :],
                    op0=ALU.mult, op1=ALU.add), R=[r_St, r_kv], W=[r_St])
                if u + 1 < NU:
                    nxt = vbc(u + 1)
                k.op("act", lambda: nc.scalar.copy(out=Sh[:, :, :], in_=St[:, :, :]), R=[r_St], W=[r_Sh])
                r_o2 = Reg("o_sret"); out_regs.append(r_o2)
                k.dma("pool", sret_d[b_, h_].rearrange("(i p) v -> p i v", p=P), St, R=[r_St], W=[r_o2])
                for i in range(2):
                    d = 2 * h_ + i
                    first = (b_ == 0 and i == 0); last = (b_ == NS - 1 and i == 1)
                    k.op("pe", lambda: mm(po[h_][0:NS, :], lhsT=qpad[:, d, b_, :], rhs=Sh[:, i, :], start=first, stop=last),
                         R=[r_qpad, r_Sh], W=[r_po[h_]], inc=True)
            for h_ in range(4):
                norm_gate(po[h_], r_po[h_], NS, h_, 0, NS, h_)
            nonlocal_nb[0] = 8
            out_proj(TP, n, 4)
            ps_.end()
            dmy = Reg("ovdummy2"); dmy.readers = list(ovc[0]); ph.regs.append(dmy)

        nseg = TP // SEG
        for s in range(nseg):
            t0 = s * SEG
            ti = t0 // 512
            cur_t0[0] = t0
            seg_front(t0, SEG, ti, False)
            vtok_all(SEG)
            seg_chunks(SEG)
            out_proj(t0, SEG, ti)
        r_o = Reg("o_pret"); out_regs.append(r_o)
        k.dma("sp", pret_d.rearrange("h (i p) v -> p h i v", p=P), S32, R=r_S + [r_o], W=[r_o])
        if do_samples:
            ret_samples(r_o)
        ph.end()

    def final_out():
        ph = Phase()
        nb = norm_bufs(ph)
        yfm = [ph.alloc([P, KC, 512]) for _ in range(2)]
        r_yfm = [[ph.reg("yfm%d_%d" % (i, d)) for d in range(KC)] for i in range(2)]
        yout = [ph.alloc([P, D]) for _ in range(2)]
        r_yout = [ph.reg("yout0"), ph.reg("yout1")]
        st = 0
        for ti, (t0, tn) in enumerate(TILES):
            yb = ti % 2
            rmsnorm_tile(nb, 6, ti, lambda d, t0, tn: yfm[yb][:, d, 0:tn], lambda d: [r_yfm[yb][d]])
            nsub = (tn + P - 1) // P
            for s in range(nsub):
                rows = min(P, tn - s * P)
                ob = st % 2; st += 1
                for d0 in range(0, KC, 4):
                    bk2, rb2 = next_bank()
                    for j in range(4):
                        d = d0 + j
                        k.op("pe", lambda: nc.tensor.transpose(
                            bk2[0:rows, j * P:(j + 1) * P], yfm[yb][:, d, s * P:s * P + rows], ident[:, :]),
                            R=[r_yfm[yb][d], r_ident], W=[rb2])
                    evac_copy(yout[ob][0:rows, d0 * P:(d0 + 4) * P], bk2[0:rows, :], R=[rb2], W=[r_yout[ob]])
                dst = yp_d[t0 + s * P:t0 + s * P + rows, :] if ti < 4 else ys_d[:, :]
                r_o = Reg("o"); out_regs.append(r_o)
                k.dma("sp", dst, yout[ob][0:rows, :], R=[r_yout[ob]], W=[r_o])
        ph.end()

    load_x()
    for l in range(2):
        if "ffn1_%d" % l in stages:
            ffn(l, 1)
        if l == 0 and lx_carry[0]:
            best = {}
            for t_ in list(carry[0]) + list(lx_carry[0]):
                kk_ = (t_[0], t_[1])
                if kk_ not in best or best[kk_][2] < t_[2]:
                    best[kk_] = t_
            carry[0] = list(best.values())
            lx_carry[0] = []
        if "mix_%d" % l in stages:
            if l == 0:
                rwkv_mixer(do_samples=("samples" in stages))
            else:
                ret_mixer(do_samples=("samples" in stages))
        if "ffn2_%d" % l in stages:
            ffn(l, 2)
    final_out()

    k.finish(out_regs)
    es.close()
    print("instructions:", k.n_inst, "waits:", k.n_wait)
    return nc


def host_consts():
    c = {"c_ident": np.eye(P, dtype=np.float32)}
    bones = np.zeros((P, P), np.float32); bones[:64, :64] = 1.0; bones[64:, 64:] = 1.0
    c["c_bones"] = bones
    s_ = np.arange(P)[:, None]; t_ = np.arange(P)[None, :]
    c["c_mask_ur"] = np.concatenate([(s_ < t_), (s_ <= t_)], axis=1).astype(np.float32)
    c["c_mask_l"] = (s_ > t_).astype(np.float32)
    sm = np.ones((P, 256), np.float32); sm[:, 0] = 0.0; sm[:, 128] = 0.0
    c["c_scanm"] = sm
    f32 = np.float32
    half = 128
    inv = (f32(10000.0) ** (-(np.arange(half, dtype=f32) / f32(half)))).astype(f32)
    pos = np.concatenate([np.arange(TP, dtype=f32), np.full(NS, 16384.0, dtype=f32)])
    ang = (pos[None, :] * inv[:, None]).astype(f32)
    c["c_cos"] = np.cos(ang.astype(np.float64)).astype(f32)
    c["c_sin"] = np.sin(ang.astype(np.float64)).astype(f32)
    c["c_cosT"] = np.ascontiguousarray(c["c_cos"][:, :TP].T)
    c["c_sinT"] = np.ascontiguousarray(c["c_sin"][:, :TP].T)
    log_g = np.log1p(-np.exp2(-5.0 - np.arange(4, dtype=f32))).astype(f32)
    idx = np.arange(P, dtype=f32)
    diff = idx[None, :] - idx[:, None]
    dm = np.where(diff[:, None, :] >= 0, np.exp(log_g[None, :, None] * np.maximum(diff, 0)[:, None, :]), 0.0)
    c["c_dmatT"] = dm.astype(f32)
    qd = np.exp(log_g[:, None] * (idx[None, :] + 1.0)).astype(f32)
    c["c_qd"] = np.ascontiguousarray(np.broadcast_to(qd[None], (P, 4, P))).astype(f32)
    c["c_kdec"] = np.exp(log_g[None, :] * (127.0 - idx[:, None])).astype(f32)
    sel = np.zeros((NS, NS, P), f32)
    for b_ in range(NS):
        sel[b_, b_, :] = 1.0
    c["c_sel"] = sel
    c["c_i16"] = np.ascontiguousarray(np.tile(np.eye(NS, dtype=f32).reshape(1, NS * NS), (P, 1)))
    return c


W_NAMES = ("ln_ffn1", "ln_mix", "ln_ffn2", "ln_final", "ff1_wg", "ff1_wu", "ff1_wd", "ff2_wg", "ff2_wu", "ff2_wd")
RT_NAMES = ("ret_wq", "ret_wk", "ret_wv", "ret_wg", "ret_wo")
RW_NAMES = ("rw_mu", "rw_w0", "rw_w1", "rw_w2", "rw_a0", "rw_a1", "rw_a2", "rw_g1", "rw_g2", "rw_kk", "rw_ka",
            "rw_rk", "rw_wr", "rw_wk", "rw_wv", "rw_wo", "rw_lnx_g", "rw_lnx_b")


def make_in_maps(inputs, n_cores=N_CORES):
    consts = host_consts()
    shared = {}
    for nm in W_NAMES:
        shared[nm] = np.ascontiguousarray(inputs[nm], dtype=np.float32)
    for nm in RW_NAMES:
        a = np.asarray(inputs[nm], dtype=np.float32)[0]
        if nm == "rw_rk":
            a = a.reshape(D)
        shared[nm] = np.ascontiguousarray(a)
    for nm in RT_NAMES:
        shared[nm] = np.ascontiguousarray(np.asarray(inputs[nm], dtype=np.float32)[0])
    in_maps = []
    for c in range(n_cores):
        m = dict(consts)
        m.update(shared)
        m["x_prompt"] = np.ascontiguousarray(inputs["x_prompt"][c])
        m["x_sample"] = np.ascontiguousarray(inputs["x_sample"][c * NS:(c + 1) * NS, 0, :])
        m["st_shift"] = np.ascontiguousarray(inputs["state_rwkv_shift"][0, c * NS:(c + 1) * NS])
        m["st_wkv"] = np.ascontiguousarray(inputs["state_rwkv_wkv"][0, c * NS:(c + 1) * NS])
        m["st_ret"] = np.ascontiguousarray(inputs["state_ret"][0, c * NS:(c + 1) * NS])
        in_maps.append(m)
    return in_maps


def kernel(**inputs):
    nc = build()
    in_maps = make_in_maps(inputs)
    res = run_bass_kernel_spmd(nc, in_maps, core_ids=list(range(N_CORES)))
    R_ = res.results
    cat = lambda nm: np.concatenate([np.asarray(R_[c][nm]) for c in range(N_CORES)], 0)
    stk = lambda nm: np.stack([np.asarray(R_[c][nm]) for c in range(N_CORES)], 0)
    y_prompt = stk("y_prompt").astype(np.float32)
    y_sample = cat("y_sample")[:, None, :].astype(np.float32)
    p_shift = stk("p_shift")[None].astype(np.float32)
    p_wkv = stk("p_wkv")[None].astype(np.float32)
    p_ret = stk("p_ret")[None].astype(np.float32)
    s_shift = cat("s_shift")[None].astype(np.float32)
    s_wkv = cat("s_wkv")[None].astype(np.float32)
    s_ret = cat("s_ret")[None].astype(np.float32)
    return (y_prompt, y_sample, p_shift, p_wkv, p_ret, s_shift, s_wkv, s_ret)
```
